# Optimizing a Trainium2 kernel written in Bass

```python
import jax, jax.numpy as jnp
from jax import lax
import numpy as np

D_MODEL = 1024
BATCH = 2
SEQ = 16384
DEPTH = 1
DEC_BATCH = 16
DEC_SEQ = 4096
PAST_LEN = 128

D_MIX = D_MODEL
D_RWKV = D_MIX // 2
D_ATTN = D_MIX - D_RWKV
RWKV_HEAD = 64
N_RWKV_HEADS = D_RWKV // RWKV_HEAD
ATTN_HEAD = 64
N_ATTN_HEADS = D_ATTN // ATTN_HEAD
DECAY_LORA = 64
ICLR_LORA = 64
RWKV_COLS = 4 * D_RWKV + DECAY_LORA + ICLR_LORA
ATTN_COLS = 4 * D_ATTN
D_IN = RWKV_COLS + ATTN_COLS
N_DIR = 2
GRID_W = 64
NA_ROWS = 8
NA_COLS = 16
RMS_EPS = 1e-6
LNX_EPS = 64e-5

kernel_name = 'hybrid_rwkv7_natten2d_encoder'


def rms_norm(x, g):
    xf = x.astype(jnp.float32)
    return xf * lax.rsqrt(jnp.mean(xf * xf, -1, keepdims=True) + RMS_EPS) * g.astype(jnp.float32)


def centred_shift(u, mu):
    prev = jnp.pad(u[:, :-1], ((0, 0), (1, 0), (0, 0)))
    nxt = jnp.pad(u[:, 1:], ((0, 0), (0, 1), (0, 0)))
    return u + mu * (0.5 * (prev + nxt) - u)


def to_scan(u):
    u = jnp.stack([u[0], jnp.flip(u[1], 1)])
    return jnp.moveaxis(u, 2, 0)


def rwkv7_step(S, inp):
    r_t, w_t, k_t, v_t, kk_t, kka_t = inp
    s_kk = jnp.einsum('zbhvk,zbhk->zbhv', S, kk_t)
    S = S * w_t[..., None, :] - s_kk[..., :, None] * kka_t[..., None, :] + v_t[..., :, None] * k_t[..., None, :]
    return S, jnp.einsum('zbhvk,zbhk->zbhv', S, r_t)


def rwkv7_bidir(zr, shift_mu, w0, w_up, a0, a_up, k_k, k_a, r_k, lnx_g, lnx_b):
    f32 = jnp.float32
    B, T, _ = zr.shape
    H, N = N_RWKV_HEADS, RWKV_HEAD
    zr = centred_shift(zr.astype(f32), shift_mu.astype(f32))
    r, k, v, gate = (zr[..., i * D_RWKV:(i + 1) * D_RWKV] for i in range(4))
    w_dn = jnp.tanh(zr[..., 4 * D_RWKV:4 * D_RWKV + DECAY_LORA])
    a_dn = zr[..., 4 * D_RWKV + DECAY_LORA:]
    w_log = -jax.nn.softplus(-(w0.astype(f32)[:, None, None, :]
                               + jnp.einsum('btl,zlc->zbtc', w_dn, w_up.astype(f32)))) - 0.5
    decay = jnp.exp(-jnp.exp(w_log))
    a = jax.nn.sigmoid(a0.astype(f32)[:, None, None, :]
                       + jnp.einsum('btl,zlc->zbtc', a_dn, a_up.astype(f32)))
    heads = lambda u: u.reshape(u.shape[:-1] + (H, N))
    kk = heads(k * k_k.astype(f32))
    kk = kk / jnp.maximum(jnp.sqrt(jnp.sum(kk * kk, -1, keepdims=True)), 1e-12)
    k_dir = heads(k[None] * (1.0 + (a - 1.0) * k_a.astype(f32)))
    r_h, v_h = heads(r), heads(v)
    a_h = heads(a)
    xs = (to_scan(jnp.stack([r_h, r_h])), to_scan(heads(decay)), to_scan(k_dir),
          to_scan(jnp.stack([v_h, v_h])), to_scan(jnp.stack([kk, kk])), to_scan(kk[None] * a_h))
    S0 = jnp.zeros((N_DIR, B, H, N, N), f32)
    _, y = lax.scan(rwkv7_step, S0, xs)
    y = jnp.moveaxis(y, 0, 2)
    y = y[0] + jnp.flip(y[1], 1)
    mu = jnp.mean(y, -1, keepdims=True)
    var = jnp.mean(jnp.square(y - mu), -1, keepdims=True)
    yn = ((y - mu) * lax.rsqrt(var + LNX_EPS)).reshape(B, T, D_RWKV)
    yn = yn * lnx_g.astype(f32) + lnx_b.astype(f32)
    bonus = jnp.sum(r_h[None] * k_dir * r_k.astype(f32), -1, keepdims=True).sum(0) * v_h
    return (yn + bonus.reshape(B, T, D_RWKV)) * jax.nn.silu(gate)


def neighbourhood_attention(q, k, v, rpb):
    f32 = jnp.float32
    B, T, H, Dh = q.shape
    rows = T // GRID_W
    kh = min(NA_ROWS, rows)
    grid = lambda u: u.reshape(B, rows, GRID_W, H, Dh)
    qg, kg, vg = grid(q), grid(k), grid(v)
    row = jnp.arange(rows)
    row_idx = jnp.clip(row - kh // 2, 0, rows - kh)[:, None] + jnp.arange(kh)[None, :]
    k_blk = kg[:, row_idx]
    v_blk = vg[:, row_idx]
    col = jnp.arange(GRID_W)
    col_start = jnp.clip(col - NA_COLS // 2, 0, GRID_W - NA_COLS)
    in_win = (col[None, :] >= col_start[:, None]) & (col[None, :] < col_start[:, None] + NA_COLS)
    dc_idx = jnp.clip(col[None, :] - col[:, None] + NA_COLS - 1, 0, 2 * NA_COLS - 2)
    dr_idx = row_idx - row[:, None] + NA_ROWS - 1
    bias = rpb.astype(f32)[:, :, dc_idx][:, dr_idx]
    bias = bias.transpose(1, 0, 3, 2, 4)
    scale = Dh ** -0.5
    s = jnp.einsum('brqhd,brkwhd->brhqkw', qg, k_blk).astype(f32) * scale + bias
    s = jnp.where(in_win[:, None, :], s, -jnp.inf)
    p = jax.nn.softmax(s.reshape(B, rows, H, GRID_W, kh * GRID_W), axis=-1).reshape(s.shape)
    o = jnp.einsum('brhqkw,brkwhd->brqhd', p.astype(v_blk.dtype), v_blk)
    return o.reshape(B, T, H, Dh)


def attn_branch(za, q_norm_g, k_norm_g, rpb):
    f32 = jnp.float32
    B, T, _ = za.shape
    q, k, v, gate = (za[..., i * D_ATTN:(i + 1) * D_ATTN].astype(f32) for i in range(4))
    heads = lambda u: u.reshape(B, T, N_ATTN_HEADS, ATTN_HEAD)
    q = rms_norm(heads(q), q_norm_g)
    k = rms_norm(heads(k), k_norm_g)
    o = neighbourhood_attention(q, k, heads(v), rpb)
    return o.reshape(B, T, D_ATTN) * jax.nn.silu(gate)


def mixer_layer(x, norm_g, w_in, shift_mu, w0, w_up, a0, a_up, k_k, k_a, r_k,
                lnx_g, lnx_b, q_norm_g, k_norm_g, rpb, w_out):
    h = rms_norm(x, norm_g).astype(x.dtype)
    z = jnp.einsum('btd,de->bte', h, w_in)
    y_r = rwkv7_bidir(z[..., :RWKV_COLS], shift_mu, w0, w_up, a0, a_up, k_k, k_a, r_k, lnx_g, lnx_b)
    y_a = attn_branch(z[..., RWKV_COLS:], q_norm_g, k_norm_g, rpb)
    mixed = jnp.concatenate([y_r, y_a], -1).astype(x.dtype)
    return x + jnp.einsum('bte,ed->btd', mixed, w_out)


def run_trunk(x, norm_g, w_in, shift_mu, w0, w_up, a0, a_up, k_k, k_a, r_k,
              lnx_g, lnx_b, q_norm_g, k_norm_g, rpb, w_out):
    for l in range(DEPTH):
        x = mixer_layer(x, norm_g[l], w_in[l], shift_mu[l], w0[l], w_up[l], a0[l], a_up[l],
                        k_k[l], k_a[l], r_k[l], lnx_g[l], lnx_b[l], q_norm_g[l], k_norm_g[l],
                        rpb[l], w_out[l])
    return x


def setup_inputs(seed: int = 0) -> dict:
    key = jax.random.key(seed)
    ks = jax.random.split(key, 20)
    nrm = jax.random.normal
    return {
        'x_prompt': nrm(ks[0], (BATCH, SEQ, D_MODEL), jnp.float32),
        'x_sample': nrm(ks[1], (DEC_BATCH, DEC_SEQ, D_MODEL), jnp.float32),
        'norm_g': 1.0 + 0.02 * nrm(ks[2], (DEPTH, D_MODEL), jnp.float32),
        'w_in': nrm(ks[3], (DEPTH, D_MODEL, D_IN), jnp.float32) * D_MODEL ** -0.5,
        'shift_mu': jax.random.uniform(ks[4], (DEPTH, RWKV_COLS), jnp.float32),
        'w0': jax.random.uniform(ks[5], (DEPTH, N_DIR, D_RWKV), jnp.float32, minval=-6.0, maxval=0.0),
        'w_up': 0.1 * nrm(ks[6], (DEPTH, N_DIR, DECAY_LORA, D_RWKV), jnp.float32),
        'a0': 0.5 * nrm(ks[7], (DEPTH, N_DIR, D_RWKV), jnp.float32),
        'a_up': 0.1 * nrm(ks[8], (DEPTH, N_DIR, ICLR_LORA, D_RWKV), jnp.float32),
        'k_k': 0.85 + 0.05 * nrm(ks[9], (DEPTH, D_RWKV), jnp.float32),
        'k_a': 1.0 + 0.05 * nrm(ks[10], (DEPTH, D_RWKV), jnp.float32),
        'r_k': 0.1 * nrm(ks[11], (DEPTH, N_RWKV_HEADS, RWKV_HEAD), jnp.float32),
        'lnx_g': 1.0 + 0.02 * nrm(ks[12], (DEPTH, D_RWKV), jnp.float32),
        'lnx_b': 0.02 * nrm(ks[13], (DEPTH, D_RWKV), jnp.float32),
        'q_norm_g': 1.0 + 0.02 * nrm(ks[14], (DEPTH, ATTN_HEAD), jnp.float32),
        'k_norm_g': 1.0 + 0.02 * nrm(ks[15], (DEPTH, ATTN_HEAD), jnp.float32),
        'rpb': 0.1 * nrm(ks[16], (DEPTH, N_ATTN_HEADS, 2 * NA_ROWS - 1, 2 * NA_COLS - 1), jnp.float32),
        'w_out': nrm(ks[17], (DEPTH, D_MIX, D_MODEL), jnp.float32) * D_MIX ** -0.5,
    }


def reference(x_prompt, x_sample, norm_g, w_in, shift_mu, w0, w_up, a0, a_up, k_k, k_a, r_k,
              lnx_g, lnx_b, q_norm_g, k_norm_g, rpb, w_out):
    y_prompt = run_trunk(x_prompt, norm_g, w_in, shift_mu, w0, w_up, a0, a_up, k_k, k_a, r_k,
                         lnx_g, lnx_b, q_norm_g, k_norm_g, rpb, w_out)
    y_sample = run_trunk(x_sample, norm_g, w_in, shift_mu, w0, w_up, a0, a_up, k_k, k_a, r_k,
                         lnx_g, lnx_b, q_norm_g, k_norm_g, rpb, w_out)
    return (y_prompt, y_sample)
```

```python
import numpy as np
from contextlib import ExitStack
import concourse.bass as bass
import concourse.mybir as mybir
from concourse.bass_utils import run_bass_kernel_spmd

F32 = mybir.dt.float32
BF16 = mybir.dt.bfloat16
ALU = mybir.AluOpType
AF = mybir.ActivationFunctionType
AX = mybir.AxisListType

D = 1024
DIN = 4224
NBLK = 33
RMS_EPS = 1e-6
LNX_EPS = 64e-5
CDEC = 0.6065306597126334

ENGS = ["pe", "act", "dve", "pool", "sp"]
NDMASEM = 48


class Op:
    __slots__ = ("eng", "fn", "chan", "seq", "signal", "waits", "vc", "is_dma", "sigval", "rows",
                 "deps", "idx", "dur", "start", "finish", "succs", "npred", "ready")

    def __init__(self, eng, fn, is_dma):
        self.eng = eng
        self.fn = fn
        self.is_dma = is_dma
        self.signal = False
        self.waits = []
        self.vc = None
        self.chan = None
        self.seq = None
        self.sigval = None
        self.rows = None
        self.deps = {}
        self.dur = 500.0
        self.start = 0.0
        self.finish = 0.0
        self.succs = []
        self.npred = 0
        self.ready = 0.0


def _region(ap):
    name = ap.tensor.name
    pat = ap.ap
    off = ap.offset
    es = mybir.dt.size(ap.dtype)
    if "DRAM" in str(ap.space).upper():
        ext = 0
        for st, cnt in pat:
            ext += (cnt - 1) * abs(st)
        return name, 0, 1, off * es, (off + ext + 1) * es
    row = pat[0][0]
    npart = pat[0][1]
    if row == 0:
        p0 = 0
        f0 = off
    else:
        p0 = off // row
        f0 = off - p0 * row
    ext = 0
    for st, cnt in pat[1:]:
        ext += (cnt - 1) * abs(st)
    return name, p0, p0 + npart, f0 * es, (f0 + ext + 1) * es


def _free_elems(ap):
    n = 1
    for st, cnt in ap.ap[1:]:
        n *= cnt
    return n


class Sched:
    def __init__(self, nc, same_engine_sync=True, reorder=True):
        self.nc = nc
        self.ops = {e: [] for e in ENGS}
        self.all_ops = []
        self.hist = {}
        self.dma_uses = [0] * NDMASEM
        self.same_engine_sync = same_engine_sync
        self.reorder = reorder
        import os
        self.max_ops = int(os.environ.get("MAXOPS", "100000000"))
        if os.environ.get("NOREORDER"):
            self.reorder = False
        self.nadd = 0
        self.log = []

    @staticmethod
    def _dep(op, A, kind):
        r = op.deps.get(id(A))
        if r is None:
            op.deps[id(A)] = [A, kind]
        elif kind < r[1] or (kind == 1 and r[1] == 2):
            r[1] = kind if not (r[1] == 0) else 0

    def add(self, eng, fn, reads=(), writes=(), dma=False, rows=None):
        self.nadd += 1
        if self.nadd > self.max_ops:
            return None
        op = Op(eng, fn, dma)
        op.rows = rows
        op.idx = len(self.all_ops)
        accesses = []
        nel = 0
        for ap, is_w in [(a, False) for a in reads] + [(a, True) for a in writes]:
            if ap is None or isinstance(ap, (int, float)):
                continue
            name, p0, p1, f0, f1 = _region(ap)
            if is_w and nel == 0:
                nel = _free_elems(ap)
                if dma:
                    nel = nel * mybir.dt.size(ap.dtype) * (p1 - p0)
            accesses.append((name, p0, p1, f0, f1, is_w, False))
            if "PSUM" in str(ap.space).upper():
                b0 = f0 // 2048
                b1 = (f1 - 1) // 2048
                accesses.append((name + "#bank", 0, 128, b0 * 2048, (b1 + 1) * 2048, True, True))
        if dma:
            op.dur = 2000.0 + nel / 150.0
        elif eng == "pe":
            op.dur = 40.0 + nel * 0.45
        elif eng == "act":
            op.dur = 220.0 + nel * 0.6
        elif eng == "dve":
            op.dur = 120.0 + nel * 0.9
        else:
            op.dur = 350.0 + nel * 1.6
        acc2 = []
        for name, p0, p1, f0, f1, is_w, cross_only in accesses:
            if name == "big":
                b = f0 // 512
                while b * 512 < f1:
                    acc2.append(((name, b), p0, p1, max(f0, b * 512), min(f1, (b + 1) * 512), is_w, cross_only))
                    b += 1
            else:
                acc2.append((name, p0, p1, f0, f1, is_w, cross_only))
        for name, p0, p1, f0, f1, is_w, cross_only in acc2:
            h = self.hist.get(name)
            if h is None:
                h = []
            newh = []
            for rec in h:
                rp0, rp1, rf0, rf1, rop, rw = rec
                if rop is op:
                    newh.append(rec)
                    continue
                ov = (rp0 < p1 and p0 < rp1 and rf0 < f1 and f0 < rf1)
                if cross_only:
                    if ov:
                        if rop.eng != eng or rop.is_dma or dma:
                            self._dep(op, rop, 0)
                        elif eng == "pe" and rows is not None and rop.rows is not None \
                                and (rop.rows[1] <= rows[0] or rows[1] <= rop.rows[0]):
                            self._dep(op, rop, 1)
                        else:
                            self._dep(op, rop, 2)
                    if rop.eng == eng and rp0 >= p0 and rp1 <= p1 and rf0 >= f0 and rf1 <= f1:
                        continue
                    newh.append(rec)
                    continue
                if ov and (rw or is_w):
                    self._dep(op, rop, 0)
                if is_w and rp0 >= p0 and rp1 <= p1 and rf0 >= f0 and rf1 <= f1:
                    continue
                if (not is_w) and (not rw) and (not rop.is_dma) and (not dma) and rop.eng == eng \
                        and rp0 == p0 and rp1 == p1 and rf0 == f0 and rf1 == f1:
                    self._dep(op, rop, 2)
                    continue
                newh.append(rec)
            newh.append((p0, p1, f0, f1, op, is_w))
            self.hist[name] = newh
        self.all_ops.append(op)
        self.ops[eng].append(op)
        return op

    def _schedule(self):
        import heapq
        ops = self.all_ops
        for op in ops:
            op.npred = len(op.deps)
            op.ready = 0.0
        for op in ops:
            for A, kind in op.deps.values():
                A.succs.append((op, kind))
        heaps = {e: [] for e in ENGS}
        free = {e: 0.0 for e in ENGS}
        for op in ops:
            if op.npred == 0:
                heapq.heappush(heaps[op.eng], (op.ready, op.idx, op))
        order = []
        new_streams = {e: [] for e in ENGS}
        n = len(ops)
        while len(order) < n:
            best_e = None
            best_t = None
            for e in ENGS:
                hp = heaps[e]
                if not hp:
                    continue
                t = max(free[e], hp[0][0])
                if best_t is None or t < best_t or (t == best_t and hp[0][1] < heaps[best_e][0][1]):
                    best_t = t
                    best_e = e
            assert best_e is not None, "scheduler deadlock (cyclic hazards?)"
            _, _, op = heapq.heappop(heaps[best_e])
            op.start = best_t
            if op.is_dma:
                free[best_e] = best_t + 60.0
                op.finish = best_t + op.dur
            else:
                free[best_e] = best_t + op.dur
                op.finish = best_t + op.dur
            order.append(op)
            new_streams[best_e].append(op)
            for sop, kind in op.succs:
                if kind == 2:
                    r = op.start + 1.0
                elif op.eng == "pe" and sop.eng == "pe" and kind == 0:
                    r = op.start + min(op.dur, 64.0)
                else:
                    r = op.finish + 180.0
                if r > sop.ready:
                    sop.ready = r
                sop.npred -= 1
                if sop.npred == 0:
                    heapq.heappush(heaps[sop.eng], (sop.ready, sop.idx, sop))
        self.ops = new_streams
        return order

    def finalize(self):
        if self.reorder:
            order = self._schedule()
        else:
            order = self.all_ops
        last = {e: None for e in ENGS}
        dma_last = [None] * NDMASEM
        rr = 0
        cnt = {e: 0 for e in ENGS}
        for op in order:
            eng = op.eng
            prev = last[eng]
            vc = dict(prev.vc) if prev is not None else {}
            deps = list(op.deps.values())
            if op.is_dma:
                j = rr
                rr = (rr + 1) % NDMASEM
                if dma_last[j] is not None:
                    deps.append([dma_last[j], 0])
                op.chan = ("d", j)
                op.seq = self.dma_uses[j]
                self.dma_uses[j] += 1
                dma_last[j] = op
                op.signal = True
            else:
                op.chan = eng
                op.seq = cnt[eng]
            cnt[eng] += 1
            best = {}
            for A, kind in deps:
                if kind == 2:
                    continue
                if A.chan == op.chan and not op.is_dma:
                    if kind != 1 and (eng == "pe" or not self.same_engine_sync):
                        continue
                if vc.get(A.chan, -1) >= A.seq:
                    continue
                if A.chan not in best or best[A.chan].seq < A.seq:
                    best[A.chan] = A
            op.waits = []
            for ch, A in best.items():
                if vc.get(A.chan, -1) >= A.seq:
                    continue
                A.signal = True
                op.waits.append(A)
                for kk_, v in A.vc.items():
                    if vc.get(kk_, -1) < v:
                        vc[kk_] = v
                if vc.get(A.chan, -1) < A.seq:
                    vc[A.chan] = A.seq
            op.vc = vc
            last[eng] = op

    def emit(self, sems, dsems, block):
        self.finalize()
        for e in ENGS:
            c = 0
            for op in self.ops[e]:
                if op.is_dma:
                    op.sigval = 16 * (op.seq + 1)
                elif op.signal:
                    c += 1
                    op.sigval = c
        sched = self

        def run(engname, engobj):
            for op in sched.ops[engname]:
                for A in op.waits:
                    if A.is_dma:
                        engobj.wait_ge(dsems[A.chan[1]], A.sigval)
                    else:
                        engobj.wait_ge(sems[A.chan], A.sigval)
                inst = op.fn(engobj)
                if op.is_dma:
                    inst.then_inc(dsems[op.chan[1]], 16)
                elif op.signal:
                    inst.then_inc(sems[op.chan], 1)

        @block.tensor
        def _(e):
            run("pe", e)

        @block.scalar
        def _(e):
            run("act", e)

        @block.vector
        def _(e):
            run("dve", e)

        @block.gpsimd
        def _(e):
            run("pool", e)

        @block.sync
        def _(e):
            run("sp", e)
            for j in range(NDMASEM):
                if sched.dma_uses[j] > 0:
                    e.wait_ge(dsems[j], 16 * sched.dma_uses[j])


class K:
    def __init__(self, S):
        self.S = S
        self.cap = None

    def _add(self, eng, fn, reads=(), writes=(), dma=False, rows=None):
        if self.cap is not None:
            self.cap.append((eng, fn, reads, writes, dma, rows))
            return None
        return self.S.add(eng, fn, reads=reads, writes=writes, dma=dma, rows=rows)

    def merge(self, lists, store_delay=0):
        idx = [0] * len(lists)
        alive = True
        held = []
        n = 0
        while alive:
            alive = False
            for c, l in enumerate(lists):
                if idx[c] < len(l):
                    item = l[idx[c]]
                    idx[c] += 1
                    alive = True
                    eng, fn, reads, writes, dma, rows = item
                    if dma and store_delay > 0 and "DRAM" in str(writes[0].space).upper():
                        held.append((n + store_delay, item))
                    else:
                        self.S.add(eng, fn, reads=reads, writes=writes, dma=dma, rows=rows)
                    n += 1
                    while held and held[0][0] <= n:
                        eng, fn, reads, writes, dma, rows = held.pop(0)[1]
                        self.S.add(eng, fn, reads=reads, writes=writes, dma=dma, rows=rows)
        for _, item in held:
            eng, fn, reads, writes, dma, rows = item
            self.S.add(eng, fn, reads=reads, writes=writes, dma=dma, rows=rows)

    def dma(self, out, in_, eng="sp"):
        return self._add(eng, lambda e: e.dma_start(out=out, in_=in_), reads=[in_], writes=[out], dma=True)

    def mm(self, out, lhsT, rhs, start=True, stop=True, skip=False):
        rg = _region(lhsT)
        rows = (rg[1], rg[2])
        if skip:
            return self._add("pe", lambda e: e.matmul(out, lhsT=lhsT, rhs=rhs, start=start, stop=stop,
                                                       skip_group_check=True),
                              reads=[lhsT, rhs], writes=[out], rows=rows)
        return self._add("pe", lambda e: e.matmul(out, lhsT=lhsT, rhs=rhs, start=start, stop=stop),
                          reads=[lhsT, rhs], writes=[out], rows=rows)

    def red(self, out, in_, op=ALU.add):
        return self._add("dve", lambda e: e.tensor_reduce(out=out, in_=in_, axis=AX.X, op=op),
                          reads=[in_], writes=[out])

    def recip(self, out, in_):
        return self._add("dve", lambda e: e.reciprocal(out=out, in_=in_), reads=[in_], writes=[out])

    def tr(self, out, in_, ident):
        rg = _region(in_)
        return self._add("pe", lambda e: e.transpose(out, in_, ident), reads=[in_, ident], writes=[out],
                          rows=(rg[1], rg[2]))

    def act(self, out, in_, func, bias=None, scale=None, eng="act"):
        kw = {}
        rd = [in_]
        if bias is not None:
            kw["bias"] = bias
            rd.append(bias)
        if scale is not None:
            kw["scale"] = scale
            rd.append(scale)
        return self._add("act", lambda e: e.activation(out=out, in_=in_, func=func, **kw), reads=rd, writes=[out])

    def tt(self, eng, out, a, b, op):
        return self._add(eng, lambda e: e.tensor_tensor(out=out, in0=a, in1=b, op=op), reads=[a, b], writes=[out])

    def ts(self, eng, out, a, s1, op0, s2=None, op1=None):
        rd = [a, s1, s2]
        if op1 is None:
            return self._add(eng, lambda e: e.tensor_scalar(out=out, in0=a, scalar1=s1, scalar2=None, op0=op0),
                              reads=rd, writes=[out])
        return self._add(eng, lambda e: e.tensor_scalar(out=out, in0=a, scalar1=s1, scalar2=s2, op0=op0, op1=op1),
                          reads=rd, writes=[out])

    def stt(self, out, a, scalar, b, op0, op1):
        return self._add("dve", lambda e: e.scalar_tensor_tensor(out=out, in0=a, scalar=scalar, in1=b, op0=op0, op1=op1),
                          reads=[a, scalar, b], writes=[out])

    def cp(self, eng, out, in_):
        if eng == "act":
            return self.act(out, in_, AF.Copy)
        return self._add(eng, lambda e: e.tensor_copy(out=out, in_=in_), reads=[in_], writes=[out])

    def memset(self, eng, out, val):
        return self._add(eng, lambda e: e.memset(out, val), writes=[out])

    def scan(self, out, d0, d1, init, op0, op1):
        return self._add("dve", lambda e: e.tensor_tensor_scan(out=out, data0=d0, data1=d1, initial=init, op0=op0, op1=op1),
                          reads=[d0, d1, init], writes=[out])

    def ttr(self, out, a, b, accum):
        return self._add("dve", lambda e: e.tensor_tensor_reduce(out=out, in0=a, in1=b, scale=1.0, scalar=0.0,
                                                                  op0=ALU.mult, op1=ALU.add, accum_out=accum),
                          reads=[a, b], writes=[out, accum])


class Cfg:
    def __init__(self, NT, SEG, debug=False, upto="all"):
        self.upto = upto
        self.NT = NT
        self.SEG = SEG
        self.NM = 4
        assert NT % SEG == 0 and SEG % self.NM == 0
        self.NSEG = NT // SEG
        self.NMT = NT // self.NM
        self.debug = debug


PC_MU = 0
PC_W0 = 17
PC_A0 = 25
PC_KK = 33
PC_KA = 37
PC_RK = 41
PC_LG = 45
PC_LB = 49
PC_QG = 53
PC_KG = 54
PC_NG = 55
NPC = 63


def build_program(cfg):
    NT, SEG, NM, NSEG, NMT = cfg.NT, cfg.SEG, cfg.NM, cfg.NSEG, cfg.NMT
    TOK = NM * 128
    nc = bass.Bass("TRN2", target_bir_lowering=False)
    okind = "ExternalOutput" if cfg.debug else "Internal"

    def din(name, shape, dt=F32):
        return nc.dram_tensor(name, list(shape), dt, kind="ExternalInput").ap()

    def dscr(name, shape, dt):
        return nc.dram_tensor(name, list(shape), dt, kind=okind).ap()

    xs = din("xs", [NT * 128, D])
    w_in = din("w_in", [D, DIN])
    w_out = din("w_out", [D, D])
    pcols = din("pcols", [128, NPC])
    flags = din("flags", [128, NSEG + 1])
    wup = din("wup", [128, 2, 512])
    consts = din("consts", [128, 6, 128])
    ohot = din("ohot", [31, 4096])
    rpbT = din("rpbT", [31, 120])
    vtab = din("vtab", [128, NT * 14])
    ys = nc.dram_tensor("ys", [NT * 128, D], F32, kind="ExternalOutput").ap()

    Wb = dscr("Wb", [NBLK, 128, 8 * 128], BF16)
    S_AR = dscr("S_AR", [2, NMT, 4, 128, NM * 256], BF16)
    S_BT = dscr("S_BT", [2, NMT, 4, 128, NM * 128], BF16)
    S_KT = dscr("S_KT", [2, NMT, 4, 128, NM * 128], BF16)
    S_BH = dscr("S_BH", [2, NMT, 4, 128, NM * 128], BF16)
    S_KH = dscr("S_KH", [2, NMT, 4, 128, NM * 128], BF16)
    S_V = dscr("S_V", [NMT, 4, 128, NM * 128], BF16)
    S_WC = dscr("S_WC", [2, NMT, 4, 128, NM], F32)
    S_GR = dscr("S_GR", [NMT, 4, 128, TOK], F32)
    S_BON = dscr("S_BON", [NMT, 4, 128, TOK], F32)
    S_Q = dscr("S_Q", [NMT, 4, 128, TOK], BF16)
    S_K = dscr("S_K", [NMT, 4, 128, TOK], BF16)
    S_VA = dscr("S_VA", [NMT, 4, 128, NM * 128], BF16)
    S_GA = dscr("S_GA", [NMT, 4, 128, TOK], F32)
    S_YF = dscr("S_YF", [NT, 128, 512], F32)
    S_MR = dscr("S_MR", [NT, 4, 128, 128], BF16)

    with ExitStack() as es:
        BIGB = 186 * 1024
        big = es.enter_context(nc.sbuf_tensor("big", [128, BIGB // 2], BF16))
        alloc = {"off": 0, "base": 0}

        def sb(name, shape, dt):
            n = 1
            for d_ in shape[1:]:
                n *= d_
            nbytes = n * mybir.dt.size(dt)
            nbytes = (nbytes + 31) // 32 * 32
            off = alloc["off"]
            assert off + nbytes <= BIGB, (name, off, nbytes)
            alloc["off"] = off + nbytes
            v = big[:, off // 2:(off + nbytes) // 2]
            if dt != BF16:
                v = v.bitcast(dt)
            v = v[:, 0:n]
            if len(shape) == 2:
                return v
            names = " ".join("a%d" % i for i in range(len(shape) - 1))
            kw = {"a%d" % i: shape[i + 1] for i in range(len(shape) - 1)}
            return v.rearrange("p (%s) -> p %s" % (names, names), **kw)

        def pass_reset():
            alloc["off"] = alloc["base"]

        def pst(name, shape, dt=F32):
            return es.enter_context(nc.psum_tensor(name, list(shape), dt))

        sems = {e: es.enter_context(nc.semaphore("s_" + e)) for e in ENGS}
        dsems = [es.enter_context(nc.semaphore("d%d" % j)) for j in range(NDMASEM)]
        S = Sched(nc)
        k = K(S)

        psW = [pst("psW0", [128, 1024]), pst("psW1", [128, 1024])]
        psB = [pst("ps%d" % i, [128, 512]) for i in range(4)]

        cst = sb("cst", [128, 6, 128], F32)
        cstb = sb("cstb", [128, 6, 128], BF16)
        pc = sb("pc", [128, NPC], F32)
        pcd = sb("pcd", [128, 64], F32)
        flg = sb("flg", [128, NSEG + 1], F32)
        wupf = sb("wupf", [128, 2, 512], F32)
        wupb = sb("wupb", [128, 2, 512], BF16)
        cm05 = sb("cm05", [128, 8], F32)
        k.memset("pool", cm05[:], -0.5)
        k.dma(cst[:], consts)
        k.dma(pc[:], pcols)
        k.dma(flg[:], flags)
        k.dma(wupf[:], wup)
        k.cp("dve", cstb[:], cst[:])
        k.cp("dve", wupb[:], wupf[:])
        identb = cstb[:, 0, :]
        identf = cst[:, 0, :]
        blockones = cstb[:, 5, :]
        PD_OMM, PD_HM, PD_W0H, PD_A0H, PD_OMKA = 0, 17, 34, 42, 50
        k.ts("dve", pcd[:, PD_OMM:PD_OMM + 17], pc[:, PC_MU:PC_MU + 17], -1.0, ALU.mult, 1.0, ALU.add)
        k.ts("dve", pcd[:, PD_HM:PD_HM + 17], pc[:, PC_MU:PC_MU + 17], 0.5, ALU.mult)
        k.ts("dve", pcd[:, PD_W0H:PD_W0H + 16], pc[:, PC_W0:PC_W0 + 16], 0.5, ALU.mult)
        k.ts("dve", pcd[:, PD_OMKA:PD_OMKA + 4], pc[:, PC_KA:PC_KA + 4], -1.0, ALU.mult, 1.0, ALU.add)

        alloc["base"] = alloc["off"]
        arena = sb("arena", [128, 2 * 12 * 512], F32)
        wst = arena[:, 0:DIN]
        wstb = arena[:, DIN:DIN + DIN // 2].bitcast(BF16)
        CH = [0]
        for dk in range(8):
            k.dma(wst, w_in[dk * 128:(dk + 1) * 128, :])
            if dk % 2 == 0:
                k.ts("dve", wstb[:], wst, pc[:, PC_NG + dk:PC_NG + dk + 1], ALU.mult)
            else:
                k.act(wstb[:], wst, AF.Copy, scale=pc[:, PC_NG + dk:PC_NG + dk + 1])
            k.dma(Wb.rearrange("b p (k c) -> p b k c", k=8)[:, :, dk, :],
                  wstb[:].rearrange("p (b c) -> p b c", c=128))

        xt = sb("xt", [128, 2, D], F32)
        hb = sb("hb", [128, 2, D], BF16)
        junk = sb("junk", [128, D], BF16)
        ssq = sb("ssq", [128, NT], F32)
        rstd = sb("rstd", [128, NT], F32)
        hT = sb("hT", [128, 2, 8, 514], BF16)
        wblk = sb("wblk", [128, 4, 8 * 128], BF16)
        zraw = sb("zraw", [128, 2, 514], F32)
        ztmp = sb("ztmp", [128, 2, 512], F32)
        zsl = sb("zsl", [128, 512], F32)
        zsc = sb("zsc", [128, 2, 4, 512], F32)
        lorab = sb("lorab", [128, 512], BF16)
        def T(i):
            return arena[:, (CH[0] * 12 + i) * 512:(CH[0] * 12 + i + 1) * 512]
        resetm = sb("resetm", [128, 512], F32)
        k.memset("pool", resetm[:], 1.0)
        for c in range(NM):
            k.memset("pool", resetm[:, c * 128:c * 128 + 1], 0.0)
        oAR = sb("oAR", [128, 2, 2, NM, 256], BF16)
        oBT = sb("oBT", [128, 2, 2, NM, 128], BF16)
        oKT = sb("oKT", [128, 2, 2, NM, 128], BF16)
        oBHc2 = sb("oBHc", [128, 2, 2, 512], BF16)
        oKHc2 = sb("oKHc", [128, 2, 2, 512], BF16)
        oVc2 = sb("oVc", [128, 2, 512], BF16)
        oBH = sb("oBH", [128, 2, 2, NM, 128], BF16)
        oKH = sb("oKH", [128, 2, 2, NM, 128], BF16)
        oV = sb("oV", [128, 2, NM, 128], BF16)
        oWC = sb("oWC", [128, 2, 2, NM], F32)
        oGR = sb("oGR", [128, 2, 512], F32)
        oBON = sb("oBON", [128, 2, 512], F32)
        oQ = sb("oQ", [128, 2, 512], BF16)
        oVAc2 = sb("oVAc", [128, 2, 512], BF16)
        oVA = sb("oVA", [128, 2, NM, 128], BF16)
        oGA = sb("oGA", [128, 2, 512], F32)

        ptr = [psB[2][:].bitcast(BF16), psB[3][:].bitcast(BF16)]

        def stage_x(i):
            sl = i % 2
            m = i // NM
            j = i % NM
            k.dma(xt[:, sl, :], xs[i * 128:(i + 1) * 128, :])
            S.add("act", lambda e, sl=sl, i=i: e.activation(out=junk[:], in_=xt[:, sl, :], func=AF.Square,
                                                          accum_out=ssq[:, i:i + 1]),
                  reads=[xt[:, sl, :]], writes=[junk[:], ssq[:, i:i + 1]])
            k.ts("dve", rstd[:, i:i + 1], ssq[:, i:i + 1], 1.0 / D, ALU.mult, RMS_EPS, ALU.add)
            k.tt("pool", rstd[:, i:i + 1], rstd[:, i:i + 1], cm05[:, 0:1], ALU.pow)
            k.act(hb[:, sl, :], xt[:, sl, :], AF.Copy, scale=rstd[:, i:i + 1])
            pt = ptr[i % 2].rearrange("p (a b) -> p a b", a=8)
            for dk in range(8):
                k.tr(pt[:, dk, :], hb[:, sl, dk * 128:(dk + 1) * 128], identb)
            k.cp("dve" if i % 2 == 0 else "act", hT[:, m % 2, :, 1 + j * 128:1 + (j + 1) * 128], pt)

        def halo_left(m):
            sl = m % 2
            t0 = m * NM
            if m == 0:
                k.memset("pool", hT[:, sl, :, 0:1], 0.0)
            elif t0 % SEG == 0:
                s = t0 // SEG
                k.ts("pool", hT[:, sl, :, 0:1], hT[:, 1 - sl, :, 512:513], flg[:, s:s + 1], ALU.mult)
            else:
                k.cp("pool", hT[:, sl, :, 0:1], hT[:, 1 - sl, :, 512:513])

        def halo_right(m):
            sl = m % 2
            t0 = m * NM
            if m == NMT - 1:
                k.memset("pool", hT[:, sl, :, 513:514], 0.0)
            elif (t0 + NM) % SEG == 0:
                s = (t0 + NM) // SEG
                k.ts("pool", hT[:, sl, :, 513:514], hT[:, 1 - sl, :, 1:2], flg[:, s:s + 1], ALU.mult)
            else:
                k.cp("pool", hT[:, sl, :, 513:514], hT[:, 1 - sl, :, 1:2])

        wcount = [0, 0]
        wq = [[], []]
        wready = [[], []]

        def w_begin(blocks):
            c = CH[0]
            wq[c] = list(blocks)
            wready[c] = []
            w_prefetch()
            w_prefetch()

        def w_prefetch():
            c = CH[0]
            if wq[c]:
                b = wq[c].pop(0)
                sl = c * 2 + wcount[c] % 2
                wcount[c] += 1
                k.dma(wblk[:, sl, :], Wb[b])
                wready[c].append((b, sl))

        def load_w(b):
            c = CH[0]
            if not wready[c]:
                w_begin([b])
            bb, sl = wready[c].pop(0)
            assert bb == b, (bb, b)
            return wblk[:, sl, :].rearrange("p (k c) -> p k c", k=8)

        zcount = [0]

        def proj_rwkv(m, b, dst):
            w = load_w(b)
            pz = psW[CH[0]]
            zr = zraw[:, CH[0], :]
            zt = ztmp[:, CH[0], :]
            sl = m % 2
            for half in range(2):
                for dk in range(8):
                    k.mm(pz[:, half * 512:half * 512 + 257], w[:, dk, :], hT[:, sl, dk, half * 257:(half + 1) * 257],
                         start=(dk == 0), stop=(dk == 7))
            w_prefetch()
            k.act(zr.rearrange("p (a b) -> p a b", a=2), pz.rearrange("p (a b) -> p a b", a=2)[:, :, 0:257], AF.Copy)
            k.tt("pool", zt, zr[:, 0:512], zr[:, 2:514], ALU.add)
            k.act(dst, zr[:, 1:513], AF.Copy, scale=pcd[:, PD_OMM + b:PD_OMM + b + 1])
            k.stt(dst, zt, pcd[:, PD_HM + b:PD_HM + b + 1], dst, ALU.mult, ALU.add)

        def proj_attn(m, b):
            w = load_w(b)
            pz = psW[CH[0]]
            sl = m % 2
            for dk in range(8):
                k.mm(pz[:, 0:512], w[:, dk, :], hT[:, sl, dk, 1:513], start=(dk == 0), stop=(dk == 7))
            w_prefetch()
            return pz[:, 0:512]

        def sigm_from_tanh(eng, out, th):
            k.ts(eng, out, th, 0.5, ALU.mult, 0.5, ALU.add)

        def passA_macro(m):
            CH[0] = 0
            proj_rwkv(m, 16, zsl[:])
            k.act(lorab[0:64, :], zsl[0:64, :], AF.Tanh)
            k.cp("pool", lorab[64:128, :], zsl[64:128, :])

            def rwkv_cb(cb):
                osl = cb % 2
                CH[0] = osl
                oVc = oVc2[:, osl]
                oBHc = oBHc2[:, osl]
                oKHc = oKHc2[:, osl]
                zs_ = zsc[:, osl]
                w_begin([cb, 4 + cb, 8 + cb, 12 + cb])
                proj_rwkv(m, 0 + cb, zs_[:, 0, :])
                proj_rwkv(m, 4 + cb, zs_[:, 1, :])
                proj_rwkv(m, 8 + cb, zs_[:, 2, :])
                proj_rwkv(m, 12 + cb, zs_[:, 3, :])
                r_, k_, v_, g_ = zs_[:, 0, :], zs_[:, 1, :], zs_[:, 2, :], zs_[:, 3, :]
                th = T(0)
                k.act(th, g_, AF.Tanh, scale=0.5)
                sigm_from_tanh("dve", th, th)
                k.tt("pool", oGR[:, osl, :], th, g_, ALU.mult)
                k.dma(S_GR[m, cb], oGR[:, osl, :])
                k.cp("dve", oVc, v_)
                pt = ptr[CH[0]].rearrange("p (a b) -> p a b", a=8)
                for j in range(NM):
                    k.tr(pt[:, j, :], oVc[:, j * 128:(j + 1) * 128], identb)
                k.cp("act", oV[:, osl, :, :], pt[:, 0:NM, :])
                k.dma(S_V[m, cb].rearrange("p (a b) -> p a b", a=NM), oV[:, osl, :, :])
                kk = T(1)
                k.ts("dve", kk, k_, pc[:, PC_KK + cb:PC_KK + cb + 1], ALU.mult)
                kk2 = T(2).bitcast(BF16)[:, 0:512]
                k.tt("pool", kk2, kk, kk, ALU.mult)
                pn = psB[CH[0]]
                k.mm(pn[:, :], blockones, kk2, start=True, stop=True)
                rn = T(2)
                k.ts("dve", rn, pn[:, :], 1e-18, ALU.max)
                k.act(rn, rn, AF.Ln)
                k.act(rn, rn, AF.Exp, scale=-0.5)
                kkn = T(1)
                k.tt("pool", kkn, kk, rn, ALU.mult)
                bon = T(3)
                for d in range(2):
                    pw = psB[CH[0]]
                    pa = psB[CH[0]]
                    k.mm(pw[:, :], wupb[0:64, d, cb * 128:(cb + 1) * 128], lorab[0:64, :])
                    lw = T(4)
                    k.act(lw, pw[:, :], AF.Tanh, bias=pcd[:, PD_W0H + d * 4 + cb:PD_W0H + d * 4 + cb + 1], scale=0.5)
                    k.mm(pa[:, :], wupb[64:128, d, cb * 128:(cb + 1) * 128], lorab[64:128, :])
                    k.ts("dve", lw, lw, -0.5 * CDEC, ALU.mult, -0.5 * CDEC, ALU.add)
                    a_ = T(5)
                    k.act(a_, pa[:, :], AF.Tanh, bias=pcd[:, PD_A0H + d * 4 + cb:PD_A0H + d * 4 + cb + 1], scale=0.5)
                    sigm_from_tanh("dve", a_, a_)
                    kd = T(6)
                    k.ts("dve", kd, a_, pc[:, PC_KA + cb:PC_KA + cb + 1], ALU.mult,
                         pcd[:, PD_OMKA + cb:PD_OMKA + cb + 1], ALU.add)
                    k.tt("pool", kd, kd, k_, ALU.mult)
                    kka = T(5)
                    k.tt("pool", kka, kkn, a_, ALU.mult)
                    rk = T(7) if d == 1 else bon
                    k.stt(rk, kd, pc[:, PC_RK + cb:PC_RK + cb + 1], r_, ALU.mult, ALU.mult)
                    if d == 1:
                        k.tt("pool", bon, bon, rk, ALU.add)
                    cl = T(7)
                    k.scan(cl, resetm[:], lw, 0.0, ALU.mult, ALU.add)
                    cl3 = cl.rearrange("p (a b) -> p a b", a=NM)
                    lw3 = lw.rearrange("p (a b) -> p a b", a=NM)
                    if d == 1:
                        tot = T(8)[:, 0:NM]
                        k.cp("dve", tot, cl3[:, :, 127])
                        k.tt("dve", cl, lw, cl, ALU.subtract)
                        k.tt("dve", cl3, cl3, tot.unsqueeze(2).to_broadcast([128, NM, 128]), ALU.add)
                    clC = T(8)[:, 8:8 + NM]
                    k.cp("dve", clC, cl3[:, :, 127] if d == 0 else cl3[:, :, 0])
                    k.act(oWC[:, osl, d, :], clC, AF.Exp)
                    Ep = T(9)
                    Em = T(10)
                    Eh = T(11)
                    k.act(Ep, cl, AF.Exp)
                    k.act(Em, cl, AF.Exp, scale=-1.0)
                    for j in range(NM):
                        k.act(Eh[:, j * 128:(j + 1) * 128], cl[:, j * 128:(j + 1) * 128], AF.Exp,
                              bias=clC[:, j:j + 1], scale=-1.0)
                    Ep3 = Ep.rearrange("p (a b) -> p a b", a=NM)
                    kkn3 = kkn.rearrange("p (a b) -> p a b", a=NM)
                    oA = oAR[:, osl, d, :, 0:128]
                    oR = oAR[:, osl, d, :, 128:256]
                    if d == 0:
                        k.stt(oA[:, :, 1:128], kkn3[:, :, 1:128], -1.0, Ep3[:, :, 0:127], ALU.mult, ALU.mult)
                        k.ts("pool", oA[:, :, 0:1], kkn3[:, :, 0:1], -1.0, ALU.mult)
                    else:
                        k.stt(oA[:, :, 0:127], kkn3[:, :, 0:127], -1.0, Ep3[:, :, 1:128], ALU.mult, ALU.mult)
                        k.ts("pool", oA[:, :, 127:128], kkn3[:, :, 127:128], -1.0, ALU.mult)
                    k.tt("pool", oR, r_.rearrange("p (a b) -> p a b", a=NM), Ep3, ALU.mult)
                    k.tt("dve", oBT[:, osl, d, :, :], kka.rearrange("p (a b) -> p a b", a=NM),
                         Em.rearrange("p (a b) -> p a b", a=NM), ALU.mult)
                    k.tt("pool", oKT[:, osl, d, :, :], kd.rearrange("p (a b) -> p a b", a=NM),
                         Em.rearrange("p (a b) -> p a b", a=NM), ALU.mult)
                    k.tt("dve", oBHc[:, d, :], kka, Eh, ALU.mult)
                    k.tt("pool", oKHc[:, d, :], kd, Eh, ALU.mult)
                    pt2 = ptr[CH[0]].rearrange("p (a b) -> p a b", a=8)
                    for j in range(NM):
                        k.tr(pt2[:, j, :], oBHc[:, d, j * 128:(j + 1) * 128], identb)
                        k.tr(pt2[:, NM + j, :], oKHc[:, d, j * 128:(j + 1) * 128], identb)
                    k.cp("act", oBH[:, osl, d, :, :], pt2[:, 0:NM, :])
                    k.cp("dve", oKH[:, osl, d, :, :], pt2[:, NM:2 * NM, :])
                    k.dma(S_AR[d, m, cb].rearrange("p (a b) -> p a b", a=NM), oAR[:, osl, d, :, :])
                    k.dma(S_BT[d, m, cb].rearrange("p (a b) -> p a b", a=NM), oBT[:, osl, d, :, :])
                    k.dma(S_KT[d, m, cb].rearrange("p (a b) -> p a b", a=NM), oKT[:, osl, d, :, :])
                    k.dma(S_BH[d, m, cb].rearrange("p (a b) -> p a b", a=NM), oBH[:, osl, d, :, :])
                    k.dma(S_KH[d, m, cb].rearrange("p (a b) -> p a b", a=NM), oKH[:, osl, d, :, :])
                    k.dma(S_WC[d, m, cb], oWC[:, osl, d, :])
                pb = psB[CH[0]]
                bonb = T(7).bitcast(BF16)[:, 0:512]
                k.cp("dve", bonb, bon)
                k.mm(pb[:, :], blockones, bonb, start=True, stop=True)
                k.tt("dve", oBON[:, osl, :], pb[:, :], v_, ALU.mult)
                k.dma(S_BON[m, cb], oBON[:, osl, :])
            def attn_cb(cb):
                osl = cb % 2
                CH[0] = osl
                oVAc = oVAc2[:, osl]
                w_begin([17 + cb, 21 + cb, 25 + cb, 29 + cb])
                for which, dst, gcol in ((0, S_Q, PC_QG), (1, S_K, PC_KG)):
                    pz = proj_attn(m, 17 + which * 4 + cb)
                    q_ = T(0)
                    k.cp("act", q_, pz)
                    q2 = T(1).bitcast(BF16)[:, 0:512]
                    k.tt("pool", q2, q_, q_, ALU.mult)
                    pn = psB[CH[0]]
                    k.mm(pn[:, :], blockones, q2, start=True, stop=True)
                    rn = T(1)
                    k.ts("dve", rn, pn[:, :], 1.0 / 64, ALU.mult, RMS_EPS, ALU.add)
                    k.act(rn, rn, AF.Ln)
                    k.act(rn, rn, AF.Exp, scale=-0.5)
                    k.stt(oQ[:, osl, :], q_, pc[:, gcol:gcol + 1], rn, ALU.mult, ALU.mult)
                    k.dma(dst[m, cb], oQ[:, osl, :])
                pz = proj_attn(m, 25 + cb)
                k.cp("act", oVAc, pz)
                pt = ptr[CH[0]].rearrange("p (a b) -> p a b", a=8)
                for j in range(NM):
                    k.tr(pt[:, j, :], oVAc[:, j * 128:(j + 1) * 128], identb)
                k.cp("dve", oVA[:, osl, :, :], pt[:, 0:NM, :])
                k.dma(S_VA[m, cb].rearrange("p (a b) -> p a b", a=NM), oVA[:, osl, :, :])
                pz = proj_attn(m, 29 + cb)
                g_ = T(2)
                k.cp("act", g_, pz)
                th = T(3)
                k.act(th, g_, AF.Tanh, scale=0.5)
                sigm_from_tanh("dve", th, th)
                k.tt("pool", oGA[:, osl, :], th, g_, ALU.mult)
                k.dma(S_GA[m, cb], oGA[:, osl, :])


            for pair in ((0, 1), (2, 3)):
                lists = []
                for cb in pair:
                    k.cap = []
                    rwkv_cb(cb)
                    lists.append(k.cap)
                    k.cap = None
                k.merge(lists, store_delay=0)
            for pair in ((0, 1), (2, 3)):
                lists = []
                for cb in pair:
                    k.cap = []
                    attn_cb(cb)
                    lists.append(k.cap)
                    k.cap = None
                k.merge(lists, store_delay=0)
            CH[0] = 0

        print("SBUF pass A bytes", alloc["off"], "base", alloc["base"])
        if cfg.upto == "W":
            k.dma(ys[0:128, :], xs[0:128, :])
            with nc.Block() as block:
                S.emit(sems, dsems, block)
            return nc
        for i in range(min(NM, NT)):
            stage_x(i)
        if cfg.upto == "X":
            k.dma(ys[0:128, :], xs[0:128, :])
            with nc.Block() as block:
                S.emit(sems, dsems, block)
            return nc
        for m in range(NMT):
            halo_left(m)
            if m + 1 < NMT:
                for j in range(NM):
                    stage_x((m + 1) * NM + j)
            halo_right(m)
            passA_macro(m)

        if cfg.upto == "A":
            with nc.Block() as block:
                S.emit(sems, dsems, block)
            return nc

        W0a, W0b = psW[0][:, 0:512], psW[0][:, 512:1024]
        W1a, W1b = psW[1][:, 0:512], psW[1][:, 512:1024]

        def rwkv_pass(d):
            pass_reset()
            ldAR = sb("ldAR", [128, 2, 4, NM, 256], BF16)
            ldBT = sb("ldBT", [128, 2, 4, NM, 128], BF16)
            ldKT = sb("ldKT", [128, 2, 4, NM, 128], BF16)
            ldBH = sb("ldBH", [128, 2, 4, NM, 128], BF16)
            ldKH = sb("ldKH", [128, 2, 4, NM, 128], BF16)
            ldV = sb("ldV", [128, 2, 4, NM, 128], BF16)
            ldWC = sb("ldWC", [128, 2, 4, NM], F32)
            MA = sb("MA", [128, 8, 256], BF16)
            AK = sb("AK", [128, 8, 256], BF16)
            N0 = sb("N0", [128, 8, 128], BF16)
            Mn = sb("Mn", [128, 2, 8, 128], BF16)
            Nn = sb("Nn", [128, 2, 8, 128], BF16)
            Ttb = sb("Ttb", [128, 2, 8, 128], BF16)
            G1s = sb("G1s", [128, 512], F32)
            G12 = sb("G12", [128, 512], BF16)
            Ubf = sb("Ubf", [128, 512], BF16)
            Hs = sb("Hs", [128, 256], F32)
            Hblk = sb("Hblk", [128, 4, 128], BF16)
            yo = sb("yo", [128, 2, 512], F32)
            mk2 = sb("mk2", [128, 256], F32)
            mkN = sb("mkN", [128, 128], F32)
            k.cp("pool", mk2[:, 0:128], cst[:, 1 if d == 0 else 2, :])
            k.cp("pool", mk2[:, 128:256], cst[:, 3 if d == 0 else 4, :])
            k.cp("pool", mkN[:], cst[:, 2 if d == 0 else 1, :])
            k.memset("pool", Hs[:], 0.0)
            k.memset("pool", Hblk[:], 0.0)
            if d == 1:
                yf = sb("yf", [128, 2, 512], F32)
                ysum = sb("ysum", [128, 512], F32)
                lnt = sb("lnt", [128, 512], F32)
                lnm = sb("lnm", [128, 16], F32)
                ynT = sb("ynT", [128, 512], F32)
                ldBON = sb("ldBON", [128, 2, 4, 128], F32)
                ldGR = sb("ldGR", [128, 2, 4, 128], F32)
                oMR = sb("oMR", [128, 2, 4, 128], BF16)

            def load(m, sl):
                k.dma(ldAR[:, sl].rearrange("p c t x -> p c (t x)"), S_AR[d, m].rearrange("c p x -> p c x"))
                k.dma(ldBT[:, sl].rearrange("p c t x -> p c (t x)"), S_BT[d, m].rearrange("c p x -> p c x"))
                k.dma(ldKT[:, sl].rearrange("p c t x -> p c (t x)"), S_KT[d, m].rearrange("c p x -> p c x"))
                k.dma(ldBH[:, sl].rearrange("p c t x -> p c (t x)"), S_BH[d, m].rearrange("c p x -> p c x"))
                k.dma(ldKH[:, sl].rearrange("p c t x -> p c (t x)"), S_KH[d, m].rearrange("c p x -> p c x"))
                k.dma(ldV[:, sl].rearrange("p c t x -> p c (t x)"), S_V[m].rearrange("c p x -> p c x"))
                k.dma(ldWC[:, sl], S_WC[d, m].rearrange("c p x -> p c x"))

            def tile(m, sl, j, cnt):
                i = m * NM + j
                AR = ldAR[:, sl]
                BT = ldBT[:, sl]
                KT = ldKT[:, sl]
                BH = ldBH[:, sl]
                KH = ldKH[:, sl]
                V = ldV[:, sl]
                bnd = None
                if d == 0 and i % SEG == 0 and i > 0:
                    bnd = i // SEG
                if d == 1 and (i + 1) % SEG == 0 and i < NT - 1:
                    bnd = (i + 1) // SEG
                if bnd is not None:
                    k.ts("pool", Hs[:], Hs[:], flg[:, bnd:bnd + 1], ALU.mult)
                    k.ts("pool", Hblk[:], Hblk[:], flg[:, bnd:bnd + 1], ALU.mult)
                for cb in range(4):
                    bs = [psB[0], psB[1], psB[2]] if cb % 2 == 0 else [psB[3], W0a, W0b]
                    for hh in range(2):
                        po = 64 * hh
                        k.mm(bs[0][:, hh * 256:(hh + 1) * 256], BT[po:po + 64, cb, j, :], AR[po:po + 64, cb, j, :])
                        k.mm(bs[1][:, hh * 256:(hh + 1) * 256], KT[po:po + 64, cb, j, :], AR[po:po + 64, cb, j, :])
                        k.mm(bs[2][:, hh * 128:(hh + 1) * 128], AR[po:po + 64, cb, j, 0:128], BT[po:po + 64, cb, j, :])
                    m2 = mk2[:].unsqueeze(1).to_broadcast([128, 2, 256])
                    k.tt("dve", MA[:, 2 * cb:2 * cb + 2, :], bs[0].rearrange("p (a b) -> p a b", a=2), m2, ALU.mult)
                    k.tt("dve", AK[:, 2 * cb:2 * cb + 2, :], bs[1].rearrange("p (a b) -> p a b", a=2), m2, ALU.mult)
                    k.tt("dve", N0[:, 2 * cb:2 * cb + 2, :], bs[2][:, 0:256].rearrange("p (a b) -> p a b", a=2),
                         mkN[:].unsqueeze(1).to_broadcast([128, 2, 128]), ALU.mult)
                k.tt("pool", Ttb[:, 0], MA[:, :, 0:128], cst[:, 0, :].unsqueeze(1).to_broadcast([128, 8, 128]), ALU.add)
                Ttbank = [psB[2], psB[3]]
                Mbank = [W1a, psB[0]]
                Nbank = [W1b, psB[1]]
                for half in range(2):
                    for q in range(4):
                        h = 4 * half + q
                        k.mm(Ttbank[half][:, q * 128:(q + 1) * 128], identb, Ttb[:, 0, h, :],
                             start=(q == 0), stop=False, skip=True)
                Mcur = [MA[:, h, 0:128] for h in range(8)]
                Ncur = [N0[:, h, :] for h in range(8)]
                for lev in range(7):
                    for half in range(2):
                        if lev <= 5:
                            for q in range(4):
                                h = 4 * half + q
                                k.mm(Nbank[half][:, q * 128:(q + 1) * 128], Mcur[h], Ncur[h])
                            if lev <= 4:
                                for q in range(4):
                                    h = 4 * half + q
                                    k.mm(Mbank[half][:, q * 128:(q + 1) * 128], Ncur[h], Mcur[h])
                        if lev >= 1:
                            for q in range(4):
                                h = 4 * half + q
                                k.mm(Ttbank[half][:, q * 128:(q + 1) * 128], Ncur[h], Ttb[:, (lev - 1) % 2, h, :],
                                     start=False, stop=(lev == 6), skip=True)
                            k.cp("act" if half == 0 else "dve", Ttb[:, lev % 2, 4 * half:4 * half + 4, :],
                                 Ttbank[half].rearrange("p (a b) -> p a b", a=4))
                        if lev <= 5:
                            k.cp("act" if (half == 0 and lev % 2 == 0) else "dve", Nn[:, lev % 2, 4 * half:4 * half + 4, :],
                                 Nbank[half].rearrange("p (a b) -> p a b", a=4))
                            if lev <= 4:
                                k.cp("act", Mn[:, lev % 2, 4 * half:4 * half + 4, :],
                                     Mbank[half].rearrange("p (a b) -> p a b", a=4))
                    if lev <= 5:
                        Ncur = [Nn[:, lev % 2, h, :] for h in range(8)]
                        if lev <= 4:
                            Mcur = [Mn[:, lev % 2, h, :] for h in range(8)]
                TT = Ttb[:, 0]
                for h in range(8):
                    cb, po = h // 2, 64 * (h % 2)
                    k.mm(W0a[:, h * 64:(h + 1) * 64], AK[:, h, 0:128], V[:, cb, j, po:po + 64])
                k.cp("act", G1s[:], W0a)
                for cb in range(4):
                    k.mm(W0b[:, cb * 128:(cb + 1) * 128], AR[:, cb, j, 0:128], Hblk[:, cb, :])
                k.tt("dve", G12[:], W0b, G1s[:], ALU.add)
                for h in range(8):
                    k.mm(W0a[:, h * 64:(h + 1) * 64], TT[:, h, :], G12[:, h * 64:(h + 1) * 64])
                k.cp("act", Ubf[:], W0a)
                for cb in range(4):
                    h0, h1 = 2 * cb, 2 * cb + 1
                    o0 = W0b[:, h0 * 64:(h0 + 1) * 64]
                    o1 = W0b[:, h1 * 64:(h1 + 1) * 64]
                    k.mm(o0, AK[:, h0, 128:256], V[:, cb, j, 0:64], start=True, stop=False, skip=True)
                    k.mm(o1, AK[:, h1, 128:256], V[:, cb, j, 64:128], start=False, stop=False, skip=True)
                    k.mm(W0b[:, cb * 128:(cb + 1) * 128], AR[:, cb, j, 128:256], Hblk[:, cb, :], start=False, stop=False, skip=True)
                    k.mm(o0, MA[:, h0, 128:256], Ubf[:, h0 * 64:(h0 + 1) * 64], start=False, stop=False, skip=True)
                    k.mm(o1, MA[:, h1, 128:256], Ubf[:, h1 * 64:(h1 + 1) * 64], start=False, stop=True, skip=True)
                for h in range(8):
                    cb, po = h // 2, 64 * (h % 2)
                    o = W1a[po:po + 64, cb * 64:(cb + 1) * 64]
                    k.mm(o, KH[:, cb, j, po:po + 64], V[:, cb, j, po:po + 64], start=True, stop=False)
                    k.mm(o, BH[:, cb, j, po:po + 64], Ubf[:, h * 64:(h + 1) * 64], start=False, stop=True)
                ysl = cnt % 2
                if d == 0:
                    k.cp("act", yo[:, ysl, :], W0b)
                    k.dma(S_YF[i], yo[:, ysl, :])
                else:
                    k.dma(yf[:, ysl, :], S_YF[i])
                    k.tt("dve", ysum[:], W0b, yf[:, ysl, :], ALU.add)
                for cb in range(4):
                    k.stt(Hs[:, cb * 64:(cb + 1) * 64], Hs[:, cb * 64:(cb + 1) * 64], ldWC[:, sl, cb, j:j + 1],
                          W1a[:, cb * 64:(cb + 1) * 64], ALU.mult, ALU.add)
                Hs3 = Hs[:].rearrange("p (a b) -> p a b", a=4)
                k.cp("pool", Hblk[0:64, :, 0:64], Hs3[0:64])
                k.cp("pool", Hblk[64:128, :, 64:128], Hs3[64:128])
                if d == 1:
                    y3 = ysum[:].rearrange("p (a b) -> p a b", a=8)
                    k.red(lnm[:, 0:8], y3)
                    k.ts("dve", lnm[:, 0:8], lnm[:, 0:8], 1.0 / 64, ALU.mult)
                    k.tt("dve", lnt[:].rearrange("p (a b) -> p a b", a=8), y3,
                         lnm[:, 0:8].unsqueeze(2).to_broadcast([128, 8, 64]), ALU.subtract)
                    k.tt("pool", ysum[:], lnt[:], lnt[:], ALU.mult)
                    k.red(lnm[:, 8:16], y3)
                    k.ts("dve", lnm[:, 8:16], lnm[:, 8:16], 1.0 / 64, ALU.mult, LNX_EPS, ALU.add)
                    k.tt("pool", lnm[:, 8:16], lnm[:, 8:16], cm05[:, 0:1].to_broadcast([128, 8]), ALU.pow)
                    k.tt("dve", lnt[:].rearrange("p (a b) -> p a b", a=8), lnt[:].rearrange("p (a b) -> p a b", a=8),
                         lnm[:, 8:16].unsqueeze(2).to_broadcast([128, 8, 64]), ALU.mult)
                    for cb in range(4):
                        k.tr(W1b[:, cb * 128:(cb + 1) * 128], lnt[:, cb * 128:(cb + 1) * 128], identf)
                    k.dma(ldBON[:, ysl], S_BON[m].rearrange("c p (t x) -> p c t x", t=NM)[:, :, j, :])
                    k.dma(ldGR[:, ysl], S_GR[m].rearrange("c p (t x) -> p c t x", t=NM)[:, :, j, :])
                    for cb in range(4):
                        k.act(ynT[:, cb * 128:(cb + 1) * 128], W1b[:, cb * 128:(cb + 1) * 128], AF.Identity,
                              bias=pc[:, PC_LB + cb:PC_LB + cb + 1], scale=pc[:, PC_LG + cb:PC_LG + cb + 1])
                    yn3 = ynT[:].rearrange("p (a b) -> p a b", a=4)
                    k.tt("pool", yn3, yn3, ldBON[:, ysl], ALU.add)
                    k.tt("pool", oMR[:, ysl], yn3, ldGR[:, ysl], ALU.mult)
                    k.dma(S_MR[i].rearrange("c p x -> p c x"), oMR[:, ysl])

            order = list(range(NMT)) if d == 0 else list(range(NMT - 1, -1, -1))
            load(order[0], 0)
            cnt = 0
            for idx, m in enumerate(order):
                sl = idx % 2
                if idx + 1 < len(order):
                    load(order[idx + 1], 1 - sl)
                js = range(NM) if d == 0 else range(NM - 1, -1, -1)
                for j in js:
                    tile(m, sl, j, cnt)
                    cnt += 1

        rwkv_pass(0)
        if cfg.upto == "B":
            with nc.Block() as block:
                S.emit(sems, dsems, block)
            return nc
        rwkv_pass(1)
        if cfg.upto == "C":
            with nc.Block() as block:
                S.emit(sems, dsems, block)
            return nc

        pass_reset()
        RS = 8
        EG = sb("EG", [128, 8, 15, 64], BF16)
        vts = sb("vts", [128, NT * 14], F32)
        Wo = sb("Wo", [128, 8, D], BF16)
        Kring = sb("Kring", [128, RS, 4, 128], BF16)
        Vring = sb("Vring", [128, RS, 4, 128], BF16)
        Qt = sb("Qt", [128, 2, 4, 128], BF16)
        ebuf = sb("ebuf", [128, 2, 8, 128], F32)
        pT = sb("pT", [128, 2, 7, 8, 128], BF16)
        gaT = sb("gaT", [128, 2, 4, 128], F32)
        mrT = sb("mrT", [128, 2, 4, 128], BF16)
        maT = sb("maT", [128, 2, 4, 128], BF16)
        dsb = sb("dsb", [128, 512], F32)
        otmp = sb("otmp", [128, 512], F32)
        xr = sb("xr", [128, 2, D], F32)
        yout = sb("yout", [128, 2, D], F32)
        onesb = sb("onesb", [128, 64], BF16)
        cm1 = sb("cm1", [128, 1], F32)
        k.memset("pool", onesb[:], 1.0)
        k.memset("pool", cm1[:], -1.0)
        k.dma(vts[:], vtab)
        wtmp = sb("wtmp", [128, 2, D], F32)
        for ek in range(8):
            k.dma(wtmp[:, ek % 2, :], w_out[ek * 128:(ek + 1) * 128, :])
            k.cp("pool" if ek % 2 == 0 else "dve", Wo[:, ek, :], wtmp[:, ek % 2, :])
        ohf = sb("ohf", [128, 4096], F32)
        ohb = sb("ohb", [128, 4096], BF16)
        rpf = sb("rpf", [128, 120], F32)
        rpe = sb("rpe", [128, 120], BF16)
        k.dma(ohf[0:31, :], ohot)
        k.dma(rpf[0:31, :], rpbT)
        k.cp("dve", ohb[0:31, :], ohf[0:31, :])
        k.act(rpe[0:31, :], rpf[0:31, :], AF.Exp)
        k.memset("pool", EG[64:128, :, 0:1, :], 0.0)
        for g in range(16):
            bank = psB[g % 2]
            for c in range(4):
                cq = g * 4 + c
                k.mm(bank[0:64, c * 120:(c + 1) * 120], ohb[0:31, cq * 64:(cq + 1) * 64], rpe[0:31, :])
                k.mm(bank[64:128, c * 120:(c + 1) * 120], ohb[0:31, cq * 64:(cq + 1) * 64], rpe[0:31, :])
            k.cp("act", EG[0:64, :, :, g * 4:(g + 1) * 4],
                 bank[0:64, 0:480].rearrange("p (c h u) -> p h u c", c=4, h=8))
            k.cp("dve", EG[64:128, :, 1:15, g * 4:(g + 1) * 4],
                 bank[64:128, 0:480].rearrange("p (c h u) -> p h u c", c=4, h=8)[:, :, 0:14, :])

        loaded = [-1]

        def ensure_keys(upto):
            while loaded[0] < min(upto, NT - 1):
                t = loaded[0] + 1
                mk, jk = t // NM, t % NM
                k.dma(Kring[:, t % RS], S_K[mk].rearrange("c p (t x) -> p c t x", t=NM)[:, :, jk, :])
                k.dma(Vring[:, t % RS], S_VA[mk].rearrange("c p (t x) -> p c t x", t=NM)[:, :, jk, :])
                loaded[0] = t

        zc = [0]
        for i in range(NT):
            m, j = i // NM, i % NM
            sl = i % 2
            pos = i % SEG
            offs = [-2, -1, 0, 1, 2]
            if pos == 0:
                offs.append(3)
            if pos == SEG - 1:
                offs.insert(0, -3)
            offs = [o for o in offs if 0 <= i + o < NT]
            ensure_keys(i + 3)
            k.dma(Qt[:, sl], S_Q[m].rearrange("c p (t x) -> p c t x", t=NM)[:, :, j, :])
            k.dma(gaT[:, sl], S_GA[m].rearrange("c p (t x) -> p c t x", t=NM)[:, :, j, :])
            k.dma(mrT[:, sl], S_MR[i].rearrange("c p x -> p c x"))
            k.dma(xr[:, sl, :], xs[i * 128:(i + 1) * 128, :])
            for ki, o in enumerate(offs):
                t = i + o
                ks = t % RS
                pz = psW[zc[0] % 2]
                es_ = zc[0] % 2
                zc[0] += 1
                for hh in range(2):
                    po = 64 * hh
                    for cb in range(4):
                        k.mm(pz[:, (hh * 4 + cb) * 128:(hh * 4 + cb + 1) * 128], Kring[po:po + 64, ks, cb, :],
                             Qt[po:po + 64, sl, cb, :])
                k.act(ebuf[:, es_].rearrange("p a b -> p (a b)"), pz[:], AF.Exp, scale=0.125)
                u0 = 7 - 2 * o
                for qh in range(2):
                    col = (i * 7 + (o + 3)) * 2 + qh
                    k.stt(pT[:, sl, ki, :, qh * 64:(qh + 1) * 64], ebuf[:, es_, :, qh * 64:(qh + 1) * 64],
                          vts[:, col:col + 1], EG[:, :, u0 + qh, :], ALU.mult, ALU.mult)
            nk = len(offs)
            for hh in range(2):
                po = 64 * hh
                for cb in range(4):
                    idx = hh * 4 + cb
                    oo = psB[0][po:po + 64, cb * 128:(cb + 1) * 128]
                    dd = psB[1][po:po + 64, cb * 128:(cb + 1) * 128]
                    for ki, o in enumerate(offs):
                        ks = (i + o) % RS
                        k.mm(oo, Vring[:, ks, cb, po:po + 64], pT[:, sl, ki, idx, :], start=(ki == 0), stop=(ki == nk - 1))
                    for ki, o in enumerate(offs):
                        k.mm(dd, onesb[:, :], pT[:, sl, ki, idx, :], start=(ki == 0), stop=(ki == nk - 1))
            k.recip(dsb[:], psB[1][:])
            k.tt("dve", otmp[:], psB[0][:], dsb[:], ALU.mult)
            k.tt("pool", maT[:, sl].rearrange("p a b -> p (a b)"), otmp[:], gaT[:, sl].rearrange("p a b -> p (a b)"), ALU.mult)
            for half in range(2):
                ob = psB[2 + half][:]
                for ek in range(8):
                    lh = mrT[:, sl, ek, :] if ek < 4 else maT[:, sl, ek - 4, :]
                    k.mm(ob, lh, Wo[:, ek, half * 512:(half + 1) * 512], start=(ek == 0), stop=(ek == 7))
                k.tt("dve", yout[:, sl, half * 512:(half + 1) * 512], ob, xr[:, sl, half * 512:(half + 1) * 512], ALU.add)
            k.dma(ys[i * 128:(i + 1) * 128, :], yout[:, sl, :])

        with nc.Block() as block:
            S.emit(sems, dsems, block)
    return nc


def _cols(v, nblk):
    return np.ascontiguousarray(np.asarray(v, np.float32).reshape(nblk, 128).T)


def make_consts():
    c = np.zeros((128, 6, 128), np.float32)
    i = np.arange(128)
    c[:, 0, :] = np.eye(128, dtype=np.float32)
    c[:, 1, :] = (i[:, None] < i[None, :])
    c[:, 2, :] = (i[:, None] > i[None, :])
    c[:, 3, :] = (i[:, None] <= i[None, :])
    c[:, 4, :] = (i[:, None] >= i[None, :])
    c[:, 5, :] = ((i[:, None] // 64) == (i[None, :] // 64))
    return c


def pack_params(p):
    pcols = np.zeros((128, NPC), np.float32)
    pcols[:, PC_MU:PC_MU + 17] = _cols(p["shift_mu"], 17)
    for d in range(2):
        pcols[:, PC_W0 + 4 * d:PC_W0 + 4 * d + 4] = _cols(p["w0"][d], 4)
        pcols[:, PC_A0 + 4 * d:PC_A0 + 4 * d + 4] = _cols(p["a0"][d], 4)
    pcols[:, PC_KK:PC_KK + 4] = _cols(p["k_k"], 4)
    pcols[:, PC_KA:PC_KA + 4] = _cols(p["k_a"], 4)
    pcols[:, PC_RK:PC_RK + 4] = _cols(p["r_k"].reshape(-1), 4)
    pcols[:, PC_LG:PC_LG + 4] = _cols(p["lnx_g"], 4)
    pcols[:, PC_LB:PC_LB + 4] = _cols(p["lnx_b"], 4)
    pcols[:, PC_QG] = np.tile(p["q_norm_g"], 2)
    pcols[:, PC_KG] = np.tile(p["k_norm_g"], 2)
    pcols[:, PC_NG:PC_NG + 8] = _cols(p["norm_g"], 8)
    wup = np.zeros((128, 2, 512), np.float32)
    wup[0:64] = np.transpose(p["w_up"], (1, 0, 2))
    wup[64:128] = np.transpose(p["a_up"], (1, 0, 2))
    cq = np.arange(64)
    ck = np.arange(64)
    cs = np.clip(cq - 8, 0, 48)
    inw = (ck[None, :] >= cs[:, None]) & (ck[None, :] < cs[:, None] + 16)
    dc = ck[None, :] - cq[:, None] + 15
    oh = np.zeros((31, 64, 64), np.float32)
    for q_ in range(64):
        for k_ in range(64):
            if inw[q_, k_]:
                oh[dc[q_, k_], q_, k_] = 1.0
    rp = np.asarray(p["rpb"], np.float32)
    rp = rp[:, ::-1, :]
    rp = rp.reshape(4, 2, 15, 31).transpose(3, 1, 0, 2)
    return {
        "ohot": np.ascontiguousarray(oh.reshape(31, 4096)),
        "rpbT": np.ascontiguousarray(rp.reshape(31, 120)),
        "w_in": np.ascontiguousarray(p["w_in"], np.float32),
        "w_out": np.ascontiguousarray(p["w_out"], np.float32),
        "pcols": pcols,
        "wup": wup,
        "consts": make_consts(),
    }


def make_core_meta(cfg, slots):
    NT, SEG, NSEG = cfg.NT, cfg.SEG, cfg.NSEG
    flags = np.zeros((128, NSEG + 1), np.float32)
    for s_ in range(1, NSEG):
        a, b = slots[s_ - 1], slots[s_]
        if a is not None and b is not None and a[0] == b[0] and b[1] == a[1] + 1:
            flags[:, s_] = 1.0
    vt = np.zeros((128, NT, 7, 2), np.float32)
    for i in range(NT):
        sl = slots[i // SEG]
        if sl is None:
            vt[:, i, 3, :] = 1.0
            continue
        sid, sidx, nseg = sl
        R = nseg * SEG * 2
        for qh in range(2):
            r = (sidx * SEG + (i % SEG)) * 2 + qh
            R0 = min(max(r - 4, 0), R - 8)
            for o in range(-3, 4):
                t = i + o
                if t < 0 or t >= NT:
                    continue
                sk = slots[t // SEG]
                if sk is None or sk[0] != sid:
                    continue
                if sk[1] - sidx != (t // SEG) - (i // SEG):
                    continue
                for kh in range(2):
                    rho = (sk[1] * SEG + (t % SEG)) * 2 + kh
                    if R0 <= rho < R0 + 8:
                        vt[kh * 64:(kh + 1) * 64, i, o + 3, qh] = 1.0
    return flags, np.ascontiguousarray(vt.reshape(128, NT * 14))


PARAM_NAMES = ["norm_g", "w_in", "shift_mu", "w0", "w_up", "a0", "a_up", "k_k", "k_a", "r_k",
               "lnx_g", "lnx_b", "q_norm_g", "k_norm_g", "rpb", "w_out"]


def run_layer(cfg, seqs, p, n_cores):
    NT, SEG, NSEG = cfg.NT, cfg.SEG, cfg.NSEG
    segtok = SEG * 128
    order = sorted(range(len(seqs)), key=lambda i: -seqs[i].shape[0])
    cores = [[] for _ in range(n_cores)]
    used = [0] * n_cores
    for si in order:
        ns = seqs[si].shape[0] // segtok
        c = min(range(n_cores), key=lambda c_: (used[c_] + ns > NSEG, used[c_]))
        assert used[c] + ns <= NSEG
        for g in range(ns):
            cores[c].append((si, g, ns))
        used[c] += ns
    shared = pack_params(p)
    in_maps = []
    for c in range(n_cores):
        slots = cores[c] + [None] * (NSEG - len(cores[c]))
        x = np.zeros((NT * 128, D), np.float32)
        for s_, sl in enumerate(slots):
            if sl is not None:
                x[s_ * segtok:(s_ + 1) * segtok] = seqs[sl[0]][sl[1] * segtok:(sl[1] + 1) * segtok]
        flags, vt = make_core_meta(cfg, slots)
        im = dict(shared)
        im["xs"] = x
        im["flags"] = flags
        im["vtab"] = vt
        in_maps.append(im)
    nc = build_program(cfg)
    res = run_bass_kernel_spmd(nc, in_maps, core_ids=list(range(n_cores)))
    outs = [np.zeros_like(s_) for s_ in seqs]
    for c in range(n_cores):
        y = np.asarray(res.results[c]["ys"])
        for s_, sl in enumerate(cores[c]):
            outs[sl[0]][sl[1] * segtok:(sl[1] + 1) * segtok] = y[s_ * segtok:(s_ + 1) * segtok]
    return outs, res


def kernel(**inputs):
    cfg = Cfg(128, 32)
    p = {n_: np.asarray(inputs[n_], np.float32)[0] for n_ in PARAM_NAMES}
    xp = np.asarray(inputs["x_prompt"], np.float32)
    xsm = np.asarray(inputs["x_sample"], np.float32)
    seqs = [xp[b] for b in range(xp.shape[0])] + [xsm[b] for b in range(xsm.shape[0])]
    outs, _ = run_layer(cfg, seqs, p, 8)
    yp = np.stack(outs[:xp.shape[0]], 0)
    ysm = np.stack(outs[xp.shape[0]:], 0)
    return (yp, ysm)
```

```python
import numpy as np
from contextlib import ExitStack
import concourse.bass as bass
import concourse.mybir as mybir
from concourse.bass_utils import run_bass_kernel_spmd

F32 = mybir.dt.float32
BF16 = mybir.dt.bfloat16
ALU = mybir.AluOpType
AF = mybir.ActivationFunctionType
AX = mybir.AxisListType

D = 1024
DIN = 4224
NBLK = 33
RMS_EPS = 1e-6
LNX_EPS = 64e-5
CDEC = 0.6065306597126334

ENGS = ["pe", "act", "dve", "pool", "sp"]
NDMASEM = 48


class Op:
    __slots__ = ("eng", "fn", "chan", "seq", "signal", "waits", "vc", "is_dma", "sigval", "rows",
                 "deps", "idx", "dur", "start", "finish", "succs", "npred", "ready")

    def __init__(self, eng, fn, is_dma):
        self.eng = eng
        self.fn = fn
        self.is_dma = is_dma
        self.signal = False
        self.waits = []
        self.vc = None
        self.chan = None
        self.seq = None
        self.sigval = None
        self.rows = None
        self.deps = {}
        self.dur = 500.0
        self.start = 0.0
        self.finish = 0.0
        self.succs = []
        self.npred = 0
        self.ready = 0.0


def _region(ap):
    name = ap.tensor.name
    pat = ap.ap
    off = ap.offset
    es = mybir.dt.size(ap.dtype)
    if "DRAM" in str(ap.space).upper():
        ext = 0
        for st, cnt in pat:
            ext += (cnt - 1) * abs(st)
        return name, 0, 1, off * es, (off + ext + 1) * es
    row = pat[0][0]
    npart = pat[0][1]
    if row == 0:
        p0 = 0
        f0 = off
    else:
        p0 = off // row
        f0 = off - p0 * row
    ext = 0
    for st, cnt in pat[1:]:
        ext += (cnt - 1) * abs(st)
    return name, p0, p0 + npart, f0 * es, (f0 + ext + 1) * es


def _free_elems(ap):
    n = 1
    for st, cnt in ap.ap[1:]:
        n *= cnt
    return n


class Sched:
    def __init__(self, nc, same_engine_sync=True, reorder=True):
        self.nc = nc
        self.ops = {e: [] for e in ENGS}
        self.all_ops = []
        self.hist = {}
        self.dma_uses = [0] * NDMASEM
        self.same_engine_sync = same_engine_sync
        self.reorder = reorder
        import os
        self.max_ops = int(os.environ.get("MAXOPS", "100000000"))
        if os.environ.get("NOREORDER"):
            self.reorder = False
        self.nadd = 0
        self.log = []

    @staticmethod
    def _dep(op, A, kind):
        r = op.deps.get(id(A))
        if r is None:
            op.deps[id(A)] = [A, kind]
        elif kind < r[1] or (kind == 1 and r[1] == 2):
            r[1] = kind if not (r[1] == 0) else 0

    def add(self, eng, fn, reads=(), writes=(), dma=False, rows=None):
        self.nadd += 1
        if self.nadd > self.max_ops:
            return None
        op = Op(eng, fn, dma)
        op.rows = rows
        op.idx = len(self.all_ops)
        accesses = []
        nel = 0
        for ap, is_w in [(a, False) for a in reads] + [(a, True) for a in writes]:
            if ap is None or isinstance(ap, (int, float)):
                continue
            name, p0, p1, f0, f1 = _region(ap)
            if is_w and nel == 0:
                nel = _free_elems(ap)
                if dma:
                    nel = nel * mybir.dt.size(ap.dtype) * (p1 - p0)
            accesses.append((name, p0, p1, f0, f1, is_w, False))
            if "PSUM" in str(ap.space).upper():
                b0 = f0 // 2048
                b1 = (f1 - 1) // 2048
                accesses.append((name + "#bank", 0, 128, b0 * 2048, (b1 + 1) * 2048, True, True))
        if dma:
            op.dur = 2000.0 + nel / 150.0
        elif eng == "pe":
            op.dur = 45.0 + nel * 0.5
        elif eng == "act":
            op.dur = 250.0 + nel * 0.8
        elif eng == "dve":
            op.dur = 150.0 + nel * 1.0
        else:
            op.dur = 400.0 + nel * 2.0
        acc2 = []
        for name, p0, p1, f0, f1, is_w, cross_only in accesses:
            if name == "big":
                b = f0 // 512
                while b * 512 < f1:
                    acc2.append(((name, b), p0, p1, max(f0, b * 512), min(f1, (b + 1) * 512), is_w, cross_only))
                    b += 1
            else:
                acc2.append((name, p0, p1, f0, f1, is_w, cross_only))
        for name, p0, p1, f0, f1, is_w, cross_only in acc2:
            h = self.hist.get(name)
            if h is None:
                h = []
            newh = []
            for rec in h:
                rp0, rp1, rf0, rf1, rop, rw = rec
                if rop is op:
                    newh.append(rec)
                    continue
                ov = (rp0 < p1 and p0 < rp1 and rf0 < f1 and f0 < rf1)
                if cross_only:
                    if ov:
                        if rop.eng != eng or rop.is_dma or dma:
                            self._dep(op, rop, 0)
                        elif eng == "pe" and rows is not None and rop.rows is not None \
                                and (rop.rows[1] <= rows[0] or rows[1] <= rop.rows[0]):
                            self._dep(op, rop, 1)
                        else:
                            self._dep(op, rop, 2)
                    if rop.eng == eng and rp0 >= p0 and rp1 <= p1 and rf0 >= f0 and rf1 <= f1:
                        continue
                    newh.append(rec)
                    continue
                if ov and (rw or is_w):
                    self._dep(op, rop, 0)
                if is_w and rp0 >= p0 and rp1 <= p1 and rf0 >= f0 and rf1 <= f1:
                    continue
                if (not is_w) and (not rw) and (not rop.is_dma) and (not dma) and rop.eng == eng \
                        and rp0 == p0 and rp1 == p1 and rf0 == f0 and rf1 == f1:
                    self._dep(op, rop, 2)
                    continue
                newh.append(rec)
            newh.append((p0, p1, f0, f1, op, is_w))
            self.hist[name] = newh
        self.all_ops.append(op)
        self.ops[eng].append(op)
        return op

    def _schedule(self):
        import heapq
        ops = self.all_ops
        for op in ops:
            op.npred = len(op.deps)
            op.ready = 0.0
        for op in ops:
            for A, kind in op.deps.values():
                A.succs.append((op, kind))
        heaps = {e: [] for e in ENGS}
        free = {e: 0.0 for e in ENGS}
        for op in ops:
            if op.npred == 0:
                heapq.heappush(heaps[op.eng], (op.ready, op.idx, op))
        order = []
        new_streams = {e: [] for e in ENGS}
        n = len(ops)
        while len(order) < n:
            best_e = None
            best_t = None
            for e in ENGS:
                hp = heaps[e]
                if not hp:
                    continue
                t = max(free[e], hp[0][0])
                if best_t is None or t < best_t or (t == best_t and hp[0][1] < heaps[best_e][0][1]):
                    best_t = t
                    best_e = e
            assert best_e is not None, "scheduler deadlock (cyclic hazards?)"
            _, _, op = heapq.heappop(heaps[best_e])
            op.start = best_t
            if op.is_dma:
                free[best_e] = best_t + 60.0
                op.finish = best_t + op.dur
            else:
                free[best_e] = best_t + op.dur
                op.finish = best_t + op.dur
            order.append(op)
            new_streams[best_e].append(op)
            for sop, kind in op.succs:
                if kind == 2:
                    r = op.start + 1.0
                elif op.eng == "pe" and sop.eng == "pe" and kind == 0:
                    r = op.start + min(op.dur, 64.0)
                else:
                    r = op.finish + 180.0
                if r > sop.ready:
                    sop.ready = r
                sop.npred -= 1
                if sop.npred == 0:
                    heapq.heappush(heaps[sop.eng], (sop.ready, sop.idx, sop))
        self.ops = new_streams
        return order

    def finalize(self):
        if self.reorder:
            order = self._schedule()
        else:
            order = self.all_ops
        last = {e: None for e in ENGS}
        dma_last = [None] * NDMASEM
        rr = 0
        cnt = {e: 0 for e in ENGS}
        for op in order:
            eng = op.eng
            prev = last[eng]
            vc = dict(prev.vc) if prev is not None else {}
            deps = list(op.deps.values())
            if op.is_dma:
                j = rr
                rr = (rr + 1) % NDMASEM
                if dma_last[j] is not None:
                    deps.append([dma_last[j], 0])
                op.chan = ("d", j)
                op.seq = self.dma_uses[j]
                self.dma_uses[j] += 1
                dma_last[j] = op
                op.signal = True
            else:
                op.chan = eng
                op.seq = cnt[eng]
            cnt[eng] += 1
            best = {}
            for A, kind in deps:
                if kind == 2:
                    continue
                if A.chan == op.chan and not op.is_dma:
                    if kind != 1 and (eng == "pe" or not self.same_engine_sync):
                        continue
                if vc.get(A.chan, -1) >= A.seq:
                    continue
                if A.chan not in best or best[A.chan].seq < A.seq:
                    best[A.chan] = A
            op.waits = []
            for ch, A in best.items():
                if vc.get(A.chan, -1) >= A.seq:
                    continue
                A.signal = True
                op.waits.append(A)
                for kk_, v in A.vc.items():
                    if vc.get(kk_, -1) < v:
                        vc[kk_] = v
                if vc.get(A.chan, -1) < A.seq:
                    vc[A.chan] = A.seq
            op.vc = vc
            last[eng] = op

    def emit(self, sems, dsems, block):
        self.finalize()
        for e in ENGS:
            c = 0
            for op in self.ops[e]:
                if op.is_dma:
                    op.sigval = 16 * (op.seq + 1)
                elif op.signal:
                    c += 1
                    op.sigval = c
        sched = self

        def run(engname, engobj):
            for op in sched.ops[engname]:
                for A in op.waits:
                    if A.is_dma:
                        engobj.wait_ge(dsems[A.chan[1]], A.sigval)
                    else:
                        engobj.wait_ge(sems[A.chan], A.sigval)
                inst = op.fn(engobj)
                if op.is_dma:
                    inst.then_inc(dsems[op.chan[1]], 16)
                elif op.signal:
                    inst.then_inc(sems[op.chan], 1)

        @block.tensor
        def _(e):
            run("pe", e)

        @block.scalar
        def _(e):
            run("act", e)

        @block.vector
        def _(e):
            run("dve", e)

        @block.gpsimd
        def _(e):
            run("pool", e)

        @block.sync
        def _(e):
            run("sp", e)
            for j in range(NDMASEM):
                if sched.dma_uses[j] > 0:
                    e.wait_ge(dsems[j], 16 * sched.dma_uses[j])


class K:
    def __init__(self, S):
        self.S = S
        self.cap = None

    def _add(self, eng, fn, reads=(), writes=(), dma=False, rows=None):
        if self.cap is not None:
            self.cap.append((eng, fn, reads, writes, dma, rows))
            return None
        return self.S.add(eng, fn, reads=reads, writes=writes, dma=dma, rows=rows)

    def merge(self, lists, store_delay=0):
        idx = [0] * len(lists)
        alive = True
        held = []
        n = 0
        while alive:
            alive = False
            for c, l in enumerate(lists):
                if idx[c] < len(l):
                    item = l[idx[c]]
                    idx[c] += 1
                    alive = True
                    eng, fn, reads, writes, dma, rows = item
                    if dma and store_delay > 0 and "DRAM" in str(writes[0].space).upper():
                        held.append((n + store_delay, item))
                    else:
                        self.S.add(eng, fn, reads=reads, writes=writes, dma=dma, rows=rows)
                    n += 1
                    while held and held[0][0] <= n:
                        eng, fn, reads, writes, dma, rows = held.pop(0)[1]
                        self.S.add(eng, fn, reads=reads, writes=writes, dma=dma, rows=rows)
        for _, item in held:
            eng, fn, reads, writes, dma, rows = item
            self.S.add(eng, fn, reads=reads, writes=writes, dma=dma, rows=rows)

    def dma(self, out, in_, eng="sp"):
        return self._add(eng, lambda e: e.dma_start(out=out, in_=in_), reads=[in_], writes=[out], dma=True)

    def mm(self, out, lhsT, rhs, start=True, stop=True, skip=False):
        rg = _region(lhsT)
        rows = (rg[1], rg[2])
        if skip:
            return self._add("pe", lambda e: e.matmul(out, lhsT=lhsT, rhs=rhs, start=start, stop=stop,
                                                       skip_group_check=True),
                              reads=[lhsT, rhs], writes=[out], rows=rows)
        return self._add("pe", lambda e: e.matmul(out, lhsT=lhsT, rhs=rhs, start=start, stop=stop),
                          reads=[lhsT, rhs], writes=[out], rows=rows)

    def red(self, out, in_, op=ALU.add):
        return self._add("dve", lambda e: e.tensor_reduce(out=out, in_=in_, axis=AX.X, op=op),
                          reads=[in_], writes=[out])

    def recip(self, out, in_):
        return self._add("dve", lambda e: e.reciprocal(out=out, in_=in_), reads=[in_], writes=[out])

    def tr(self, out, in_, ident):
        rg = _region(in_)
        return self._add("pe", lambda e: e.transpose(out, in_, ident), reads=[in_, ident], writes=[out],
                          rows=(rg[1], rg[2]))

    def act(self, out, in_, func, bias=None, scale=None, eng="act"):
        kw = {}
        rd = [in_]
        if bias is not None:
            kw["bias"] = bias
            rd.append(bias)
        if scale is not None:
            kw["scale"] = scale
            rd.append(scale)
        return self._add("act", lambda e: e.activation(out=out, in_=in_, func=func, **kw), reads=rd, writes=[out])

    def tt(self, eng, out, a, b, op):
        return self._add(eng, lambda e: e.tensor_tensor(out=out, in0=a, in1=b, op=op), reads=[a, b], writes=[out])

    def ts(self, eng, out, a, s1, op0, s2=None, op1=None):
        rd = [a, s1, s2]
        if op1 is None:
            return self._add(eng, lambda e: e.tensor_scalar(out=out, in0=a, scalar1=s1, scalar2=None, op0=op0),
                              reads=rd, writes=[out])
        return self._add(eng, lambda e: e.tensor_scalar(out=out, in0=a, scalar1=s1, scalar2=s2, op0=op0, op1=op1),
                          reads=rd, writes=[out])

    def stt(self, out, a, scalar, b, op0, op1):
        return self._add("dve", lambda e: e.scalar_tensor_tensor(out=out, in0=a, scalar=scalar, in1=b, op0=op0, op1=op1),
                          reads=[a, scalar, b], writes=[out])

    def cp(self, eng, out, in_):
        if eng == "act":
            return self.act(out, in_, AF.Copy)
        return self._add(eng, lambda e: e.tensor_copy(out=out, in_=in_), reads=[in_], writes=[out])

    def memset(self, eng, out, val):
        return self._add(eng, lambda e: e.memset(out, val), writes=[out])

    def scan(self, out, d0, d1, init, op0, op1):
        return self._add("dve", lambda e: e.tensor_tensor_scan(out=out, data0=d0, data1=d1, initial=init, op0=op0, op1=op1),
                          reads=[d0, d1, init], writes=[out])

    def ttr(self, out, a, b, accum):
        return self._add("dve", lambda e: e.tensor_tensor_reduce(out=out, in0=a, in1=b, scale=1.0, scalar=0.0,
                                                                  op0=ALU.mult, op1=ALU.add, accum_out=accum),
                          reads=[a, b], writes=[out, accum])


class Cfg:
    def __init__(self, NT, SEG, debug=False, upto="all"):
        self.upto = upto
        self.NT = NT
        self.SEG = SEG
        self.NM = 4
        assert NT % SEG == 0 and SEG % self.NM == 0
        self.NSEG = NT // SEG
        self.NMT = NT // self.NM
        self.debug = debug


PC_MU = 0
PC_W0 = 17
PC_A0 = 25
PC_KK = 33
PC_KA = 37
PC_RK = 41
PC_LG = 45
PC_LB = 49
PC_QG = 53
PC_KG = 54
PC_NG = 55
NPC = 63


def build_program(cfg):
    NT, SEG, NM, NSEG, NMT = cfg.NT, cfg.SEG, cfg.NM, cfg.NSEG, cfg.NMT
    TOK = NM * 128
    nc = bass.Bass("TRN2", target_bir_lowering=False)
    okind = "ExternalOutput" if cfg.debug else "Internal"

    def din(name, shape, dt=F32):
        return nc.dram_tensor(name, list(shape), dt, kind="ExternalInput").ap()

    def dscr(name, shape, dt):
        return nc.dram_tensor(name, list(shape), dt, kind=okind).ap()

    xs = din("xs", [NT * 128, D])
    w_in = din("w_in", [D, DIN])
    w_out = din("w_out", [D, D])
    pcols = din("pcols", [128, NPC])
    flags = din("flags", [128, NSEG + 1])
    wup = din("wup", [128, 2, 512])
    consts = din("consts", [128, 6, 128])
    ohot = din("ohot", [31, 4096])
    rpbT = din("rpbT", [31, 120])
    vtab = din("vtab", [128, NT * 14])
    ys = nc.dram_tensor("ys", [NT * 128, D], F32, kind="ExternalOutput").ap()

    Wb = dscr("Wb", [NBLK, 128, 8 * 128], BF16)
    S_AR = dscr("S_AR", [2, NMT, 4, 128, NM * 256], BF16)
    S_BT = dscr("S_BT", [2, NMT, 4, 128, NM * 128], BF16)
    S_KT = dscr("S_KT", [2, NMT, 4, 128, NM * 128], BF16)
    S_BH = dscr("S_BH", [2, NMT, 4, 128, NM * 128], BF16)
    S_KH = dscr("S_KH", [2, NMT, 4, 128, NM * 128], BF16)
    S_V = dscr("S_V", [NMT, 4, 128, NM * 128], BF16)
    S_WC = dscr("S_WC", [2, NMT, 4, 128, NM], F32)
    S_GR = dscr("S_GR", [NMT, 4, 128, TOK], F32)
    S_BON = dscr("S_BON", [NMT, 4, 128, TOK], F32)
    S_Q = dscr("S_Q", [NMT, 4, 128, TOK], BF16)
    S_K = dscr("S_K", [NMT, 4, 128, TOK], BF16)
    S_VA = dscr("S_VA", [NMT, 4, 128, NM * 128], BF16)
    S_GA = dscr("S_GA", [NMT, 4, 128, TOK], F32)
    S_YF = dscr("S_YF", [NT, 128, 512], F32)
    S_MR = dscr("S_MR", [NT, 4, 128, 128], BF16)

    with ExitStack() as es:
        BIGB = 186 * 1024
        big = es.enter_context(nc.sbuf_tensor("big", [128, BIGB // 2], BF16))
        alloc = {"off": 0, "base": 0}

        def sb(name, shape, dt):
            n = 1
            for d_ in shape[1:]:
                n *= d_
            nbytes = n * mybir.dt.size(dt)
            nbytes = (nbytes + 31) // 32 * 32
            off = alloc["off"]
            assert off + nbytes <= BIGB, (name, off, nbytes)
            alloc["off"] = off + nbytes
            v = big[:, off // 2:(off + nbytes) // 2]
            if dt != BF16:
                v = v.bitcast(dt)
            v = v[:, 0:n]
            if len(shape) == 2:
                return v
            names = " ".join("a%d" % i for i in range(len(shape) - 1))
            kw = {"a%d" % i: shape[i + 1] for i in range(len(shape) - 1)}
            return v.rearrange("p (%s) -> p %s" % (names, names), **kw)

        def pass_reset():
            alloc["off"] = alloc["base"]

        def pst(name, shape, dt=F32):
            return es.enter_context(nc.psum_tensor(name, list(shape), dt))

        sems = {e: es.enter_context(nc.semaphore("s_" + e)) for e in ENGS}
        dsems = [es.enter_context(nc.semaphore("d%d" % j)) for j in range(NDMASEM)]
        S = Sched(nc)
        k = K(S)

        psW = [pst("psW0", [128, 1024]), pst("psW1", [128, 1024])]
        psB = [pst("ps%d" % i, [128, 512]) for i in range(4)]

        cst = sb("cst", [128, 6, 128], F32)
        cstb = sb("cstb", [128, 6, 128], BF16)
        pc = sb("pc", [128, NPC], F32)
        pcd = sb("pcd", [128, 64], F32)
        flg = sb("flg", [128, NSEG + 1], F32)
        wupf = sb("wupf", [128, 2, 512], F32)
        wupb = sb("wupb", [128, 2, 512], BF16)
        cm05 = sb("cm05", [128, 8], F32)
        k.memset("pool", cm05[:], -0.5)
        k.dma(cst[:], consts)
        k.dma(pc[:], pcols)
        k.dma(flg[:], flags)
        k.dma(wupf[:], wup)
        k.cp("dve", cstb[:], cst[:])
        k.cp("dve", wupb[:], wupf[:])
        identb = cstb[:, 0, :]
        identf = cst[:, 0, :]
        blockones = cstb[:, 5, :]
        PD_OMM, PD_HM, PD_W0H, PD_A0H, PD_OMKA = 0, 17, 34, 42, 50
        k.ts("dve", pcd[:, PD_OMM:PD_OMM + 17], pc[:, PC_MU:PC_MU + 17], -1.0, ALU.mult, 1.0, ALU.add)
        k.ts("dve", pcd[:, PD_HM:PD_HM + 17], pc[:, PC_MU:PC_MU + 17], 0.5, ALU.mult)
        k.ts("dve", pcd[:, PD_W0H:PD_W0H + 16], pc[:, PC_W0:PC_W0 + 16], 0.5, ALU.mult)
        k.ts("dve", pcd[:, PD_OMKA:PD_OMKA + 4], pc[:, PC_KA:PC_KA + 4], -1.0, ALU.mult, 1.0, ALU.add)

        alloc["base"] = alloc["off"]
        arena = sb("arena", [128, 2 * 12 * 512], F32)
        wst = arena[:, 0:DIN]
        wstb = arena[:, DIN:DIN + DIN // 2].bitcast(BF16)
        CH = [0]
        for dk in range(8):
            k.dma(wst, w_in[dk * 128:(dk + 1) * 128, :])
            if dk % 2 == 0:
                k.ts("dve", wstb[:], wst, pc[:, PC_NG + dk:PC_NG + dk + 1], ALU.mult)
            else:
                k.act(wstb[:], wst, AF.Copy, scale=pc[:, PC_NG + dk:PC_NG + dk + 1])
            k.dma(Wb.rearrange("b p (k c) -> p b k c", k=8)[:, :, dk, :],
                  wstb[:].rearrange("p (b c) -> p b c", c=128))

        xt = sb("xt", [128, 2, D], F32)
        hb = sb("hb", [128, 2, D], BF16)
        junk = sb("junk", [128, D], BF16)
        ssq = sb("ssq", [128, NT], F32)
        rstd = sb("rstd", [128, NT], F32)
        hT = sb("hT", [128, 2, 8, 514], BF16)
        wblk = sb("wblk", [128, 4, 8 * 128], BF16)
        zraw = sb("zraw", [128, 2, 514], F32)
        ztmp = sb("ztmp", [128, 2, 512], F32)
        zsl = sb("zsl", [128, 512], F32)
        zsc = sb("zsc", [128, 2, 4, 512], F32)
        lorab = sb("lorab", [128, 512], BF16)
        def T(i):
            return arena[:, (CH[0] * 12 + i) * 512:(CH[0] * 12 + i + 1) * 512]
        resetm = sb("resetm", [128, 512], F32)
        k.memset("pool", resetm[:], 1.0)
        for c in range(NM):
            k.memset("pool", resetm[:, c * 128:c * 128 + 1], 0.0)
        oAR = sb("oAR", [128, 2, 2, NM, 256], BF16)
        oBT = sb("oBT", [128, 2, 2, NM, 128], BF16)
        oKT = sb("oKT", [128, 2, 2, NM, 128], BF16)
        oBHc2 = sb("oBHc", [128, 2, 2, 512], BF16)
        oKHc2 = sb("oKHc", [128, 2, 2, 512], BF16)
        oVc2 = sb("oVc", [128, 2, 512], BF16)
        oBH = sb("oBH", [128, 2, 2, NM, 128], BF16)
        oKH = sb("oKH", [128, 2, 2, NM, 128], BF16)
        oV = sb("oV", [128, 2, NM, 128], BF16)
        oWC = sb("oWC", [128, 2, 2, NM], F32)
        oGR = sb("oGR", [128, 2, 512], F32)
        oBON = sb("oBON", [128, 2, 512], F32)
        oQ = sb("oQ", [128, 2, 512], BF16)
        oVAc2 = sb("oVAc", [128, 2, 512], BF16)
        oVA = sb("oVA", [128, 2, NM, 128], BF16)
        oGA = sb("oGA", [128, 2, 512], F32)

        ptr = [psB[2][:].bitcast(BF16), psB[3][:].bitcast(BF16)]

        def stage_x(i):
            sl = i % 2
            m = i // NM
            j = i % NM
            k.dma(xt[:, sl, :], xs[i * 128:(i + 1) * 128, :])
            S.add("act", lambda e, sl=sl, i=i: e.activation(out=junk[:], in_=xt[:, sl, :], func=AF.Square,
                                                          accum_out=ssq[:, i:i + 1]),
                  reads=[xt[:, sl, :]], writes=[junk[:], ssq[:, i:i + 1]])
            k.ts("dve", rstd[:, i:i + 1], ssq[:, i:i + 1], 1.0 / D, ALU.mult, RMS_EPS, ALU.add)
            k.tt("pool", rstd[:, i:i + 1], rstd[:, i:i + 1], cm05[:, 0:1], ALU.pow)
            k.act(hb[:, sl, :], xt[:, sl, :], AF.Copy, scale=rstd[:, i:i + 1])
            pt = ptr[i % 2].rearrange("p (a b) -> p a b", a=8)
            for dk in range(8):
                k.tr(pt[:, dk, :], hb[:, sl, dk * 128:(dk + 1) * 128], identb)
            k.cp("dve" if i % 2 == 0 else "act", hT[:, m % 2, :, 1 + j * 128:1 + (j + 1) * 128], pt)

        def halo_left(m):
            sl = m % 2
            t0 = m * NM
            if m == 0:
                k.memset("pool", hT[:, sl, :, 0:1], 0.0)
            elif t0 % SEG == 0:
                s = t0 // SEG
                k.ts("pool", hT[:, sl, :, 0:1], hT[:, 1 - sl, :, 512:513], flg[:, s:s + 1], ALU.mult)
            else:
                k.cp("pool", hT[:, sl, :, 0:1], hT[:, 1 - sl, :, 512:513])

        def halo_right(m):
            sl = m % 2
            t0 = m * NM
            if m == NMT - 1:
                k.memset("pool", hT[:, sl, :, 513:514], 0.0)
            elif (t0 + NM) % SEG == 0:
                s = (t0 + NM) // SEG
                k.ts("pool", hT[:, sl, :, 513:514], hT[:, 1 - sl, :, 1:2], flg[:, s:s + 1], ALU.mult)
            else:
                k.cp("pool", hT[:, sl, :, 513:514], hT[:, 1 - sl, :, 1:2])

        wcount = [0, 0]
        wq = [[], []]
        wready = [[], []]

        def w_begin(blocks):
            c = CH[0]
            wq[c] = list(blocks)
            wready[c] = []
            w_prefetch()
            w_prefetch()

        def w_prefetch():
            c = CH[0]
            if wq[c]:
                b = wq[c].pop(0)
                sl = c * 2 + wcount[c] % 2
                wcount[c] += 1
                k.dma(wblk[:, sl, :], Wb[b])
                wready[c].append((b, sl))

        def load_w(b):
            c = CH[0]
            if not wready[c]:
                w_begin([b])
            bb, sl = wready[c].pop(0)
            assert bb == b, (bb, b)
            return wblk[:, sl, :].rearrange("p (k c) -> p k c", k=8)

        zcount = [0]

        def proj_rwkv(m, b, dst):
            w = load_w(b)
            pz = psW[CH[0]]
            zr = zraw[:, CH[0], :]
            zt = ztmp[:, CH[0], :]
            sl = m % 2
            for half in range(2):
                for dk in range(8):
                    k.mm(pz[:, half * 512:half * 512 + 257], w[:, dk, :], hT[:, sl, dk, half * 257:(half + 1) * 257],
                         start=(dk == 0), stop=(dk == 7))
            w_prefetch()
            k.act(zr.rearrange("p (a b) -> p a b", a=2), pz.rearrange("p (a b) -> p a b", a=2)[:, :, 0:257], AF.Copy)
            k.act(dst, zr[:, 1:513], AF.Copy, scale=pcd[:, PD_OMM + b:PD_OMM + b + 1])
            if b % 3 == 0:
                k.tt("pool", zt, zr[:, 0:512], zr[:, 2:514], ALU.add)
                k.stt(dst, zt, pcd[:, PD_HM + b:PD_HM + b + 1], dst, ALU.mult, ALU.add)
            else:
                k.stt(dst, zr[:, 0:512], pcd[:, PD_HM + b:PD_HM + b + 1], dst, ALU.mult, ALU.add)
                k.stt(dst, zr[:, 2:514], pcd[:, PD_HM + b:PD_HM + b + 1], dst, ALU.mult, ALU.add)

        def proj_attn(m, b):
            w = load_w(b)
            pz = psW[CH[0]]
            sl = m % 2
            for dk in range(8):
                k.mm(pz[:, 0:512], w[:, dk, :], hT[:, sl, dk, 1:513], start=(dk == 0), stop=(dk == 7))
            w_prefetch()
            return pz[:, 0:512]

        def sigm_from_tanh(eng, out, th):
            k.ts(eng, out, th, 0.5, ALU.mult, 0.5, ALU.add)

        def passA_macro(m):
            CH[0] = 0
            proj_rwkv(m, 16, zsl[:])
            k.act(lorab[0:64, :], zsl[0:64, :], AF.Tanh)
            k.cp("pool", lorab[64:128, :], zsl[64:128, :])

            def rwkv_cb(cb):
                osl = cb % 2
                CH[0] = osl
                oVc = oVc2[:, osl]
                oBHc = oBHc2[:, osl]
                oKHc = oKHc2[:, osl]
                zs_ = zsc[:, osl]
                w_begin([cb, 4 + cb, 8 + cb, 12 + cb])
                proj_rwkv(m, 0 + cb, zs_[:, 0, :])
                proj_rwkv(m, 4 + cb, zs_[:, 1, :])
                proj_rwkv(m, 8 + cb, zs_[:, 2, :])
                proj_rwkv(m, 12 + cb, zs_[:, 3, :])
                r_, k_, v_, g_ = zs_[:, 0, :], zs_[:, 1, :], zs_[:, 2, :], zs_[:, 3, :]
                th = T(0)
                k.act(th, g_, AF.Tanh, scale=0.5)
                sigm_from_tanh("dve", th, th)
                k.tt("pool", oGR[:, osl, :], th, g_, ALU.mult)
                k.dma(S_GR[m, cb], oGR[:, osl, :])
                k.cp("dve", oVc, v_)
                pt = ptr[CH[0]].rearrange("p (a b) -> p a b", a=8)
                for j in range(NM):
                    k.tr(pt[:, j, :], oVc[:, j * 128:(j + 1) * 128], identb)
                k.cp("act", oV[:, osl, :, :], pt[:, 0:NM, :])
                k.dma(S_V[m, cb].rearrange("p (a b) -> p a b", a=NM), oV[:, osl, :, :])
                kk = T(1)
                k.ts("dve", kk, k_, pc[:, PC_KK + cb:PC_KK + cb + 1], ALU.mult)
                kk2 = T(2).bitcast(BF16)[:, 0:512]
                k.tt("pool", kk2, kk, kk, ALU.mult)
                pn = psB[CH[0]]
                k.mm(pn[:, :], blockones, kk2, start=True, stop=True)
                rn = T(2)
                k.ts("dve", rn, pn[:, :], 1e-18, ALU.max)
                k.act(rn, rn, AF.Ln)
                k.act(rn, rn, AF.Exp, scale=-0.5)
                kkn = T(1)
                k.tt("pool", kkn, kk, rn, ALU.mult)
                bon = T(3)
                for d in range(2):
                    pw = psB[CH[0]]
                    pa = psB[CH[0]]
                    k.mm(pw[:, :], wupb[0:64, d, cb * 128:(cb + 1) * 128], lorab[0:64, :])
                    lw = T(4)
                    k.act(lw, pw[:, :], AF.Tanh, bias=pcd[:, PD_W0H + d * 4 + cb:PD_W0H + d * 4 + cb + 1], scale=0.5)
                    k.mm(pa[:, :], wupb[64:128, d, cb * 128:(cb + 1) * 128], lorab[64:128, :])
                    k.ts("dve", lw, lw, -0.5 * CDEC, ALU.mult, -0.5 * CDEC, ALU.add)
                    a_ = T(5)
                    k.act(a_, pa[:, :], AF.Tanh, bias=pcd[:, PD_A0H + d * 4 + cb:PD_A0H + d * 4 + cb + 1], scale=0.5)
                    sigm_from_tanh("dve", a_, a_)
                    kd = T(6)
                    k.ts("dve", kd, a_, pc[:, PC_KA + cb:PC_KA + cb + 1], ALU.mult,
                         pcd[:, PD_OMKA + cb:PD_OMKA + cb + 1], ALU.add)
                    k.tt("pool", kd, kd, k_, ALU.mult)
                    kka = T(5)
                    k.tt("pool", kka, kkn, a_, ALU.mult)
                    rk = T(7) if d == 1 else bon
                    k.stt(rk, kd, pc[:, PC_RK + cb:PC_RK + cb + 1], r_, ALU.mult, ALU.mult)
                    if d == 1:
                        k.tt("pool", bon, bon, rk, ALU.add)
                    cl = T(7)
                    k.scan(cl, resetm[:], lw, 0.0, ALU.mult, ALU.add)
                    cl3 = cl.rearrange("p (a b) -> p a b", a=NM)
                    lw3 = lw.rearrange("p (a b) -> p a b", a=NM)
                    if d == 1:
                        tot = T(8)[:, 0:NM]
                        k.cp("dve", tot, cl3[:, :, 127])
                        k.tt("dve", cl, lw, cl, ALU.subtract)
                        k.tt("dve", cl3, cl3, tot.unsqueeze(2).to_broadcast([128, NM, 128]), ALU.add)
                    clC = T(8)[:, 8:8 + NM]
                    k.cp("dve", clC, cl3[:, :, 127] if d == 0 else cl3[:, :, 0])
                    k.act(oWC[:, osl, d, :], clC, AF.Exp)
                    Ep = T(9)
                    Em = T(10)
                    Eh = T(11)
                    k.act(Ep, cl, AF.Exp)
                    k.act(Em, cl, AF.Exp, scale=-1.0)
                    for j in range(NM):
                        k.act(Eh[:, j * 128:(j + 1) * 128], cl[:, j * 128:(j + 1) * 128], AF.Exp,
                              bias=clC[:, j:j + 1], scale=-1.0)
                    Ep3 = Ep.rearrange("p (a b) -> p a b", a=NM)
                    kkn3 = kkn.rearrange("p (a b) -> p a b", a=NM)
                    oA = oAR[:, osl, d, :, 0:128]
                    oR = oAR[:, osl, d, :, 128:256]
                    if d == 0:
                        k.stt(oA[:, :, 1:128], kkn3[:, :, 1:128], -1.0, Ep3[:, :, 0:127], ALU.mult, ALU.mult)
                        k.ts("pool", oA[:, :, 0:1], kkn3[:, :, 0:1], -1.0, ALU.mult)
                    else:
                        k.stt(oA[:, :, 0:127], kkn3[:, :, 0:127], -1.0, Ep3[:, :, 1:128], ALU.mult, ALU.mult)
                        k.ts("pool", oA[:, :, 127:128], kkn3[:, :, 127:128], -1.0, ALU.mult)
                    k.tt("pool", oR, r_.rearrange("p (a b) -> p a b", a=NM), Ep3, ALU.mult)
                    k.tt("dve", oBT[:, osl, d, :, :], kka.rearrange("p (a b) -> p a b", a=NM),
                         Em.rearrange("p (a b) -> p a b", a=NM), ALU.mult)
                    k.tt("pool", oKT[:, osl, d, :, :], kd.rearrange("p (a b) -> p a b", a=NM),
                         Em.rearrange("p (a b) -> p a b", a=NM), ALU.mult)
                    k.tt("dve", oBHc[:, d, :], kka, Eh, ALU.mult)
                    k.tt("pool", oKHc[:, d, :], kd, Eh, ALU.mult)
                    pt2 = ptr[CH[0]].rearrange("p (a b) -> p a b", a=8)
                    for j in range(NM):
                        k.tr(pt2[:, j, :], oBHc[:, d, j * 128:(j + 1) * 128], identb)
                        k.tr(pt2[:, NM + j, :], oKHc[:, d, j * 128:(j + 1) * 128], identb)
                    k.cp("act", oBH[:, osl, d, :, :], pt2[:, 0:NM, :])
                    k.cp("dve", oKH[:, osl, d, :, :], pt2[:, NM:2 * NM, :])
                    k.dma(S_AR[d, m, cb].rearrange("p (a b) -> p a b", a=NM), oAR[:, osl, d, :, :])
                    k.dma(S_BT[d, m, cb].rearrange("p (a b) -> p a b", a=NM), oBT[:, osl, d, :, :])
                    k.dma(S_KT[d, m, cb].rearrange("p (a b) -> p a b", a=NM), oKT[:, osl, d, :, :])
                    k.dma(S_BH[d, m, cb].rearrange("p (a b) -> p a b", a=NM), oBH[:, osl, d, :, :])
                    k.dma(S_KH[d, m, cb].rearrange("p (a b) -> p a b", a=NM), oKH[:, osl, d, :, :])
                    k.dma(S_WC[d, m, cb], oWC[:, osl, d, :])
                pb = psB[CH[0]]
                bonb = T(7).bitcast(BF16)[:, 0:512]
                k.cp("dve", bonb, bon)
                k.mm(pb[:, :], blockones, bonb, start=True, stop=True)
                k.tt("dve", oBON[:, osl, :], pb[:, :], v_, ALU.mult)
                k.dma(S_BON[m, cb], oBON[:, osl, :])
            def attn_cb(cb):
                osl = cb % 2
                CH[0] = osl
                oVAc = oVAc2[:, osl]
                w_begin([17 + cb, 21 + cb, 25 + cb, 29 + cb])
                for which, dst, gcol in ((0, S_Q, PC_QG), (1, S_K, PC_KG)):
                    pz = proj_attn(m, 17 + which * 4 + cb)
                    q_ = T(0)
                    k.cp("act", q_, pz)
                    q2 = T(1).bitcast(BF16)[:, 0:512]
                    k.tt("pool", q2, q_, q_, ALU.mult)
                    pn = psB[CH[0]]
                    k.mm(pn[:, :], blockones, q2, start=True, stop=True)
                    rn = T(1)
                    k.ts("dve", rn, pn[:, :], 1.0 / 64, ALU.mult, RMS_EPS, ALU.add)
                    k.act(rn, rn, AF.Ln)
                    k.act(rn, rn, AF.Exp, scale=-0.5)
                    k.stt(oQ[:, osl, :], q_, pc[:, gcol:gcol + 1], rn, ALU.mult, ALU.mult)
                    k.dma(dst[m, cb], oQ[:, osl, :])
                pz = proj_attn(m, 25 + cb)
                k.cp("act", oVAc, pz)
                pt = ptr[CH[0]].rearrange("p (a b) -> p a b", a=8)
                for j in range(NM):
                    k.tr(pt[:, j, :], oVAc[:, j * 128:(j + 1) * 128], identb)
                k.cp("dve", oVA[:, osl, :, :], pt[:, 0:NM, :])
                k.dma(S_VA[m, cb].rearrange("p (a b) -> p a b", a=NM), oVA[:, osl, :, :])
                pz = proj_attn(m, 29 + cb)
                g_ = T(2)
                k.cp("act", g_, pz)
                th = T(3)
                k.act(th, g_, AF.Tanh, scale=0.5)
                sigm_from_tanh("dve", th, th)
                k.tt("pool", oGA[:, osl, :], th, g_, ALU.mult)
                k.dma(S_GA[m, cb], oGA[:, osl, :])


            for pair in ((0, 1), (2, 3)):
                lists = []
                for cb in pair:
                    k.cap = []
                    rwkv_cb(cb)
                    lists.append(k.cap)
                    k.cap = None
                k.merge(lists, store_delay=0)
            for pair in ((0, 1), (2, 3)):
                lists = []
                for cb in pair:
                    k.cap = []
                    attn_cb(cb)
                    lists.append(k.cap)
                    k.cap = None
                k.merge(lists, store_delay=0)
            CH[0] = 0

        print("SBUF pass A bytes", alloc["off"], "base", alloc["base"])
        if cfg.upto == "W":
            k.dma(ys[0:128, :], xs[0:128, :])
            with nc.Block() as block:
                S.emit(sems, dsems, block)
            return nc
        for i in range(min(NM, NT)):
            stage_x(i)
        if cfg.upto == "X":
            k.dma(ys[0:128, :], xs[0:128, :])
            with nc.Block() as block:
                S.emit(sems, dsems, block)
            return nc
        for m in range(NMT):
            halo_left(m)
            if m + 1 < NMT:
                for j in range(NM):
                    stage_x((m + 1) * NM + j)
            halo_right(m)
            passA_macro(m)

        if cfg.upto == "A":
            with nc.Block() as block:
                S.emit(sems, dsems, block)
            return nc

        W0a, W0b = psW[0][:, 0:512], psW[0][:, 512:1024]
        W1a, W1b = psW[1][:, 0:512], psW[1][:, 512:1024]

        def rwkv_pass(d):
            pass_reset()
            ldAR = sb("ldAR", [128, 2, 4, NM, 256], BF16)
            ldBT = sb("ldBT", [128, 2, 4, NM, 128], BF16)
            ldKT = sb("ldKT", [128, 2, 4, NM, 128], BF16)
            ldBH = sb("ldBH", [128, 2, 4, NM, 128], BF16)
            ldKH = sb("ldKH", [128, 2, 4, NM, 128], BF16)
            ldV = sb("ldV", [128, 2, 4, NM, 128], BF16)
            ldWC = sb("ldWC", [128, 2, 4, NM], F32)
            MA = sb("MA", [128, 8, 256], BF16)
            AK = sb("AK", [128, 8, 256], BF16)
            N0 = sb("N0", [128, 8, 128], BF16)
            Mn = sb("Mn", [128, 2, 8, 128], BF16)
            Nn = sb("Nn", [128, 2, 8, 128], BF16)
            Ttb = sb("Ttb", [128, 2, 8, 128], BF16)
            G1s = sb("G1s", [128, 512], F32)
            G12 = sb("G12", [128, 512], BF16)
            Ubf = sb("Ubf", [128, 512], BF16)
            Hs = sb("Hs", [128, 256], F32)
            Hblk = sb("Hblk", [128, 4, 128], BF16)
            yo = sb("yo", [128, 2, 512], F32)
            mk2 = sb("mk2", [128, 256], F32)
            mkN = sb("mkN", [128, 128], F32)
            k.cp("pool", mk2[:, 0:128], cst[:, 1 if d == 0 else 2, :])
            k.cp("pool", mk2[:, 128:256], cst[:, 3 if d == 0 else 4, :])
            k.cp("pool", mkN[:], cst[:, 2 if d == 0 else 1, :])
            k.memset("pool", Hs[:], 0.0)
            k.memset("pool", Hblk[:], 0.0)
            import os as _os
            for _w in range(int(_os.environ.get("WARM", "0"))):
                k.mm(W0a, identb, MA[:, 0:2, :].rearrange("p a b -> p (a b)"), start=True, stop=True)
            if d == 1:
                yf = sb("yf", [128, 2, 512], F32)
                ysum = sb("ysum", [128, 512], F32)
                lnt = sb("lnt", [128, 512], F32)
                lnm = sb("lnm", [128, 16], F32)
                ynT = sb("ynT", [128, 512], F32)
                ldBON = sb("ldBON", [128, 2, 4, 128], F32)
                ldGR = sb("ldGR", [128, 2, 4, 128], F32)
                oMR = sb("oMR", [128, 2, 4, 128], BF16)

            def load(m, sl):
                k.dma(ldAR[:, sl].rearrange("p c t x -> p c (t x)"), S_AR[d, m].rearrange("c p x -> p c x"))
                k.dma(ldBT[:, sl].rearrange("p c t x -> p c (t x)"), S_BT[d, m].rearrange("c p x -> p c x"))
                k.dma(ldKT[:, sl].rearrange("p c t x -> p c (t x)"), S_KT[d, m].rearrange("c p x -> p c x"))
                k.dma(ldBH[:, sl].rearrange("p c t x -> p c (t x)"), S_BH[d, m].rearrange("c p x -> p c x"))
                k.dma(ldKH[:, sl].rearrange("p c t x -> p c (t x)"), S_KH[d, m].rearrange("c p x -> p c x"))
                k.dma(ldV[:, sl].rearrange("p c t x -> p c (t x)"), S_V[m].rearrange("c p x -> p c x"))
                k.dma(ldWC[:, sl], S_WC[d, m].rearrange("c p x -> p c x"))

            def tile(m, sl, j, cnt):
                i = m * NM + j
                AR = ldAR[:, sl]
                BT = ldBT[:, sl]
                KT = ldKT[:, sl]
                BH = ldBH[:, sl]
                KH = ldKH[:, sl]
                V = ldV[:, sl]
                bnd = None
                if d == 0 and i % SEG == 0 and i > 0:
                    bnd = i // SEG
                if d == 1 and (i + 1) % SEG == 0 and i < NT - 1:
                    bnd = (i + 1) // SEG
                if bnd is not None:
                    k.ts("pool", Hs[:], Hs[:], flg[:, bnd:bnd + 1], ALU.mult)
                    k.ts("pool", Hblk[:], Hblk[:], flg[:, bnd:bnd + 1], ALU.mult)
                m2 = mk2[:].unsqueeze(1).to_broadcast([128, 2, 256])
                mN = mkN[:].unsqueeze(1).to_broadcast([128, 2, 128])
                sbanks = {0: [psB[0], psB[1], psB[2]], 1: [psB[3], W0a, W0b]}
                for cp_ in range(2):
                    for hh in range(2):
                        po = 64 * hh
                        bs = sbanks[hh]
                        for ci in range(2):
                            cb = 2 * cp_ + ci
                            k.mm(bs[0][:, ci * 256:(ci + 1) * 256], BT[po:po + 64, cb, j, :], AR[po:po + 64, cb, j, :])
                            k.mm(bs[1][:, ci * 256:(ci + 1) * 256], KT[po:po + 64, cb, j, :], AR[po:po + 64, cb, j, :])
                            k.mm(bs[2][:, ci * 128:(ci + 1) * 128], AR[po:po + 64, cb, j, 0:128], BT[po:po + 64, cb, j, :])
                        h0 = 4 * cp_ + hh
                        k.tt("dve", MA[:, h0:h0 + 3:2, :], bs[0].rearrange("p (a b) -> p a b", a=2), m2, ALU.mult)
                        k.tt("dve", AK[:, h0:h0 + 3:2, :], bs[1].rearrange("p (a b) -> p a b", a=2), m2, ALU.mult)
                        k.tt("dve", N0[:, h0:h0 + 3:2, :], bs[2][:, 0:256].rearrange("p (a b) -> p a b", a=2), mN, ALU.mult)
                k.tt("pool", Ttb[:, 0], MA[:, :, 0:128], cst[:, 0, :].unsqueeze(1).to_broadcast([128, 8, 128]), ALU.add)
                Ttbank = [psB[2], psB[3]]
                Mbank = [W1a, psB[0]]
                Nbank = [W1b, psB[1]]
                for half in range(2):
                    for q in range(4):
                        h = 4 * half + q
                        k.mm(Ttbank[half][:, q * 128:(q + 1) * 128], identb, Ttb[:, 0, h, :],
                             start=(q == 0), stop=False, skip=True)
                Mcur = [MA[:, h, 0:128] for h in range(8)]
                Ncur = [N0[:, h, :] for h in range(8)]
                for lev in range(7):
                    for half in range(2):
                        if lev <= 5:
                            for q in range(4):
                                h = 4 * half + q
                                k.mm(Nbank[half][:, q * 128:(q + 1) * 128], Mcur[h], Ncur[h])
                            if lev <= 4:
                                for q in range(4):
                                    h = 4 * half + q
                                    k.mm(Mbank[half][:, q * 128:(q + 1) * 128], Ncur[h], Mcur[h])
                        if lev >= 1:
                            for q in range(4):
                                h = 4 * half + q
                                k.mm(Ttbank[half][:, q * 128:(q + 1) * 128], Ncur[h], Ttb[:, (lev - 1) % 2, h, :],
                                     start=False, stop=(lev == 6), skip=True)
                            k.cp("act" if half == 0 else "dve", Ttb[:, lev % 2, 4 * half:4 * half + 4, :],
                                 Ttbank[half].rearrange("p (a b) -> p a b", a=4))
                        if lev <= 5:
                            k.cp("act" if (half == 0 and lev % 2 == 0) else "dve", Nn[:, lev % 2, 4 * half:4 * half + 4, :],
                                 Nbank[half].rearrange("p (a b) -> p a b", a=4))
                            if lev <= 4:
                                k.cp("act", Mn[:, lev % 2, 4 * half:4 * half + 4, :],
                                     Mbank[half].rearrange("p (a b) -> p a b", a=4))
                    if lev <= 5:
                        Ncur = [Nn[:, lev % 2, h, :] for h in range(8)]
                        if lev <= 4:
                            Mcur = [Mn[:, lev % 2, h, :] for h in range(8)]
                TT = Ttb[:, 0]
                for h in range(8):
                    cb, po = h // 2, 64 * (h % 2)
                    k.mm(W0a[:, h * 64:(h + 1) * 64], AK[:, h, 0:128], V[:, cb, j, po:po + 64])
                k.cp("act", G1s[:], W0a)
                for cb in range(4):
                    k.mm(W0b[:, cb * 128:(cb + 1) * 128], AR[:, cb, j, 0:128], Hblk[:, cb, :])
                k.tt("dve", G12[:], W0b, G1s[:], ALU.add)
                for h in range(8):
                    k.mm(W0a[:, h * 64:(h + 1) * 64], TT[:, h, :], G12[:, h * 64:(h + 1) * 64])
                k.cp("act", Ubf[:], W0a)
                for cb in range(4):
                    h0, h1 = 2 * cb, 2 * cb + 1
                    o0 = W0b[:, h0 * 64:(h0 + 1) * 64]
                    o1 = W0b[:, h1 * 64:(h1 + 1) * 64]
                    k.mm(o0, AK[:, h0, 128:256], V[:, cb, j, 0:64], start=True, stop=False, skip=True)
                    k.mm(o1, AK[:, h1, 128:256], V[:, cb, j, 64:128], start=False, stop=False, skip=True)
                    k.mm(W0b[:, cb * 128:(cb + 1) * 128], AR[:, cb, j, 128:256], Hblk[:, cb, :], start=False, stop=False, skip=True)
                    k.mm(o0, MA[:, h0, 128:256], Ubf[:, h0 * 64:(h0 + 1) * 64], start=False, stop=False, skip=True)
                    k.mm(o1, MA[:, h1, 128:256], Ubf[:, h1 * 64:(h1 + 1) * 64], start=False, stop=True, skip=True)
                for h in range(8):
                    cb, po = h // 2, 64 * (h % 2)
                    o = W1a[po:po + 64, cb * 64:(cb + 1) * 64]
                    k.mm(o, KH[:, cb, j, po:po + 64], V[:, cb, j, po:po + 64], start=True, stop=False)
                    k.mm(o, BH[:, cb, j, po:po + 64], Ubf[:, h * 64:(h + 1) * 64], start=False, stop=True)
                ysl = cnt % 2
                if d == 0:
                    k.cp("act", yo[:, ysl, :], W0b)
                    k.dma(S_YF[i], yo[:, ysl, :])
                else:
                    k.dma(yf[:, ysl, :], S_YF[i])
                    k.tt("dve", ysum[:], W0b, yf[:, ysl, :], ALU.add)
                for cb in range(4):
                    k.stt(Hs[:, cb * 64:(cb + 1) * 64], Hs[:, cb * 64:(cb + 1) * 64], ldWC[:, sl, cb, j:j + 1],
                          W1a[:, cb * 64:(cb + 1) * 64], ALU.mult, ALU.add)
                Hs3 = Hs[:].rearrange("p (a b) -> p a b", a=4)
                k.cp("pool", Hblk[0:64, :, 0:64], Hs3[0:64])
                k.cp("pool", Hblk[64:128, :, 64:128], Hs3[64:128])
                if d == 1:
                    y3 = ysum[:].rearrange("p (a b) -> p a b", a=8)
                    k.red(lnm[:, 0:8], y3)
                    k.ts("dve", lnm[:, 0:8], lnm[:, 0:8], 1.0 / 64, ALU.mult)
                    k.tt("dve", lnt[:].rearrange("p (a b) -> p a b", a=8), y3,
                         lnm[:, 0:8].unsqueeze(2).to_broadcast([128, 8, 64]), ALU.subtract)
                    k.tt("pool", ysum[:], lnt[:], lnt[:], ALU.mult)
                    k.red(lnm[:, 8:16], y3)
                    k.ts("dve", lnm[:, 8:16], lnm[:, 8:16], 1.0 / 64, ALU.mult, LNX_EPS, ALU.add)
                    k.tt("pool", lnm[:, 8:16], lnm[:, 8:16], cm05[:, 0:1].to_broadcast([128, 8]), ALU.pow)
                    k.tt("dve", lnt[:].rearrange("p (a b) -> p a b", a=8), lnt[:].rearrange("p (a b) -> p a b", a=8),
                         lnm[:, 8:16].unsqueeze(2).to_broadcast([128, 8, 64]), ALU.mult)
                    for cb in range(4):
                        k.tr(W1b[:, cb * 128:(cb + 1) * 128], lnt[:, cb * 128:(cb + 1) * 128], identf)
                    k.dma(ldBON[:, ysl], S_BON[m].rearrange("c p (t x) -> p c t x", t=NM)[:, :, j, :])
                    k.dma(ldGR[:, ysl], S_GR[m].rearrange("c p (t x) -> p c t x", t=NM)[:, :, j, :])
                    for cb in range(4):
                        k.act(ynT[:, cb * 128:(cb + 1) * 128], W1b[:, cb * 128:(cb + 1) * 128], AF.Identity,
                              bias=pc[:, PC_LB + cb:PC_LB + cb + 1], scale=pc[:, PC_LG + cb:PC_LG + cb + 1])
                    yn3 = ynT[:].rearrange("p (a b) -> p a b", a=4)
                    k.tt("pool", yn3, yn3, ldBON[:, ysl], ALU.add)
                    k.tt("pool", oMR[:, ysl], yn3, ldGR[:, ysl], ALU.mult)
                    k.dma(S_MR[i].rearrange("c p x -> p c x"), oMR[:, ysl])

            order = list(range(NMT)) if d == 0 else list(range(NMT - 1, -1, -1))
            load(order[0], 0)
            cnt = 0
            for idx, m in enumerate(order):
                sl = idx % 2
                if idx + 1 < len(order):
                    load(order[idx + 1], 1 - sl)
                js = range(NM) if d == 0 else range(NM - 1, -1, -1)
                for j in js:
                    tile(m, sl, j, cnt)
                    cnt += 1

        rwkv_pass(0)
        if cfg.upto == "B":
            with nc.Block() as block:
                S.emit(sems, dsems, block)
            return nc
        rwkv_pass(1)
        if cfg.upto == "C":
            with nc.Block() as block:
                S.emit(sems, dsems, block)
            return nc

        pass_reset()
        RS = 8
        EG = sb("EG", [128, 8, 15, 64], BF16)
        vts = sb("vts", [128, NT * 14], F32)
        Wo = sb("Wo", [128, 8, D], BF16)
        Kring = sb("Kring", [128, RS, 4, 128], BF16)
        Vring = sb("Vring", [128, RS, 4, 128], BF16)
        Qt = sb("Qt", [128, 2, 4, 128], BF16)
        ebuf = sb("ebuf", [128, 2, 8, 128], F32)
        pT = sb("pT", [128, 2, 7, 8, 128], BF16)
        gaT = sb("gaT", [128, 2, 4, 128], F32)
        mrT = sb("mrT", [128, 2, 4, 128], BF16)
        maT = sb("maT", [128, 2, 4, 128], BF16)
        dsb2 = sb("dsb", [128, 2, 512], F32)
        otmp2 = sb("otmp", [128, 2, 512], F32)
        xr = sb("xr", [128, 2, D], F32)
        yout = sb("yout", [128, 2, D], F32)
        onesb = sb("onesb", [128, 64], BF16)
        cm1 = sb("cm1", [128, 1], F32)
        k.memset("pool", onesb[:], 1.0)
        k.memset("pool", cm1[:], -1.0)
        k.dma(vts[:], vtab)
        wtmp = sb("wtmp", [128, 2, D], F32)
        for ek in range(8):
            k.dma(wtmp[:, ek % 2, :], w_out[ek * 128:(ek + 1) * 128, :])
            k.cp("pool" if ek % 2 == 0 else "dve", Wo[:, ek, :], wtmp[:, ek % 2, :])
        ohf = sb("ohf", [128, 4096], F32)
        ohb = sb("ohb", [128, 4096], BF16)
        rpf = sb("rpf", [128, 120], F32)
        rpe = sb("rpe", [128, 120], BF16)
        k.dma(ohf[0:31, :], ohot)
        k.dma(rpf[0:31, :], rpbT)
        k.cp("dve", ohb[0:31, :], ohf[0:31, :])
        k.act(rpe[0:31, :], rpf[0:31, :], AF.Exp)
        k.memset("pool", EG[64:128, :, 0:1, :], 0.0)
        for g in range(16):
            bank = psB[g % 2]
            for c in range(4):
                cq = g * 4 + c
                k.mm(bank[0:64, c * 120:(c + 1) * 120], ohb[0:31, cq * 64:(cq + 1) * 64], rpe[0:31, :])
                k.mm(bank[64:128, c * 120:(c + 1) * 120], ohb[0:31, cq * 64:(cq + 1) * 64], rpe[0:31, :])
            k.cp("act", EG[0:64, :, :, g * 4:(g + 1) * 4],
                 bank[0:64, 0:480].rearrange("p (c h u) -> p h u c", c=4, h=8))
            k.cp("dve", EG[64:128, :, 1:15, g * 4:(g + 1) * 4],
                 bank[64:128, 0:480].rearrange("p (c h u) -> p h u c", c=4, h=8)[:, :, 0:14, :])

        loaded = [-1]

        def ensure_keys(upto):
            while loaded[0] < min(upto, NT - 1):
                t = loaded[0] + 1
                mk, jk = t // NM, t % NM
                k.dma(Kring[:, t % RS], S_K[mk].rearrange("c p (t x) -> p c t x", t=NM)[:, :, jk, :])
                k.dma(Vring[:, t % RS], S_VA[mk].rearrange("c p (t x) -> p c t x", t=NM)[:, :, jk, :])
                loaded[0] = t

        zc = [0]
        for i in range(NT):
            m, j = i // NM, i % NM
            sl = i % 2
            pos = i % SEG
            offs = [-2, -1, 0, 1, 2]
            if pos == 0:
                offs.append(3)
            if pos == SEG - 1:
                offs.insert(0, -3)
            offs = [o for o in offs if 0 <= i + o < NT]
            ensure_keys(i + 3)
            k.dma(Qt[:, sl], S_Q[m].rearrange("c p (t x) -> p c t x", t=NM)[:, :, j, :])
            k.dma(gaT[:, sl], S_GA[m].rearrange("c p (t x) -> p c t x", t=NM)[:, :, j, :])
            k.dma(mrT[:, sl], S_MR[i].rearrange("c p x -> p c x"))
            k.dma(xr[:, sl, :], xs[i * 128:(i + 1) * 128, :])
            for ki, o in enumerate(offs):
                t = i + o
                ks = t % RS
                pz = psW[zc[0] % 2]
                es_ = zc[0] % 2
                zc[0] += 1
                for hh in range(2):
                    po = 64 * hh
                    for cb in range(4):
                        k.mm(pz[:, (hh * 4 + cb) * 128:(hh * 4 + cb + 1) * 128], Kring[po:po + 64, ks, cb, :],
                             Qt[po:po + 64, sl, cb, :])
                k.act(ebuf[:, es_].rearrange("p a b -> p (a b)"), pz[:], AF.Exp, scale=0.125)
                u0 = 7 - 2 * o
                for qh in range(2):
                    col = (i * 7 + (o + 3)) * 2 + qh
                    k.stt(pT[:, sl, ki, :, qh * 64:(qh + 1) * 64], ebuf[:, es_, :, qh * 64:(qh + 1) * 64],
                          vts[:, col:col + 1], EG[:, :, u0 + qh, :], ALU.mult, ALU.mult)
            nk = len(offs)
            bO = psB[2 * (i % 2)]
            bD = psB[2 * (i % 2) + 1]
            dsb = dsb2[:, i % 2]
            otmp = otmp2[:, i % 2]
            for hh in range(2):
                po = 64 * hh
                for cb in range(4):
                    idx = hh * 4 + cb
                    oo = bO[po:po + 64, cb * 128:(cb + 1) * 128]
                    dd = bD[po:po + 64, cb * 128:(cb + 1) * 128]
                    for ki, o in enumerate(offs):
                        ks = (i + o) % RS
                        k.mm(oo, Vring[:, ks, cb, po:po + 64], pT[:, sl, ki, idx, :], start=(ki == 0), stop=(ki == nk - 1))
                    for ki, o in enumerate(offs):
                        k.mm(dd, onesb[:, :], pT[:, sl, ki, idx, :], start=(ki == 0), stop=(ki == nk - 1))
            k.recip(dsb, bD[:])
            k.tt("dve", otmp, bO[:], dsb, ALU.mult)
            k.tt("pool", maT[:, sl].rearrange("p a b -> p (a b)"), otmp, gaT[:, sl].rearrange("p a b -> p (a b)"), ALU.mult)
            for half in range(2):
                ob = (bO if half == 0 else bD)[:]
                for ek in range(8):
                    lh = mrT[:, sl, ek, :] if ek < 4 else maT[:, sl, ek - 4, :]
                    k.mm(ob, lh, Wo[:, ek, half * 512:(half + 1) * 512], start=(ek == 0), stop=(ek == 7))
                k.tt("dve", yout[:, sl, half * 512:(half + 1) * 512], ob, xr[:, sl, half * 512:(half + 1) * 512], ALU.add)
            k.dma(ys[i * 128:(i + 1) * 128, :], yout[:, sl, :])

        with nc.Block() as block:
            S.emit(sems, dsems, block)
    return nc


def _cols(v, nblk):
    return np.ascontiguousarray(np.asarray(v, np.float32).reshape(nblk, 128).T)


def make_consts():
    c = np.zeros((128, 6, 128), np.float32)
    i = np.arange(128)
    c[:, 0, :] = np.eye(128, dtype=np.float32)
    c[:, 1, :] = (i[:, None] < i[None, :])
    c[:, 2, :] = (i[:, None] > i[None, :])
    c[:, 3, :] = (i[:, None] <= i[None, :])
    c[:, 4, :] = (i[:, None] >= i[None, :])
    c[:, 5, :] = ((i[:, None] // 64) == (i[None, :] // 64))
    return c


def pack_params(p):
    pcols = np.zeros((128, NPC), np.float32)
    pcols[:, PC_MU:PC_MU + 17] = _cols(p["shift_mu"], 17)
    for d in range(2):
        pcols[:, PC_W0 + 4 * d:PC_W0 + 4 * d + 4] = _cols(p["w0"][d], 4)
        pcols[:, PC_A0 + 4 * d:PC_A0 + 4 * d + 4] = _cols(p["a0"][d], 4)
    pcols[:, PC_KK:PC_KK + 4] = _cols(p["k_k"], 4)
    pcols[:, PC_KA:PC_KA + 4] = _cols(p["k_a"], 4)
    pcols[:, PC_RK:PC_RK + 4] = _cols(p["r_k"].reshape(-1), 4)
    pcols[:, PC_LG:PC_LG + 4] = _cols(p["lnx_g"], 4)
    pcols[:, PC_LB:PC_LB + 4] = _cols(p["lnx_b"], 4)
    pcols[:, PC_QG] = np.tile(p["q_norm_g"], 2)
    pcols[:, PC_KG] = np.tile(p["k_norm_g"], 2)
    pcols[:, PC_NG:PC_NG + 8] = _cols(p["norm_g"], 8)
    wup = np.zeros((128, 2, 512), np.float32)
    wup[0:64] = np.transpose(p["w_up"], (1, 0, 2))
    wup[64:128] = np.transpose(p["a_up"], (1, 0, 2))
    cq = np.arange(64)
    ck = np.arange(64)
    cs = np.clip(cq - 8, 0, 48)
    inw = (ck[None, :] >= cs[:, None]) & (ck[None, :] < cs[:, None] + 16)
    dc = ck[None, :] - cq[:, None] + 15
    oh = np.zeros((31, 64, 64), np.float32)
    for q_ in range(64):
        for k_ in range(64):
            if inw[q_, k_]:
                oh[dc[q_, k_], q_, k_] = 1.0
    rp = np.asarray(p["rpb"], np.float32)
    rp = rp[:, ::-1, :]
    rp = rp.reshape(4, 2, 15, 31).transpose(3, 1, 0, 2)
    return {
        "ohot": np.ascontiguousarray(oh.reshape(31, 4096)),
        "rpbT": np.ascontiguousarray(rp.reshape(31, 120)),
        "w_in": np.ascontiguousarray(p["w_in"], np.float32),
        "w_out": np.ascontiguousarray(p["w_out"], np.float32),
        "pcols": pcols,
        "wup": wup,
        "consts": make_consts(),
    }


def make_core_meta(cfg, slots):
    NT, SEG, NSEG = cfg.NT, cfg.SEG, cfg.NSEG
    flags = np.zeros((128, NSEG + 1), np.float32)
    for s_ in range(1, NSEG):
        a, b = slots[s_ - 1], slots[s_]
        if a is not None and b is not None and a[0] == b[0] and b[1] == a[1] + 1:
            flags[:, s_] = 1.0
    vt = np.zeros((128, NT, 7, 2), np.float32)
    for i in range(NT):
        sl = slots[i // SEG]
        if sl is None:
            vt[:, i, 3, :] = 1.0
            continue
        sid, sidx, nseg = sl
        R = nseg * SEG * 2
        for qh in range(2):
            r = (sidx * SEG + (i % SEG)) * 2 + qh
            R0 = min(max(r - 4, 0), R - 8)
            for o in range(-3, 4):
                t = i + o
                if t < 0 or t >= NT:
                    continue
                sk = slots[t // SEG]
                if sk is None or sk[0] != sid:
                    continue
                if sk[1] - sidx != (t // SEG) - (i // SEG):
                    continue
                for kh in range(2):
                    rho = (sk[1] * SEG + (t % SEG)) * 2 + kh
                    if R0 <= rho < R0 + 8:
                        vt[kh * 64:(kh + 1) * 64, i, o + 3, qh] = 1.0
    return flags, np.ascontiguousarray(vt.reshape(128, NT * 14))


PARAM_NAMES = ["norm_g", "w_in", "shift_mu", "w0", "w_up", "a0", "a_up", "k_k", "k_a", "r_k",
               "lnx_g", "lnx_b", "q_norm_g", "k_norm_g", "rpb", "w_out"]


def run_layer(cfg, seqs, p, n_cores):
    NT, SEG, NSEG = cfg.NT, cfg.SEG, cfg.NSEG
    segtok = SEG * 128
    order = sorted(range(len(seqs)), key=lambda i: -seqs[i].shape[0])
    cores = [[] for _ in range(n_cores)]
    used = [0] * n_cores
    for si in order:
        ns = seqs[si].shape[0] // segtok
        c = min(range(n_cores), key=lambda c_: (used[c_] + ns > NSEG, used[c_]))
        assert used[c] + ns <= NSEG
        for g in range(ns):
            cores[c].append((si, g, ns))
        used[c] += ns
    shared = pack_params(p)
    in_maps = []
    for c in range(n_cores):
        slots = cores[c] + [None] * (NSEG - len(cores[c]))
        x = np.zeros((NT * 128, D), np.float32)
        for s_, sl in enumerate(slots):
            if sl is not None:
                x[s_ * segtok:(s_ + 1) * segtok] = seqs[sl[0]][sl[1] * segtok:(sl[1] + 1) * segtok]
        flags, vt = make_core_meta(cfg, slots)
        im = dict(shared)
        im["xs"] = x
        im["flags"] = flags
        im["vtab"] = vt
        in_maps.append(im)
    nc = build_program(cfg)
    res = run_bass_kernel_spmd(nc, in_maps, core_ids=list(range(n_cores)))
    outs = [np.zeros_like(s_) for s_ in seqs]
    for c in range(n_cores):
        y = np.asarray(res.results[c]["ys"])
        for s_, sl in enumerate(cores[c]):
            outs[sl[0]][sl[1] * segtok:(sl[1] + 1) * segtok] = y[s_ * segtok:(s_ + 1) * segtok]
    return outs, res


def kernel(**inputs):
    cfg = Cfg(128, 32)
    p = {n_: np.asarray(inputs[n_], np.float32)[0] for n_ in PARAM_NAMES}
    xp = np.asarray(inputs["x_prompt"], np.float32)
    xsm = np.asarray(inputs["x_sample"], np.float32)
    seqs = [xp[b] for b in range(xp.shape[0])] + [xsm[b] for b in range(xsm.shape[0])]
    outs, _ = run_layer(cfg, seqs, p, 8)
    yp = np.stack(outs[:xp.shape[0]], 0)
    ysm = np.stack(outs[xp.shape[0]:], 0)
    return (yp, ysm)
```

```python
import numpy as np
from contextlib import ExitStack
import concourse.bass as bass
import concourse.mybir as mybir
from concourse.bass_utils import run_bass_kernel_spmd

F32 = mybir.dt.float32
BF16 = mybir.dt.bfloat16
ALU = mybir.AluOpType
AF = mybir.ActivationFunctionType
AX = mybir.AxisListType

D = 1024
DIN = 4224
NBLK = 33
RMS_EPS = 1e-6
LNX_EPS = 64e-5
CDEC = 0.6065306597126334

ENGS = ["pe", "act", "dve", "pool", "sp"]
NDMASEM = 48


class Op:
    __slots__ = ("eng", "fn", "chan", "seq", "signal", "waits", "vc", "is_dma", "sigval", "rows",
                 "deps", "idx", "dur", "start", "finish", "succs", "npred", "ready")

    def __init__(self, eng, fn, is_dma):
        self.eng = eng
        self.fn = fn
        self.is_dma = is_dma
        self.signal = False
        self.waits = []
        self.vc = None
        self.chan = None
        self.seq = None
        self.sigval = None
        self.rows = None
        self.deps = {}
        self.dur = 500.0
        self.start = 0.0
        self.finish = 0.0
        self.succs = []
        self.npred = 0
        self.ready = 0.0


def _region(ap):
    name = ap.tensor.name
    pat = ap.ap
    off = ap.offset
    es = mybir.dt.size(ap.dtype)
    if "DRAM" in str(ap.space).upper():
        ext = 0
        for st, cnt in pat:
            ext += (cnt - 1) * abs(st)
        return name, 0, 1, off * es, (off + ext + 1) * es
    row = pat[0][0]
    npart = pat[0][1]
    if row == 0:
        p0 = 0
        f0 = off
    else:
        p0 = off // row
        f0 = off - p0 * row
    ext = 0
    for st, cnt in pat[1:]:
        ext += (cnt - 1) * abs(st)
    return name, p0, p0 + npart, f0 * es, (f0 + ext + 1) * es


def _free_elems(ap):
    n = 1
    for st, cnt in ap.ap[1:]:
        n *= cnt
    return n


class Sched:
    def __init__(self, nc, same_engine_sync=True, reorder=True):
        self.nc = nc
        self.ops = {e: [] for e in ENGS}
        self.all_ops = []
        self.hist = {}
        self.dma_uses = [0] * NDMASEM
        self.same_engine_sync = same_engine_sync
        self.reorder = reorder
        import os
        self.max_ops = int(os.environ.get("MAXOPS", "100000000"))
        if os.environ.get("NOREORDER"):
            self.reorder = False
        self.nadd = 0
        self.log = []

    @staticmethod
    def _dep(op, A, kind):
        r = op.deps.get(id(A))
        if r is None:
            op.deps[id(A)] = [A, kind]
        elif kind < r[1] or (kind == 1 and r[1] == 2):
            r[1] = kind if not (r[1] == 0) else 0

    def add(self, eng, fn, reads=(), writes=(), dma=False, rows=None):
        self.nadd += 1
        if self.nadd > self.max_ops:
            return None
        op = Op(eng, fn, dma)
        op.rows = rows
        op.idx = len(self.all_ops)
        accesses = []
        nel = 0
        for ap, is_w in [(a, False) for a in reads] + [(a, True) for a in writes]:
            if ap is None or isinstance(ap, (int, float)):
                continue
            name, p0, p1, f0, f1 = _region(ap)
            if is_w and nel == 0:
                nel = _free_elems(ap)
                if dma:
                    nel = nel * mybir.dt.size(ap.dtype) * (p1 - p0)
            accesses.append((name, p0, p1, f0, f1, is_w, False))
            if "PSUM" in str(ap.space).upper():
                b0 = f0 // 2048
                b1 = (f1 - 1) // 2048
                accesses.append((name + "#bank", 0, 128, b0 * 2048, (b1 + 1) * 2048, True, True))
        if dma:
            op.dur = 2000.0 + nel / 150.0
        elif eng == "pe":
            op.dur = 45.0 + nel * 0.5
        elif eng == "act":
            op.dur = 250.0 + nel * 0.8
        elif eng == "dve":
            op.dur = 150.0 + nel * 1.0
        else:
            op.dur = 400.0 + nel * 2.0
        acc2 = []
        for name, p0, p1, f0, f1, is_w, cross_only in accesses:
            if name == "big":
                b = f0 // 512
                while b * 512 < f1:
                    acc2.append(((name, b), p0, p1, max(f0, b * 512), min(f1, (b + 1) * 512), is_w, cross_only))
                    b += 1
            else:
                acc2.append((name, p0, p1, f0, f1, is_w, cross_only))
        for name, p0, p1, f0, f1, is_w, cross_only in acc2:
            h = self.hist.get(name)
            if h is None:
                h = []
            newh = []
            for rec in h:
                rp0, rp1, rf0, rf1, rop, rw = rec
                if rop is op:
                    newh.append(rec)
                    continue
                ov = (rp0 < p1 and p0 < rp1 and rf0 < f1 and f0 < rf1)
                if cross_only:
                    if ov:
                        if rop.eng != eng or rop.is_dma or dma:
                            self._dep(op, rop, 0)
                        elif eng == "pe" and rows is not None and rop.rows is not None \
                                and (rop.rows[1] <= rows[0] or rows[1] <= rop.rows[0]):
                            self._dep(op, rop, 1)
                        else:
                            self._dep(op, rop, 2)
                    if rop.eng == eng and rp0 >= p0 and rp1 <= p1 and rf0 >= f0 and rf1 <= f1:
                        continue
                    newh.append(rec)
                    continue
                if ov and (rw or is_w):
                    self._dep(op, rop, 0)
                if is_w and rp0 >= p0 and rp1 <= p1 and rf0 >= f0 and rf1 <= f1:
                    continue
                if (not is_w) and (not rw) and (not rop.is_dma) and (not dma) and rop.eng == eng \
                        and rp0 == p0 and rp1 == p1 and rf0 == f0 and rf1 == f1:
                    self._dep(op, rop, 2)
                    continue
                newh.append(rec)
            newh.append((p0, p1, f0, f1, op, is_w))
            self.hist[name] = newh
        self.all_ops.append(op)
        self.ops[eng].append(op)
        return op

    def _schedule(self):
        import heapq
        ops = self.all_ops
        for op in ops:
            op.npred = len(op.deps)
            op.ready = 0.0
        for op in ops:
            for A, kind in op.deps.values():
                A.succs.append((op, kind))
        heaps = {e: [] for e in ENGS}
        free = {e: 0.0 for e in ENGS}
        for op in ops:
            if op.npred == 0:
                heapq.heappush(heaps[op.eng], (op.ready, op.idx, op))
        order = []
        new_streams = {e: [] for e in ENGS}
        n = len(ops)
        while len(order) < n:
            best_e = None
            best_t = None
            for e in ENGS:
                hp = heaps[e]
                if not hp:
                    continue
                t = max(free[e], hp[0][0])
                if best_t is None or t < best_t or (t == best_t and hp[0][1] < heaps[best_e][0][1]):
                    best_t = t
                    best_e = e
            assert best_e is not None, "scheduler deadlock (cyclic hazards?)"
            _, _, op = heapq.heappop(heaps[best_e])
            op.start = best_t
            if op.is_dma:
                free[best_e] = best_t + 60.0
                op.finish = best_t + op.dur
            else:
                free[best_e] = best_t + op.dur
                op.finish = best_t + op.dur
            order.append(op)
            new_streams[best_e].append(op)
            for sop, kind in op.succs:
                if kind == 2:
                    r = op.start + 1.0
                elif op.eng == "pe" and sop.eng == "pe" and kind == 0:
                    r = op.start + min(op.dur, 64.0)
                else:
                    r = op.finish + 180.0
                if r > sop.ready:
                    sop.ready = r
                sop.npred -= 1
                if sop.npred == 0:
                    heapq.heappush(heaps[sop.eng], (sop.ready, sop.idx, sop))
        self.ops = new_streams
        return order

    def finalize(self):
        if self.reorder:
            order = self._schedule()
        else:
            order = self.all_ops
        last = {e: None for e in ENGS}
        dma_last = [None] * NDMASEM
        rr = 0
        cnt = {e: 0 for e in ENGS}
        for op in order:
            eng = op.eng
            prev = last[eng]
            vc = dict(prev.vc) if prev is not None else {}
            deps = list(op.deps.values())
            if op.is_dma:
                j = rr
                rr = (rr + 1) % NDMASEM
                if dma_last[j] is not None:
                    deps.append([dma_last[j], 0])
                op.chan = ("d", j)
                op.seq = self.dma_uses[j]
                self.dma_uses[j] += 1
                dma_last[j] = op
                op.signal = True
            else:
                op.chan = eng
                op.seq = cnt[eng]
            cnt[eng] += 1
            best = {}
            for A, kind in deps:
                if kind == 2:
                    continue
                if A.chan == op.chan and not op.is_dma:
                    if kind != 1 and (eng == "pe" or not self.same_engine_sync):
                        continue
                if vc.get(A.chan, -1) >= A.seq:
                    continue
                if A.chan not in best or best[A.chan].seq < A.seq:
                    best[A.chan] = A
            op.waits = []
            for ch, A in best.items():
                if vc.get(A.chan, -1) >= A.seq:
                    continue
                A.signal = True
                op.waits.append(A)
                for kk_, v in A.vc.items():
                    if vc.get(kk_, -1) < v:
                        vc[kk_] = v
                if vc.get(A.chan, -1) < A.seq:
                    vc[A.chan] = A.seq
            op.vc = vc
            last[eng] = op

    def emit(self, sems, dsems, block):
        self.finalize()
        for e in ENGS:
            c = 0
            for op in self.ops[e]:
                if op.is_dma:
                    op.sigval = 16 * (op.seq + 1)
                elif op.signal:
                    c += 1
                    op.sigval = c
        sched = self

        def run(engname, engobj):
            for op in sched.ops[engname]:
                for A in op.waits:
                    if A.is_dma:
                        engobj.wait_ge(dsems[A.chan[1]], A.sigval)
                    else:
                        engobj.wait_ge(sems[A.chan], A.sigval)
                inst = op.fn(engobj)
                if op.is_dma:
                    inst.then_inc(dsems[op.chan[1]], 16)
                elif op.signal:
                    inst.then_inc(sems[op.chan], 1)

        @block.tensor
        def _(e):
            run("pe", e)

        @block.scalar
        def _(e):
            run("act", e)

        @block.vector
        def _(e):
            run("dve", e)

        @block.gpsimd
        def _(e):
            run("pool", e)

        @block.sync
        def _(e):
            run("sp", e)
            for j in range(NDMASEM):
                if sched.dma_uses[j] > 0:
                    e.wait_ge(dsems[j], 16 * sched.dma_uses[j])


class K:
    def __init__(self, S):
        self.S = S
        self.cap = None

    def _add(self, eng, fn, reads=(), writes=(), dma=False, rows=None):
        if self.cap is not None:
            self.cap.append((eng, fn, reads, writes, dma, rows))
            return None
        return self.S.add(eng, fn, reads=reads, writes=writes, dma=dma, rows=rows)

    def merge(self, lists, store_delay=0):
        idx = [0] * len(lists)
        alive = True
        held = []
        n = 0
        while alive:
            alive = False
            for c, l in enumerate(lists):
                if idx[c] < len(l):
                    item = l[idx[c]]
                    idx[c] += 1
                    alive = True
                    eng, fn, reads, writes, dma, rows = item
                    if dma and store_delay > 0 and "DRAM" in str(writes[0].space).upper():
                        held.append((n + store_delay, item))
                    else:
                        self.S.add(eng, fn, reads=reads, writes=writes, dma=dma, rows=rows)
                    n += 1
                    while held and held[0][0] <= n:
                        eng, fn, reads, writes, dma, rows = held.pop(0)[1]
                        self.S.add(eng, fn, reads=reads, writes=writes, dma=dma, rows=rows)
        for _, item in held:
            eng, fn, reads, writes, dma, rows = item
            self.S.add(eng, fn, reads=reads, writes=writes, dma=dma, rows=rows)

    def dma(self, out, in_, eng="sp"):
        return self._add(eng, lambda e: e.dma_start(out=out, in_=in_), reads=[in_], writes=[out], dma=True)

    def mm(self, out, lhsT, rhs, start=True, stop=True, skip=False):
        rg = _region(lhsT)
        rows = (rg[1], rg[2])
        if skip:
            return self._add("pe", lambda e: e.matmul(out, lhsT=lhsT, rhs=rhs, start=start, stop=stop,
                                                       skip_group_check=True),
                              reads=[lhsT, rhs], writes=[out], rows=rows)
        return self._add("pe", lambda e: e.matmul(out, lhsT=lhsT, rhs=rhs, start=start, stop=stop),
                          reads=[lhsT, rhs], writes=[out], rows=rows)

    def red(self, out, in_, op=ALU.add):
        return self._add("dve", lambda e: e.tensor_reduce(out=out, in_=in_, axis=AX.X, op=op),
                          reads=[in_], writes=[out])

    def recip(self, out, in_):
        return self._add("dve", lambda e: e.reciprocal(out=out, in_=in_), reads=[in_], writes=[out])

    def tr(self, out, in_, ident):
        rg = _region(in_)
        return self._add("pe", lambda e: e.transpose(out, in_, ident), reads=[in_, ident], writes=[out],
                          rows=(rg[1], rg[2]))

    def act(self, out, in_, func, bias=None, scale=None, eng="act"):
        kw = {}
        rd = [in_]
        if bias is not None:
            kw["bias"] = bias
            rd.append(bias)
        if scale is not None:
            kw["scale"] = scale
            rd.append(scale)
        return self._add("act", lambda e: e.activation(out=out, in_=in_, func=func, **kw), reads=rd, writes=[out])

    def tt(self, eng, out, a, b, op):
        return self._add(eng, lambda e: e.tensor_tensor(out=out, in0=a, in1=b, op=op), reads=[a, b], writes=[out])

    def ts(self, eng, out, a, s1, op0, s2=None, op1=None):
        rd = [a, s1, s2]
        if op1 is None:
            return self._add(eng, lambda e: e.tensor_scalar(out=out, in0=a, scalar1=s1, scalar2=None, op0=op0),
                              reads=rd, writes=[out])
        return self._add(eng, lambda e: e.tensor_scalar(out=out, in0=a, scalar1=s1, scalar2=s2, op0=op0, op1=op1),
                          reads=rd, writes=[out])

    def stt(self, out, a, scalar, b, op0, op1):
        return self._add("dve", lambda e: e.scalar_tensor_tensor(out=out, in0=a, scalar=scalar, in1=b, op0=op0, op1=op1),
                          reads=[a, scalar, b], writes=[out])

    def cp(self, eng, out, in_):
        if eng == "act":
            return self.act(out, in_, AF.Copy)
        return self._add(eng, lambda e: e.tensor_copy(out=out, in_=in_), reads=[in_], writes=[out])

    def memset(self, eng, out, val):
        return self._add(eng, lambda e: e.memset(out, val), writes=[out])

    def scan(self, out, d0, d1, init, op0, op1):
        return self._add("dve", lambda e: e.tensor_tensor_scan(out=out, data0=d0, data1=d1, initial=init, op0=op0, op1=op1),
                          reads=[d0, d1, init], writes=[out])

    def ttr(self, out, a, b, accum):
        return self._add("dve", lambda e: e.tensor_tensor_reduce(out=out, in0=a, in1=b, scale=1.0, scalar=0.0,
                                                                  op0=ALU.mult, op1=ALU.add, accum_out=accum),
                          reads=[a, b], writes=[out, accum])


class Cfg:
    def __init__(self, NT, SEG, debug=False, upto="all"):
        self.upto = upto
        self.NT = NT
        self.SEG = SEG
        self.NM = 4
        assert NT % SEG == 0 and SEG % self.NM == 0
        self.NSEG = NT // SEG
        self.NMT = NT // self.NM
        self.debug = debug


PC_MU = 0
PC_W0 = 17
PC_A0 = 25
PC_KK = 33
PC_KA = 37
PC_RK = 41
PC_LG = 45
PC_LB = 49
PC_QG = 53
PC_KG = 54
PC_NG = 55
NPC = 63


def build_program(cfg):
    NT, SEG, NM, NSEG, NMT = cfg.NT, cfg.SEG, cfg.NM, cfg.NSEG, cfg.NMT
    TOK = NM * 128
    nc = bass.Bass("TRN2", target_bir_lowering=False)
    okind = "ExternalOutput" if cfg.debug else "Internal"

    def din(name, shape, dt=F32):
        return nc.dram_tensor(name, list(shape), dt, kind="ExternalInput").ap()

    def dscr(name, shape, dt):
        return nc.dram_tensor(name, list(shape), dt, kind=okind).ap()

    xs = din("xs", [NT * 128, D])
    w_in = din("w_in", [D, DIN])
    w_out = din("w_out", [D, D])
    pcols = din("pcols", [128, NPC])
    flags = din("flags", [128, NSEG + 1])
    wup = din("wup", [128, 2, 512])
    consts = din("consts", [128, 6, 128])
    ohot = din("ohot", [31, 4096])
    rpbT = din("rpbT", [31, 120])
    vtab = din("vtab", [128, NT * 14])
    ys = nc.dram_tensor("ys", [NT * 128, D], F32, kind="ExternalOutput").ap()

    Wb = dscr("Wb", [NBLK, 128, 8 * 128], BF16)
    S_AR = dscr("S_AR", [2, NMT, 4, 128, NM * 256], BF16)
    S_BT = dscr("S_BT", [2, NMT, 4, 128, NM * 128], BF16)
    S_KT = dscr("S_KT", [2, NMT, 4, 128, NM * 128], BF16)
    S_BH = dscr("S_BH", [2, NMT, 4, 128, NM * 128], BF16)
    S_KH = dscr("S_KH", [2, NMT, 4, 128, NM * 128], BF16)
    S_V = dscr("S_V", [NMT, 4, 128, NM * 128], BF16)
    S_WC = dscr("S_WC", [2, NMT, 4, 128, NM], F32)
    S_GR = dscr("S_GR", [NMT, 4, 128, TOK], F32)
    S_BON = dscr("S_BON", [NMT, 4, 128, TOK], F32)
    S_Q = dscr("S_Q", [NMT, 4, 128, TOK], BF16)
    S_K = dscr("S_K", [NMT, 4, 128, TOK], BF16)
    S_VA = dscr("S_VA", [NMT, 4, 128, NM * 128], BF16)
    S_GA = dscr("S_GA", [NMT, 4, 128, TOK], F32)
    S_YF = dscr("S_YF", [NT, 128, 512], F32)
    S_MR = dscr("S_MR", [NT, 4, 128, 128], BF16)

    with ExitStack() as es:
        BIGB = 186 * 1024
        big = es.enter_context(nc.sbuf_tensor("big", [128, BIGB // 2], BF16))
        alloc = {"off": 0, "base": 0}

        def sb(name, shape, dt):
            n = 1
            for d_ in shape[1:]:
                n *= d_
            nbytes = n * mybir.dt.size(dt)
            nbytes = (nbytes + 31) // 32 * 32
            off = alloc["off"]
            assert off + nbytes <= BIGB, (name, off, nbytes)
            alloc["off"] = off + nbytes
            v = big[:, off // 2:(off + nbytes) // 2]
            if dt != BF16:
                v = v.bitcast(dt)
            v = v[:, 0:n]
            if len(shape) == 2:
                return v
            names = " ".join("a%d" % i for i in range(len(shape) - 1))
            kw = {"a%d" % i: shape[i + 1] for i in range(len(shape) - 1)}
            return v.rearrange("p (%s) -> p %s" % (names, names), **kw)

        def pass_reset():
            alloc["off"] = alloc["base"]

        def pst(name, shape, dt=F32):
            return es.enter_context(nc.psum_tensor(name, list(shape), dt))

        sems = {e: es.enter_context(nc.semaphore("s_" + e)) for e in ENGS}
        dsems = [es.enter_context(nc.semaphore("d%d" % j)) for j in range(NDMASEM)]
        S = Sched(nc)
        k = K(S)

        psW = [pst("psW0", [128, 1024]), pst("psW1", [128, 1024])]
        psB = [pst("ps%d" % i, [128, 512]) for i in range(4)]

        cst = sb("cst", [128, 6, 128], F32)
        cstb = sb("cstb", [128, 6, 128], BF16)
        pc = sb("pc", [128, NPC], F32)
        pcd = sb("pcd", [128, 64], F32)
        flg = sb("flg", [128, NSEG + 1], F32)
        wupf = sb("wupf", [128, 2, 512], F32)
        wupb = sb("wupb", [128, 2, 512], BF16)
        cm05 = sb("cm05", [128, 8], F32)
        k.memset("pool", cm05[:], -0.5)
        k.dma(cst[:], consts)
        k.dma(pc[:], pcols)
        k.dma(flg[:], flags)
        k.dma(wupf[:], wup)
        k.cp("dve", cstb[:], cst[:])
        k.cp("dve", wupb[:], wupf[:])
        identb = cstb[:, 0, :]
        identf = cst[:, 0, :]
        blockones = cstb[:, 5, :]
        PD_OMM, PD_HM, PD_W0H, PD_A0H, PD_OMKA = 0, 17, 34, 42, 50
        k.ts("dve", pcd[:, PD_OMM:PD_OMM + 17], pc[:, PC_MU:PC_MU + 17], -1.0, ALU.mult, 1.0, ALU.add)
        k.ts("dve", pcd[:, PD_HM:PD_HM + 17], pc[:, PC_MU:PC_MU + 17], 0.5, ALU.mult)
        k.ts("dve", pcd[:, PD_W0H:PD_W0H + 16], pc[:, PC_W0:PC_W0 + 16], 0.5, ALU.mult)
        k.ts("dve", pcd[:, PD_OMKA:PD_OMKA + 4], pc[:, PC_KA:PC_KA + 4], -1.0, ALU.mult, 1.0, ALU.add)

        alloc["base"] = alloc["off"]
        arena = sb("arena", [128, 2 * 12 * 512], F32)
        wst = arena[:, 0:DIN]
        wstb = arena[:, DIN:DIN + DIN // 2].bitcast(BF16)
        CH = [0]
        for dk in range(8):
            k.dma(wst, w_in[dk * 128:(dk + 1) * 128, :])
            if dk % 2 == 0:
                k.ts("dve", wstb[:], wst, pc[:, PC_NG + dk:PC_NG + dk + 1], ALU.mult)
            else:
                k.act(wstb[:], wst, AF.Copy, scale=pc[:, PC_NG + dk:PC_NG + dk + 1])
            k.dma(Wb.rearrange("b p (k c) -> p b k c", k=8)[:, :, dk, :],
                  wstb[:].rearrange("p (b c) -> p b c", c=128))

        xt = sb("xt", [128, 2, D], F32)
        hb = sb("hb", [128, 2, D], BF16)
        junk = sb("junk", [128, D], BF16)
        ssq = sb("ssq", [128, NT], F32)
        rstd = sb("rstd", [128, NT], F32)
        hT = sb("hT", [128, 2, 8, 514], BF16)
        wblk = sb("wblk", [128, 4, 8 * 128], BF16)
        zraw = sb("zraw", [128, 2, 514], F32)
        ztmp = sb("ztmp", [128, 2, 512], F32)
        zsl = sb("zsl", [128, 512], F32)
        zsc = sb("zsc", [128, 2, 4, 512], F32)
        lorab = sb("lorab", [128, 512], BF16)
        def T(i):
            return arena[:, (CH[0] * 12 + i) * 512:(CH[0] * 12 + i + 1) * 512]
        resetm = sb("resetm", [128, 512], F32)
        k.memset("pool", resetm[:], 1.0)
        for c in range(NM):
            k.memset("pool", resetm[:, c * 128:c * 128 + 1], 0.0)
        oAR = sb("oAR", [128, 2, 2, NM, 256], BF16)
        oBT = sb("oBT", [128, 2, 2, NM, 128], BF16)
        oKT = sb("oKT", [128, 2, 2, NM, 128], BF16)
        oBHc2 = sb("oBHc", [128, 2, 2, 512], BF16)
        oKHc2 = sb("oKHc", [128, 2, 2, 512], BF16)
        oVc2 = sb("oVc", [128, 2, 512], BF16)
        oBH = sb("oBH", [128, 2, 2, NM, 128], BF16)
        oKH = sb("oKH", [128, 2, 2, NM, 128], BF16)
        oV = sb("oV", [128, 2, NM, 128], BF16)
        oWC = sb("oWC", [128, 2, 2, NM], F32)
        oGR = sb("oGR", [128, 2, 512], F32)
        oBON = sb("oBON", [128, 2, 512], F32)
        oQ = sb("oQ", [128, 2, 512], BF16)
        oVAc2 = sb("oVAc", [128, 2, 512], BF16)
        oVA = sb("oVA", [128, 2, NM, 128], BF16)
        oGA = sb("oGA", [128, 2, 512], F32)

        ptr = [psB[2][:].bitcast(BF16), psB[3][:].bitcast(BF16)]

        def stage_x(i):
            sl = i % 2
            m = i // NM
            j = i % NM
            k.dma(xt[:, sl, :], xs[i * 128:(i + 1) * 128, :])
            S.add("act", lambda e, sl=sl, i=i: e.activation(out=junk[:], in_=xt[:, sl, :], func=AF.Square,
                                                          accum_out=ssq[:, i:i + 1]),
                  reads=[xt[:, sl, :]], writes=[junk[:], ssq[:, i:i + 1]])
            k.ts("dve", rstd[:, i:i + 1], ssq[:, i:i + 1], 1.0 / D, ALU.mult, RMS_EPS, ALU.add)
            k.tt("pool", rstd[:, i:i + 1], rstd[:, i:i + 1], cm05[:, 0:1], ALU.pow)
            k.act(hb[:, sl, :], xt[:, sl, :], AF.Copy, scale=rstd[:, i:i + 1])
            pt = ptr[i % 2].rearrange("p (a b) -> p a b", a=8)
            for dk in range(8):
                k.tr(pt[:, dk, :], hb[:, sl, dk * 128:(dk + 1) * 128], identb)
            k.cp("dve" if i % 2 == 0 else "act", hT[:, m % 2, :, 1 + j * 128:1 + (j + 1) * 128], pt)

        def halo_left(m):
            sl = m % 2
            t0 = m * NM
            if m == 0:
                k.memset("pool", hT[:, sl, :, 0:1], 0.0)
            elif t0 % SEG == 0:
                s = t0 // SEG
                k.ts("pool", hT[:, sl, :, 0:1], hT[:, 1 - sl, :, 512:513], flg[:, s:s + 1], ALU.mult)
            else:
                k.cp("pool", hT[:, sl, :, 0:1], hT[:, 1 - sl, :, 512:513])

        def halo_right(m):
            sl = m % 2
            t0 = m * NM
            if m == NMT - 1:
                k.memset("pool", hT[:, sl, :, 513:514], 0.0)
            elif (t0 + NM) % SEG == 0:
                s = (t0 + NM) // SEG
                k.ts("pool", hT[:, sl, :, 513:514], hT[:, 1 - sl, :, 1:2], flg[:, s:s + 1], ALU.mult)
            else:
                k.cp("pool", hT[:, sl, :, 513:514], hT[:, 1 - sl, :, 1:2])

        wcount = [0, 0]
        wq = [[], []]
        wready = [[], []]

        def w_begin(blocks):
            c = CH[0]
            wq[c] = list(blocks)
            wready[c] = []
            w_prefetch()
            w_prefetch()

        def w_prefetch():
            c = CH[0]
            if wq[c]:
                b = wq[c].pop(0)
                sl = c * 2 + wcount[c] % 2
                wcount[c] += 1
                k.dma(wblk[:, sl, :], Wb[b])
                wready[c].append((b, sl))

        def load_w(b):
            c = CH[0]
            if not wready[c]:
                w_begin([b])
            bb, sl = wready[c].pop(0)
            assert bb == b, (bb, b)
            return wblk[:, sl, :].rearrange("p (k c) -> p k c", k=8)

        zcount = [0]

        def proj_rwkv(m, b, dst):
            w = load_w(b)
            pz = psW[CH[0]]
            zr = zraw[:, CH[0], :]
            zt = ztmp[:, CH[0], :]
            sl = m % 2
            for half in range(2):
                for dk in range(8):
                    k.mm(pz[:, half * 512:half * 512 + 257], w[:, dk, :], hT[:, sl, dk, half * 257:(half + 1) * 257],
                         start=(dk == 0), stop=(dk == 7))
            w_prefetch()
            k.act(zr.rearrange("p (a b) -> p a b", a=2), pz.rearrange("p (a b) -> p a b", a=2)[:, :, 0:257], AF.Copy)
            k.act(dst, zr[:, 1:513], AF.Copy, scale=pcd[:, PD_OMM + b:PD_OMM + b + 1])
            if b % 3 == 0:
                k.tt("pool", zt, zr[:, 0:512], zr[:, 2:514], ALU.add)
                k.stt(dst, zt, pcd[:, PD_HM + b:PD_HM + b + 1], dst, ALU.mult, ALU.add)
            else:
                k.stt(dst, zr[:, 0:512], pcd[:, PD_HM + b:PD_HM + b + 1], dst, ALU.mult, ALU.add)
                k.stt(dst, zr[:, 2:514], pcd[:, PD_HM + b:PD_HM + b + 1], dst, ALU.mult, ALU.add)

        def proj_attn(m, b):
            w = load_w(b)
            pz = psW[CH[0]]
            sl = m % 2
            for dk in range(8):
                k.mm(pz[:, 0:512], w[:, dk, :], hT[:, sl, dk, 1:513], start=(dk == 0), stop=(dk == 7))
            w_prefetch()
            return pz[:, 0:512]

        def sigm_from_tanh(eng, out, th):
            k.ts(eng, out, th, 0.5, ALU.mult, 0.5, ALU.add)

        def passA_macro(m):
            CH[0] = 0
            proj_rwkv(m, 16, zsl[:])
            k.act(lorab[0:64, :], zsl[0:64, :], AF.Tanh)
            k.cp("pool", lorab[64:128, :], zsl[64:128, :])

            def rwkv_cb(cb):
                osl = cb % 2
                CH[0] = osl
                oVc = oVc2[:, osl]
                oBHc = oBHc2[:, osl]
                oKHc = oKHc2[:, osl]
                zs_ = zsc[:, osl]
                w_begin([cb, 4 + cb, 8 + cb, 12 + cb])
                proj_rwkv(m, 0 + cb, zs_[:, 0, :])
                proj_rwkv(m, 4 + cb, zs_[:, 1, :])
                proj_rwkv(m, 8 + cb, zs_[:, 2, :])
                proj_rwkv(m, 12 + cb, zs_[:, 3, :])
                r_, k_, v_, g_ = zs_[:, 0, :], zs_[:, 1, :], zs_[:, 2, :], zs_[:, 3, :]
                th = T(0)
                k.act(th, g_, AF.Tanh, scale=0.5)
                sigm_from_tanh("dve", th, th)
                k.tt("pool", oGR[:, osl, :], th, g_, ALU.mult)
                k.dma(S_GR[m, cb], oGR[:, osl, :])
                k.cp("dve", oVc, v_)
                pt = ptr[CH[0]].rearrange("p (a b) -> p a b", a=8)
                for j in range(NM):
                    k.tr(pt[:, j, :], oVc[:, j * 128:(j + 1) * 128], identb)
                k.cp("act", oV[:, osl, :, :], pt[:, 0:NM, :])
                k.dma(S_V[m, cb].rearrange("p (a b) -> p a b", a=NM), oV[:, osl, :, :])
                kk = T(1)
                k.ts("dve", kk, k_, pc[:, PC_KK + cb:PC_KK + cb + 1], ALU.mult)
                kk2 = T(2).bitcast(BF16)[:, 0:512]
                k.tt("pool", kk2, kk, kk, ALU.mult)
                pn = psB[CH[0]]
                k.mm(pn[:, :], blockones, kk2, start=True, stop=True)
                rn = T(2)
                k.ts("dve", rn, pn[:, :], 1e-18, ALU.max)
                k.act(rn, rn, AF.Ln)
                k.act(rn, rn, AF.Exp, scale=-0.5)
                kkn = T(1)
                k.tt("pool", kkn, kk, rn, ALU.mult)
                bon = T(3)
                for d in range(2):
                    pw = psB[CH[0]]
                    pa = psB[CH[0]]
                    k.mm(pw[:, :], wupb[0:64, d, cb * 128:(cb + 1) * 128], lorab[0:64, :])
                    lw = T(4)
                    k.act(lw, pw[:, :], AF.Tanh, bias=pcd[:, PD_W0H + d * 4 + cb:PD_W0H + d * 4 + cb + 1], scale=0.5)
                    k.mm(pa[:, :], wupb[64:128, d, cb * 128:(cb + 1) * 128], lorab[64:128, :])
                    k.ts("dve", lw, lw, -0.5 * CDEC, ALU.mult, -0.5 * CDEC, ALU.add)
                    a_ = T(5)
                    k.act(a_, pa[:, :], AF.Tanh, bias=pcd[:, PD_A0H + d * 4 + cb:PD_A0H + d * 4 + cb + 1], scale=0.5)
                    sigm_from_tanh("dve", a_, a_)
                    kd = T(6)
                    k.ts("dve", kd, a_, pc[:, PC_KA + cb:PC_KA + cb + 1], ALU.mult,
                         pcd[:, PD_OMKA + cb:PD_OMKA + cb + 1], ALU.add)
                    k.tt("pool", kd, kd, k_, ALU.mult)
                    kka = T(5)
                    k.tt("pool", kka, kkn, a_, ALU.mult)
                    rk = T(7) if d == 1 else bon
                    k.stt(rk, kd, pc[:, PC_RK + cb:PC_RK + cb + 1], r_, ALU.mult, ALU.mult)
                    if d == 1:
                        k.tt("pool", bon, bon, rk, ALU.add)
                    cl = T(7)
                    k.scan(cl, resetm[:], lw, 0.0, ALU.mult, ALU.add)
                    cl3 = cl.rearrange("p (a b) -> p a b", a=NM)
                    lw3 = lw.rearrange("p (a b) -> p a b", a=NM)
                    if d == 1:
                        tot = T(8)[:, 0:NM]
                        k.cp("dve", tot, cl3[:, :, 127])
                        k.tt("dve", cl, lw, cl, ALU.subtract)
                        k.tt("dve", cl3, cl3, tot.unsqueeze(2).to_broadcast([128, NM, 128]), ALU.add)
                    clC = T(8)[:, 8:8 + NM]
                    k.cp("dve", clC, cl3[:, :, 127] if d == 0 else cl3[:, :, 0])
                    k.act(oWC[:, osl, d, :], clC, AF.Exp)
                    Ep = T(9)
                    Em = T(10)
                    Eh = T(11)
                    k.act(Ep, cl, AF.Exp)
                    k.act(Em, cl, AF.Exp, scale=-1.0)
                    for j in range(NM):
                        k.act(Eh[:, j * 128:(j + 1) * 128], cl[:, j * 128:(j + 1) * 128], AF.Exp,
                              bias=clC[:, j:j + 1], scale=-1.0)
                    Ep3 = Ep.rearrange("p (a b) -> p a b", a=NM)
                    kkn3 = kkn.rearrange("p (a b) -> p a b", a=NM)
                    oA = oAR[:, osl, d, :, 0:128]
                    oR = oAR[:, osl, d, :, 128:256]
                    if d == 0:
                        k.stt(oA[:, :, 1:128], kkn3[:, :, 1:128], -1.0, Ep3[:, :, 0:127], ALU.mult, ALU.mult)
                        k.ts("pool", oA[:, :, 0:1], kkn3[:, :, 0:1], -1.0, ALU.mult)
                    else:
                        k.stt(oA[:, :, 0:127], kkn3[:, :, 0:127], -1.0, Ep3[:, :, 1:128], ALU.mult, ALU.mult)
                        k.ts("pool", oA[:, :, 127:128], kkn3[:, :, 127:128], -1.0, ALU.mult)
                    k.tt("pool", oR, r_.rearrange("p (a b) -> p a b", a=NM), Ep3, ALU.mult)
                    k.tt("dve", oBT[:, osl, d, :, :], kka.rearrange("p (a b) -> p a b", a=NM),
                         Em.rearrange("p (a b) -> p a b", a=NM), ALU.mult)
                    k.tt("pool", oKT[:, osl, d, :, :], kd.rearrange("p (a b) -> p a b", a=NM),
                         Em.rearrange("p (a b) -> p a b", a=NM), ALU.mult)
                    k.tt("dve", oBHc[:, d, :], kka, Eh, ALU.mult)
                    k.tt("pool", oKHc[:, d, :], kd, Eh, ALU.mult)
                    pt2 = ptr[CH[0]].rearrange("p (a b) -> p a b", a=8)
                    for j in range(NM):
                        k.tr(pt2[:, j, :], oBHc[:, d, j * 128:(j + 1) * 128], identb)
                        k.tr(pt2[:, NM + j, :], oKHc[:, d, j * 128:(j + 1) * 128], identb)
                    k.cp("act", oBH[:, osl, d, :, :], pt2[:, 0:NM, :])
                    k.cp("dve", oKH[:, osl, d, :, :], pt2[:, NM:2 * NM, :])
                    k.dma(S_AR[d, m, cb].rearrange("p (a b) -> p a b", a=NM), oAR[:, osl, d, :, :])
                    k.dma(S_BT[d, m, cb].rearrange("p (a b) -> p a b", a=NM), oBT[:, osl, d, :, :])
                    k.dma(S_KT[d, m, cb].rearrange("p (a b) -> p a b", a=NM), oKT[:, osl, d, :, :])
                    k.dma(S_BH[d, m, cb].rearrange("p (a b) -> p a b", a=NM), oBH[:, osl, d, :, :])
                    k.dma(S_KH[d, m, cb].rearrange("p (a b) -> p a b", a=NM), oKH[:, osl, d, :, :])
                    k.dma(S_WC[d, m, cb], oWC[:, osl, d, :])
                pb = psB[CH[0]]
                bonb = T(7).bitcast(BF16)[:, 0:512]
                k.cp("dve", bonb, bon)
                k.mm(pb[:, :], blockones, bonb, start=True, stop=True)
                k.tt("dve", oBON[:, osl, :], pb[:, :], v_, ALU.mult)
                k.dma(S_BON[m, cb], oBON[:, osl, :])
            def attn_cb(cb):
                osl = cb % 2
                CH[0] = osl
                oVAc = oVAc2[:, osl]
                w_begin([17 + cb, 21 + cb, 25 + cb, 29 + cb])
                for which, dst, gcol in ((0, S_Q, PC_QG), (1, S_K, PC_KG)):
                    pz = proj_attn(m, 17 + which * 4 + cb)
                    q_ = T(0)
                    k.cp("act", q_, pz)
                    q2 = T(1).bitcast(BF16)[:, 0:512]
                    k.tt("pool", q2, q_, q_, ALU.mult)
                    pn = psB[CH[0]]
                    k.mm(pn[:, :], blockones, q2, start=True, stop=True)
                    rn = T(1)
                    k.ts("dve", rn, pn[:, :], 1.0 / 64, ALU.mult, RMS_EPS, ALU.add)
                    k.act(rn, rn, AF.Ln)
                    k.act(rn, rn, AF.Exp, scale=-0.5)
                    k.stt(oQ[:, osl, :], q_, pc[:, gcol:gcol + 1], rn, ALU.mult, ALU.mult)
                    k.dma(dst[m, cb], oQ[:, osl, :])
                pz = proj_attn(m, 25 + cb)
                k.cp("act", oVAc, pz)
                pt = ptr[CH[0]].rearrange("p (a b) -> p a b", a=8)
                for j in range(NM):
                    k.tr(pt[:, j, :], oVAc[:, j * 128:(j + 1) * 128], identb)
                k.cp("dve", oVA[:, osl, :, :], pt[:, 0:NM, :])
                k.dma(S_VA[m, cb].rearrange("p (a b) -> p a b", a=NM), oVA[:, osl, :, :])
                pz = proj_attn(m, 29 + cb)
                g_ = T(2)
                k.cp("act", g_, pz)
                th = T(3)
                k.act(th, g_, AF.Tanh, scale=0.5)
                sigm_from_tanh("dve", th, th)
                k.tt("pool", oGA[:, osl, :], th, g_, ALU.mult)
                k.dma(S_GA[m, cb], oGA[:, osl, :])


            for pair in ((0, 1), (2, 3)):
                lists = []
                for cb in pair:
                    k.cap = []
                    rwkv_cb(cb)
                    lists.append(k.cap)
                    k.cap = None
                k.merge(lists, store_delay=0)
            for pair in ((0, 1), (2, 3)):
                lists = []
                for cb in pair:
                    k.cap = []
                    attn_cb(cb)
                    lists.append(k.cap)
                    k.cap = None
                k.merge(lists, store_delay=0)
            CH[0] = 0

        print("SBUF pass A bytes", alloc["off"], "base", alloc["base"])
        if cfg.upto == "W":
            k.dma(ys[0:128, :], xs[0:128, :])
            with nc.Block() as block:
                S.emit(sems, dsems, block)
            return nc
        for i in range(min(NM, NT)):
            stage_x(i)
        if cfg.upto == "X":
            k.dma(ys[0:128, :], xs[0:128, :])
            with nc.Block() as block:
                S.emit(sems, dsems, block)
            return nc
        for m in range(NMT):
            halo_left(m)
            if m + 1 < NMT:
                for j in range(NM):
                    stage_x((m + 1) * NM + j)
            halo_right(m)
            passA_macro(m)

        if cfg.upto == "A":
            with nc.Block() as block:
                S.emit(sems, dsems, block)
            return nc

        W0a, W0b = psW[0][:, 0:512], psW[0][:, 512:1024]
        W1a, W1b = psW[1][:, 0:512], psW[1][:, 512:1024]

        def rwkv_pass(d):
            pass_reset()
            ldAR = sb("ldAR", [128, 2, 4, NM, 256], BF16)
            ldBT = sb("ldBT", [128, 2, 4, NM, 128], BF16)
            ldKT = sb("ldKT", [128, 2, 4, NM, 128], BF16)
            ldBH = sb("ldBH", [128, 2, 4, NM, 128], BF16)
            ldKH = sb("ldKH", [128, 2, 4, NM, 128], BF16)
            ldV = sb("ldV", [128, 2, 4, NM, 128], BF16)
            ldWC = sb("ldWC", [128, 2, 4, NM], F32)
            MAb = sb("MA", [128, 2, 8, 256], BF16)
            AKb = sb("AK", [128, 2, 8, 256], BF16)
            N0b = sb("N0", [128, 2, 8, 128], BF16)
            Mn = sb("Mn", [128, 2, 8, 128], BF16)
            Nn = sb("Nn", [128, 2, 8, 128], BF16)
            Ttb = sb("Ttb", [128, 2, 8, 128], BF16)
            G1s = sb("G1s", [128, 512], F32)
            G12 = sb("G12", [128, 512], BF16)
            Ubf = sb("Ubf", [128, 512], BF16)
            Hs = sb("Hs", [128, 256], F32)
            Hblk = sb("Hblk", [128, 4, 128], BF16)
            yo = sb("yo", [128, 2, 512], F32)
            mk2 = sb("mk2", [128, 256], F32)
            mkN = sb("mkN", [128, 128], F32)
            k.cp("pool", mk2[:, 0:128], cst[:, 1 if d == 0 else 2, :])
            k.cp("pool", mk2[:, 128:256], cst[:, 3 if d == 0 else 4, :])
            k.cp("pool", mkN[:], cst[:, 2 if d == 0 else 1, :])
            k.memset("pool", Hs[:], 0.0)
            k.memset("pool", Hblk[:], 0.0)
            import os as _os
            for _w in range(int(_os.environ.get("WARM", "0"))):
                k.mm(W0a, identb, MAb[:, 0, 0:2, :].rearrange("p a b -> p (a b)"), start=True, stop=True)
            if d == 1:
                yf = sb("yf", [128, 2, 512], F32)
                ysum = sb("ysum", [128, 512], F32)
                lnt = sb("lnt", [128, 512], F32)
                lnm = sb("lnm", [128, 16], F32)
                ynT = sb("ynT", [128, 512], F32)
                ldBON = sb("ldBON", [128, 2, 4, 128], F32)
                ldGR = sb("ldGR", [128, 2, 4, 128], F32)
                oMR = sb("oMR", [128, 2, 4, 128], BF16)

            def load(m, sl):
                k.dma(ldAR[:, sl].rearrange("p c t x -> p c (t x)"), S_AR[d, m].rearrange("c p x -> p c x"))
                k.dma(ldBT[:, sl].rearrange("p c t x -> p c (t x)"), S_BT[d, m].rearrange("c p x -> p c x"))
                k.dma(ldKT[:, sl].rearrange("p c t x -> p c (t x)"), S_KT[d, m].rearrange("c p x -> p c x"))
                k.dma(ldBH[:, sl].rearrange("p c t x -> p c (t x)"), S_BH[d, m].rearrange("c p x -> p c x"))
                k.dma(ldKH[:, sl].rearrange("p c t x -> p c (t x)"), S_KH[d, m].rearrange("c p x -> p c x"))
                k.dma(ldV[:, sl].rearrange("p c t x -> p c (t x)"), S_V[m].rearrange("c p x -> p c x"))
                k.dma(ldWC[:, sl], S_WC[d, m].rearrange("c p x -> p c x"))

            def tile(m, sl, j, cnt):
                i = m * NM + j
                AR = ldAR[:, sl]
                BT = ldBT[:, sl]
                KT = ldKT[:, sl]
                BH = ldBH[:, sl]
                KH = ldKH[:, sl]
                V = ldV[:, sl]
                MA = MAb[:, cnt % 2]
                AK = AKb[:, cnt % 2]
                N0 = N0b[:, cnt % 2]
                bnd = None
                if d == 0 and i % SEG == 0 and i > 0:
                    bnd = i // SEG
                if d == 1 and (i + 1) % SEG == 0 and i < NT - 1:
                    bnd = (i + 1) // SEG
                if bnd is not None:
                    k.ts("pool", Hs[:], Hs[:], flg[:, bnd:bnd + 1], ALU.mult)
                    k.ts("pool", Hblk[:], Hblk[:], flg[:, bnd:bnd + 1], ALU.mult)
                m2 = mk2[:].unsqueeze(1).to_broadcast([128, 2, 256])
                mN = mkN[:].unsqueeze(1).to_broadcast([128, 2, 128])
                sbanks = {0: [psB[0], psB[1], psB[2][:, 0:256]], 1: [psB[3], W1b, psB[2][:, 256:512]]}
                for cp_ in range(2):
                    for hh in range(2):
                        po = 64 * hh
                        bs = sbanks[hh]
                        for ci in range(2):
                            cb = 2 * cp_ + ci
                            k.mm(bs[0][:, ci * 256:(ci + 1) * 256], BT[po:po + 64, cb, j, :], AR[po:po + 64, cb, j, :])
                            k.mm(bs[1][:, ci * 256:(ci + 1) * 256], KT[po:po + 64, cb, j, :], AR[po:po + 64, cb, j, :])
                            k.mm(bs[2][:, ci * 128:(ci + 1) * 128], AR[po:po + 64, cb, j, 0:128], BT[po:po + 64, cb, j, :])
                        h0 = 4 * cp_ + hh
                        k.tt("dve", MA[:, h0:h0 + 3:2, :], bs[0].rearrange("p (a b) -> p a b", a=2), m2, ALU.mult)
                        k.tt("dve", AK[:, h0:h0 + 3:2, :], bs[1].rearrange("p (a b) -> p a b", a=2), m2, ALU.mult)
                        k.tt("dve", N0[:, h0:h0 + 3:2, :], bs[2].rearrange("p (a b) -> p a b", a=2), mN, ALU.mult)
                k.tt("pool", Ttb[:, 0], MA[:, :, 0:128], cst[:, 0, :].unsqueeze(1).to_broadcast([128, 8, 128]), ALU.add)
                Ttbank = [psB[2], psB[3]]
                Mbank = [W1a, psB[0]]
                Nbank = [W1b, psB[1]]
                for half in range(2):
                    for q in range(4):
                        h = 4 * half + q
                        k.mm(Ttbank[half][:, q * 128:(q + 1) * 128], identb, Ttb[:, 0, h, :],
                             start=(q == 0), stop=False, skip=True)
                Mcur = [MA[:, h, 0:128] for h in range(8)]
                Ncur = [N0[:, h, :] for h in range(8)]
                for lev in range(7):
                    for half in range(2):
                        if lev <= 5:
                            for q in range(4):
                                h = 4 * half + q
                                k.mm(Nbank[half][:, q * 128:(q + 1) * 128], Mcur[h], Ncur[h])
                            if lev <= 4:
                                for q in range(4):
                                    h = 4 * half + q
                                    k.mm(Mbank[half][:, q * 128:(q + 1) * 128], Ncur[h], Mcur[h])
                        if lev >= 1:
                            for q in range(4):
                                h = 4 * half + q
                                k.mm(Ttbank[half][:, q * 128:(q + 1) * 128], Ncur[h], Ttb[:, (lev - 1) % 2, h, :],
                                     start=False, stop=(lev == 6), skip=True)
                            k.cp("act" if half == 0 else "dve", Ttb[:, lev % 2, 4 * half:4 * half + 4, :],
                                 Ttbank[half].rearrange("p (a b) -> p a b", a=4))
                        if lev <= 5:
                            k.cp("act" if (half == 0 and lev % 2 == 0) else "dve", Nn[:, lev % 2, 4 * half:4 * half + 4, :],
                                 Nbank[half].rearrange("p (a b) -> p a b", a=4))
                            if lev <= 4:
                                k.cp("act", Mn[:, lev % 2, 4 * half:4 * half + 4, :],
                                     Mbank[half].rearrange("p (a b) -> p a b", a=4))
                    if lev <= 5:
                        Ncur = [Nn[:, lev % 2, h, :] for h in range(8)]
                        if lev <= 4:
                            Mcur = [Mn[:, lev % 2, h, :] for h in range(8)]
                TT = Ttb[:, 0]
                for h in range(8):
                    cb, po = h // 2, 64 * (h % 2)
                    k.mm(W0a[:, h * 64:(h + 1) * 64], AK[:, h, 0:128], V[:, cb, j, po:po + 64])
                k.cp("act", G1s[:], W0a)
                for cb in range(4):
                    k.mm(W0b[:, cb * 128:(cb + 1) * 128], AR[:, cb, j, 0:128], Hblk[:, cb, :])
                k.tt("dve", G12[:], W0b, G1s[:], ALU.add)
                for h in range(8):
                    k.mm(W0a[:, h * 64:(h + 1) * 64], TT[:, h, :], G12[:, h * 64:(h + 1) * 64])
                k.cp("act", Ubf[:], W0a)
                for cb in range(4):
                    h0, h1 = 2 * cb, 2 * cb + 1
                    o0 = W0b[:, h0 * 64:(h0 + 1) * 64]
                    o1 = W0b[:, h1 * 64:(h1 + 1) * 64]
                    k.mm(o0, AK[:, h0, 128:256], V[:, cb, j, 0:64], start=True, stop=False, skip=True)
                    k.mm(o1, AK[:, h1, 128:256], V[:, cb, j, 64:128], start=False, stop=False, skip=True)
                    k.mm(W0b[:, cb * 128:(cb + 1) * 128], AR[:, cb, j, 128:256], Hblk[:, cb, :], start=False, stop=False, skip=True)
                    k.mm(o0, MA[:, h0, 128:256], Ubf[:, h0 * 64:(h0 + 1) * 64], start=False, stop=False, skip=True)
                    k.mm(o1, MA[:, h1, 128:256], Ubf[:, h1 * 64:(h1 + 1) * 64], start=False, stop=True, skip=True)
                for h in range(8):
                    cb, po = h // 2, 64 * (h % 2)
                    o = W1a[po:po + 64, cb * 64:(cb + 1) * 64]
                    k.mm(o, KH[:, cb, j, po:po + 64], V[:, cb, j, po:po + 64], start=True, stop=False)
                    k.mm(o, BH[:, cb, j, po:po + 64], Ubf[:, h * 64:(h + 1) * 64], start=False, stop=True)
                ysl = cnt % 2
                if d == 0:
                    k.cp("act", yo[:, ysl, :], W0b)
                    k.dma(S_YF[i], yo[:, ysl, :])
                else:
                    k.dma(yf[:, ysl, :], S_YF[i])
                    k.tt("dve", ysum[:], W0b, yf[:, ysl, :], ALU.add)
                for cb in range(4):
                    k.stt(Hs[:, cb * 64:(cb + 1) * 64], Hs[:, cb * 64:(cb + 1) * 64], ldWC[:, sl, cb, j:j + 1],
                          W1a[:, cb * 64:(cb + 1) * 64], ALU.mult, ALU.add)
                Hs3 = Hs[:].rearrange("p (a b) -> p a b", a=4)
                k.cp("pool", Hblk[0:64, :, 0:64], Hs3[0:64])
                k.cp("pool", Hblk[64:128, :, 64:128], Hs3[64:128])
                if d == 1:
                    y3 = ysum[:].rearrange("p (a b) -> p a b", a=8)
                    k.red(lnm[:, 0:8], y3)
                    k.ts("dve", lnm[:, 0:8], lnm[:, 0:8], 1.0 / 64, ALU.mult)
                    k.tt("dve", lnt[:].rearrange("p (a b) -> p a b", a=8), y3,
                         lnm[:, 0:8].unsqueeze(2).to_broadcast([128, 8, 64]), ALU.subtract)
                    k.tt("pool", ysum[:], lnt[:], lnt[:], ALU.mult)
                    k.red(lnm[:, 8:16], y3)
                    k.ts("dve", lnm[:, 8:16], lnm[:, 8:16], 1.0 / 64, ALU.mult, LNX_EPS, ALU.add)
                    k.tt("pool", lnm[:, 8:16], lnm[:, 8:16], cm05[:, 0:1].to_broadcast([128, 8]), ALU.pow)
                    k.tt("dve", lnt[:].rearrange("p (a b) -> p a b", a=8), lnt[:].rearrange("p (a b) -> p a b", a=8),
                         lnm[:, 8:16].unsqueeze(2).to_broadcast([128, 8, 64]), ALU.mult)
                    for cb in range(4):
                        k.tr(W1b[:, cb * 128:(cb + 1) * 128], lnt[:, cb * 128:(cb + 1) * 128], identf)
                    k.dma(ldBON[:, ysl], S_BON[m].rearrange("c p (t x) -> p c t x", t=NM)[:, :, j, :])
                    k.dma(ldGR[:, ysl], S_GR[m].rearrange("c p (t x) -> p c t x", t=NM)[:, :, j, :])
                    for cb in range(4):
                        k.act(ynT[:, cb * 128:(cb + 1) * 128], W1b[:, cb * 128:(cb + 1) * 128], AF.Identity,
                              bias=pc[:, PC_LB + cb:PC_LB + cb + 1], scale=pc[:, PC_LG + cb:PC_LG + cb + 1])
                    yn3 = ynT[:].rearrange("p (a b) -> p a b", a=4)
                    k.tt("pool", yn3, yn3, ldBON[:, ysl], ALU.add)
                    k.tt("pool", oMR[:, ysl], yn3, ldGR[:, ysl], ALU.mult)
                    k.dma(S_MR[i].rearrange("c p x -> p c x"), oMR[:, ysl])

            order = list(range(NMT)) if d == 0 else list(range(NMT - 1, -1, -1))
            load(order[0], 0)
            cnt = 0
            for idx, m in enumerate(order):
                sl = idx % 2
                if idx + 1 < len(order):
                    load(order[idx + 1], 1 - sl)
                js = range(NM) if d == 0 else range(NM - 1, -1, -1)
                for j in js:
                    tile(m, sl, j, cnt)
                    cnt += 1

        rwkv_pass(0)
        if cfg.upto == "B":
            with nc.Block() as block:
                S.emit(sems, dsems, block)
            return nc
        rwkv_pass(1)
        if cfg.upto == "C":
            with nc.Block() as block:
                S.emit(sems, dsems, block)
            return nc

        pass_reset()
        RS = 8
        EG = sb("EG", [128, 8, 15, 64], BF16)
        vts = sb("vts", [128, NT * 14], F32)
        Wo = sb("Wo", [128, 8, D], BF16)
        Kring = sb("Kring", [128, RS, 4, 128], BF16)
        Vring = sb("Vring", [128, RS, 4, 128], BF16)
        Qt = sb("Qt", [128, 2, 4, 128], BF16)
        ebuf = sb("ebuf", [128, 2, 8, 128], F32)
        pT = sb("pT", [128, 2, 7, 8, 128], BF16)
        gaT = sb("gaT", [128, 2, 4, 128], F32)
        mrT = sb("mrT", [128, 2, 4, 128], BF16)
        maT = sb("maT", [128, 2, 4, 128], BF16)
        dsb2 = sb("dsb", [128, 2, 512], F32)
        otmp2 = sb("otmp", [128, 2, 512], F32)
        xr = sb("xr", [128, 2, D], F32)
        yout = sb("yout", [128, 2, D], F32)
        onesb = sb("onesb", [128, 64], BF16)
        cm1 = sb("cm1", [128, 1], F32)
        k.memset("pool", onesb[:], 1.0)
        k.memset("pool", cm1[:], -1.0)
        k.dma(vts[:], vtab)
        wtmp = sb("wtmp", [128, 2, D], F32)
        for ek in range(8):
            k.dma(wtmp[:, ek % 2, :], w_out[ek * 128:(ek + 1) * 128, :])
            k.cp("pool" if ek % 2 == 0 else "dve", Wo[:, ek, :], wtmp[:, ek % 2, :])
        ohf = sb("ohf", [128, 4096], F32)
        ohb = sb("ohb", [128, 4096], BF16)
        rpf = sb("rpf", [128, 120], F32)
        rpe = sb("rpe", [128, 120], BF16)
        k.dma(ohf[0:31, :], ohot)
        k.dma(rpf[0:31, :], rpbT)
        k.cp("dve", ohb[0:31, :], ohf[0:31, :])
        k.act(rpe[0:31, :], rpf[0:31, :], AF.Exp)
        k.memset("pool", EG[64:128, :, 0:1, :], 0.0)
        for g in range(16):
            bank = psB[g % 2]
            for c in range(4):
                cq = g * 4 + c
                k.mm(bank[0:64, c * 120:(c + 1) * 120], ohb[0:31, cq * 64:(cq + 1) * 64], rpe[0:31, :])
                k.mm(bank[64:128, c * 120:(c + 1) * 120], ohb[0:31, cq * 64:(cq + 1) * 64], rpe[0:31, :])
            k.cp("act", EG[0:64, :, :, g * 4:(g + 1) * 4],
                 bank[0:64, 0:480].rearrange("p (c h u) -> p h u c", c=4, h=8))
            k.cp("dve", EG[64:128, :, 1:15, g * 4:(g + 1) * 4],
                 bank[64:128, 0:480].rearrange("p (c h u) -> p h u c", c=4, h=8)[:, :, 0:14, :])

        loaded = [-1]

        def ensure_keys(upto):
            while loaded[0] < min(upto, NT - 1):
                t = loaded[0] + 1
                mk, jk = t // NM, t % NM
                k.dma(Kring[:, t % RS], S_K[mk].rearrange("c p (t x) -> p c t x", t=NM)[:, :, jk, :])
                k.dma(Vring[:, t % RS], S_VA[mk].rearrange("c p (t x) -> p c t x", t=NM)[:, :, jk, :])
                loaded[0] = t

        zc = [0]
        for i in range(NT):
            m, j = i // NM, i % NM
            sl = i % 2
            pos = i % SEG
            offs = [-2, -1, 0, 1, 2]
            if pos == 0:
                offs.append(3)
            if pos == SEG - 1:
                offs.insert(0, -3)
            offs = [o for o in offs if 0 <= i + o < NT]
            ensure_keys(i + 3)
            k.dma(Qt[:, sl], S_Q[m].rearrange("c p (t x) -> p c t x", t=NM)[:, :, j, :])
            k.dma(gaT[:, sl], S_GA[m].rearrange("c p (t x) -> p c t x", t=NM)[:, :, j, :])
            k.dma(mrT[:, sl], S_MR[i].rearrange("c p x -> p c x"))
            k.dma(xr[:, sl, :], xs[i * 128:(i + 1) * 128, :])
            for ki, o in enumerate(offs):
                t = i + o
                ks = t % RS
                pz = psW[zc[0] % 2]
                es_ = zc[0] % 2
                zc[0] += 1
                for hh in range(2):
                    po = 64 * hh
                    for cb in range(4):
                        k.mm(pz[:, (hh * 4 + cb) * 128:(hh * 4 + cb + 1) * 128], Kring[po:po + 64, ks, cb, :],
                             Qt[po:po + 64, sl, cb, :])
                k.act(ebuf[:, es_].rearrange("p a b -> p (a b)"), pz[:], AF.Exp, scale=0.125)
                u0 = 7 - 2 * o
                for qh in range(2):
                    col = (i * 7 + (o + 3)) * 2 + qh
                    k.stt(pT[:, sl, ki, :, qh * 64:(qh + 1) * 64], ebuf[:, es_, :, qh * 64:(qh + 1) * 64],
                          vts[:, col:col + 1], EG[:, :, u0 + qh, :], ALU.mult, ALU.mult)
            nk = len(offs)
            bO = psB[2 * (i % 2)]
            bD = psB[2 * (i % 2) + 1]
            dsb = dsb2[:, i % 2]
            otmp = otmp2[:, i % 2]
            for hh in range(2):
                po = 64 * hh
                for cb in range(4):
                    idx = hh * 4 + cb
                    oo = bO[po:po + 64, cb * 128:(cb + 1) * 128]
                    dd = bD[po:po + 64, cb * 128:(cb + 1) * 128]
                    for ki, o in enumerate(offs):
                        ks = (i + o) % RS
                        k.mm(oo, Vring[:, ks, cb, po:po + 64], pT[:, sl, ki, idx, :], start=(ki == 0), stop=(ki == nk - 1))
                    for ki, o in enumerate(offs):
                        k.mm(dd, onesb[:, :], pT[:, sl, ki, idx, :], start=(ki == 0), stop=(ki == nk - 1))
            k.recip(dsb, bD[:])
            k.tt("dve", otmp, bO[:], dsb, ALU.mult)
            k.tt("pool", maT[:, sl].rearrange("p a b -> p (a b)"), otmp, gaT[:, sl].rearrange("p a b -> p (a b)"), ALU.mult)
            for half in range(2):
                ob = (bO if half == 0 else bD)[:]
                for ek in range(8):
                    lh = mrT[:, sl, ek, :] if ek < 4 else maT[:, sl, ek - 4, :]
                    k.mm(ob, lh, Wo[:, ek, half * 512:(half + 1) * 512], start=(ek == 0), stop=(ek == 7))
                k.tt("dve", yout[:, sl, half * 512:(half + 1) * 512], ob, xr[:, sl, half * 512:(half + 1) * 512], ALU.add)
            k.dma(ys[i * 128:(i + 1) * 128, :], yout[:, sl, :])

        with nc.Block() as block:
            S.emit(sems, dsems, block)
    return nc


def _cols(v, nblk):
    return np.ascontiguousarray(np.asarray(v, np.float32).reshape(nblk, 128).T)


def make_consts():
    c = np.zeros((128, 6, 128), np.float32)
    i = np.arange(128)
    c[:, 0, :] = np.eye(128, dtype=np.float32)
    c[:, 1, :] = (i[:, None] < i[None, :])
    c[:, 2, :] = (i[:, None] > i[None, :])
    c[:, 3, :] = (i[:, None] <= i[None, :])
    c[:, 4, :] = (i[:, None] >= i[None, :])
    c[:, 5, :] = ((i[:, None] // 64) == (i[None, :] // 64))
    return c


def pack_params(p):
    pcols = np.zeros((128, NPC), np.float32)
    pcols[:, PC_MU:PC_MU + 17] = _cols(p["shift_mu"], 17)
    for d in range(2):
        pcols[:, PC_W0 + 4 * d:PC_W0 + 4 * d + 4] = _cols(p["w0"][d], 4)
        pcols[:, PC_A0 + 4 * d:PC_A0 + 4 * d + 4] = _cols(p["a0"][d], 4)
    pcols[:, PC_KK:PC_KK + 4] = _cols(p["k_k"], 4)
    pcols[:, PC_KA:PC_KA + 4] = _cols(p["k_a"], 4)
    pcols[:, PC_RK:PC_RK + 4] = _cols(p["r_k"].reshape(-1), 4)
    pcols[:, PC_LG:PC_LG + 4] = _cols(p["lnx_g"], 4)
    pcols[:, PC_LB:PC_LB + 4] = _cols(p["lnx_b"], 4)
    pcols[:, PC_QG] = np.tile(p["q_norm_g"], 2)
    pcols[:, PC_KG] = np.tile(p["k_norm_g"], 2)
    pcols[:, PC_NG:PC_NG + 8] = _cols(p["norm_g"], 8)
    wup = np.zeros((128, 2, 512), np.float32)
    wup[0:64] = np.transpose(p["w_up"], (1, 0, 2))
    wup[64:128] = np.transpose(p["a_up"], (1, 0, 2))
    cq = np.arange(64)
    ck = np.arange(64)
    cs = np.clip(cq - 8, 0, 48)
    inw = (ck[None, :] >= cs[:, None]) & (ck[None, :] < cs[:, None] + 16)
    dc = ck[None, :] - cq[:, None] + 15
    oh = np.zeros((31, 64, 64), np.float32)
    for q_ in range(64):
        for k_ in range(64):
            if inw[q_, k_]:
                oh[dc[q_, k_], q_, k_] = 1.0
    rp = np.asarray(p["rpb"], np.float32)
    rp = rp[:, ::-1, :]
    rp = rp.reshape(4, 2, 15, 31).transpose(3, 1, 0, 2)
    return {
        "ohot": np.ascontiguousarray(oh.reshape(31, 4096)),
        "rpbT": np.ascontiguousarray(rp.reshape(31, 120)),
        "w_in": np.ascontiguousarray(p["w_in"], np.float32),
        "w_out": np.ascontiguousarray(p["w_out"], np.float32),
        "pcols": pcols,
        "wup": wup,
        "consts": make_consts(),
    }


def make_core_meta(cfg, slots):
    NT, SEG, NSEG = cfg.NT, cfg.SEG, cfg.NSEG
    flags = np.zeros((128, NSEG + 1), np.float32)
    for s_ in range(1, NSEG):
        a, b = slots[s_ - 1], slots[s_]
        if a is not None and b is not None and a[0] == b[0] and b[1] == a[1] + 1:
            flags[:, s_] = 1.0
    vt = np.zeros((128, NT, 7, 2), np.float32)
    for i in range(NT):
        sl = slots[i // SEG]
        if sl is None:
            vt[:, i, 3, :] = 1.0
            continue
        sid, sidx, nseg = sl
        R = nseg * SEG * 2
        for qh in range(2):
            r = (sidx * SEG + (i % SEG)) * 2 + qh
            R0 = min(max(r - 4, 0), R - 8)
            for o in range(-3, 4):
                t = i + o
                if t < 0 or t >= NT:
                    continue
                sk = slots[t // SEG]
                if sk is None or sk[0] != sid:
                    continue
                if sk[1] - sidx != (t // SEG) - (i // SEG):
                    continue
                for kh in range(2):
                    rho = (sk[1] * SEG + (t % SEG)) * 2 + kh
                    if R0 <= rho < R0 + 8:
                        vt[kh * 64:(kh + 1) * 64, i, o + 3, qh] = 1.0
    return flags, np.ascontiguousarray(vt.reshape(128, NT * 14))


PARAM_NAMES = ["norm_g", "w_in", "shift_mu", "w0", "w_up", "a0", "a_up", "k_k", "k_a", "r_k",
               "lnx_g", "lnx_b", "q_norm_g", "k_norm_g", "rpb", "w_out"]


def run_layer(cfg, seqs, p, n_cores):
    NT, SEG, NSEG = cfg.NT, cfg.SEG, cfg.NSEG
    segtok = SEG * 128
    order = sorted(range(len(seqs)), key=lambda i: -seqs[i].shape[0])
    cores = [[] for _ in range(n_cores)]
    used = [0] * n_cores
    for si in order:
        ns = seqs[si].shape[0] // segtok
        c = min(range(n_cores), key=lambda c_: (used[c_] + ns > NSEG, used[c_]))
        assert used[c] + ns <= NSEG
        for g in range(ns):
            cores[c].append((si, g, ns))
        used[c] += ns
    shared = pack_params(p)
    in_maps = []
    for c in range(n_cores):
        slots = cores[c] + [None] * (NSEG - len(cores[c]))
        x = np.zeros((NT * 128, D), np.float32)
        for s_, sl in enumerate(slots):
            if sl is not None:
                x[s_ * segtok:(s_ + 1) * segtok] = seqs[sl[0]][sl[1] * segtok:(sl[1] + 1) * segtok]
        flags, vt = make_core_meta(cfg, slots)
        im = dict(shared)
        im["xs"] = x
        im["flags"] = flags
        im["vtab"] = vt
        in_maps.append(im)
    nc = build_program(cfg)
    res = run_bass_kernel_spmd(nc, in_maps, core_ids=list(range(n_cores)))
    outs = [np.zeros_like(s_) for s_ in seqs]
    for c in range(n_cores):
        y = np.asarray(res.results[c]["ys"])
        for s_, sl in enumerate(cores[c]):
            outs[sl[0]][sl[1] * segtok:(sl[1] + 1) * segtok] = y[s_ * segtok:(s_ + 1) * segtok]
    return outs, res


def kernel(**inputs):
    cfg = Cfg(128, 32)
    p = {n_: np.asarray(inputs[n_], np.float32)[0] for n_ in PARAM_NAMES}
    xp = np.asarray(inputs["x_prompt"], np.float32)
    xsm = np.asarray(inputs["x_sample"], np.float32)
    seqs = [xp[b] for b in range(xp.shape[0])] + [xsm[b] for b in range(xsm.shape[0])]
    outs, _ = run_layer(cfg, seqs, p, 8)
    yp = np.stack(outs[:xp.shape[0]], 0)
    ysm = np.stack(outs[xp.shape[0]:], 0)
    return (yp, ysm)
```

```python
import numpy as np
from contextlib import ExitStack
import concourse.bass as bass
import concourse.mybir as mybir
from concourse.bass_utils import run_bass_kernel_spmd

F32 = mybir.dt.float32
BF16 = mybir.dt.bfloat16
ALU = mybir.AluOpType
AF = mybir.ActivationFunctionType
AX = mybir.AxisListType

D = 1024
DIN = 4224
NBLK = 33
RMS_EPS = 1e-6
LNX_EPS = 64e-5
CDEC = 0.6065306597126334

ENGS = ["pe", "act", "dve", "pool", "sp"]
NDMASEM = 48


class Op:
    __slots__ = ("eng", "fn", "chan", "seq", "signal", "waits", "vc", "is_dma", "sigval", "rows",
                 "deps", "idx", "dur", "start", "finish", "succs", "npred", "ready")

    def __init__(self, eng, fn, is_dma):
        self.eng = eng
        self.fn = fn
        self.is_dma = is_dma
        self.signal = False
        self.waits = []
        self.vc = None
        self.chan = None
        self.seq = None
        self.sigval = None
        self.rows = None
        self.deps = {}
        self.dur = 500.0
        self.start = 0.0
        self.finish = 0.0
        self.succs = []
        self.npred = 0
        self.ready = 0.0


def _region(ap):
    name = ap.tensor.name
    pat = ap.ap
    off = ap.offset
    es = mybir.dt.size(ap.dtype)
    if "DRAM" in str(ap.space).upper():
        ext = 0
        for st, cnt in pat:
            ext += (cnt - 1) * abs(st)
        return name, 0, 1, off * es, (off + ext + 1) * es
    row = pat[0][0]
    npart = pat[0][1]
    if row == 0:
        p0 = 0
        f0 = off
    else:
        p0 = off // row
        f0 = off - p0 * row
    ext = 0
    for st, cnt in pat[1:]:
        ext += (cnt - 1) * abs(st)
    return name, p0, p0 + npart, f0 * es, (f0 + ext + 1) * es


def _free_elems(ap):
    n = 1
    for st, cnt in ap.ap[1:]:
        n *= cnt
    return n


class Sched:
    def __init__(self, nc, same_engine_sync=True, reorder=True):
        self.nc = nc
        self.ops = {e: [] for e in ENGS}
        self.all_ops = []
        self.hist = {}
        self.dma_uses = [0] * NDMASEM
        self.same_engine_sync = same_engine_sync
        self.reorder = reorder
        import os
        self.max_ops = int(os.environ.get("MAXOPS", "100000000"))
        if os.environ.get("NOREORDER"):
            self.reorder = False
        if os.environ.get("NOSES"):
            self.same_engine_sync = False
        self.nadd = 0
        self.log = []

    @staticmethod
    def _dep(op, A, kind):
        r = op.deps.get(id(A))
        if r is None:
            op.deps[id(A)] = [A, kind]
        elif kind < r[1] or (kind == 1 and r[1] == 2):
            r[1] = kind if not (r[1] == 0) else 0

    def add(self, eng, fn, reads=(), writes=(), dma=False, rows=None):
        self.nadd += 1
        if self.nadd > self.max_ops:
            return None
        op = Op(eng, fn, dma)
        op.rows = rows
        op.idx = len(self.all_ops)
        accesses = []
        nel = 0
        for ap, is_w in [(a, False) for a in reads] + [(a, True) for a in writes]:
            if ap is None or isinstance(ap, (int, float)):
                continue
            name, p0, p1, f0, f1 = _region(ap)
            if is_w and nel == 0:
                nel = _free_elems(ap)
                if dma:
                    nel = nel * mybir.dt.size(ap.dtype) * (p1 - p0)
            accesses.append((name, p0, p1, f0, f1, is_w, False))
            if "PSUM" in str(ap.space).upper():
                b0 = f0 // 2048
                b1 = (f1 - 1) // 2048
                accesses.append((name + "#bank", 0, 128, b0 * 2048, (b1 + 1) * 2048, True, True))
        if dma:
            op.dur = 2000.0 + nel / 150.0
        elif eng == "pe":
            op.dur = 45.0 + nel * 0.5
        elif eng == "act":
            op.dur = 250.0 + nel * 0.8
        elif eng == "dve":
            op.dur = 150.0 + nel * 1.0
        else:
            op.dur = 400.0 + nel * 2.0
        acc2 = []
        for name, p0, p1, f0, f1, is_w, cross_only in accesses:
            if name == "big":
                b = f0 // 512
                while b * 512 < f1:
                    acc2.append(((name, b), p0, p1, max(f0, b * 512), min(f1, (b + 1) * 512), is_w, cross_only))
                    b += 1
            else:
                acc2.append((name, p0, p1, f0, f1, is_w, cross_only))
        for name, p0, p1, f0, f1, is_w, cross_only in acc2:
            h = self.hist.get(name)
            if h is None:
                h = []
            newh = []
            for rec in h:
                rp0, rp1, rf0, rf1, rop, rw = rec
                if rop is op:
                    newh.append(rec)
                    continue
                ov = (rp0 < p1 and p0 < rp1 and rf0 < f1 and f0 < rf1)
                if cross_only:
                    if ov:
                        if rop.eng != eng or rop.is_dma or dma:
                            self._dep(op, rop, 0)
                        elif eng == "pe" and rows is not None and rop.rows is not None \
                                and (rop.rows[1] <= rows[0] or rows[1] <= rop.rows[0]):
                            self._dep(op, rop, 1)
                        else:
                            self._dep(op, rop, 2)
                    if rop.eng == eng and rp0 >= p0 and rp1 <= p1 and rf0 >= f0 and rf1 <= f1:
                        continue
                    newh.append(rec)
                    continue
                if ov and (rw or is_w):
                    self._dep(op, rop, 0)
                if is_w and rp0 >= p0 and rp1 <= p1 and rf0 >= f0 and rf1 <= f1:
                    continue
                if (not is_w) and (not rw) and (not rop.is_dma) and (not dma) and rop.eng == eng \
                        and rp0 == p0 and rp1 == p1 and rf0 == f0 and rf1 == f1:
                    self._dep(op, rop, 2)
                    continue
                newh.append(rec)
            newh.append((p0, p1, f0, f1, op, is_w))
            self.hist[name] = newh
        self.all_ops.append(op)
        self.ops[eng].append(op)
        return op

    def _schedule(self):
        import heapq, os
        lat = float(os.environ.get("SYNCLAT", "1000"))
        use_bl = not os.environ.get("NOBLEVEL")
        ops = self.all_ops
        for op in ops:
            op.npred = len(op.deps)
            op.ready = 0.0
        for op in ops:
            for A, kind in op.deps.values():
                A.succs.append((op, kind))
        bl = [0.0] * len(ops)
        for op in reversed(ops):
            m = 0.0
            for sop, kind in op.succs:
                v = bl[sop.idx] + (0.0 if kind == 2 else 200.0)
                if v > m:
                    m = v
            bl[op.idx] = (200.0 if op.is_dma else op.dur) + m
        fut = {e: [] for e in ENGS}
        avail = {e: [] for e in ENGS}
        free = {e: 0.0 for e in ENGS}
        for op in ops:
            if op.npred == 0:
                heapq.heappush(fut[op.eng], (op.ready, op.idx, op))
        order = []
        new_streams = {e: [] for e in ENGS}
        n = len(ops)
        while len(order) < n:
            best_e = None
            best_t = None
            for e in ENGS:
                f = fut[e]
                av = avail[e]
                while f and f[0][0] <= free[e]:
                    r_, i_, o_ = heapq.heappop(f)
                    heapq.heappush(av, ((-bl[i_] if use_bl else r_), i_, o_))
                if av:
                    t = free[e]
                elif f:
                    t = f[0][0]
                else:
                    continue
                if best_t is None or t < best_t:
                    best_t = t
                    best_e = e
            assert best_e is not None, "scheduler deadlock (cyclic hazards?)"
            if avail[best_e]:
                _, _, op = heapq.heappop(avail[best_e])
            else:
                _, _, op = heapq.heappop(fut[best_e])
            op.start = best_t
            if op.is_dma:
                free[best_e] = best_t + 60.0
                op.finish = best_t + op.dur
            else:
                free[best_e] = best_t + op.dur
                op.finish = best_t + op.dur
            order.append(op)
            new_streams[best_e].append(op)
            for sop, kind in op.succs:
                if kind == 2:
                    r = op.start + 1.0
                elif op.eng == "pe" and sop.eng == "pe" and kind == 0:
                    r = op.start + min(op.dur, 64.0)
                else:
                    r = op.finish + lat
                if r > sop.ready:
                    sop.ready = r
                sop.npred -= 1
                if sop.npred == 0:
                    heapq.heappush(fut[sop.eng], (sop.ready, sop.idx, sop))
        self.ops = new_streams
        return order

    def finalize(self):
        if self.reorder:
            order = self._schedule()
        else:
            order = self.all_ops
        last = {e: None for e in ENGS}
        dma_last = [None] * NDMASEM
        rr = 0
        cnt = {e: 0 for e in ENGS}
        for op in order:
            eng = op.eng
            prev = last[eng]
            vc = dict(prev.vc) if prev is not None else {}
            deps = list(op.deps.values())
            if op.is_dma:
                j = rr
                rr = (rr + 1) % NDMASEM
                if dma_last[j] is not None:
                    deps.append([dma_last[j], 0])
                op.chan = ("d", j)
                op.seq = self.dma_uses[j]
                self.dma_uses[j] += 1
                dma_last[j] = op
                op.signal = True
            else:
                op.chan = eng
                op.seq = cnt[eng]
            cnt[eng] += 1
            best = {}
            for A, kind in deps:
                if kind == 2:
                    continue
                if A.chan == op.chan and not op.is_dma:
                    if kind != 1 and (eng == "pe" or not self.same_engine_sync):
                        continue
                if vc.get(A.chan, -1) >= A.seq:
                    continue
                if A.chan not in best or best[A.chan].seq < A.seq:
                    best[A.chan] = A
            op.waits = []
            for ch, A in best.items():
                if vc.get(A.chan, -1) >= A.seq:
                    continue
                A.signal = True
                op.waits.append(A)
                for kk_, v in A.vc.items():
                    if vc.get(kk_, -1) < v:
                        vc[kk_] = v
                if vc.get(A.chan, -1) < A.seq:
                    vc[A.chan] = A.seq
            op.vc = vc
            last[eng] = op

    def emit(self, sems, dsems, block):
        self.finalize()
        for e in ENGS:
            c = 0
            for op in self.ops[e]:
                if op.is_dma:
                    op.sigval = 16 * (op.seq + 1)
                elif op.signal:
                    c += 1
                    op.sigval = c
        sched = self

        def run(engname, engobj):
            for op in sched.ops[engname]:
                for A in op.waits:
                    if A.is_dma:
                        engobj.wait_ge(dsems[A.chan[1]], A.sigval)
                    else:
                        engobj.wait_ge(sems[A.chan], A.sigval)
                inst = op.fn(engobj)
                if op.is_dma:
                    inst.then_inc(dsems[op.chan[1]], 16)
                elif op.signal:
                    inst.then_inc(sems[op.chan], 1)

        @block.tensor
        def _(e):
            run("pe", e)

        @block.scalar
        def _(e):
            run("act", e)

        @block.vector
        def _(e):
            run("dve", e)

        @block.gpsimd
        def _(e):
            run("pool", e)

        @block.sync
        def _(e):
            run("sp", e)
            for j in range(NDMASEM):
                if sched.dma_uses[j] > 0:
                    e.wait_ge(dsems[j], 16 * sched.dma_uses[j])


class K:
    def __init__(self, S):
        self.S = S
        self.cap = None

    def _add(self, eng, fn, reads=(), writes=(), dma=False, rows=None):
        if self.cap is not None:
            self.cap.append((eng, fn, reads, writes, dma, rows))
            return None
        return self.S.add(eng, fn, reads=reads, writes=writes, dma=dma, rows=rows)

    def merge(self, lists, store_delay=0):
        idx = [0] * len(lists)
        alive = True
        held = []
        n = 0
        while alive:
            alive = False
            for c, l in enumerate(lists):
                if idx[c] < len(l):
                    item = l[idx[c]]
                    idx[c] += 1
                    alive = True
                    eng, fn, reads, writes, dma, rows = item
                    if dma and store_delay > 0 and "DRAM" in str(writes[0].space).upper():
                        held.append((n + store_delay, item))
                    else:
                        self.S.add(eng, fn, reads=reads, writes=writes, dma=dma, rows=rows)
                    n += 1
                    while held and held[0][0] <= n:
                        eng, fn, reads, writes, dma, rows = held.pop(0)[1]
                        self.S.add(eng, fn, reads=reads, writes=writes, dma=dma, rows=rows)
        for _, item in held:
            eng, fn, reads, writes, dma, rows = item
            self.S.add(eng, fn, reads=reads, writes=writes, dma=dma, rows=rows)

    def dma(self, out, in_, eng="sp"):
        return self._add(eng, lambda e: e.dma_start(out=out, in_=in_), reads=[in_], writes=[out], dma=True)

    def mm(self, out, lhsT, rhs, start=True, stop=True, skip=False):
        rg = _region(lhsT)
        rows = (rg[1], rg[2])
        if skip:
            return self._add("pe", lambda e: e.matmul(out, lhsT=lhsT, rhs=rhs, start=start, stop=stop,
                                                       skip_group_check=True),
                              reads=[lhsT, rhs], writes=[out], rows=rows)
        return self._add("pe", lambda e: e.matmul(out, lhsT=lhsT, rhs=rhs, start=start, stop=stop),
                          reads=[lhsT, rhs], writes=[out], rows=rows)

    def red(self, out, in_, op=ALU.add):
        return self._add("dve", lambda e: e.tensor_reduce(out=out, in_=in_, axis=AX.X, op=op),
                          reads=[in_], writes=[out])

    def recip(self, out, in_):
        return self._add("dve", lambda e: e.reciprocal(out=out, in_=in_), reads=[in_], writes=[out])

    def tr(self, out, in_, ident):
        rg = _region(in_)
        return self._add("pe", lambda e: e.transpose(out, in_, ident), reads=[in_, ident], writes=[out],
                          rows=(rg[1], rg[2]))

    def act(self, out, in_, func, bias=None, scale=None, eng="act"):
        kw = {}
        rd = [in_]
        if bias is not None:
            kw["bias"] = bias
            rd.append(bias)
        if scale is not None:
            kw["scale"] = scale
            rd.append(scale)
        return self._add("act", lambda e: e.activation(out=out, in_=in_, func=func, **kw), reads=rd, writes=[out])

    def tt(self, eng, out, a, b, op):
        return self._add(eng, lambda e: e.tensor_tensor(out=out, in0=a, in1=b, op=op), reads=[a, b], writes=[out])

    def ts(self, eng, out, a, s1, op0, s2=None, op1=None):
        rd = [a, s1, s2]
        if op1 is None:
            return self._add(eng, lambda e: e.tensor_scalar(out=out, in0=a, scalar1=s1, scalar2=None, op0=op0),
                              reads=rd, writes=[out])
        return self._add(eng, lambda e: e.tensor_scalar(out=out, in0=a, scalar1=s1, scalar2=s2, op0=op0, op1=op1),
                          reads=rd, writes=[out])

    def stt(self, out, a, scalar, b, op0, op1):
        return self._add("dve", lambda e: e.scalar_tensor_tensor(out=out, in0=a, scalar=scalar, in1=b, op0=op0, op1=op1),
                          reads=[a, scalar, b], writes=[out])

    def cp(self, eng, out, in_):
        if eng == "act":
            return self.act(out, in_, AF.Copy)
        return self._add(eng, lambda e: e.tensor_copy(out=out, in_=in_), reads=[in_], writes=[out])

    def memset(self, eng, out, val):
        return self._add(eng, lambda e: e.memset(out, val), writes=[out])

    def scan(self, out, d0, d1, init, op0, op1):
        return self._add("dve", lambda e: e.tensor_tensor_scan(out=out, data0=d0, data1=d1, initial=init, op0=op0, op1=op1),
                          reads=[d0, d1, init], writes=[out])

    def ttr(self, out, a, b, accum):
        return self._add("dve", lambda e: e.tensor_tensor_reduce(out=out, in0=a, in1=b, scale=1.0, scalar=0.0,
                                                                  op0=ALU.mult, op1=ALU.add, accum_out=accum),
                          reads=[a, b], writes=[out, accum])


class Cfg:
    def __init__(self, NT, SEG, debug=False, upto="all"):
        self.upto = upto
        self.NT = NT
        self.SEG = SEG
        self.NM = 4
        assert NT % SEG == 0 and SEG % self.NM == 0
        self.NSEG = NT // SEG
        self.NMT = NT // self.NM
        self.debug = debug


PC_MU = 0
PC_W0 = 17
PC_A0 = 25
PC_KK = 33
PC_KA = 37
PC_RK = 41
PC_LG = 45
PC_LB = 49
PC_QG = 53
PC_KG = 54
PC_NG = 55
NPC = 63


def build_program(cfg):
    NT, SEG, NM, NSEG, NMT = cfg.NT, cfg.SEG, cfg.NM, cfg.NSEG, cfg.NMT
    TOK = NM * 128
    nc = bass.Bass("TRN2", target_bir_lowering=False)
    okind = "ExternalOutput" if cfg.debug else "Internal"

    def din(name, shape, dt=F32):
        return nc.dram_tensor(name, list(shape), dt, kind="ExternalInput").ap()

    def dscr(name, shape, dt):
        return nc.dram_tensor(name, list(shape), dt, kind=okind).ap()

    xs = din("xs", [NT * 128, D])
    w_in = din("w_in", [D, DIN])
    w_out = din("w_out", [D, D])
    pcols = din("pcols", [128, NPC])
    flags = din("flags", [128, NSEG + 1])
    wup = din("wup", [128, 2, 512])
    consts = din("consts", [128, 6, 128])
    ohot = din("ohot", [31, 4096])
    rpbT = din("rpbT", [31, 120])
    vtab = din("vtab", [128, NT * 14])
    ys = nc.dram_tensor("ys", [NT * 128, D], F32, kind="ExternalOutput").ap()

    Wb = dscr("Wb", [NBLK, 128, 8 * 128], BF16)
    S_AR = dscr("S_AR", [2, NMT, 4, 128, NM * 256], BF16)
    S_BT = dscr("S_BT", [2, NMT, 4, 128, NM * 128], BF16)
    S_KT = dscr("S_KT", [2, NMT, 4, 128, NM * 128], BF16)
    S_BH = dscr("S_BH", [2, NMT, 4, 128, NM * 128], BF16)
    S_KH = dscr("S_KH", [2, NMT, 4, 128, NM * 128], BF16)
    S_V = dscr("S_V", [NMT, 4, 128, NM * 128], BF16)
    S_WC = dscr("S_WC", [2, NMT, 4, 128, NM], F32)
    S_GR = dscr("S_GR", [NMT, 4, 128, TOK], F32)
    S_BON = dscr("S_BON", [NMT, 4, 128, TOK], F32)
    S_Q = dscr("S_Q", [NMT, 4, 128, TOK], BF16)
    S_K = dscr("S_K", [NMT, 4, 128, TOK], BF16)
    S_VA = dscr("S_VA", [NMT, 4, 128, NM * 128], BF16)
    S_GA = dscr("S_GA", [NMT, 4, 128, TOK], F32)
    S_YF = dscr("S_YF", [NT, 128, 512], F32)
    S_MR = dscr("S_MR", [NT, 4, 128, 128], BF16)

    with ExitStack() as es:
        BIGB = 186 * 1024
        big = es.enter_context(nc.sbuf_tensor("big", [128, BIGB // 2], BF16))
        alloc = {"off": 0, "base": 0}

        def sb(name, shape, dt):
            n = 1
            for d_ in shape[1:]:
                n *= d_
            nbytes = n * mybir.dt.size(dt)
            nbytes = (nbytes + 31) // 32 * 32
            off = alloc["off"]
            assert off + nbytes <= BIGB, (name, off, nbytes)
            alloc["off"] = off + nbytes
            v = big[:, off // 2:(off + nbytes) // 2]
            if dt != BF16:
                v = v.bitcast(dt)
            v = v[:, 0:n]
            if len(shape) == 2:
                return v
            names = " ".join("a%d" % i for i in range(len(shape) - 1))
            kw = {"a%d" % i: shape[i + 1] for i in range(len(shape) - 1)}
            return v.rearrange("p (%s) -> p %s" % (names, names), **kw)

        def pass_reset():
            alloc["off"] = alloc["base"]

        def pst(name, shape, dt=F32):
            return es.enter_context(nc.psum_tensor(name, list(shape), dt))

        sems = {e: es.enter_context(nc.semaphore("s_" + e)) for e in ENGS}
        dsems = [es.enter_context(nc.semaphore("d%d" % j)) for j in range(NDMASEM)]
        S = Sched(nc)
        k = K(S)

        psW = [pst("psW0", [128, 1024]), pst("psW1", [128, 1024])]
        psB = [pst("ps%d" % i, [128, 512]) for i in range(4)]

        cst = sb("cst", [128, 6, 128], F32)
        cstb = sb("cstb", [128, 6, 128], BF16)
        pc = sb("pc", [128, NPC], F32)
        pcd = sb("pcd", [128, 64], F32)
        flg = sb("flg", [128, NSEG + 1], F32)
        wupf = sb("wupf", [128, 2, 512], F32)
        wupb = sb("wupb", [128, 2, 512], BF16)
        cm05 = sb("cm05", [128, 8], F32)
        k.memset("pool", cm05[:], -0.5)
        k.dma(cst[:], consts)
        k.dma(pc[:], pcols)
        k.dma(flg[:], flags)
        k.dma(wupf[:], wup)
        k.cp("dve", cstb[:], cst[:])
        k.cp("dve", wupb[:], wupf[:])
        identb = cstb[:, 0, :]
        identf = cst[:, 0, :]
        blockones = cstb[:, 5, :]
        PD_OMM, PD_HM, PD_W0H, PD_A0H, PD_OMKA = 0, 17, 34, 42, 50
        k.ts("dve", pcd[:, PD_OMM:PD_OMM + 17], pc[:, PC_MU:PC_MU + 17], -1.0, ALU.mult, 1.0, ALU.add)
        k.ts("dve", pcd[:, PD_HM:PD_HM + 17], pc[:, PC_MU:PC_MU + 17], 0.5, ALU.mult)
        k.ts("dve", pcd[:, PD_W0H:PD_W0H + 16], pc[:, PC_W0:PC_W0 + 16], 0.5, ALU.mult)
        k.ts("dve", pcd[:, PD_OMKA:PD_OMKA + 4], pc[:, PC_KA:PC_KA + 4], -1.0, ALU.mult, 1.0, ALU.add)

        alloc["base"] = alloc["off"]
        arena = sb("arena", [128, 2 * 12 * 512], F32)
        wst = arena[:, 0:DIN]
        wstb = arena[:, DIN:DIN + DIN // 2].bitcast(BF16)
        CH = [0]
        for dk in range(8):
            k.dma(wst, w_in[dk * 128:(dk + 1) * 128, :])
            if dk % 2 == 0:
                k.ts("dve", wstb[:], wst, pc[:, PC_NG + dk:PC_NG + dk + 1], ALU.mult)
            else:
                k.act(wstb[:], wst, AF.Copy, scale=pc[:, PC_NG + dk:PC_NG + dk + 1])
            k.dma(Wb.rearrange("b p (k c) -> p b k c", k=8)[:, :, dk, :],
                  wstb[:].rearrange("p (b c) -> p b c", c=128))

        xt = sb("xt", [128, 2, D], F32)
        hb = sb("hb", [128, 2, D], BF16)
        junk = sb("junk", [128, D], BF16)
        ssq = sb("ssq", [128, NT], F32)
        rstd = sb("rstd", [128, NT], F32)
        hT = sb("hT", [128, 2, 8, 514], BF16)
        wblk = sb("wblk", [128, 4, 8 * 128], BF16)
        zraw = sb("zraw", [128, 2, 514], F32)
        ztmp = sb("ztmp", [128, 2, 512], F32)
        zsl = sb("zsl", [128, 512], F32)
        zsc = sb("zsc", [128, 2, 4, 512], F32)
        lorab = sb("lorab", [128, 512], BF16)
        def T(i):
            return arena[:, (CH[0] * 12 + i) * 512:(CH[0] * 12 + i + 1) * 512]
        resetm = sb("resetm", [128, 512], F32)
        k.memset("pool", resetm[:], 1.0)
        for c in range(NM):
            k.memset("pool", resetm[:, c * 128:c * 128 + 1], 0.0)
        oAR = sb("oAR", [128, 2, 2, NM, 256], BF16)
        oBT = sb("oBT", [128, 2, 2, NM, 128], BF16)
        oKT = sb("oKT", [128, 2, 2, NM, 128], BF16)
        oBHc2 = sb("oBHc", [128, 2, 2, 512], BF16)
        oKHc2 = sb("oKHc", [128, 2, 2, 512], BF16)
        oVc2 = sb("oVc", [128, 2, 512], BF16)
        oBH = sb("oBH", [128, 2, 2, NM, 128], BF16)
        oKH = sb("oKH", [128, 2, 2, NM, 128], BF16)
        oV = sb("oV", [128, 2, NM, 128], BF16)
        oWC = sb("oWC", [128, 2, 2, NM], F32)
        oGR = sb("oGR", [128, 2, 512], F32)
        oBON = sb("oBON", [128, 2, 512], F32)
        oQ = sb("oQ", [128, 2, 512], BF16)
        oVAc2 = sb("oVAc", [128, 2, 512], BF16)
        oVA = sb("oVA", [128, 2, NM, 128], BF16)
        oGA = sb("oGA", [128, 2, 512], F32)

        ptr = [psB[2][:].bitcast(BF16), psB[3][:].bitcast(BF16)]

        def stage_x(i):
            sl = i % 2
            m = i // NM
            j = i % NM
            k.dma(xt[:, sl, :], xs[i * 128:(i + 1) * 128, :])
            S.add("act", lambda e, sl=sl, i=i: e.activation(out=junk[:], in_=xt[:, sl, :], func=AF.Square,
                                                          accum_out=ssq[:, i:i + 1]),
                  reads=[xt[:, sl, :]], writes=[junk[:], ssq[:, i:i + 1]])
            k.ts("dve", rstd[:, i:i + 1], ssq[:, i:i + 1], 1.0 / D, ALU.mult, RMS_EPS, ALU.add)
            k.tt("pool", rstd[:, i:i + 1], rstd[:, i:i + 1], cm05[:, 0:1], ALU.pow)
            k.act(hb[:, sl, :], xt[:, sl, :], AF.Copy, scale=rstd[:, i:i + 1])
            pt = ptr[i % 2].rearrange("p (a b) -> p a b", a=8)
            for dk in range(8):
                k.tr(pt[:, dk, :], hb[:, sl, dk * 128:(dk + 1) * 128], identb)
            k.cp("dve" if i % 2 == 0 else "act", hT[:, m % 2, :, 1 + j * 128:1 + (j + 1) * 128], pt)

        def halo_left(m):
            sl = m % 2
            t0 = m * NM
            if m == 0:
                k.memset("pool", hT[:, sl, :, 0:1], 0.0)
            elif t0 % SEG == 0:
                s = t0 // SEG
                k.ts("pool", hT[:, sl, :, 0:1], hT[:, 1 - sl, :, 512:513], flg[:, s:s + 1], ALU.mult)
            else:
                k.cp("pool", hT[:, sl, :, 0:1], hT[:, 1 - sl, :, 512:513])

        def halo_right(m):
            sl = m % 2
            t0 = m * NM
            if m == NMT - 1:
                k.memset("pool", hT[:, sl, :, 513:514], 0.0)
            elif (t0 + NM) % SEG == 0:
                s = (t0 + NM) // SEG
                k.ts("pool", hT[:, sl, :, 513:514], hT[:, 1 - sl, :, 1:2], flg[:, s:s + 1], ALU.mult)
            else:
                k.cp("pool", hT[:, sl, :, 513:514], hT[:, 1 - sl, :, 1:2])

        wcount = [0, 0]
        wq = [[], []]
        wready = [[], []]

        def w_begin(blocks):
            c = CH[0]
            wq[c] = list(blocks)
            wready[c] = []
            w_prefetch()
            w_prefetch()

        def w_prefetch():
            c = CH[0]
            if wq[c]:
                b = wq[c].pop(0)
                sl = c * 2 + wcount[c] % 2
                wcount[c] += 1
                k.dma(wblk[:, sl, :], Wb[b])
                wready[c].append((b, sl))

        def load_w(b):
            c = CH[0]
            if not wready[c]:
                w_begin([b])
            bb, sl = wready[c].pop(0)
            assert bb == b, (bb, b)
            return wblk[:, sl, :].rearrange("p (k c) -> p k c", k=8)

        zcount = [0]

        def proj_rwkv(m, b, dst):
            w = load_w(b)
            pz = psW[CH[0]]
            zr = zraw[:, CH[0], :]
            zt = ztmp[:, CH[0], :]
            sl = m % 2
            for half in range(2):
                for dk in range(8):
                    k.mm(pz[:, half * 512:half * 512 + 257], w[:, dk, :], hT[:, sl, dk, half * 257:(half + 1) * 257],
                         start=(dk == 0), stop=(dk == 7))
            w_prefetch()
            k.act(zr.rearrange("p (a b) -> p a b", a=2), pz.rearrange("p (a b) -> p a b", a=2)[:, :, 0:257], AF.Copy)
            k.act(dst, zr[:, 1:513], AF.Copy, scale=pcd[:, PD_OMM + b:PD_OMM + b + 1])
            if b % 3 == 0:
                k.tt("pool", zt, zr[:, 0:512], zr[:, 2:514], ALU.add)
                k.stt(dst, zt, pcd[:, PD_HM + b:PD_HM + b + 1], dst, ALU.mult, ALU.add)
            else:
                k.stt(dst, zr[:, 0:512], pcd[:, PD_HM + b:PD_HM + b + 1], dst, ALU.mult, ALU.add)
                k.stt(dst, zr[:, 2:514], pcd[:, PD_HM + b:PD_HM + b + 1], dst, ALU.mult, ALU.add)

        def proj_attn(m, b):
            w = load_w(b)
            pz = psW[CH[0]]
            sl = m % 2
            for dk in range(8):
                k.mm(pz[:, 0:512], w[:, dk, :], hT[:, sl, dk, 1:513], start=(dk == 0), stop=(dk == 7))
            w_prefetch()
            return pz[:, 0:512]

        def sigm_from_tanh(eng, out, th):
            k.ts(eng, out, th, 0.5, ALU.mult, 0.5, ALU.add)

        def passA_macro(m):
            CH[0] = 0
            proj_rwkv(m, 16, zsl[:])
            k.act(lorab[0:64, :], zsl[0:64, :], AF.Tanh)
            k.cp("pool", lorab[64:128, :], zsl[64:128, :])

            def rwkv_cb(cb):
                osl = cb % 2
                CH[0] = osl
                oVc = oVc2[:, osl]
                oBHc = oBHc2[:, osl]
                oKHc = oKHc2[:, osl]
                zs_ = zsc[:, osl]
                w_begin([cb, 4 + cb, 8 + cb, 12 + cb])
                proj_rwkv(m, 0 + cb, zs_[:, 0, :])
                proj_rwkv(m, 4 + cb, zs_[:, 1, :])
                proj_rwkv(m, 8 + cb, zs_[:, 2, :])
                proj_rwkv(m, 12 + cb, zs_[:, 3, :])
                r_, k_, v_, g_ = zs_[:, 0, :], zs_[:, 1, :], zs_[:, 2, :], zs_[:, 3, :]
                th = T(0)
                k.act(th, g_, AF.Tanh, scale=0.5)
                sigm_from_tanh("dve", th, th)
                k.tt("pool", oGR[:, osl, :], th, g_, ALU.mult)
                k.dma(S_GR[m, cb], oGR[:, osl, :])
                k.cp("dve", oVc, v_)
                pt = ptr[CH[0]].rearrange("p (a b) -> p a b", a=8)
                for j in range(NM):
                    k.tr(pt[:, j, :], oVc[:, j * 128:(j + 1) * 128], identb)
                k.cp("act", oV[:, osl, :, :], pt[:, 0:NM, :])
                k.dma(S_V[m, cb].rearrange("p (a b) -> p a b", a=NM), oV[:, osl, :, :])
                kk = T(1)
                k.ts("dve", kk, k_, pc[:, PC_KK + cb:PC_KK + cb + 1], ALU.mult)
                kk2 = T(2).bitcast(BF16)[:, 0:512]
                k.tt("pool", kk2, kk, kk, ALU.mult)
                pn = psB[CH[0]]
                k.mm(pn[:, :], blockones, kk2, start=True, stop=True)
                rn = T(2)
                k.ts("dve", rn, pn[:, :], 1e-18, ALU.max)
                k.act(rn, rn, AF.Ln)
                k.act(rn, rn, AF.Exp, scale=-0.5)
                kkn = T(1)
                k.tt("pool", kkn, kk, rn, ALU.mult)
                bon = T(3)
                for d in range(2):
                    pw = psB[CH[0]]
                    pa = psB[CH[0]]
                    k.mm(pw[:, :], wupb[0:64, d, cb * 128:(cb + 1) * 128], lorab[0:64, :])
                    lw = T(4)
                    k.act(lw, pw[:, :], AF.Tanh, bias=pcd[:, PD_W0H + d * 4 + cb:PD_W0H + d * 4 + cb + 1], scale=0.5)
                    k.mm(pa[:, :], wupb[64:128, d, cb * 128:(cb + 1) * 128], lorab[64:128, :])
                    k.ts("dve", lw, lw, -0.5 * CDEC, ALU.mult, -0.5 * CDEC, ALU.add)
                    a_ = T(5)
                    k.act(a_, pa[:, :], AF.Tanh, bias=pcd[:, PD_A0H + d * 4 + cb:PD_A0H + d * 4 + cb + 1], scale=0.5)
                    sigm_from_tanh("dve", a_, a_)
                    kd = T(6)
                    k.ts("dve", kd, a_, pc[:, PC_KA + cb:PC_KA + cb + 1], ALU.mult,
                         pcd[:, PD_OMKA + cb:PD_OMKA + cb + 1], ALU.add)
                    k.tt("pool", kd, kd, k_, ALU.mult)
                    kka = T(5)
                    k.tt("pool", kka, kkn, a_, ALU.mult)
                    rk = T(7) if d == 1 else bon
                    k.stt(rk, kd, pc[:, PC_RK + cb:PC_RK + cb + 1], r_, ALU.mult, ALU.mult)
                    if d == 1:
                        k.tt("pool", bon, bon, rk, ALU.add)
                    cl = T(7)
                    k.scan(cl, resetm[:], lw, 0.0, ALU.mult, ALU.add)
                    cl3 = cl.rearrange("p (a b) -> p a b", a=NM)
                    lw3 = lw.rearrange("p (a b) -> p a b", a=NM)
                    if d == 1:
                        tot = T(8)[:, 0:NM]
                        k.cp("dve", tot, cl3[:, :, 127])
                        k.tt("dve", cl, lw, cl, ALU.subtract)
                        k.tt("dve", cl3, cl3, tot.unsqueeze(2).to_broadcast([128, NM, 128]), ALU.add)
                    clC = T(8)[:, 8:8 + NM]
                    k.cp("dve", clC, cl3[:, :, 127] if d == 0 else cl3[:, :, 0])
                    k.act(oWC[:, osl, d, :], clC, AF.Exp)
                    Ep = T(9)
                    Em = T(10)
                    Eh = T(11)
                    k.act(Ep, cl, AF.Exp)
                    k.act(Em, cl, AF.Exp, scale=-1.0)
                    for j in range(NM):
                        k.act(Eh[:, j * 128:(j + 1) * 128], cl[:, j * 128:(j + 1) * 128], AF.Exp,
                              bias=clC[:, j:j + 1], scale=-1.0)
                    Ep3 = Ep.rearrange("p (a b) -> p a b", a=NM)
                    kkn3 = kkn.rearrange("p (a b) -> p a b", a=NM)
                    oA = oAR[:, osl, d, :, 0:128]
                    oR = oAR[:, osl, d, :, 128:256]
                    if d == 0:
                        k.stt(oA[:, :, 1:128], kkn3[:, :, 1:128], -1.0, Ep3[:, :, 0:127], ALU.mult, ALU.mult)
                        k.ts("pool", oA[:, :, 0:1], kkn3[:, :, 0:1], -1.0, ALU.mult)
                    else:
                        k.stt(oA[:, :, 0:127], kkn3[:, :, 0:127], -1.0, Ep3[:, :, 1:128], ALU.mult, ALU.mult)
                        k.ts("pool", oA[:, :, 127:128], kkn3[:, :, 127:128], -1.0, ALU.mult)
                    k.tt("pool", oR, r_.rearrange("p (a b) -> p a b", a=NM), Ep3, ALU.mult)
                    k.tt("dve", oBT[:, osl, d, :, :], kka.rearrange("p (a b) -> p a b", a=NM),
                         Em.rearrange("p (a b) -> p a b", a=NM), ALU.mult)
                    k.tt("pool", oKT[:, osl, d, :, :], kd.rearrange("p (a b) -> p a b", a=NM),
                         Em.rearrange("p (a b) -> p a b", a=NM), ALU.mult)
                    k.tt("dve", oBHc[:, d, :], kka, Eh, ALU.mult)
                    k.tt("pool", oKHc[:, d, :], kd, Eh, ALU.mult)
                    pt2 = ptr[CH[0]].rearrange("p (a b) -> p a b", a=8)
                    for j in range(NM):
                        k.tr(pt2[:, j, :], oBHc[:, d, j * 128:(j + 1) * 128], identb)
                        k.tr(pt2[:, NM + j, :], oKHc[:, d, j * 128:(j + 1) * 128], identb)
                    k.cp("act", oBH[:, osl, d, :, :], pt2[:, 0:NM, :])
                    k.cp("dve", oKH[:, osl, d, :, :], pt2[:, NM:2 * NM, :])
                    k.dma(S_AR[d, m, cb].rearrange("p (a b) -> p a b", a=NM), oAR[:, osl, d, :, :])
                    k.dma(S_BT[d, m, cb].rearrange("p (a b) -> p a b", a=NM), oBT[:, osl, d, :, :])
                    k.dma(S_KT[d, m, cb].rearrange("p (a b) -> p a b", a=NM), oKT[:, osl, d, :, :])
                    k.dma(S_BH[d, m, cb].rearrange("p (a b) -> p a b", a=NM), oBH[:, osl, d, :, :])
                    k.dma(S_KH[d, m, cb].rearrange("p (a b) -> p a b", a=NM), oKH[:, osl, d, :, :])
                    k.dma(S_WC[d, m, cb], oWC[:, osl, d, :])
                pb = psB[CH[0]]
                bonb = T(7).bitcast(BF16)[:, 0:512]
                k.cp("dve", bonb, bon)
                k.mm(pb[:, :], blockones, bonb, start=True, stop=True)
                k.tt("dve", oBON[:, osl, :], pb[:, :], v_, ALU.mult)
                k.dma(S_BON[m, cb], oBON[:, osl, :])
            def attn_cb(cb):
                osl = cb % 2
                CH[0] = osl
                oVAc = oVAc2[:, osl]
                w_begin([17 + cb, 21 + cb, 25 + cb, 29 + cb])
                for which, dst, gcol in ((0, S_Q, PC_QG), (1, S_K, PC_KG)):
                    pz = proj_attn(m, 17 + which * 4 + cb)
                    q_ = T(0)
                    k.cp("act", q_, pz)
                    q2 = T(1).bitcast(BF16)[:, 0:512]
                    k.tt("pool", q2, q_, q_, ALU.mult)
                    pn = psB[CH[0]]
                    k.mm(pn[:, :], blockones, q2, start=True, stop=True)
                    rn = T(1)
                    k.ts("dve", rn, pn[:, :], 1.0 / 64, ALU.mult, RMS_EPS, ALU.add)
                    k.act(rn, rn, AF.Ln)
                    k.act(rn, rn, AF.Exp, scale=-0.5)
                    k.stt(oQ[:, osl, :], q_, pc[:, gcol:gcol + 1], rn, ALU.mult, ALU.mult)
                    k.dma(dst[m, cb], oQ[:, osl, :])
                pz = proj_attn(m, 25 + cb)
                k.cp("act", oVAc, pz)
                pt = ptr[CH[0]].rearrange("p (a b) -> p a b", a=8)
                for j in range(NM):
                    k.tr(pt[:, j, :], oVAc[:, j * 128:(j + 1) * 128], identb)
                k.cp("dve", oVA[:, osl, :, :], pt[:, 0:NM, :])
                k.dma(S_VA[m, cb].rearrange("p (a b) -> p a b", a=NM), oVA[:, osl, :, :])
                pz = proj_attn(m, 29 + cb)
                g_ = T(2)
                k.cp("act", g_, pz)
                th = T(3)
                k.act(th, g_, AF.Tanh, scale=0.5)
                sigm_from_tanh("dve", th, th)
                k.tt("pool", oGA[:, osl, :], th, g_, ALU.mult)
                k.dma(S_GA[m, cb], oGA[:, osl, :])


            for pair in ((0, 1), (2, 3)):
                lists = []
                for cb in pair:
                    k.cap = []
                    rwkv_cb(cb)
                    lists.append(k.cap)
                    k.cap = None
                k.merge(lists, store_delay=0)
            for pair in ((0, 1), (2, 3)):
                lists = []
                for cb in pair:
                    k.cap = []
                    attn_cb(cb)
                    lists.append(k.cap)
                    k.cap = None
                k.merge(lists, store_delay=0)
            CH[0] = 0

        print("SBUF pass A bytes", alloc["off"], "base", alloc["base"])
        if cfg.upto == "W":
            k.dma(ys[0:128, :], xs[0:128, :])
            with nc.Block() as block:
                S.emit(sems, dsems, block)
            return nc
        for i in range(min(NM, NT)):
            stage_x(i)
        if cfg.upto == "X":
            k.dma(ys[0:128, :], xs[0:128, :])
            with nc.Block() as block:
                S.emit(sems, dsems, block)
            return nc
        for m in range(NMT):
            halo_left(m)
            if m + 1 < NMT:
                for j in range(NM):
                    stage_x((m + 1) * NM + j)
            halo_right(m)
            passA_macro(m)

        if cfg.upto == "A":
            with nc.Block() as block:
                S.emit(sems, dsems, block)
            return nc

        W0a, W0b = psW[0][:, 0:512], psW[0][:, 512:1024]
        W1a, W1b = psW[1][:, 0:512], psW[1][:, 512:1024]

        def rwkv_pass(d):
            pass_reset()
            ldAR = sb("ldAR", [128, 2, 4, NM, 256], BF16)
            ldBT = sb("ldBT", [128, 2, 4, NM, 128], BF16)
            ldKT = sb("ldKT", [128, 2, 4, NM, 128], BF16)
            ldBH = sb("ldBH", [128, 2, 4, NM, 128], BF16)
            ldKH = sb("ldKH", [128, 2, 4, NM, 128], BF16)
            ldV = sb("ldV", [128, 2, 4, NM, 128], BF16)
            ldWC = sb("ldWC", [128, 2, 4, NM], F32)
            MAb = sb("MA", [128, 2, 8, 256], BF16)
            AKb = sb("AK", [128, 2, 8, 256], BF16)
            N0b = sb("N0", [128, 2, 8, 128], BF16)
            Mn = sb("Mn", [128, 2, 8, 128], BF16)
            Nn = sb("Nn", [128, 2, 8, 128], BF16)
            Ttb = sb("Ttb", [128, 2, 8, 128], BF16)
            G1s = sb("G1s", [128, 512], F32)
            G12 = sb("G12", [128, 512], BF16)
            Ubf = sb("Ubf", [128, 512], BF16)
            Hs = sb("Hs", [128, 256], F32)
            Hblk = sb("Hblk", [128, 4, 128], BF16)
            yo = sb("yo", [128, 2, 512], F32)
            mk2 = sb("mk2", [128, 256], F32)
            mkN = sb("mkN", [128, 128], F32)
            k.cp("pool", mk2[:, 0:128], cst[:, 1 if d == 0 else 2, :])
            k.cp("pool", mk2[:, 128:256], cst[:, 3 if d == 0 else 4, :])
            k.cp("pool", mkN[:], cst[:, 2 if d == 0 else 1, :])
            k.memset("pool", Hs[:], 0.0)
            k.memset("pool", Hblk[:], 0.0)
            import os as _os
            for _w in range(int(_os.environ.get("WARM", "0"))):
                k.mm(W0a, identb, MAb[:, 0, 0:2, :].rearrange("p a b -> p (a b)"), start=True, stop=True)
            if d == 1:
                yf = sb("yf", [128, 2, 512], F32)
                ysum = sb("ysum", [128, 512], F32)
                lnt = sb("lnt", [128, 512], F32)
                lnm = sb("lnm", [128, 16], F32)
                ynT = sb("ynT", [128, 512], F32)
                ldBON = sb("ldBON", [128, 2, 4, 128], F32)
                ldGR = sb("ldGR", [128, 2, 4, 128], F32)
                oMR = sb("oMR", [128, 2, 4, 128], BF16)

            def load(m, sl):
                k.dma(ldAR[:, sl].rearrange("p c t x -> p c (t x)"), S_AR[d, m].rearrange("c p x -> p c x"))
                k.dma(ldBT[:, sl].rearrange("p c t x -> p c (t x)"), S_BT[d, m].rearrange("c p x -> p c x"))
                k.dma(ldKT[:, sl].rearrange("p c t x -> p c (t x)"), S_KT[d, m].rearrange("c p x -> p c x"))
                k.dma(ldBH[:, sl].rearrange("p c t x -> p c (t x)"), S_BH[d, m].rearrange("c p x -> p c x"))
                k.dma(ldKH[:, sl].rearrange("p c t x -> p c (t x)"), S_KH[d, m].rearrange("c p x -> p c x"))
                k.dma(ldV[:, sl].rearrange("p c t x -> p c (t x)"), S_V[m].rearrange("c p x -> p c x"))
                k.dma(ldWC[:, sl], S_WC[d, m].rearrange("c p x -> p c x"))

            def tile(m, sl, j, cnt):
                i = m * NM + j
                AR = ldAR[:, sl]
                BT = ldBT[:, sl]
                KT = ldKT[:, sl]
                BH = ldBH[:, sl]
                KH = ldKH[:, sl]
                V = ldV[:, sl]
                MA = MAb[:, cnt % 2]
                AK = AKb[:, cnt % 2]
                N0 = N0b[:, cnt % 2]
                bnd = None
                if d == 0 and i % SEG == 0 and i > 0:
                    bnd = i // SEG
                if d == 1 and (i + 1) % SEG == 0 and i < NT - 1:
                    bnd = (i + 1) // SEG
                if bnd is not None:
                    k.ts("pool", Hs[:], Hs[:], flg[:, bnd:bnd + 1], ALU.mult)
                    k.ts("pool", Hblk[:], Hblk[:], flg[:, bnd:bnd + 1], ALU.mult)
                m2 = mk2[:].unsqueeze(1).to_broadcast([128, 2, 256])
                mN = mkN[:].unsqueeze(1).to_broadcast([128, 2, 128])
                sbanks = {0: [psB[0], psB[1], psB[2][:, 0:256]], 1: [psB[3], W1b, psB[2][:, 256:512]]}
                for cp_ in range(2):
                    for hh in range(2):
                        po = 64 * hh
                        bs = sbanks[hh]
                        for ci in range(2):
                            cb = 2 * cp_ + ci
                            k.mm(bs[0][:, ci * 256:(ci + 1) * 256], BT[po:po + 64, cb, j, :], AR[po:po + 64, cb, j, :])
                            k.mm(bs[1][:, ci * 256:(ci + 1) * 256], KT[po:po + 64, cb, j, :], AR[po:po + 64, cb, j, :])
                            k.mm(bs[2][:, ci * 128:(ci + 1) * 128], AR[po:po + 64, cb, j, 0:128], BT[po:po + 64, cb, j, :])
                        h0 = 4 * cp_ + hh
                        k.tt("dve", MA[:, h0:h0 + 3:2, :], bs[0].rearrange("p (a b) -> p a b", a=2), m2, ALU.mult)
                        k.tt("dve", AK[:, h0:h0 + 3:2, :], bs[1].rearrange("p (a b) -> p a b", a=2), m2, ALU.mult)
                        k.tt("dve", N0[:, h0:h0 + 3:2, :], bs[2].rearrange("p (a b) -> p a b", a=2), mN, ALU.mult)
                k.tt("pool", Ttb[:, 0], MA[:, :, 0:128], cst[:, 0, :].unsqueeze(1).to_broadcast([128, 8, 128]), ALU.add)
                Ttbank = [psB[2], psB[3]]
                Mbank = [W1a, psB[0]]
                Nbank = [W1b, psB[1]]
                for half in range(2):
                    for q in range(4):
                        h = 4 * half + q
                        k.mm(Ttbank[half][:, q * 128:(q + 1) * 128], identb, Ttb[:, 0, h, :],
                             start=(q == 0), stop=False, skip=True)
                Mcur = [MA[:, h, 0:128] for h in range(8)]
                Ncur = [N0[:, h, :] for h in range(8)]
                for lev in range(7):
                    for half in range(2):
                        if lev <= 5:
                            for q in range(4):
                                h = 4 * half + q
                                k.mm(Nbank[half][:, q * 128:(q + 1) * 128], Mcur[h], Ncur[h])
                            if lev <= 4:
                                for q in range(4):
                                    h = 4 * half + q
                                    k.mm(Mbank[half][:, q * 128:(q + 1) * 128], Ncur[h], Mcur[h])
                        if lev >= 1:
                            for q in range(4):
                                h = 4 * half + q
                                k.mm(Ttbank[half][:, q * 128:(q + 1) * 128], Ncur[h], Ttb[:, (lev - 1) % 2, h, :],
                                     start=False, stop=(lev == 6), skip=True)
                            k.cp("act" if half == 0 else "dve", Ttb[:, lev % 2, 4 * half:4 * half + 4, :],
                                 Ttbank[half].rearrange("p (a b) -> p a b", a=4))
                        if lev <= 5:
                            k.cp("act" if (half == 0 and lev % 2 == 0) else "dve", Nn[:, lev % 2, 4 * half:4 * half + 4, :],
                                 Nbank[half].rearrange("p (a b) -> p a b", a=4))
                            if lev <= 4:
                                k.cp("act", Mn[:, lev % 2, 4 * half:4 * half + 4, :],
                                     Mbank[half].rearrange("p (a b) -> p a b", a=4))
                    if lev <= 5:
                        Ncur = [Nn[:, lev % 2, h, :] for h in range(8)]
                        if lev <= 4:
                            Mcur = [Mn[:, lev % 2, h, :] for h in range(8)]
                TT = Ttb[:, 0]
                for h in range(8):
                    cb, po = h // 2, 64 * (h % 2)
                    k.mm(W0a[:, h * 64:(h + 1) * 64], AK[:, h, 0:128], V[:, cb, j, po:po + 64])
                k.cp("act", G1s[:], W0a)
                for cb in range(4):
                    k.mm(W0b[:, cb * 128:(cb + 1) * 128], AR[:, cb, j, 0:128], Hblk[:, cb, :])
                k.tt("dve", G12[:], W0b, G1s[:], ALU.add)
                for h in range(8):
                    k.mm(W0a[:, h * 64:(h + 1) * 64], TT[:, h, :], G12[:, h * 64:(h + 1) * 64])
                k.cp("act", Ubf[:], W0a)
                for cb in range(4):
                    h0, h1 = 2 * cb, 2 * cb + 1
                    o0 = W0b[:, h0 * 64:(h0 + 1) * 64]
                    o1 = W0b[:, h1 * 64:(h1 + 1) * 64]
                    k.mm(o0, AK[:, h0, 128:256], V[:, cb, j, 0:64], start=True, stop=False, skip=True)
                    k.mm(o1, AK[:, h1, 128:256], V[:, cb, j, 64:128], start=False, stop=False, skip=True)
                    k.mm(W0b[:, cb * 128:(cb + 1) * 128], AR[:, cb, j, 128:256], Hblk[:, cb, :], start=False, stop=False, skip=True)
                    k.mm(o0, MA[:, h0, 128:256], Ubf[:, h0 * 64:(h0 + 1) * 64], start=False, stop=False, skip=True)
                    k.mm(o1, MA[:, h1, 128:256], Ubf[:, h1 * 64:(h1 + 1) * 64], start=False, stop=True, skip=True)
                for h in range(8):
                    cb, po = h // 2, 64 * (h % 2)
                    o = W1a[po:po + 64, cb * 64:(cb + 1) * 64]
                    k.mm(o, KH[:, cb, j, po:po + 64], V[:, cb, j, po:po + 64], start=True, stop=False)
                    k.mm(o, BH[:, cb, j, po:po + 64], Ubf[:, h * 64:(h + 1) * 64], start=False, stop=True)
                ysl = cnt % 2
                if d == 0:
                    k.cp("act", yo[:, ysl, :], W0b)
                    k.dma(S_YF[i], yo[:, ysl, :])
                else:
                    k.dma(yf[:, ysl, :], S_YF[i])
                    k.tt("dve", ysum[:], W0b, yf[:, ysl, :], ALU.add)
                for cb in range(4):
                    k.stt(Hs[:, cb * 64:(cb + 1) * 64], Hs[:, cb * 64:(cb + 1) * 64], ldWC[:, sl, cb, j:j + 1],
                          W1a[:, cb * 64:(cb + 1) * 64], ALU.mult, ALU.add)
                Hs3 = Hs[:].rearrange("p (a b) -> p a b", a=4)
                k.cp("pool", Hblk[0:64, :, 0:64], Hs3[0:64])
                k.cp("pool", Hblk[64:128, :, 64:128], Hs3[64:128])
                if d == 1:
                    y3 = ysum[:].rearrange("p (a b) -> p a b", a=8)
                    k.red(lnm[:, 0:8], y3)
                    k.ts("dve", lnm[:, 0:8], lnm[:, 0:8], 1.0 / 64, ALU.mult)
                    k.tt("dve", lnt[:].rearrange("p (a b) -> p a b", a=8), y3,
                         lnm[:, 0:8].unsqueeze(2).to_broadcast([128, 8, 64]), ALU.subtract)
                    k.tt("pool", ysum[:], lnt[:], lnt[:], ALU.mult)
                    k.red(lnm[:, 8:16], y3)
                    k.ts("dve", lnm[:, 8:16], lnm[:, 8:16], 1.0 / 64, ALU.mult, LNX_EPS, ALU.add)
                    k.tt("pool", lnm[:, 8:16], lnm[:, 8:16], cm05[:, 0:1].to_broadcast([128, 8]), ALU.pow)
                    k.tt("dve", lnt[:].rearrange("p (a b) -> p a b", a=8), lnt[:].rearrange("p (a b) -> p a b", a=8),
                         lnm[:, 8:16].unsqueeze(2).to_broadcast([128, 8, 64]), ALU.mult)
                    for cb in range(4):
                        k.tr(W1b[:, cb * 128:(cb + 1) * 128], lnt[:, cb * 128:(cb + 1) * 128], identf)
                    k.dma(ldBON[:, ysl], S_BON[m].rearrange("c p (t x) -> p c t x", t=NM)[:, :, j, :])
                    k.dma(ldGR[:, ysl], S_GR[m].rearrange("c p (t x) -> p c t x", t=NM)[:, :, j, :])
                    for cb in range(4):
                        k.act(ynT[:, cb * 128:(cb + 1) * 128], W1b[:, cb * 128:(cb + 1) * 128], AF.Identity,
                              bias=pc[:, PC_LB + cb:PC_LB + cb + 1], scale=pc[:, PC_LG + cb:PC_LG + cb + 1])
                    yn3 = ynT[:].rearrange("p (a b) -> p a b", a=4)
                    k.tt("pool", yn3, yn3, ldBON[:, ysl], ALU.add)
                    k.tt("pool", oMR[:, ysl], yn3, ldGR[:, ysl], ALU.mult)
                    k.dma(S_MR[i].rearrange("c p x -> p c x"), oMR[:, ysl])

            order = list(range(NMT)) if d == 0 else list(range(NMT - 1, -1, -1))
            load(order[0], 0)
            cnt = 0
            for idx, m in enumerate(order):
                sl = idx % 2
                if idx + 1 < len(order):
                    load(order[idx + 1], 1 - sl)
                js = range(NM) if d == 0 else range(NM - 1, -1, -1)
                for j in js:
                    tile(m, sl, j, cnt)
                    cnt += 1

        rwkv_pass(0)
        if cfg.upto == "B":
            with nc.Block() as block:
                S.emit(sems, dsems, block)
            return nc
        rwkv_pass(1)
        if cfg.upto == "C":
            with nc.Block() as block:
                S.emit(sems, dsems, block)
            return nc

        pass_reset()
        RS = 8
        EG = sb("EG", [128, 8, 15, 64], BF16)
        vts = sb("vts", [128, NT * 14], F32)
        Wo = sb("Wo", [128, 8, D], BF16)
        Kring = sb("Kring", [128, RS, 4, 128], BF16)
        Vring = sb("Vring", [128, RS, 4, 128], BF16)
        Qt = sb("Qt", [128, 2, 4, 128], BF16)
        ebuf = sb("ebuf", [128, 2, 8, 128], F32)
        pT = sb("pT", [128, 2, 7, 8, 128], BF16)
        gaT = sb("gaT", [128, 2, 4, 128], F32)
        mrT = sb("mrT", [128, 2, 4, 128], BF16)
        maT = sb("maT", [128, 2, 4, 128], BF16)
        dsb2 = sb("dsb", [128, 2, 512], F32)
        otmp2 = sb("otmp", [128, 2, 512], F32)
        xr = sb("xr", [128, 2, D], F32)
        yout = sb("yout", [128, 2, D], F32)
        onesb = sb("onesb", [128, 64], BF16)
        cm1 = sb("cm1", [128, 1], F32)
        k.memset("pool", onesb[:], 1.0)
        k.memset("pool", cm1[:], -1.0)
        k.dma(vts[:], vtab)
        wtmp = sb("wtmp", [128, 2, D], F32)
        for ek in range(8):
            k.dma(wtmp[:, ek % 2, :], w_out[ek * 128:(ek + 1) * 128, :])
            k.cp("pool" if ek % 2 == 0 else "dve", Wo[:, ek, :], wtmp[:, ek % 2, :])
        ohf = sb("ohf", [128, 4096], F32)
        ohb = sb("ohb", [128, 4096], BF16)
        rpf = sb("rpf", [128, 120], F32)
        rpe = sb("rpe", [128, 120], BF16)
        k.dma(ohf[0:31, :], ohot)
        k.dma(rpf[0:31, :], rpbT)
        k.cp("dve", ohb[0:31, :], ohf[0:31, :])
        k.act(rpe[0:31, :], rpf[0:31, :], AF.Exp)
        k.memset("pool", EG[64:128, :, 0:1, :], 0.0)
        for g in range(16):
            bank = psB[g % 2]
            for c in range(4):
                cq = g * 4 + c
                k.mm(bank[0:64, c * 120:(c + 1) * 120], ohb[0:31, cq * 64:(cq + 1) * 64], rpe[0:31, :])
                k.mm(bank[64:128, c * 120:(c + 1) * 120], ohb[0:31, cq * 64:(cq + 1) * 64], rpe[0:31, :])
            k.cp("act", EG[0:64, :, :, g * 4:(g + 1) * 4],
                 bank[0:64, 0:480].rearrange("p (c h u) -> p h u c", c=4, h=8))
            k.cp("dve", EG[64:128, :, 1:15, g * 4:(g + 1) * 4],
                 bank[64:128, 0:480].rearrange("p (c h u) -> p h u c", c=4, h=8)[:, :, 0:14, :])

        loaded = [-1]

        def ensure_keys(upto):
            while loaded[0] < min(upto, NT - 1):
                t = loaded[0] + 1
                mk, jk = t // NM, t % NM
                k.dma(Kring[:, t % RS], S_K[mk].rearrange("c p (t x) -> p c t x", t=NM)[:, :, jk, :])
                k.dma(Vring[:, t % RS], S_VA[mk].rearrange("c p (t x) -> p c t x", t=NM)[:, :, jk, :])
                loaded[0] = t

        zc = [0]
        for i in range(NT):
            m, j = i // NM, i % NM
            sl = i % 2
            pos = i % SEG
            offs = [-2, -1, 0, 1, 2]
            if pos == 0:
                offs.append(3)
            if pos == SEG - 1:
                offs.insert(0, -3)
            offs = [o for o in offs if 0 <= i + o < NT]
            ensure_keys(i + 3)
            k.dma(Qt[:, sl], S_Q[m].rearrange("c p (t x) -> p c t x", t=NM)[:, :, j, :])
            k.dma(gaT[:, sl], S_GA[m].rearrange("c p (t x) -> p c t x", t=NM)[:, :, j, :])
            k.dma(mrT[:, sl], S_MR[i].rearrange("c p x -> p c x"))
            k.dma(xr[:, sl, :], xs[i * 128:(i + 1) * 128, :])
            for ki, o in enumerate(offs):
                t = i + o
                ks = t % RS
                pz = psW[zc[0] % 2]
                es_ = zc[0] % 2
                zc[0] += 1
                for hh in range(2):
                    po = 64 * hh
                    for cb in range(4):
                        k.mm(pz[:, (hh * 4 + cb) * 128:(hh * 4 + cb + 1) * 128], Kring[po:po + 64, ks, cb, :],
                             Qt[po:po + 64, sl, cb, :])
                k.act(ebuf[:, es_].rearrange("p a b -> p (a b)"), pz[:], AF.Exp, scale=0.125)
                u0 = 7 - 2 * o
                for qh in range(2):
                    col = (i * 7 + (o + 3)) * 2 + qh
                    k.stt(pT[:, sl, ki, :, qh * 64:(qh + 1) * 64], ebuf[:, es_, :, qh * 64:(qh + 1) * 64],
                          vts[:, col:col + 1], EG[:, :, u0 + qh, :], ALU.mult, ALU.mult)
            nk = len(offs)
            bO = psB[2 * (i % 2)]
            bD = psB[2 * (i % 2) + 1]
            dsb = dsb2[:, i % 2]
            otmp = otmp2[:, i % 2]
            for hh in range(2):
                po = 64 * hh
                for cb in range(4):
                    idx = hh * 4 + cb
                    oo = bO[po:po + 64, cb * 128:(cb + 1) * 128]
                    dd = bD[po:po + 64, cb * 128:(cb + 1) * 128]
                    for ki, o in enumerate(offs):
                        ks = (i + o) % RS
                        k.mm(oo, Vring[:, ks, cb, po:po + 64], pT[:, sl, ki, idx, :], start=(ki == 0), stop=(ki == nk - 1))
                    for ki, o in enumerate(offs):
                        k.mm(dd, onesb[:, :], pT[:, sl, ki, idx, :], start=(ki == 0), stop=(ki == nk - 1))
            k.recip(dsb, bD[:])
            k.tt("dve", otmp, bO[:], dsb, ALU.mult)
            k.tt("pool", maT[:, sl].rearrange("p a b -> p (a b)"), otmp, gaT[:, sl].rearrange("p a b -> p (a b)"), ALU.mult)
            for half in range(2):
                ob = (bO if half == 0 else bD)[:]
                for ek in range(8):
                    lh = mrT[:, sl, ek, :] if ek < 4 else maT[:, sl, ek - 4, :]
                    k.mm(ob, lh, Wo[:, ek, half * 512:(half + 1) * 512], start=(ek == 0), stop=(ek == 7))
                k.tt("dve", yout[:, sl, half * 512:(half + 1) * 512], ob, xr[:, sl, half * 512:(half + 1) * 512], ALU.add)
            k.dma(ys[i * 128:(i + 1) * 128, :], yout[:, sl, :])

        with nc.Block() as block:
            S.emit(sems, dsems, block)
    return nc


def _cols(v, nblk):
    return np.ascontiguousarray(np.asarray(v, np.float32).reshape(nblk, 128).T)


def make_consts():
    c = np.zeros((128, 6, 128), np.float32)
    i = np.arange(128)
    c[:, 0, :] = np.eye(128, dtype=np.float32)
    c[:, 1, :] = (i[:, None] < i[None, :])
    c[:, 2, :] = (i[:, None] > i[None, :])
    c[:, 3, :] = (i[:, None] <= i[None, :])
    c[:, 4, :] = (i[:, None] >= i[None, :])
    c[:, 5, :] = ((i[:, None] // 64) == (i[None, :] // 64))
    return c


def pack_params(p):
    pcols = np.zeros((128, NPC), np.float32)
    pcols[:, PC_MU:PC_MU + 17] = _cols(p["shift_mu"], 17)
    for d in range(2):
        pcols[:, PC_W0 + 4 * d:PC_W0 + 4 * d + 4] = _cols(p["w0"][d], 4)
        pcols[:, PC_A0 + 4 * d:PC_A0 + 4 * d + 4] = _cols(p["a0"][d], 4)
    pcols[:, PC_KK:PC_KK + 4] = _cols(p["k_k"], 4)
    pcols[:, PC_KA:PC_KA + 4] = _cols(p["k_a"], 4)
    pcols[:, PC_RK:PC_RK + 4] = _cols(p["r_k"].reshape(-1), 4)
    pcols[:, PC_LG:PC_LG + 4] = _cols(p["lnx_g"], 4)
    pcols[:, PC_LB:PC_LB + 4] = _cols(p["lnx_b"], 4)
    pcols[:, PC_QG] = np.tile(p["q_norm_g"], 2)
    pcols[:, PC_KG] = np.tile(p["k_norm_g"], 2)
    pcols[:, PC_NG:PC_NG + 8] = _cols(p["norm_g"], 8)
    wup = np.zeros((128, 2, 512), np.float32)
    wup[0:64] = np.transpose(p["w_up"], (1, 0, 2))
    wup[64:128] = np.transpose(p["a_up"], (1, 0, 2))
    cq = np.arange(64)
    ck = np.arange(64)
    cs = np.clip(cq - 8, 0, 48)
    inw = (ck[None, :] >= cs[:, None]) & (ck[None, :] < cs[:, None] + 16)
    dc = ck[None, :] - cq[:, None] + 15
    oh = np.zeros((31, 64, 64), np.float32)
    for q_ in range(64):
        for k_ in range(64):
            if inw[q_, k_]:
                oh[dc[q_, k_], q_, k_] = 1.0
    rp = np.asarray(p["rpb"], np.float32)
    rp = rp[:, ::-1, :]
    rp = rp.reshape(4, 2, 15, 31).transpose(3, 1, 0, 2)
    return {
        "ohot": np.ascontiguousarray(oh.reshape(31, 4096)),
        "rpbT": np.ascontiguousarray(rp.reshape(31, 120)),
        "w_in": np.ascontiguousarray(p["w_in"], np.float32),
        "w_out": np.ascontiguousarray(p["w_out"], np.float32),
        "pcols": pcols,
        "wup": wup,
        "consts": make_consts(),
    }


def make_core_meta(cfg, slots):
    NT, SEG, NSEG = cfg.NT, cfg.SEG, cfg.NSEG
    flags = np.zeros((128, NSEG + 1), np.float32)
    for s_ in range(1, NSEG):
        a, b = slots[s_ - 1], slots[s_]
        if a is not None and b is not None and a[0] == b[0] and b[1] == a[1] + 1:
            flags[:, s_] = 1.0
    vt = np.zeros((128, NT, 7, 2), np.float32)
    for i in range(NT):
        sl = slots[i // SEG]
        if sl is None:
            vt[:, i, 3, :] = 1.0
            continue
        sid, sidx, nseg = sl
        R = nseg * SEG * 2
        for qh in range(2):
            r = (sidx * SEG + (i % SEG)) * 2 + qh
            R0 = min(max(r - 4, 0), R - 8)
            for o in range(-3, 4):
                t = i + o
                if t < 0 or t >= NT:
                    continue
                sk = slots[t // SEG]
                if sk is None or sk[0] != sid:
                    continue
                if sk[1] - sidx != (t // SEG) - (i // SEG):
                    continue
                for kh in range(2):
                    rho = (sk[1] * SEG + (t % SEG)) * 2 + kh
                    if R0 <= rho < R0 + 8:
                        vt[kh * 64:(kh + 1) * 64, i, o + 3, qh] = 1.0
    return flags, np.ascontiguousarray(vt.reshape(128, NT * 14))


PARAM_NAMES = ["norm_g", "w_in", "shift_mu", "w0", "w_up", "a0", "a_up", "k_k", "k_a", "r_k",
               "lnx_g", "lnx_b", "q_norm_g", "k_norm_g", "rpb", "w_out"]


def run_layer(cfg, seqs, p, n_cores):
    NT, SEG, NSEG = cfg.NT, cfg.SEG, cfg.NSEG
    segtok = SEG * 128
    order = sorted(range(len(seqs)), key=lambda i: -seqs[i].shape[0])
    cores = [[] for _ in range(n_cores)]
    used = [0] * n_cores
    for si in order:
        ns = seqs[si].shape[0] // segtok
        c = min(range(n_cores), key=lambda c_: (used[c_] + ns > NSEG, used[c_]))
        assert used[c] + ns <= NSEG
        for g in range(ns):
            cores[c].append((si, g, ns))
        used[c] += ns
    shared = pack_params(p)
    in_maps = []
    for c in range(n_cores):
        slots = cores[c] + [None] * (NSEG - len(cores[c]))
        x = np.zeros((NT * 128, D), np.float32)
        for s_, sl in enumerate(slots):
            if sl is not None:
                x[s_ * segtok:(s_ + 1) * segtok] = seqs[sl[0]][sl[1] * segtok:(sl[1] + 1) * segtok]
        flags, vt = make_core_meta(cfg, slots)
        im = dict(shared)
        im["xs"] = x
        im["flags"] = flags
        im["vtab"] = vt
        in_maps.append(im)
    nc = build_program(cfg)
    res = run_bass_kernel_spmd(nc, in_maps, core_ids=list(range(n_cores)))
    outs = [np.zeros_like(s_) for s_ in seqs]
    for c in range(n_cores):
        y = np.asarray(res.results[c]["ys"])
        for s_, sl in enumerate(cores[c]):
            outs[sl[0]][sl[1] * segtok:(sl[1] + 1) * segtok] = y[s_ * segtok:(s_ + 1) * segtok]
    return outs, res


def kernel(**inputs):
    cfg = Cfg(128, 32)
    p = {n_: np.asarray(inputs[n_], np.float32)[0] for n_ in PARAM_NAMES}
    xp = np.asarray(inputs["x_prompt"], np.float32)
    xsm = np.asarray(inputs["x_sample"], np.float32)
    seqs = [xp[b] for b in range(xp.shape[0])] + [xsm[b] for b in range(xsm.shape[0])]
    outs, _ = run_layer(cfg, seqs, p, 8)
    yp = np.stack(outs[:xp.shape[0]], 0)
    ysm = np.stack(outs[xp.shape[0]:], 0)
    return (yp, ysm)
```

```python
import numpy as np
from contextlib import ExitStack
import concourse.bass as bass
import concourse.mybir as mybir
from concourse.bass_utils import run_bass_kernel_spmd

F32 = mybir.dt.float32
BF16 = mybir.dt.bfloat16
ALU = mybir.AluOpType
AF = mybir.ActivationFunctionType
AX = mybir.AxisListType

D = 1024
DIN = 4224
NBLK = 33
RMS_EPS = 1e-6
LNX_EPS = 64e-5
CDEC = 0.6065306597126334

ENGS = ["pe", "act", "dve", "pool", "sp"]
NDMASEM = 48


class Op:
    __slots__ = ("eng", "fn", "chan", "seq", "signal", "waits", "vc", "is_dma", "sigval", "rows",
                 "deps", "idx", "dur", "start", "finish", "succs", "npred", "ready")

    def __init__(self, eng, fn, is_dma):
        self.eng = eng
        self.fn = fn
        self.is_dma = is_dma
        self.signal = False
        self.waits = []
        self.vc = None
        self.chan = None
        self.seq = None
        self.sigval = None
        self.rows = None
        self.deps = {}
        self.dur = 500.0
        self.start = 0.0
        self.finish = 0.0
        self.succs = []
        self.npred = 0
        self.ready = 0.0


def _region(ap):
    name = ap.tensor.name
    pat = ap.ap
    off = ap.offset
    es = mybir.dt.size(ap.dtype)
    if "DRAM" in str(ap.space).upper():
        ext = 0
        for st, cnt in pat:
            ext += (cnt - 1) * abs(st)
        return name, 0, 1, off * es, (off + ext + 1) * es
    row = pat[0][0]
    npart = pat[0][1]
    if row == 0:
        p0 = 0
        f0 = off
    else:
        p0 = off // row
        f0 = off - p0 * row
    ext = 0
    for st, cnt in pat[1:]:
        ext += (cnt - 1) * abs(st)
    return name, p0, p0 + npart, f0 * es, (f0 + ext + 1) * es


def _free_elems(ap):
    n = 1
    for st, cnt in ap.ap[1:]:
        n *= cnt
    return n


class Sched:
    def __init__(self, nc, same_engine_sync=True, reorder=True):
        self.nc = nc
        self.ops = {e: [] for e in ENGS}
        self.all_ops = []
        self.hist = {}
        self.dma_uses = [0] * NDMASEM
        self.same_engine_sync = same_engine_sync
        self.reorder = reorder
        import os
        self.max_ops = int(os.environ.get("MAXOPS", "100000000"))
        if os.environ.get("NOREORDER"):
            self.reorder = False
        if os.environ.get("NOSES"):
            self.same_engine_sync = False
        self.nadd = 0
        self.log = []

    @staticmethod
    def _dep(op, A, kind):
        r = op.deps.get(id(A))
        if r is None:
            op.deps[id(A)] = [A, kind]
        elif kind < r[1] or (kind == 1 and r[1] == 2):
            r[1] = kind if not (r[1] == 0) else 0

    def add(self, eng, fn, reads=(), writes=(), dma=False, rows=None):
        self.nadd += 1
        if self.nadd > self.max_ops:
            return None
        op = Op(eng, fn, dma)
        op.rows = rows
        op.idx = len(self.all_ops)
        accesses = []
        nel = 0
        for ap, is_w in [(a, False) for a in reads] + [(a, True) for a in writes]:
            if ap is None or isinstance(ap, (int, float)):
                continue
            name, p0, p1, f0, f1 = _region(ap)
            if is_w and nel == 0:
                nel = _free_elems(ap)
                if dma:
                    nel = nel * mybir.dt.size(ap.dtype) * (p1 - p0)
            accesses.append((name, p0, p1, f0, f1, is_w, False))
            if "PSUM" in str(ap.space).upper():
                b0 = f0 // 2048
                b1 = (f1 - 1) // 2048
                accesses.append((name + "#bank", 0, 128, b0 * 2048, (b1 + 1) * 2048, True, True))
        if dma:
            op.dur = 2000.0 + nel / 150.0
        elif eng == "pe":
            op.dur = 45.0 + nel * 0.5
        elif eng == "act":
            op.dur = 250.0 + nel * 0.8
        elif eng == "dve":
            op.dur = 150.0 + nel * 1.0
        else:
            op.dur = 400.0 + nel * 2.0
        acc2 = []
        for name, p0, p1, f0, f1, is_w, cross_only in accesses:
            if name == "big":
                b = f0 // 512
                while b * 512 < f1:
                    acc2.append(((name, b), p0, p1, max(f0, b * 512), min(f1, (b + 1) * 512), is_w, cross_only))
                    b += 1
            else:
                acc2.append((name, p0, p1, f0, f1, is_w, cross_only))
        for name, p0, p1, f0, f1, is_w, cross_only in acc2:
            h = self.hist.get(name)
            if h is None:
                h = []
            newh = []
            for rec in h:
                rp0, rp1, rf0, rf1, rop, rw = rec
                if rop is op:
                    newh.append(rec)
                    continue
                ov = (rp0 < p1 and p0 < rp1 and rf0 < f1 and f0 < rf1)
                if cross_only:
                    if ov:
                        if rop.eng != eng or rop.is_dma or dma:
                            self._dep(op, rop, 0)
                        elif eng == "pe" and rows is not None and rop.rows is not None \
                                and (rop.rows[1] <= rows[0] or rows[1] <= rop.rows[0]):
                            self._dep(op, rop, 1)
                        else:
                            self._dep(op, rop, 2)
                    if rop.eng == eng and rp0 >= p0 and rp1 <= p1 and rf0 >= f0 and rf1 <= f1:
                        continue
                    newh.append(rec)
                    continue
                if ov and (rw or is_w):
                    self._dep(op, rop, 0)
                if is_w and rp0 >= p0 and rp1 <= p1 and rf0 >= f0 and rf1 <= f1:
                    continue
                if (not is_w) and (not rw) and (not rop.is_dma) and (not dma) and rop.eng == eng \
                        and rp0 == p0 and rp1 == p1 and rf0 == f0 and rf1 == f1:
                    self._dep(op, rop, 2)
                    continue
                newh.append(rec)
            newh.append((p0, p1, f0, f1, op, is_w))
            self.hist[name] = newh
        self.all_ops.append(op)
        self.ops[eng].append(op)
        return op

    def _schedule(self):
        import heapq, os
        lat = float(os.environ.get("SYNCLAT", "300"))
        use_bl = not os.environ.get("NOBLEVEL")
        ops = self.all_ops
        for op in ops:
            op.npred = len(op.deps)
            op.ready = 0.0
        for op in ops:
            for A, kind in op.deps.values():
                A.succs.append((op, kind))
        bl = [0.0] * len(ops)
        for op in reversed(ops):
            m = 0.0
            for sop, kind in op.succs:
                v = bl[sop.idx] + (0.0 if kind == 2 else 200.0)
                if v > m:
                    m = v
            bl[op.idx] = (200.0 if op.is_dma else op.dur) + m
        fut = {e: [] for e in ENGS}
        avail = {e: [] for e in ENGS}
        free = {e: 0.0 for e in ENGS}
        for op in ops:
            if op.npred == 0:
                heapq.heappush(fut[op.eng], (op.ready, op.idx, op))
        order = []
        new_streams = {e: [] for e in ENGS}
        n = len(ops)
        while len(order) < n:
            best_e = None
            best_t = None
            for e in ENGS:
                f = fut[e]
                av = avail[e]
                while f and f[0][0] <= free[e]:
                    r_, i_, o_ = heapq.heappop(f)
                    heapq.heappush(av, ((-bl[i_] if use_bl else r_), i_, o_))
                if av:
                    t = free[e]
                elif f:
                    t = f[0][0]
                else:
                    continue
                if best_t is None or t < best_t:
                    best_t = t
                    best_e = e
            assert best_e is not None, "scheduler deadlock (cyclic hazards?)"
            if avail[best_e]:
                _, _, op = heapq.heappop(avail[best_e])
            else:
                _, _, op = heapq.heappop(fut[best_e])
            op.start = best_t
            if op.is_dma:
                free[best_e] = best_t + 60.0
                op.finish = best_t + op.dur
            else:
                free[best_e] = best_t + op.dur
                op.finish = best_t + op.dur
            order.append(op)
            new_streams[best_e].append(op)
            for sop, kind in op.succs:
                if kind == 2:
                    r = op.start + 1.0
                elif op.eng == "pe" and sop.eng == "pe" and kind == 0:
                    r = op.start + min(op.dur, 64.0)
                else:
                    r = op.finish + lat
                if r > sop.ready:
                    sop.ready = r
                sop.npred -= 1
                if sop.npred == 0:
                    heapq.heappush(fut[sop.eng], (sop.ready, sop.idx, sop))
        self.ops = new_streams
        return order

    def finalize(self):
        if self.reorder:
            order = self._schedule()
        else:
            order = self.all_ops
        last = {e: None for e in ENGS}
        dma_last = [None] * NDMASEM
        rr = 0
        cnt = {e: 0 for e in ENGS}
        for op in order:
            eng = op.eng
            prev = last[eng]
            vc = dict(prev.vc) if prev is not None else {}
            deps = list(op.deps.values())
            if op.is_dma:
                j = rr
                rr = (rr + 1) % NDMASEM
                if dma_last[j] is not None:
                    deps.append([dma_last[j], 0])
                op.chan = ("d", j)
                op.seq = self.dma_uses[j]
                self.dma_uses[j] += 1
                dma_last[j] = op
                op.signal = True
            else:
                op.chan = eng
                op.seq = cnt[eng]
            cnt[eng] += 1
            best = {}
            for A, kind in deps:
                if kind == 2:
                    continue
                if A.chan == op.chan and not op.is_dma:
                    if kind != 1 and (eng == "pe" or not self.same_engine_sync):
                        continue
                if vc.get(A.chan, -1) >= A.seq:
                    continue
                if A.chan not in best or best[A.chan].seq < A.seq:
                    best[A.chan] = A
            op.waits = []
            for ch, A in best.items():
                if vc.get(A.chan, -1) >= A.seq:
                    continue
                A.signal = True
                op.waits.append(A)
                for kk_, v in A.vc.items():
                    if vc.get(kk_, -1) < v:
                        vc[kk_] = v
                if vc.get(A.chan, -1) < A.seq:
                    vc[A.chan] = A.seq
            op.vc = vc
            last[eng] = op

    def emit(self, sems, dsems, block):
        self.finalize()
        for e in ENGS:
            c = 0
            for op in self.ops[e]:
                if op.is_dma:
                    op.sigval = 16 * (op.seq + 1)
                elif op.signal:
                    c += 1
                    op.sigval = c
        sched = self

        def run(engname, engobj):
            for op in sched.ops[engname]:
                for A in op.waits:
                    if A.is_dma:
                        engobj.wait_ge(dsems[A.chan[1]], A.sigval)
                    else:
                        engobj.wait_ge(sems[A.chan], A.sigval)
                inst = op.fn(engobj)
                if op.is_dma:
                    inst.then_inc(dsems[op.chan[1]], 16)
                elif op.signal:
                    inst.then_inc(sems[op.chan], 1)

        @block.tensor
        def _(e):
            run("pe", e)

        @block.scalar
        def _(e):
            run("act", e)

        @block.vector
        def _(e):
            run("dve", e)

        @block.gpsimd
        def _(e):
            run("pool", e)

        @block.sync
        def _(e):
            run("sp", e)
            for j in range(NDMASEM):
                if sched.dma_uses[j] > 0:
                    e.wait_ge(dsems[j], 16 * sched.dma_uses[j])


class K:
    def __init__(self, S):
        self.S = S
        self.cap = None

    def _add(self, eng, fn, reads=(), writes=(), dma=False, rows=None):
        if self.cap is not None:
            self.cap.append((eng, fn, reads, writes, dma, rows))
            return None
        return self.S.add(eng, fn, reads=reads, writes=writes, dma=dma, rows=rows)

    def merge(self, lists, store_delay=0):
        idx = [0] * len(lists)
        alive = True
        held = []
        n = 0
        while alive:
            alive = False
            for c, l in enumerate(lists):
                if idx[c] < len(l):
                    item = l[idx[c]]
                    idx[c] += 1
                    alive = True
                    eng, fn, reads, writes, dma, rows = item
                    if dma and store_delay > 0 and "DRAM" in str(writes[0].space).upper():
                        held.append((n + store_delay, item))
                    else:
                        self.S.add(eng, fn, reads=reads, writes=writes, dma=dma, rows=rows)
                    n += 1
                    while held and held[0][0] <= n:
                        eng, fn, reads, writes, dma, rows = held.pop(0)[1]
                        self.S.add(eng, fn, reads=reads, writes=writes, dma=dma, rows=rows)
        for _, item in held:
            eng, fn, reads, writes, dma, rows = item
            self.S.add(eng, fn, reads=reads, writes=writes, dma=dma, rows=rows)

    def dma(self, out, in_, eng="sp"):
        return self._add(eng, lambda e: e.dma_start(out=out, in_=in_), reads=[in_], writes=[out], dma=True)

    def mm(self, out, lhsT, rhs, start=True, stop=True, skip=False):
        rg = _region(lhsT)
        rows = (rg[1], rg[2])
        if skip:
            return self._add("pe", lambda e: e.matmul(out, lhsT=lhsT, rhs=rhs, start=start, stop=stop,
                                                       skip_group_check=True),
                              reads=[lhsT, rhs], writes=[out], rows=rows)
        return self._add("pe", lambda e: e.matmul(out, lhsT=lhsT, rhs=rhs, start=start, stop=stop),
                          reads=[lhsT, rhs], writes=[out], rows=rows)

    def red(self, out, in_, op=ALU.add):
        return self._add("dve", lambda e: e.tensor_reduce(out=out, in_=in_, axis=AX.X, op=op),
                          reads=[in_], writes=[out])

    def recip(self, out, in_):
        return self._add("dve", lambda e: e.reciprocal(out=out, in_=in_), reads=[in_], writes=[out])

    def tr(self, out, in_, ident):
        rg = _region(in_)
        return self._add("pe", lambda e: e.transpose(out, in_, ident), reads=[in_, ident], writes=[out],
                          rows=(rg[1], rg[2]))

    def act(self, out, in_, func, bias=None, scale=None, eng="act"):
        kw = {}
        rd = [in_]
        if bias is not None:
            kw["bias"] = bias
            rd.append(bias)
        if scale is not None:
            kw["scale"] = scale
            rd.append(scale)
        return self._add("act", lambda e: e.activation(out=out, in_=in_, func=func, **kw), reads=rd, writes=[out])

    def tt(self, eng, out, a, b, op):
        return self._add(eng, lambda e: e.tensor_tensor(out=out, in0=a, in1=b, op=op), reads=[a, b], writes=[out])

    def ts(self, eng, out, a, s1, op0, s2=None, op1=None):
        rd = [a, s1, s2]
        if op1 is None:
            return self._add(eng, lambda e: e.tensor_scalar(out=out, in0=a, scalar1=s1, scalar2=None, op0=op0),
                              reads=rd, writes=[out])
        return self._add(eng, lambda e: e.tensor_scalar(out=out, in0=a, scalar1=s1, scalar2=s2, op0=op0, op1=op1),
                          reads=rd, writes=[out])

    def stt(self, out, a, scalar, b, op0, op1):
        return self._add("dve", lambda e: e.scalar_tensor_tensor(out=out, in0=a, scalar=scalar, in1=b, op0=op0, op1=op1),
                          reads=[a, scalar, b], writes=[out])

    def cp(self, eng, out, in_):
        if eng == "act":
            return self.act(out, in_, AF.Copy)
        return self._add(eng, lambda e: e.tensor_copy(out=out, in_=in_), reads=[in_], writes=[out])

    def memset(self, eng, out, val):
        return self._add(eng, lambda e: e.memset(out, val), writes=[out])

    def scan(self, out, d0, d1, init, op0, op1):
        return self._add("dve", lambda e: e.tensor_tensor_scan(out=out, data0=d0, data1=d1, initial=init, op0=op0, op1=op1),
                          reads=[d0, d1, init], writes=[out])

    def ttr(self, out, a, b, accum):
        return self._add("dve", lambda e: e.tensor_tensor_reduce(out=out, in0=a, in1=b, scale=1.0, scalar=0.0,
                                                                  op0=ALU.mult, op1=ALU.add, accum_out=accum),
                          reads=[a, b], writes=[out, accum])


class Cfg:
    def __init__(self, NT, SEG, debug=False, upto="all"):
        self.upto = upto
        self.NT = NT
        self.SEG = SEG
        self.NM = 4
        assert NT % SEG == 0 and SEG % self.NM == 0
        self.NSEG = NT // SEG
        self.NMT = NT // self.NM
        self.debug = debug


PC_MU = 0
PC_W0 = 17
PC_A0 = 25
PC_KK = 33
PC_KA = 37
PC_RK = 41
PC_LG = 45
PC_LB = 49
PC_QG = 53
PC_KG = 54
PC_NG = 55
NPC = 63


def build_program(cfg):
    NT, SEG, NM, NSEG, NMT = cfg.NT, cfg.SEG, cfg.NM, cfg.NSEG, cfg.NMT
    TOK = NM * 128
    nc = bass.Bass("TRN2", target_bir_lowering=False)
    okind = "ExternalOutput" if cfg.debug else "Internal"

    def din(name, shape, dt=F32):
        return nc.dram_tensor(name, list(shape), dt, kind="ExternalInput").ap()

    def dscr(name, shape, dt):
        return nc.dram_tensor(name, list(shape), dt, kind=okind).ap()

    xs = din("xs", [NT * 128, D])
    w_in = din("w_in", [D, DIN])
    w_out = din("w_out", [D, D])
    pcols = din("pcols", [128, NPC])
    flags = din("flags", [128, NSEG + 1])
    wup = din("wup", [128, 2, 512])
    consts = din("consts", [128, 6, 128])
    ohot = din("ohot", [31, 4096])
    rpbT = din("rpbT", [31, 120])
    vtab = din("vtab", [128, NT * 14])
    ys = nc.dram_tensor("ys", [NT * 128, D], F32, kind="ExternalOutput").ap()

    Wb = dscr("Wb", [NBLK, 128, 8 * 128], BF16)
    S_AR = dscr("S_AR", [2, NMT, 4, 128, NM * 256], BF16)
    S_BT = dscr("S_BT", [2, NMT, 4, 128, NM * 128], BF16)
    S_KT = dscr("S_KT", [2, NMT, 4, 128, NM * 128], BF16)
    S_BH = dscr("S_BH", [2, NMT, 4, 128, NM * 128], BF16)
    S_KH = dscr("S_KH", [2, NMT, 4, 128, NM * 128], BF16)
    S_V = dscr("S_V", [NMT, 4, 128, NM * 128], BF16)
    S_WC = dscr("S_WC", [2, NMT, 4, 128, NM], F32)
    S_GR = dscr("S_GR", [NMT, 4, 128, TOK], F32)
    S_BON = dscr("S_BON", [NMT, 4, 128, TOK], F32)
    S_Q = dscr("S_Q", [NMT, 4, 128, TOK], BF16)
    S_K = dscr("S_K", [NMT, 4, 128, TOK], BF16)
    S_VA = dscr("S_VA", [NMT, 4, 128, NM * 128], BF16)
    S_GA = dscr("S_GA", [NMT, 4, 128, TOK], F32)
    S_YF = dscr("S_YF", [NT, 128, 512], F32)
    S_MR = dscr("S_MR", [NT, 4, 128, 128], BF16)

    with ExitStack() as es:
        BIGB = 186 * 1024
        big = es.enter_context(nc.sbuf_tensor("big", [128, BIGB // 2], BF16))
        alloc = {"off": 0, "base": 0}

        def sb(name, shape, dt):
            n = 1
            for d_ in shape[1:]:
                n *= d_
            nbytes = n * mybir.dt.size(dt)
            nbytes = (nbytes + 31) // 32 * 32
            off = alloc["off"]
            assert off + nbytes <= BIGB, (name, off, nbytes)
            alloc["off"] = off + nbytes
            v = big[:, off // 2:(off + nbytes) // 2]
            if dt != BF16:
                v = v.bitcast(dt)
            v = v[:, 0:n]
            if len(shape) == 2:
                return v
            names = " ".join("a%d" % i for i in range(len(shape) - 1))
            kw = {"a%d" % i: shape[i + 1] for i in range(len(shape) - 1)}
            return v.rearrange("p (%s) -> p %s" % (names, names), **kw)

        def pass_reset():
            alloc["off"] = alloc["base"]

        def pst(name, shape, dt=F32):
            return es.enter_context(nc.psum_tensor(name, list(shape), dt))

        sems = {e: es.enter_context(nc.semaphore("s_" + e)) for e in ENGS}
        dsems = [es.enter_context(nc.semaphore("d%d" % j)) for j in range(NDMASEM)]
        S = Sched(nc)
        k = K(S)

        psW = [pst("psW0", [128, 1024]), pst("psW1", [128, 1024])]
        psB = [pst("ps%d" % i, [128, 512]) for i in range(4)]

        cst = sb("cst", [128, 6, 128], F32)
        cstb = sb("cstb", [128, 6, 128], BF16)
        pc = sb("pc", [128, NPC], F32)
        pcd = sb("pcd", [128, 64], F32)
        flg = sb("flg", [128, NSEG + 1], F32)
        wupf = sb("wupf", [128, 2, 512], F32)
        wupb = sb("wupb", [128, 2, 512], BF16)
        cm05 = sb("cm05", [128, 8], F32)
        k.memset("pool", cm05[:], -0.5)
        k.dma(cst[:], consts)
        k.dma(pc[:], pcols)
        k.dma(flg[:], flags)
        k.dma(wupf[:], wup)
        k.cp("dve", cstb[:], cst[:])
        k.cp("dve", wupb[:], wupf[:])
        identb = cstb[:, 0, :]
        identf = cst[:, 0, :]
        blockones = cstb[:, 5, :]
        PD_OMM, PD_HM, PD_W0H, PD_A0H, PD_OMKA = 0, 17, 34, 42, 50
        k.ts("dve", pcd[:, PD_OMM:PD_OMM + 17], pc[:, PC_MU:PC_MU + 17], -1.0, ALU.mult, 1.0, ALU.add)
        k.ts("dve", pcd[:, PD_HM:PD_HM + 17], pc[:, PC_MU:PC_MU + 17], 0.5, ALU.mult)
        k.ts("dve", pcd[:, PD_W0H:PD_W0H + 16], pc[:, PC_W0:PC_W0 + 16], 0.5, ALU.mult)
        k.ts("dve", pcd[:, PD_OMKA:PD_OMKA + 4], pc[:, PC_KA:PC_KA + 4], -1.0, ALU.mult, 1.0, ALU.add)

        alloc["base"] = alloc["off"]
        arena = sb("arena", [128, 2 * 12 * 512], F32)
        wst = arena[:, 0:DIN]
        wstb = arena[:, DIN:DIN + DIN // 2].bitcast(BF16)
        CH = [0]
        for dk in range(8):
            k.dma(wst, w_in[dk * 128:(dk + 1) * 128, :])
            if dk % 2 == 0:
                k.ts("dve", wstb[:], wst, pc[:, PC_NG + dk:PC_NG + dk + 1], ALU.mult)
            else:
                k.act(wstb[:], wst, AF.Copy, scale=pc[:, PC_NG + dk:PC_NG + dk + 1])
            k.dma(Wb.rearrange("b p (k c) -> p b k c", k=8)[:, :, dk, :],
                  wstb[:].rearrange("p (b c) -> p b c", c=128))

        xt = sb("xt", [128, 2, D], F32)
        hb = sb("hb", [128, 2, D], BF16)
        junk = sb("junk", [128, D], BF16)
        ssq = sb("ssq", [128, NT], F32)
        rstd = sb("rstd", [128, NT], F32)
        hT = sb("hT", [128, 2, 8, 514], BF16)
        wblk = sb("wblk", [128, 4, 8 * 128], BF16)
        zraw = sb("zraw", [128, 2, 514], F32)
        ztmp = sb("ztmp", [128, 2, 512], F32)
        zsl = sb("zsl", [128, 512], F32)
        zsc = sb("zsc", [128, 2, 4, 512], F32)
        lorab = sb("lorab", [128, 512], BF16)
        def T(i):
            return arena[:, (CH[0] * 12 + i) * 512:(CH[0] * 12 + i + 1) * 512]
        resetm = sb("resetm", [128, 512], F32)
        k.memset("pool", resetm[:], 1.0)
        for c in range(NM):
            k.memset("pool", resetm[:, c * 128:c * 128 + 1], 0.0)
        oAR = sb("oAR", [128, 2, 2, NM, 256], BF16)
        oBT = sb("oBT", [128, 2, 2, NM, 128], BF16)
        oKT = sb("oKT", [128, 2, 2, NM, 128], BF16)
        oBHc2 = sb("oBHc", [128, 2, 2, 512], BF16)
        oKHc2 = sb("oKHc", [128, 2, 2, 512], BF16)
        oVc2 = sb("oVc", [128, 2, 512], BF16)
        oBH = sb("oBH", [128, 2, 2, NM, 128], BF16)
        oKH = sb("oKH", [128, 2, 2, NM, 128], BF16)
        oV = sb("oV", [128, 2, NM, 128], BF16)
        oWC = sb("oWC", [128, 2, 2, NM], F32)
        oGR = sb("oGR", [128, 2, 512], F32)
        oBON = sb("oBON", [128, 2, 512], F32)
        oQ = sb("oQ", [128, 2, 512], BF16)
        oVAc2 = sb("oVAc", [128, 2, 512], BF16)
        oVA = sb("oVA", [128, 2, NM, 128], BF16)
        oGA = sb("oGA", [128, 2, 512], F32)

        ptr = [psB[2][:].bitcast(BF16), psB[3][:].bitcast(BF16)]

        def stage_x(i):
            sl = i % 2
            m = i // NM
            j = i % NM
            k.dma(xt[:, sl, :], xs[i * 128:(i + 1) * 128, :])
            S.add("act", lambda e, sl=sl, i=i: e.activation(out=junk[:], in_=xt[:, sl, :], func=AF.Square,
                                                          accum_out=ssq[:, i:i + 1]),
                  reads=[xt[:, sl, :]], writes=[junk[:], ssq[:, i:i + 1]])
            k.ts("dve", rstd[:, i:i + 1], ssq[:, i:i + 1], 1.0 / D, ALU.mult, RMS_EPS, ALU.add)
            k.tt("pool", rstd[:, i:i + 1], rstd[:, i:i + 1], cm05[:, 0:1], ALU.pow)
            k.act(hb[:, sl, :], xt[:, sl, :], AF.Copy, scale=rstd[:, i:i + 1])
            pt = ptr[i % 2].rearrange("p (a b) -> p a b", a=8)
            for dk in range(8):
                k.tr(pt[:, dk, :], hb[:, sl, dk * 128:(dk + 1) * 128], identb)
            k.cp("dve" if i % 2 == 0 else "act", hT[:, m % 2, :, 1 + j * 128:1 + (j + 1) * 128], pt)

        def halo_left(m):
            sl = m % 2
            t0 = m * NM
            if m == 0:
                k.memset("pool", hT[:, sl, :, 0:1], 0.0)
            elif t0 % SEG == 0:
                s = t0 // SEG
                k.ts("pool", hT[:, sl, :, 0:1], hT[:, 1 - sl, :, 512:513], flg[:, s:s + 1], ALU.mult)
            else:
                k.cp("pool", hT[:, sl, :, 0:1], hT[:, 1 - sl, :, 512:513])

        def halo_right(m):
            sl = m % 2
            t0 = m * NM
            if m == NMT - 1:
                k.memset("pool", hT[:, sl, :, 513:514], 0.0)
            elif (t0 + NM) % SEG == 0:
                s = (t0 + NM) // SEG
                k.ts("pool", hT[:, sl, :, 513:514], hT[:, 1 - sl, :, 1:2], flg[:, s:s + 1], ALU.mult)
            else:
                k.cp("pool", hT[:, sl, :, 513:514], hT[:, 1 - sl, :, 1:2])

        wcount = [0, 0]
        wq = [[], []]
        wready = [[], []]

        def w_begin(blocks):
            c = CH[0]
            wq[c] = list(blocks)
            wready[c] = []
            w_prefetch()
            w_prefetch()

        def w_prefetch():
            c = CH[0]
            if wq[c]:
                b = wq[c].pop(0)
                sl = c * 2 + wcount[c] % 2
                wcount[c] += 1
                k.dma(wblk[:, sl, :], Wb[b])
                wready[c].append((b, sl))

        def load_w(b):
            c = CH[0]
            if not wready[c]:
                w_begin([b])
            bb, sl = wready[c].pop(0)
            assert bb == b, (bb, b)
            return wblk[:, sl, :].rearrange("p (k c) -> p k c", k=8)

        zcount = [0]

        def proj_rwkv(m, b, dst):
            w = load_w(b)
            pz = psW[CH[0]]
            zr = zraw[:, CH[0], :]
            zt = ztmp[:, CH[0], :]
            sl = m % 2
            for half in range(2):
                for dk in range(8):
                    k.mm(pz[:, half * 512:half * 512 + 257], w[:, dk, :], hT[:, sl, dk, half * 257:(half + 1) * 257],
                         start=(dk == 0), stop=(dk == 7))
            w_prefetch()
            k.act(zr.rearrange("p (a b) -> p a b", a=2), pz.rearrange("p (a b) -> p a b", a=2)[:, :, 0:257], AF.Copy)
            k.act(dst, zr[:, 1:513], AF.Copy, scale=pcd[:, PD_OMM + b:PD_OMM + b + 1])
            if b % 3 == 0:
                k.tt("pool", zt, zr[:, 0:512], zr[:, 2:514], ALU.add)
                k.stt(dst, zt, pcd[:, PD_HM + b:PD_HM + b + 1], dst, ALU.mult, ALU.add)
            else:
                k.stt(dst, zr[:, 0:512], pcd[:, PD_HM + b:PD_HM + b + 1], dst, ALU.mult, ALU.add)
                k.stt(dst, zr[:, 2:514], pcd[:, PD_HM + b:PD_HM + b + 1], dst, ALU.mult, ALU.add)

        def proj_attn(m, b):
            w = load_w(b)
            pz = psW[CH[0]]
            sl = m % 2
            for dk in range(8):
                k.mm(pz[:, 0:512], w[:, dk, :], hT[:, sl, dk, 1:513], start=(dk == 0), stop=(dk == 7))
            w_prefetch()
            return pz[:, 0:512]

        def sigm_from_tanh(eng, out, th):
            k.ts(eng, out, th, 0.5, ALU.mult, 0.5, ALU.add)

        def passA_macro(m):
            CH[0] = 0
            proj_rwkv(m, 16, zsl[:])
            k.act(lorab[0:64, :], zsl[0:64, :], AF.Tanh)
            k.cp("pool", lorab[64:128, :], zsl[64:128, :])

            def rwkv_cb(cb):
                osl = cb % 2
                CH[0] = osl
                oVc = oVc2[:, osl]
                oBHc = oBHc2[:, osl]
                oKHc = oKHc2[:, osl]
                zs_ = zsc[:, osl]
                w_begin([cb, 4 + cb, 8 + cb, 12 + cb])
                proj_rwkv(m, 0 + cb, zs_[:, 0, :])
                proj_rwkv(m, 4 + cb, zs_[:, 1, :])
                proj_rwkv(m, 8 + cb, zs_[:, 2, :])
                proj_rwkv(m, 12 + cb, zs_[:, 3, :])
                r_, k_, v_, g_ = zs_[:, 0, :], zs_[:, 1, :], zs_[:, 2, :], zs_[:, 3, :]
                th = T(0)
                k.act(th, g_, AF.Tanh, scale=0.5)
                sigm_from_tanh("dve", th, th)
                k.tt("pool", oGR[:, osl, :], th, g_, ALU.mult)
                k.dma(S_GR[m, cb], oGR[:, osl, :])
                k.cp("dve", oVc, v_)
                pt = ptr[CH[0]].rearrange("p (a b) -> p a b", a=8)
                for j in range(NM):
                    k.tr(pt[:, j, :], oVc[:, j * 128:(j + 1) * 128], identb)
                k.cp("act", oV[:, osl, :, :], pt[:, 0:NM, :])
                k.dma(S_V[m, cb].rearrange("p (a b) -> p a b", a=NM), oV[:, osl, :, :])
                kk = T(1)
                k.ts("dve", kk, k_, pc[:, PC_KK + cb:PC_KK + cb + 1], ALU.mult)
                kk2 = T(2).bitcast(BF16)[:, 0:512]
                k.tt("pool", kk2, kk, kk, ALU.mult)
                pn = psB[CH[0]]
                k.mm(pn[:, :], blockones, kk2, start=True, stop=True)
                rn = T(2)
                k.ts("dve", rn, pn[:, :], 1e-18, ALU.max)
                k.act(rn, rn, AF.Ln)
                k.act(rn, rn, AF.Exp, scale=-0.5)
                kkn = T(1)
                k.tt("pool", kkn, kk, rn, ALU.mult)
                bon = T(3)
                for d in range(2):
                    pw = psB[CH[0]]
                    pa = psB[CH[0]]
                    k.mm(pw[:, :], wupb[0:64, d, cb * 128:(cb + 1) * 128], lorab[0:64, :])
                    lw = T(4)
                    k.act(lw, pw[:, :], AF.Tanh, bias=pcd[:, PD_W0H + d * 4 + cb:PD_W0H + d * 4 + cb + 1], scale=0.5)
                    k.mm(pa[:, :], wupb[64:128, d, cb * 128:(cb + 1) * 128], lorab[64:128, :])
                    k.ts("dve", lw, lw, -0.5 * CDEC, ALU.mult, -0.5 * CDEC, ALU.add)
                    a_ = T(5)
                    k.act(a_, pa[:, :], AF.Tanh, bias=pcd[:, PD_A0H + d * 4 + cb:PD_A0H + d * 4 + cb + 1], scale=0.5)
                    sigm_from_tanh("dve", a_, a_)
                    kd = T(6)
                    k.ts("dve", kd, a_, pc[:, PC_KA + cb:PC_KA + cb + 1], ALU.mult,
                         pcd[:, PD_OMKA + cb:PD_OMKA + cb + 1], ALU.add)
                    k.tt("pool", kd, kd, k_, ALU.mult)
                    kka = T(5)
                    k.tt("pool", kka, kkn, a_, ALU.mult)
                    rk = T(7) if d == 1 else bon
                    k.stt(rk, kd, pc[:, PC_RK + cb:PC_RK + cb + 1], r_, ALU.mult, ALU.mult)
                    if d == 1:
                        k.tt("pool", bon, bon, rk, ALU.add)
                    cl = T(7)
                    k.scan(cl, resetm[:], lw, 0.0, ALU.mult, ALU.add)
                    cl3 = cl.rearrange("p (a b) -> p a b", a=NM)
                    lw3 = lw.rearrange("p (a b) -> p a b", a=NM)
                    if d == 1:
                        tot = T(8)[:, 0:NM]
                        k.cp("dve", tot, cl3[:, :, 127])
                        k.tt("dve", cl, lw, cl, ALU.subtract)
                        k.tt("dve", cl3, cl3, tot.unsqueeze(2).to_broadcast([128, NM, 128]), ALU.add)
                    clC = T(8)[:, 8:8 + NM]
                    k.cp("dve", clC, cl3[:, :, 127] if d == 0 else cl3[:, :, 0])
                    k.act(oWC[:, osl, d, :], clC, AF.Exp)
                    Ep = T(9)
                    Em = T(10)
                    Eh = T(11)
                    k.act(Ep, cl, AF.Exp)
                    k.act(Em, cl, AF.Exp, scale=-1.0)
                    for j in range(NM):
                        k.act(Eh[:, j * 128:(j + 1) * 128], cl[:, j * 128:(j + 1) * 128], AF.Exp,
                              bias=clC[:, j:j + 1], scale=-1.0)
                    Ep3 = Ep.rearrange("p (a b) -> p a b", a=NM)
                    kkn3 = kkn.rearrange("p (a b) -> p a b", a=NM)
                    oA = oAR[:, osl, d, :, 0:128]
                    oR = oAR[:, osl, d, :, 128:256]
                    if d == 0:
                        k.stt(oA[:, :, 1:128], kkn3[:, :, 1:128], -1.0, Ep3[:, :, 0:127], ALU.mult, ALU.mult)
                        k.ts("pool", oA[:, :, 0:1], kkn3[:, :, 0:1], -1.0, ALU.mult)
                    else:
                        k.stt(oA[:, :, 0:127], kkn3[:, :, 0:127], -1.0, Ep3[:, :, 1:128], ALU.mult, ALU.mult)
                        k.ts("pool", oA[:, :, 127:128], kkn3[:, :, 127:128], -1.0, ALU.mult)
                    k.tt("pool", oR, r_.rearrange("p (a b) -> p a b", a=NM), Ep3, ALU.mult)
                    k.tt("dve", oBT[:, osl, d, :, :], kka.rearrange("p (a b) -> p a b", a=NM),
                         Em.rearrange("p (a b) -> p a b", a=NM), ALU.mult)
                    k.tt("pool", oKT[:, osl, d, :, :], kd.rearrange("p (a b) -> p a b", a=NM),
                         Em.rearrange("p (a b) -> p a b", a=NM), ALU.mult)
                    k.tt("dve", oBHc[:, d, :], kka, Eh, ALU.mult)
                    k.tt("pool", oKHc[:, d, :], kd, Eh, ALU.mult)
                    pt2 = ptr[CH[0]].rearrange("p (a b) -> p a b", a=8)
                    for j in range(NM):
                        k.tr(pt2[:, j, :], oBHc[:, d, j * 128:(j + 1) * 128], identb)
                        k.tr(pt2[:, NM + j, :], oKHc[:, d, j * 128:(j + 1) * 128], identb)
                    k.cp("act", oBH[:, osl, d, :, :], pt2[:, 0:NM, :])
                    k.cp("dve", oKH[:, osl, d, :, :], pt2[:, NM:2 * NM, :])
                    k.dma(S_AR[d, m, cb].rearrange("p (a b) -> p a b", a=NM), oAR[:, osl, d, :, :])
                    k.dma(S_BT[d, m, cb].rearrange("p (a b) -> p a b", a=NM), oBT[:, osl, d, :, :])
                    k.dma(S_KT[d, m, cb].rearrange("p (a b) -> p a b", a=NM), oKT[:, osl, d, :, :])
                    k.dma(S_BH[d, m, cb].rearrange("p (a b) -> p a b", a=NM), oBH[:, osl, d, :, :])
                    k.dma(S_KH[d, m, cb].rearrange("p (a b) -> p a b", a=NM), oKH[:, osl, d, :, :])
                    k.dma(S_WC[d, m, cb], oWC[:, osl, d, :])
                pb = psB[CH[0]]
                bonb = T(7).bitcast(BF16)[:, 0:512]
                k.cp("dve", bonb, bon)
                k.mm(pb[:, :], blockones, bonb, start=True, stop=True)
                k.tt("dve", oBON[:, osl, :], pb[:, :], v_, ALU.mult)
                k.dma(S_BON[m, cb], oBON[:, osl, :])
            def attn_cb(cb):
                osl = cb % 2
                CH[0] = osl
                oVAc = oVAc2[:, osl]
                w_begin([17 + cb, 21 + cb, 25 + cb, 29 + cb])
                for which, dst, gcol in ((0, S_Q, PC_QG), (1, S_K, PC_KG)):
                    pz = proj_attn(m, 17 + which * 4 + cb)
                    q_ = T(0)
                    k.cp("act", q_, pz)
                    q2 = T(1).bitcast(BF16)[:, 0:512]
                    k.tt("pool", q2, q_, q_, ALU.mult)
                    pn = psB[CH[0]]
                    k.mm(pn[:, :], blockones, q2, start=True, stop=True)
                    rn = T(1)
                    k.ts("dve", rn, pn[:, :], 1.0 / 64, ALU.mult, RMS_EPS, ALU.add)
                    k.act(rn, rn, AF.Ln)
                    k.act(rn, rn, AF.Exp, scale=-0.5)
                    k.stt(oQ[:, osl, :], q_, pc[:, gcol:gcol + 1], rn, ALU.mult, ALU.mult)
                    k.dma(dst[m, cb], oQ[:, osl, :])
                pz = proj_attn(m, 25 + cb)
                k.cp("act", oVAc, pz)
                pt = ptr[CH[0]].rearrange("p (a b) -> p a b", a=8)
                for j in range(NM):
                    k.tr(pt[:, j, :], oVAc[:, j * 128:(j + 1) * 128], identb)
                k.cp("dve", oVA[:, osl, :, :], pt[:, 0:NM, :])
                k.dma(S_VA[m, cb].rearrange("p (a b) -> p a b", a=NM), oVA[:, osl, :, :])
                pz = proj_attn(m, 29 + cb)
                g_ = T(2)
                k.cp("act", g_, pz)
                th = T(3)
                k.act(th, g_, AF.Tanh, scale=0.5)
                sigm_from_tanh("dve", th, th)
                k.tt("pool", oGA[:, osl, :], th, g_, ALU.mult)
                k.dma(S_GA[m, cb], oGA[:, osl, :])


            for pair in ((0, 1), (2, 3)):
                lists = []
                for cb in pair:
                    k.cap = []
                    rwkv_cb(cb)
                    lists.append(k.cap)
                    k.cap = None
                k.merge(lists, store_delay=0)
            for pair in ((0, 1), (2, 3)):
                lists = []
                for cb in pair:
                    k.cap = []
                    attn_cb(cb)
                    lists.append(k.cap)
                    k.cap = None
                k.merge(lists, store_delay=0)
            CH[0] = 0

        print("SBUF pass A bytes", alloc["off"], "base", alloc["base"])
        if cfg.upto == "W":
            k.dma(ys[0:128, :], xs[0:128, :])
            with nc.Block() as block:
                S.emit(sems, dsems, block)
            return nc
        for i in range(min(NM, NT)):
            stage_x(i)
        if cfg.upto == "X":
            k.dma(ys[0:128, :], xs[0:128, :])
            with nc.Block() as block:
                S.emit(sems, dsems, block)
            return nc
        for m in range(NMT):
            halo_left(m)
            if m + 1 < NMT:
                for j in range(NM):
                    stage_x((m + 1) * NM + j)
            halo_right(m)
            passA_macro(m)

        if cfg.upto == "A":
            with nc.Block() as block:
                S.emit(sems, dsems, block)
            return nc

        W0a, W0b = psW[0][:, 0:512], psW[0][:, 512:1024]
        W1a, W1b = psW[1][:, 0:512], psW[1][:, 512:1024]

        def rwkv_pass(d):
            pass_reset()
            ldAR = sb("ldAR", [128, 2, 4, NM, 256], BF16)
            ldBT = sb("ldBT", [128, 2, 4, NM, 128], BF16)
            ldKT = sb("ldKT", [128, 2, 4, NM, 128], BF16)
            ldBH = sb("ldBH", [128, 2, 4, NM, 128], BF16)
            ldKH = sb("ldKH", [128, 2, 4, NM, 128], BF16)
            ldV = sb("ldV", [128, 2, 4, NM, 128], BF16)
            ldWC = sb("ldWC", [128, 2, 4, NM], F32)
            MAb = sb("MA", [128, 2, 8, 256], BF16)
            AKb = sb("AK", [128, 2, 8, 256], BF16)
            N0b = sb("N0", [128, 2, 8, 128], BF16)
            Mn = sb("Mn", [128, 2, 8, 128], BF16)
            Nn = sb("Nn", [128, 2, 8, 128], BF16)
            Ttb = sb("Ttb", [128, 2, 8, 128], BF16)
            G1s = sb("G1s", [128, 512], F32)
            G12 = sb("G12", [128, 512], BF16)
            Ubf = sb("Ubf", [128, 512], BF16)
            Hs = sb("Hs", [128, 256], F32)
            Hblk = sb("Hblk", [128, 4, 128], BF16)
            yo = sb("yo", [128, 2, 512], F32)
            mk2 = sb("mk2", [128, 256], F32)
            mkN = sb("mkN", [128, 128], F32)
            k.cp("pool", mk2[:, 0:128], cst[:, 1 if d == 0 else 2, :])
            k.cp("pool", mk2[:, 128:256], cst[:, 3 if d == 0 else 4, :])
            k.cp("pool", mkN[:], cst[:, 2 if d == 0 else 1, :])
            k.memset("pool", Hs[:], 0.0)
            k.memset("pool", Hblk[:], 0.0)
            import os as _os
            for _w in range(int(_os.environ.get("WARM", "0"))):
                k.mm(W0a, identb, MAb[:, 0, 0:2, :].rearrange("p a b -> p (a b)"), start=True, stop=True)
            if d == 1:
                yf = sb("yf", [128, 2, 512], F32)
                ysum = sb("ysum", [128, 512], F32)
                lnt = sb("lnt", [128, 512], F32)
                lnm = sb("lnm", [128, 16], F32)
                ynT = sb("ynT", [128, 512], F32)
                ldBON = sb("ldBON", [128, 2, 4, 128], F32)
                ldGR = sb("ldGR", [128, 2, 4, 128], F32)
                oMR = sb("oMR", [128, 2, 4, 128], BF16)

            def load(m, sl):
                k.dma(ldAR[:, sl].rearrange("p c t x -> p c (t x)"), S_AR[d, m].rearrange("c p x -> p c x"))
                k.dma(ldBT[:, sl].rearrange("p c t x -> p c (t x)"), S_BT[d, m].rearrange("c p x -> p c x"))
                k.dma(ldKT[:, sl].rearrange("p c t x -> p c (t x)"), S_KT[d, m].rearrange("c p x -> p c x"))
                k.dma(ldBH[:, sl].rearrange("p c t x -> p c (t x)"), S_BH[d, m].rearrange("c p x -> p c x"))
                k.dma(ldKH[:, sl].rearrange("p c t x -> p c (t x)"), S_KH[d, m].rearrange("c p x -> p c x"))
                k.dma(ldV[:, sl].rearrange("p c t x -> p c (t x)"), S_V[m].rearrange("c p x -> p c x"))
                k.dma(ldWC[:, sl], S_WC[d, m].rearrange("c p x -> p c x"))

            def tile(m, sl, j, cnt):
                i = m * NM + j
                AR = ldAR[:, sl]
                BT = ldBT[:, sl]
                KT = ldKT[:, sl]
                BH = ldBH[:, sl]
                KH = ldKH[:, sl]
                V = ldV[:, sl]
                MA = MAb[:, cnt % 2]
                AK = AKb[:, cnt % 2]
                N0 = N0b[:, cnt % 2]
                bnd = None
                if d == 0 and i % SEG == 0 and i > 0:
                    bnd = i // SEG
                if d == 1 and (i + 1) % SEG == 0 and i < NT - 1:
                    bnd = (i + 1) // SEG
                if bnd is not None:
                    k.ts("pool", Hs[:], Hs[:], flg[:, bnd:bnd + 1], ALU.mult)
                    k.ts("pool", Hblk[:], Hblk[:], flg[:, bnd:bnd + 1], ALU.mult)
                m2 = mk2[:].unsqueeze(1).to_broadcast([128, 2, 256])
                mN = mkN[:].unsqueeze(1).to_broadcast([128, 2, 128])
                sbanks = {0: [psB[0], psB[1], psB[2][:, 0:256]], 1: [psB[3], W1b, psB[2][:, 256:512]]}
                for cp_ in range(2):
                    for hh in range(2):
                        po = 64 * hh
                        bs = sbanks[hh]
                        for ci in range(2):
                            cb = 2 * cp_ + ci
                            k.mm(bs[0][:, ci * 256:(ci + 1) * 256], BT[po:po + 64, cb, j, :], AR[po:po + 64, cb, j, :])
                            k.mm(bs[1][:, ci * 256:(ci + 1) * 256], KT[po:po + 64, cb, j, :], AR[po:po + 64, cb, j, :])
                            k.mm(bs[2][:, ci * 128:(ci + 1) * 128], AR[po:po + 64, cb, j, 0:128], BT[po:po + 64, cb, j, :])
                        h0 = 4 * cp_ + hh
                        k.tt("dve", MA[:, h0:h0 + 3:2, :], bs[0].rearrange("p (a b) -> p a b", a=2), m2, ALU.mult)
                        k.tt("dve", AK[:, h0:h0 + 3:2, :], bs[1].rearrange("p (a b) -> p a b", a=2), m2, ALU.mult)
                        k.tt("dve", N0[:, h0:h0 + 3:2, :], bs[2].rearrange("p (a b) -> p a b", a=2), mN, ALU.mult)
                k.tt("pool", Ttb[:, 0], MA[:, :, 0:128], cst[:, 0, :].unsqueeze(1).to_broadcast([128, 8, 128]), ALU.add)
                Ttbank = [psB[2], psB[3]]
                Mbank = [W1a, psB[0]]
                Nbank = [W1b, psB[1]]
                for half in range(2):
                    for q in range(4):
                        h = 4 * half + q
                        k.mm(Ttbank[half][:, q * 128:(q + 1) * 128], identb, Ttb[:, 0, h, :],
                             start=(q == 0), stop=False, skip=True)
                Mcur = [MA[:, h, 0:128] for h in range(8)]
                Ncur = [N0[:, h, :] for h in range(8)]
                for lev in range(7):
                    for half in range(2):
                        if lev <= 5:
                            for q in range(4):
                                h = 4 * half + q
                                k.mm(Nbank[half][:, q * 128:(q + 1) * 128], Mcur[h], Ncur[h])
                            if lev <= 4:
                                for q in range(4):
                                    h = 4 * half + q
                                    k.mm(Mbank[half][:, q * 128:(q + 1) * 128], Ncur[h], Mcur[h])
                        if lev >= 1:
                            for q in range(4):
                                h = 4 * half + q
                                k.mm(Ttbank[half][:, q * 128:(q + 1) * 128], Ncur[h], Ttb[:, (lev - 1) % 2, h, :],
                                     start=False, stop=(lev == 6), skip=True)
                            k.cp("act" if half == 0 else "dve", Ttb[:, lev % 2, 4 * half:4 * half + 4, :],
                                 Ttbank[half].rearrange("p (a b) -> p a b", a=4))
                        if lev <= 5:
                            k.cp("act" if (half == 0 and lev % 2 == 0) else "dve", Nn[:, lev % 2, 4 * half:4 * half + 4, :],
                                 Nbank[half].rearrange("p (a b) -> p a b", a=4))
                            if lev <= 4:
                                k.cp("act", Mn[:, lev % 2, 4 * half:4 * half + 4, :],
                                     Mbank[half].rearrange("p (a b) -> p a b", a=4))
                    if lev <= 5:
                        Ncur = [Nn[:, lev % 2, h, :] for h in range(8)]
                        if lev <= 4:
                            Mcur = [Mn[:, lev % 2, h, :] for h in range(8)]
                TT = Ttb[:, 0]
                for h in range(8):
                    cb, po = h // 2, 64 * (h % 2)
                    k.mm(W0a[:, h * 64:(h + 1) * 64], AK[:, h, 0:128], V[:, cb, j, po:po + 64])
                k.cp("act", G1s[:], W0a)
                for cb in range(4):
                    k.mm(W0b[:, cb * 128:(cb + 1) * 128], AR[:, cb, j, 0:128], Hblk[:, cb, :])
                k.tt("dve", G12[:], W0b, G1s[:], ALU.add)
                for h in range(8):
                    k.mm(W0a[:, h * 64:(h + 1) * 64], TT[:, h, :], G12[:, h * 64:(h + 1) * 64])
                k.cp("act", Ubf[:], W0a)
                for cb in range(4):
                    h0, h1 = 2 * cb, 2 * cb + 1
                    o0 = W0b[:, h0 * 64:(h0 + 1) * 64]
                    o1 = W0b[:, h1 * 64:(h1 + 1) * 64]
                    k.mm(o0, AK[:, h0, 128:256], V[:, cb, j, 0:64], start=True, stop=False, skip=True)
                    k.mm(o1, AK[:, h1, 128:256], V[:, cb, j, 64:128], start=False, stop=False, skip=True)
                    k.mm(W0b[:, cb * 128:(cb + 1) * 128], AR[:, cb, j, 128:256], Hblk[:, cb, :], start=False, stop=False, skip=True)
                    k.mm(o0, MA[:, h0, 128:256], Ubf[:, h0 * 64:(h0 + 1) * 64], start=False, stop=False, skip=True)
                    k.mm(o1, MA[:, h1, 128:256], Ubf[:, h1 * 64:(h1 + 1) * 64], start=False, stop=True, skip=True)
                for h in range(8):
                    cb, po = h // 2, 64 * (h % 2)
                    o = W1a[po:po + 64, cb * 64:(cb + 1) * 64]
                    k.mm(o, KH[:, cb, j, po:po + 64], V[:, cb, j, po:po + 64], start=True, stop=False)
                    k.mm(o, BH[:, cb, j, po:po + 64], Ubf[:, h * 64:(h + 1) * 64], start=False, stop=True)
                ysl = cnt % 2
                if d == 0:
                    k.cp("act", yo[:, ysl, :], W0b)
                    k.dma(S_YF[i], yo[:, ysl, :])
                else:
                    k.dma(yf[:, ysl, :], S_YF[i])
                    k.tt("dve", ysum[:], W0b, yf[:, ysl, :], ALU.add)
                for cb in range(4):
                    k.stt(Hs[:, cb * 64:(cb + 1) * 64], Hs[:, cb * 64:(cb + 1) * 64], ldWC[:, sl, cb, j:j + 1],
                          W1a[:, cb * 64:(cb + 1) * 64], ALU.mult, ALU.add)
                Hs3 = Hs[:].rearrange("p (a b) -> p a b", a=4)
                k.cp("pool", Hblk[0:64, :, 0:64], Hs3[0:64])
                k.cp("pool", Hblk[64:128, :, 64:128], Hs3[64:128])
                if d == 1:
                    y3 = ysum[:].rearrange("p (a b) -> p a b", a=8)
                    k.red(lnm[:, 0:8], y3)
                    k.ts("dve", lnm[:, 0:8], lnm[:, 0:8], 1.0 / 64, ALU.mult)
                    k.tt("dve", lnt[:].rearrange("p (a b) -> p a b", a=8), y3,
                         lnm[:, 0:8].unsqueeze(2).to_broadcast([128, 8, 64]), ALU.subtract)
                    k.tt("pool", ysum[:], lnt[:], lnt[:], ALU.mult)
                    k.red(lnm[:, 8:16], y3)
                    k.ts("dve", lnm[:, 8:16], lnm[:, 8:16], 1.0 / 64, ALU.mult, LNX_EPS, ALU.add)
                    k.tt("pool", lnm[:, 8:16], lnm[:, 8:16], cm05[:, 0:1].to_broadcast([128, 8]), ALU.pow)
                    k.tt("dve", lnt[:].rearrange("p (a b) -> p a b", a=8), lnt[:].rearrange("p (a b) -> p a b", a=8),
                         lnm[:, 8:16].unsqueeze(2).to_broadcast([128, 8, 64]), ALU.mult)
                    for cb in range(4):
                        k.tr(W1b[:, cb * 128:(cb + 1) * 128], lnt[:, cb * 128:(cb + 1) * 128], identf)
                    k.dma(ldBON[:, ysl], S_BON[m].rearrange("c p (t x) -> p c t x", t=NM)[:, :, j, :])
                    k.dma(ldGR[:, ysl], S_GR[m].rearrange("c p (t x) -> p c t x", t=NM)[:, :, j, :])
                    for cb in range(4):
                        k.act(ynT[:, cb * 128:(cb + 1) * 128], W1b[:, cb * 128:(cb + 1) * 128], AF.Identity,
                              bias=pc[:, PC_LB + cb:PC_LB + cb + 1], scale=pc[:, PC_LG + cb:PC_LG + cb + 1])
                    yn3 = ynT[:].rearrange("p (a b) -> p a b", a=4)
                    k.tt("pool", yn3, yn3, ldBON[:, ysl], ALU.add)
                    k.tt("pool", oMR[:, ysl], yn3, ldGR[:, ysl], ALU.mult)
                    k.dma(S_MR[i].rearrange("c p x -> p c x"), oMR[:, ysl])

            order = list(range(NMT)) if d == 0 else list(range(NMT - 1, -1, -1))
            load(order[0], 0)
            cnt = 0
            for idx, m in enumerate(order):
                sl = idx % 2
                if idx + 1 < len(order):
                    load(order[idx + 1], 1 - sl)
                js = range(NM) if d == 0 else range(NM - 1, -1, -1)
                for j in js:
                    tile(m, sl, j, cnt)
                    cnt += 1

        rwkv_pass(0)
        if cfg.upto == "B":
            with nc.Block() as block:
                S.emit(sems, dsems, block)
            return nc
        rwkv_pass(1)
        if cfg.upto == "C":
            with nc.Block() as block:
                S.emit(sems, dsems, block)
            return nc

        pass_reset()
        RS = 8
        EG = sb("EG", [128, 8, 15, 64], BF16)
        vts = sb("vts", [128, NT * 14], F32)
        Wo = sb("Wo", [128, 8, D], BF16)
        Kring = sb("Kring", [128, RS, 4, 128], BF16)
        Vring = sb("Vring", [128, RS, 4, 128], BF16)
        Qt = sb("Qt", [128, 2, 4, 128], BF16)
        ebuf = sb("ebuf", [128, 2, 8, 128], F32)
        pT = sb("pT", [128, 2, 7, 8, 128], BF16)
        gaT = sb("gaT", [128, 2, 4, 128], F32)
        mrT = sb("mrT", [128, 2, 4, 128], BF16)
        maT = sb("maT", [128, 2, 4, 128], BF16)
        dsb2 = sb("dsb", [128, 2, 512], F32)
        otmp2 = sb("otmp", [128, 2, 512], F32)
        xr = sb("xr", [128, 2, D], F32)
        yout = sb("yout", [128, 2, D], F32)
        onesb = sb("onesb", [128, 64], BF16)
        cm1 = sb("cm1", [128, 1], F32)
        k.memset("pool", onesb[:], 1.0)
        k.memset("pool", cm1[:], -1.0)
        k.dma(vts[:], vtab)
        wtmp = sb("wtmp", [128, 2, D], F32)
        for ek in range(8):
            k.dma(wtmp[:, ek % 2, :], w_out[ek * 128:(ek + 1) * 128, :])
            k.cp("pool" if ek % 2 == 0 else "dve", Wo[:, ek, :], wtmp[:, ek % 2, :])
        ohf = sb("ohf", [128, 4096], F32)
        ohb = sb("ohb", [128, 4096], BF16)
        rpf = sb("rpf", [128, 120], F32)
        rpe = sb("rpe", [128, 120], BF16)
        k.dma(ohf[0:31, :], ohot)
        k.dma(rpf[0:31, :], rpbT)
        k.cp("dve", ohb[0:31, :], ohf[0:31, :])
        k.act(rpe[0:31, :], rpf[0:31, :], AF.Exp)
        k.memset("pool", EG[64:128, :, 0:1, :], 0.0)
        for g in range(16):
            bank = psB[g % 2]
            for c in range(4):
                cq = g * 4 + c
                k.mm(bank[0:64, c * 120:(c + 1) * 120], ohb[0:31, cq * 64:(cq + 1) * 64], rpe[0:31, :])
                k.mm(bank[64:128, c * 120:(c + 1) * 120], ohb[0:31, cq * 64:(cq + 1) * 64], rpe[0:31, :])
            k.cp("act", EG[0:64, :, :, g * 4:(g + 1) * 4],
                 bank[0:64, 0:480].rearrange("p (c h u) -> p h u c", c=4, h=8))
            k.cp("dve", EG[64:128, :, 1:15, g * 4:(g + 1) * 4],
                 bank[64:128, 0:480].rearrange("p (c h u) -> p h u c", c=4, h=8)[:, :, 0:14, :])

        loaded = [-1]

        def ensure_keys(upto):
            while loaded[0] < min(upto, NT - 1):
                t = loaded[0] + 1
                mk, jk = t // NM, t % NM
                k.dma(Kring[:, t % RS], S_K[mk].rearrange("c p (t x) -> p c t x", t=NM)[:, :, jk, :])
                k.dma(Vring[:, t % RS], S_VA[mk].rearrange("c p (t x) -> p c t x", t=NM)[:, :, jk, :])
                loaded[0] = t

        zc = [0]
        for i in range(NT):
            m, j = i // NM, i % NM
            sl = i % 2
            pos = i % SEG
            offs = [-2, -1, 0, 1, 2]
            if pos == 0:
                offs.append(3)
            if pos == SEG - 1:
                offs.insert(0, -3)
            offs = [o for o in offs if 0 <= i + o < NT]
            ensure_keys(i + 3)
            k.dma(Qt[:, sl], S_Q[m].rearrange("c p (t x) -> p c t x", t=NM)[:, :, j, :])
            k.dma(gaT[:, sl], S_GA[m].rearrange("c p (t x) -> p c t x", t=NM)[:, :, j, :])
            k.dma(mrT[:, sl], S_MR[i].rearrange("c p x -> p c x"))
            k.dma(xr[:, sl, :], xs[i * 128:(i + 1) * 128, :])
            for ki, o in enumerate(offs):
                t = i + o
                ks = t % RS
                pz = psW[zc[0] % 2]
                es_ = zc[0] % 2
                zc[0] += 1
                for hh in range(2):
                    po = 64 * hh
                    for cb in range(4):
                        k.mm(pz[:, (hh * 4 + cb) * 128:(hh * 4 + cb + 1) * 128], Kring[po:po + 64, ks, cb, :],
                             Qt[po:po + 64, sl, cb, :])
                k.act(ebuf[:, es_].rearrange("p a b -> p (a b)"), pz[:], AF.Exp, scale=0.125)
                u0 = 7 - 2 * o
                for qh in range(2):
                    col = (i * 7 + (o + 3)) * 2 + qh
                    k.stt(pT[:, sl, ki, :, qh * 64:(qh + 1) * 64], ebuf[:, es_, :, qh * 64:(qh + 1) * 64],
                          vts[:, col:col + 1], EG[:, :, u0 + qh, :], ALU.mult, ALU.mult)
            nk = len(offs)
            bO = psB[2 * (i % 2)]
            bD = psB[2 * (i % 2) + 1]
            dsb = dsb2[:, i % 2]
            otmp = otmp2[:, i % 2]
            for hh in range(2):
                po = 64 * hh
                for cb in range(4):
                    idx = hh * 4 + cb
                    oo = bO[po:po + 64, cb * 128:(cb + 1) * 128]
                    dd = bD[po:po + 64, cb * 128:(cb + 1) * 128]
                    for ki, o in enumerate(offs):
                        ks = (i + o) % RS
                        k.mm(oo, Vring[:, ks, cb, po:po + 64], pT[:, sl, ki, idx, :], start=(ki == 0), stop=(ki == nk - 1))
                    for ki, o in enumerate(offs):
                        k.mm(dd, onesb[:, :], pT[:, sl, ki, idx, :], start=(ki == 0), stop=(ki == nk - 1))
            k.recip(dsb, bD[:])
            k.tt("dve", otmp, bO[:], dsb, ALU.mult)
            k.tt("pool", maT[:, sl].rearrange("p a b -> p (a b)"), otmp, gaT[:, sl].rearrange("p a b -> p (a b)"), ALU.mult)
            for half in range(2):
                ob = (bO if half == 0 else bD)[:]
                for ek in range(8):
                    lh = mrT[:, sl, ek, :] if ek < 4 else maT[:, sl, ek - 4, :]
                    k.mm(ob, lh, Wo[:, ek, half * 512:(half + 1) * 512], start=(ek == 0), stop=(ek == 7))
                k.tt("dve", yout[:, sl, half * 512:(half + 1) * 512], ob, xr[:, sl, half * 512:(half + 1) * 512], ALU.add)
            k.dma(ys[i * 128:(i + 1) * 128, :], yout[:, sl, :])

        with nc.Block() as block:
            S.emit(sems, dsems, block)
    return nc


def _cols(v, nblk):
    return np.ascontiguousarray(np.asarray(v, np.float32).reshape(nblk, 128).T)


def make_consts():
    c = np.zeros((128, 6, 128), np.float32)
    i = np.arange(128)
    c[:, 0, :] = np.eye(128, dtype=np.float32)
    c[:, 1, :] = (i[:, None] < i[None, :])
    c[:, 2, :] = (i[:, None] > i[None, :])
    c[:, 3, :] = (i[:, None] <= i[None, :])
    c[:, 4, :] = (i[:, None] >= i[None, :])
    c[:, 5, :] = ((i[:, None] // 64) == (i[None, :] // 64))
    return c


def pack_params(p):
    pcols = np.zeros((128, NPC), np.float32)
    pcols[:, PC_MU:PC_MU + 17] = _cols(p["shift_mu"], 17)
    for d in range(2):
        pcols[:, PC_W0 + 4 * d:PC_W0 + 4 * d + 4] = _cols(p["w0"][d], 4)
        pcols[:, PC_A0 + 4 * d:PC_A0 + 4 * d + 4] = _cols(p["a0"][d], 4)
    pcols[:, PC_KK:PC_KK + 4] = _cols(p["k_k"], 4)
    pcols[:, PC_KA:PC_KA + 4] = _cols(p["k_a"], 4)
    pcols[:, PC_RK:PC_RK + 4] = _cols(p["r_k"].reshape(-1), 4)
    pcols[:, PC_LG:PC_LG + 4] = _cols(p["lnx_g"], 4)
    pcols[:, PC_LB:PC_LB + 4] = _cols(p["lnx_b"], 4)
    pcols[:, PC_QG] = np.tile(p["q_norm_g"], 2)
    pcols[:, PC_KG] = np.tile(p["k_norm_g"], 2)
    pcols[:, PC_NG:PC_NG + 8] = _cols(p["norm_g"], 8)
    wup = np.zeros((128, 2, 512), np.float32)
    wup[0:64] = np.transpose(p["w_up"], (1, 0, 2))
    wup[64:128] = np.transpose(p["a_up"], (1, 0, 2))
    cq = np.arange(64)
    ck = np.arange(64)
    cs = np.clip(cq - 8, 0, 48)
    inw = (ck[None, :] >= cs[:, None]) & (ck[None, :] < cs[:, None] + 16)
    dc = ck[None, :] - cq[:, None] + 15
    oh = np.zeros((31, 64, 64), np.float32)
    for q_ in range(64):
        for k_ in range(64):
            if inw[q_, k_]:
                oh[dc[q_, k_], q_, k_] = 1.0
    rp = np.asarray(p["rpb"], np.float32)
    rp = rp[:, ::-1, :]
    rp = rp.reshape(4, 2, 15, 31).transpose(3, 1, 0, 2)
    return {
        "ohot": np.ascontiguousarray(oh.reshape(31, 4096)),
        "rpbT": np.ascontiguousarray(rp.reshape(31, 120)),
        "w_in": np.ascontiguousarray(p["w_in"], np.float32),
        "w_out": np.ascontiguousarray(p["w_out"], np.float32),
        "pcols": pcols,
        "wup": wup,
        "consts": make_consts(),
    }


def make_core_meta(cfg, slots):
    NT, SEG, NSEG = cfg.NT, cfg.SEG, cfg.NSEG
    flags = np.zeros((128, NSEG + 1), np.float32)
    for s_ in range(1, NSEG):
        a, b = slots[s_ - 1], slots[s_]
        if a is not None and b is not None and a[0] == b[0] and b[1] == a[1] + 1:
            flags[:, s_] = 1.0
    vt = np.zeros((128, NT, 7, 2), np.float32)
    for i in range(NT):
        sl = slots[i // SEG]
        if sl is None:
            vt[:, i, 3, :] = 1.0
            continue
        sid, sidx, nseg = sl
        R = nseg * SEG * 2
        for qh in range(2):
            r = (sidx * SEG + (i % SEG)) * 2 + qh
            R0 = min(max(r - 4, 0), R - 8)
            for o in range(-3, 4):
                t = i + o
                if t < 0 or t >= NT:
                    continue
                sk = slots[t // SEG]
                if sk is None or sk[0] != sid:
                    continue
                if sk[1] - sidx != (t // SEG) - (i // SEG):
                    continue
                for kh in range(2):
                    rho = (sk[1] * SEG + (t % SEG)) * 2 + kh
                    if R0 <= rho < R0 + 8:
                        vt[kh * 64:(kh + 1) * 64, i, o + 3, qh] = 1.0
    return flags, np.ascontiguousarray(vt.reshape(128, NT * 14))


PARAM_NAMES = ["norm_g", "w_in", "shift_mu", "w0", "w_up", "a0", "a_up", "k_k", "k_a", "r_k",
               "lnx_g", "lnx_b", "q_norm_g", "k_norm_g", "rpb", "w_out"]


def run_layer(cfg, seqs, p, n_cores):
    NT, SEG, NSEG = cfg.NT, cfg.SEG, cfg.NSEG
    segtok = SEG * 128
    order = sorted(range(len(seqs)), key=lambda i: -seqs[i].shape[0])
    cores = [[] for _ in range(n_cores)]
    used = [0] * n_cores
    for si in order:
        ns = seqs[si].shape[0] // segtok
        c = min(range(n_cores), key=lambda c_: (used[c_] + ns > NSEG, used[c_]))
        assert used[c] + ns <= NSEG
        for g in range(ns):
            cores[c].append((si, g, ns))
        used[c] += ns
    shared = pack_params(p)
    in_maps = []
    for c in range(n_cores):
        slots = cores[c] + [None] * (NSEG - len(cores[c]))
        x = np.zeros((NT * 128, D), np.float32)
        for s_, sl in enumerate(slots):
            if sl is not None:
                x[s_ * segtok:(s_ + 1) * segtok] = seqs[sl[0]][sl[1] * segtok:(sl[1] + 1) * segtok]
        flags, vt = make_core_meta(cfg, slots)
        im = dict(shared)
        im["xs"] = x
        im["flags"] = flags
        im["vtab"] = vt
        in_maps.append(im)
    nc = build_program(cfg)
    res = run_bass_kernel_spmd(nc, in_maps, core_ids=list(range(n_cores)))
    outs = [np.zeros_like(s_) for s_ in seqs]
    for c in range(n_cores):
        y = np.asarray(res.results[c]["ys"])
        for s_, sl in enumerate(cores[c]):
            outs[sl[0]][sl[1] * segtok:(sl[1] + 1) * segtok] = y[s_ * segtok:(s_ + 1) * segtok]
    return outs, res


def kernel(**inputs):
    cfg = Cfg(128, 32)
    p = {n_: np.asarray(inputs[n_], np.float32)[0] for n_ in PARAM_NAMES}
    xp = np.asarray(inputs["x_prompt"], np.float32)
    xsm = np.asarray(inputs["x_sample"], np.float32)
    seqs = [xp[b] for b in range(xp.shape[0])] + [xsm[b] for b in range(xsm.shape[0])]
    outs, _ = run_layer(cfg, seqs, p, 8)
    yp = np.stack(outs[:xp.shape[0]], 0)
    ysm = np.stack(outs[xp.shape[0]:], 0)
    return (yp, ysm)
```

```python
import numpy as np
from contextlib import ExitStack
import concourse.bass as bass
import concourse.mybir as mybir
from concourse.bass_utils import run_bass_kernel_spmd

F32 = mybir.dt.float32
BF16 = mybir.dt.bfloat16
ALU = mybir.AluOpType
AF = mybir.ActivationFunctionType
AX = mybir.AxisListType

D = 1024
DIN = 4224
NBLK = 33
RMS_EPS = 1e-6
LNX_EPS = 64e-5
CDEC = 0.6065306597126334

ENGS = ["pe", "act", "dve", "pool", "sp"]
NDMASEM = 48


class Op:
    __slots__ = ("eng", "fn", "chan", "seq", "signal", "waits", "vc", "is_dma", "sigval", "rows",
                 "deps", "idx", "dur", "start", "finish", "succs", "npred", "ready")

    def __init__(self, eng, fn, is_dma):
        self.eng = eng
        self.fn = fn
        self.is_dma = is_dma
        self.signal = False
        self.waits = []
        self.vc = None
        self.chan = None
        self.seq = None
        self.sigval = None
        self.rows = None
        self.deps = {}
        self.dur = 500.0
        self.start = 0.0
        self.finish = 0.0
        self.succs = []
        self.npred = 0
        self.ready = 0.0


def _region(ap):
    name = ap.tensor.name
    pat = ap.ap
    off = ap.offset
    es = mybir.dt.size(ap.dtype)
    if "DRAM" in str(ap.space).upper():
        ext = 0
        for st, cnt in pat:
            ext += (cnt - 1) * abs(st)
        return name, 0, 1, off * es, (off + ext + 1) * es
    row = pat[0][0]
    npart = pat[0][1]
    if row == 0:
        p0 = 0
        f0 = off
    else:
        p0 = off // row
        f0 = off - p0 * row
    ext = 0
    for st, cnt in pat[1:]:
        ext += (cnt - 1) * abs(st)
    return name, p0, p0 + npart, f0 * es, (f0 + ext + 1) * es


def _free_elems(ap):
    n = 1
    for st, cnt in ap.ap[1:]:
        n *= cnt
    return n


class Sched:
    def __init__(self, nc, same_engine_sync=True, reorder=True):
        self.nc = nc
        self.ops = {e: [] for e in ENGS}
        self.all_ops = []
        self.hist = {}
        self.dma_uses = [0] * NDMASEM
        self.same_engine_sync = same_engine_sync
        self.reorder = reorder
        import os
        self.max_ops = int(os.environ.get("MAXOPS", "100000000"))
        if os.environ.get("NOREORDER"):
            self.reorder = False
        if os.environ.get("NOSES"):
            self.same_engine_sync = False
        self.nadd = 0
        self.log = []

    @staticmethod
    def _dep(op, A, kind):
        r = op.deps.get(id(A))
        if r is None:
            op.deps[id(A)] = [A, kind]
        elif kind < r[1] or (kind == 1 and r[1] == 2):
            r[1] = kind if not (r[1] == 0) else 0

    def add(self, eng, fn, reads=(), writes=(), dma=False, rows=None):
        self.nadd += 1
        if self.nadd > self.max_ops:
            return None
        op = Op(eng, fn, dma)
        op.rows = rows
        op.idx = len(self.all_ops)
        accesses = []
        nel = 0
        for ap, is_w in [(a, False) for a in reads] + [(a, True) for a in writes]:
            if ap is None or isinstance(ap, (int, float)):
                continue
            name, p0, p1, f0, f1 = _region(ap)
            if is_w and nel == 0:
                nel = _free_elems(ap)
                if dma:
                    nel = nel * mybir.dt.size(ap.dtype) * (p1 - p0)
            accesses.append((name, p0, p1, f0, f1, is_w, False))
            if "PSUM" in str(ap.space).upper():
                b0 = f0 // 2048
                b1 = (f1 - 1) // 2048
                accesses.append((name + "#bank", 0, 128, b0 * 2048, (b1 + 1) * 2048, True, True))
        if dma:
            op.dur = 2000.0 + nel / 150.0
        elif eng == "pe":
            op.dur = 45.0 + nel * 0.5
        elif eng == "act":
            op.dur = 250.0 + nel * 0.8
        elif eng == "dve":
            op.dur = 150.0 + nel * 1.0
        else:
            op.dur = 400.0 + nel * 2.0
        acc2 = []
        for name, p0, p1, f0, f1, is_w, cross_only in accesses:
            if name == "big":
                b = f0 // 512
                while b * 512 < f1:
                    acc2.append(((name, b), p0, p1, max(f0, b * 512), min(f1, (b + 1) * 512), is_w, cross_only))
                    b += 1
            else:
                acc2.append((name, p0, p1, f0, f1, is_w, cross_only))
        for name, p0, p1, f0, f1, is_w, cross_only in acc2:
            h = self.hist.get(name)
            if h is None:
                h = []
            newh = []
            for rec in h:
                rp0, rp1, rf0, rf1, rop, rw = rec
                if rop is op:
                    newh.append(rec)
                    continue
                ov = (rp0 < p1 and p0 < rp1 and rf0 < f1 and f0 < rf1)
                if cross_only:
                    if ov:
                        if rop.eng != eng or rop.is_dma or dma:
                            self._dep(op, rop, 0)
                        elif eng == "pe" and rows is not None and rop.rows is not None \
                                and (rop.rows[1] <= rows[0] or rows[1] <= rop.rows[0]):
                            self._dep(op, rop, 1)
                        else:
                            self._dep(op, rop, 2)
                    if rop.eng == eng and rp0 >= p0 and rp1 <= p1 and rf0 >= f0 and rf1 <= f1:
                        continue
                    newh.append(rec)
                    continue
                if ov and (rw or is_w):
                    self._dep(op, rop, 0)
                if is_w and rp0 >= p0 and rp1 <= p1 and rf0 >= f0 and rf1 <= f1:
                    continue
                if (not is_w) and (not rw) and (not rop.is_dma) and (not dma) and rop.eng == eng \
                        and rp0 == p0 and rp1 == p1 and rf0 == f0 and rf1 == f1:
                    self._dep(op, rop, 2)
                    continue
                newh.append(rec)
            newh.append((p0, p1, f0, f1, op, is_w))
            self.hist[name] = newh
        self.all_ops.append(op)
        self.ops[eng].append(op)
        return op

    def _schedule(self):
        import heapq, os
        lat = float(os.environ.get("SYNCLAT", "300"))
        use_bl = not os.environ.get("NOBLEVEL")
        ops = self.all_ops
        for op in ops:
            op.npred = len(op.deps)
            op.ready = 0.0
        for op in ops:
            for A, kind in op.deps.values():
                A.succs.append((op, kind))
        bl = [0.0] * len(ops)
        for op in reversed(ops):
            m = 0.0
            for sop, kind in op.succs:
                v = bl[sop.idx] + (0.0 if kind == 2 else float(os.environ.get("BLLAT", "0")))
                if v > m:
                    m = v
            bl[op.idx] = (200.0 if op.is_dma else op.dur) + m
        fut = {e: [] for e in ENGS}
        avail = {e: [] for e in ENGS}
        free = {e: 0.0 for e in ENGS}
        for op in ops:
            if op.npred == 0:
                heapq.heappush(fut[op.eng], (op.ready, op.idx, op))
        order = []
        new_streams = {e: [] for e in ENGS}
        n = len(ops)
        while len(order) < n:
            best_e = None
            best_t = None
            for e in ENGS:
                f = fut[e]
                av = avail[e]
                while f and f[0][0] <= free[e]:
                    r_, i_, o_ = heapq.heappop(f)
                    heapq.heappush(av, ((-bl[i_] if use_bl else r_), i_, o_))
                if av:
                    t = free[e]
                elif f:
                    t = f[0][0]
                else:
                    continue
                if best_t is None or t < best_t:
                    best_t = t
                    best_e = e
            assert best_e is not None, "scheduler deadlock (cyclic hazards?)"
            if avail[best_e]:
                _, _, op = heapq.heappop(avail[best_e])
            else:
                _, _, op = heapq.heappop(fut[best_e])
            op.start = best_t
            if op.is_dma:
                free[best_e] = best_t + 60.0
                op.finish = best_t + op.dur
            else:
                free[best_e] = best_t + op.dur
                op.finish = best_t + op.dur
            order.append(op)
            new_streams[best_e].append(op)
            for sop, kind in op.succs:
                if kind == 2:
                    r = op.start + 1.0
                elif op.eng == "pe" and sop.eng == "pe" and kind == 0:
                    r = op.start + min(op.dur, 64.0)
                else:
                    r = op.finish + lat
                if r > sop.ready:
                    sop.ready = r
                sop.npred -= 1
                if sop.npred == 0:
                    heapq.heappush(fut[sop.eng], (sop.ready, sop.idx, sop))
        self.ops = new_streams
        return order

    def finalize(self):
        if self.reorder:
            order = self._schedule()
        else:
            order = self.all_ops
        last = {e: None for e in ENGS}
        dma_last = [None] * NDMASEM
        rr = 0
        cnt = {e: 0 for e in ENGS}
        for op in order:
            eng = op.eng
            prev = last[eng]
            vc = dict(prev.vc) if prev is not None else {}
            deps = list(op.deps.values())
            if op.is_dma:
                j = rr
                rr = (rr + 1) % NDMASEM
                if dma_last[j] is not None:
                    deps.append([dma_last[j], 0])
                op.chan = ("d", j)
                op.seq = self.dma_uses[j]
                self.dma_uses[j] += 1
                dma_last[j] = op
                op.signal = True
            else:
                op.chan = eng
                op.seq = cnt[eng]
            cnt[eng] += 1
            best = {}
            for A, kind in deps:
                if kind == 2:
                    continue
                if A.chan == op.chan and not op.is_dma:
                    if kind != 1 and (eng == "pe" or not self.same_engine_sync):
                        continue
                if vc.get(A.chan, -1) >= A.seq:
                    continue
                if A.chan not in best or best[A.chan].seq < A.seq:
                    best[A.chan] = A
            op.waits = []
            for ch, A in best.items():
                if vc.get(A.chan, -1) >= A.seq:
                    continue
                A.signal = True
                op.waits.append(A)
                for kk_, v in A.vc.items():
                    if vc.get(kk_, -1) < v:
                        vc[kk_] = v
                if vc.get(A.chan, -1) < A.seq:
                    vc[A.chan] = A.seq
            op.vc = vc
            last[eng] = op

    def emit(self, sems, dsems, block):
        self.finalize()
        for e in ENGS:
            c = 0
            for op in self.ops[e]:
                if op.is_dma:
                    op.sigval = 16 * (op.seq + 1)
                elif op.signal:
                    c += 1
                    op.sigval = c
        sched = self

        def run(engname, engobj):
            for op in sched.ops[engname]:
                for A in op.waits:
                    if A.is_dma:
                        engobj.wait_ge(dsems[A.chan[1]], A.sigval)
                    else:
                        engobj.wait_ge(sems[A.chan], A.sigval)
                inst = op.fn(engobj)
                if op.is_dma:
                    inst.then_inc(dsems[op.chan[1]], 16)
                elif op.signal:
                    inst.then_inc(sems[op.chan], 1)

        @block.tensor
        def _(e):
            run("pe", e)

        @block.scalar
        def _(e):
            run("act", e)

        @block.vector
        def _(e):
            run("dve", e)

        @block.gpsimd
        def _(e):
            run("pool", e)

        @block.sync
        def _(e):
            run("sp", e)
            for j in range(NDMASEM):
                if sched.dma_uses[j] > 0:
                    e.wait_ge(dsems[j], 16 * sched.dma_uses[j])


class K:
    def __init__(self, S):
        self.S = S
        self.cap = None

    def _add(self, eng, fn, reads=(), writes=(), dma=False, rows=None):
        if self.cap is not None:
            self.cap.append((eng, fn, reads, writes, dma, rows))
            return None
        return self.S.add(eng, fn, reads=reads, writes=writes, dma=dma, rows=rows)

    def merge(self, lists, store_delay=0):
        idx = [0] * len(lists)
        alive = True
        held = []
        n = 0
        while alive:
            alive = False
            for c, l in enumerate(lists):
                if idx[c] < len(l):
                    item = l[idx[c]]
                    idx[c] += 1
                    alive = True
                    eng, fn, reads, writes, dma, rows = item
                    if dma and store_delay > 0 and "DRAM" in str(writes[0].space).upper():
                        held.append((n + store_delay, item))
                    else:
                        self.S.add(eng, fn, reads=reads, writes=writes, dma=dma, rows=rows)
                    n += 1
                    while held and held[0][0] <= n:
                        eng, fn, reads, writes, dma, rows = held.pop(0)[1]
                        self.S.add(eng, fn, reads=reads, writes=writes, dma=dma, rows=rows)
        for _, item in held:
            eng, fn, reads, writes, dma, rows = item
            self.S.add(eng, fn, reads=reads, writes=writes, dma=dma, rows=rows)

    def dma(self, out, in_, eng="sp"):
        return self._add(eng, lambda e: e.dma_start(out=out, in_=in_), reads=[in_], writes=[out], dma=True)

    def mm(self, out, lhsT, rhs, start=True, stop=True, skip=False):
        rg = _region(lhsT)
        rows = (rg[1], rg[2])
        if skip:
            return self._add("pe", lambda e: e.matmul(out, lhsT=lhsT, rhs=rhs, start=start, stop=stop,
                                                       skip_group_check=True),
                              reads=[lhsT, rhs], writes=[out], rows=rows)
        return self._add("pe", lambda e: e.matmul(out, lhsT=lhsT, rhs=rhs, start=start, stop=stop),
                          reads=[lhsT, rhs], writes=[out], rows=rows)

    def red(self, out, in_, op=ALU.add):
        return self._add("dve", lambda e: e.tensor_reduce(out=out, in_=in_, axis=AX.X, op=op),
                          reads=[in_], writes=[out])

    def recip(self, out, in_):
        return self._add("dve", lambda e: e.reciprocal(out=out, in_=in_), reads=[in_], writes=[out])

    def tr(self, out, in_, ident):
        rg = _region(in_)
        return self._add("pe", lambda e: e.transpose(out, in_, ident), reads=[in_, ident], writes=[out],
                          rows=(rg[1], rg[2]))

    def act(self, out, in_, func, bias=None, scale=None, eng="act"):
        kw = {}
        rd = [in_]
        if bias is not None:
            kw["bias"] = bias
            rd.append(bias)
        if scale is not None:
            kw["scale"] = scale
            rd.append(scale)
        return self._add("act", lambda e: e.activation(out=out, in_=in_, func=func, **kw), reads=rd, writes=[out])

    def tt(self, eng, out, a, b, op):
        return self._add(eng, lambda e: e.tensor_tensor(out=out, in0=a, in1=b, op=op), reads=[a, b], writes=[out])

    def ts(self, eng, out, a, s1, op0, s2=None, op1=None):
        rd = [a, s1, s2]
        if op1 is None:
            return self._add(eng, lambda e: e.tensor_scalar(out=out, in0=a, scalar1=s1, scalar2=None, op0=op0),
                              reads=rd, writes=[out])
        return self._add(eng, lambda e: e.tensor_scalar(out=out, in0=a, scalar1=s1, scalar2=s2, op0=op0, op1=op1),
                          reads=rd, writes=[out])

    def stt(self, out, a, scalar, b, op0, op1):
        return self._add("dve", lambda e: e.scalar_tensor_tensor(out=out, in0=a, scalar=scalar, in1=b, op0=op0, op1=op1),
                          reads=[a, scalar, b], writes=[out])

    def cp(self, eng, out, in_):
        if eng == "act":
            return self.act(out, in_, AF.Copy)
        return self._add(eng, lambda e: e.tensor_copy(out=out, in_=in_), reads=[in_], writes=[out])

    def memset(self, eng, out, val):
        return self._add(eng, lambda e: e.memset(out, val), writes=[out])

    def scan(self, out, d0, d1, init, op0, op1):
        return self._add("dve", lambda e: e.tensor_tensor_scan(out=out, data0=d0, data1=d1, initial=init, op0=op0, op1=op1),
                          reads=[d0, d1, init], writes=[out])

    def ttr(self, out, a, b, accum):
        return self._add("dve", lambda e: e.tensor_tensor_reduce(out=out, in0=a, in1=b, scale=1.0, scalar=0.0,
                                                                  op0=ALU.mult, op1=ALU.add, accum_out=accum),
                          reads=[a, b], writes=[out, accum])


class Cfg:
    def __init__(self, NT, SEG, debug=False, upto="all"):
        self.upto = upto
        self.NT = NT
        self.SEG = SEG
        self.NM = 4
        assert NT % SEG == 0 and SEG % self.NM == 0
        self.NSEG = NT // SEG
        self.NMT = NT // self.NM
        self.debug = debug


PC_MU = 0
PC_W0 = 17
PC_A0 = 25
PC_KK = 33
PC_KA = 37
PC_RK = 41
PC_LG = 45
PC_LB = 49
PC_QG = 53
PC_KG = 54
PC_NG = 55
NPC = 63


def build_program(cfg):
    NT, SEG, NM, NSEG, NMT = cfg.NT, cfg.SEG, cfg.NM, cfg.NSEG, cfg.NMT
    TOK = NM * 128
    nc = bass.Bass("TRN2", target_bir_lowering=False)
    okind = "ExternalOutput" if cfg.debug else "Internal"

    def din(name, shape, dt=F32):
        return nc.dram_tensor(name, list(shape), dt, kind="ExternalInput").ap()

    def dscr(name, shape, dt):
        return nc.dram_tensor(name, list(shape), dt, kind=okind).ap()

    xs = din("xs", [NT * 128, D])
    w_in = din("w_in", [D, DIN])
    w_out = din("w_out", [D, D])
    pcols = din("pcols", [128, NPC])
    flags = din("flags", [128, NSEG + 1])
    wup = din("wup", [128, 2, 512])
    consts = din("consts", [128, 6, 128])
    ohot = din("ohot", [31, 4096])
    rpbT = din("rpbT", [31, 120])
    vtab = din("vtab", [128, NT * 14])
    ys = nc.dram_tensor("ys", [NT * 128, D], F32, kind="ExternalOutput").ap()

    Wb = dscr("Wb", [NBLK, 128, 8 * 128], BF16)
    S_AR = dscr("S_AR", [2, NMT, 4, 128, NM * 256], BF16)
    S_BT = dscr("S_BT", [2, NMT, 4, 128, NM * 128], BF16)
    S_KT = dscr("S_KT", [2, NMT, 4, 128, NM * 128], BF16)
    S_BH = dscr("S_BH", [2, NMT, 4, 128, NM * 128], BF16)
    S_KH = dscr("S_KH", [2, NMT, 4, 128, NM * 128], BF16)
    S_V = dscr("S_V", [NMT, 4, 128, NM * 128], BF16)
    S_WC = dscr("S_WC", [2, NMT, 4, 128, NM], F32)
    S_GR = dscr("S_GR", [NMT, 4, 128, TOK], F32)
    S_BON = dscr("S_BON", [NMT, 4, 128, TOK], F32)
    S_Q = dscr("S_Q", [NMT, 4, 128, TOK], BF16)
    S_K = dscr("S_K", [NMT, 4, 128, TOK], BF16)
    S_VA = dscr("S_VA", [NMT, 4, 128, NM * 128], BF16)
    S_GA = dscr("S_GA", [NMT, 4, 128, TOK], F32)
    S_YF = dscr("S_YF", [NT, 128, 512], F32)
    S_MR = dscr("S_MR", [NT, 4, 128, 128], BF16)

    with ExitStack() as es:
        BIGB = 186 * 1024
        big = es.enter_context(nc.sbuf_tensor("big", [128, BIGB // 2], BF16))
        alloc = {"off": 0, "base": 0}

        def sb(name, shape, dt):
            n = 1
            for d_ in shape[1:]:
                n *= d_
            nbytes = n * mybir.dt.size(dt)
            nbytes = (nbytes + 31) // 32 * 32
            off = alloc["off"]
            assert off + nbytes <= BIGB, (name, off, nbytes)
            alloc["off"] = off + nbytes
            v = big[:, off // 2:(off + nbytes) // 2]
            if dt != BF16:
                v = v.bitcast(dt)
            v = v[:, 0:n]
            if len(shape) == 2:
                return v
            names = " ".join("a%d" % i for i in range(len(shape) - 1))
            kw = {"a%d" % i: shape[i + 1] for i in range(len(shape) - 1)}
            return v.rearrange("p (%s) -> p %s" % (names, names), **kw)

        def pass_reset():
            alloc["off"] = alloc["base"]

        def pst(name, shape, dt=F32):
            return es.enter_context(nc.psum_tensor(name, list(shape), dt))

        sems = {e: es.enter_context(nc.semaphore("s_" + e)) for e in ENGS}
        dsems = [es.enter_context(nc.semaphore("d%d" % j)) for j in range(NDMASEM)]
        S = Sched(nc)
        k = K(S)

        psW = [pst("psW0", [128, 1024]), pst("psW1", [128, 1024])]
        psB = [pst("ps%d" % i, [128, 512]) for i in range(4)]

        cst = sb("cst", [128, 6, 128], F32)
        cstb = sb("cstb", [128, 6, 128], BF16)
        pc = sb("pc", [128, NPC], F32)
        pcd = sb("pcd", [128, 64], F32)
        flg = sb("flg", [128, NSEG + 1], F32)
        wupf = sb("wupf", [128, 2, 512], F32)
        wupb = sb("wupb", [128, 2, 512], BF16)
        cm05 = sb("cm05", [128, 8], F32)
        k.memset("pool", cm05[:], -0.5)
        k.dma(cst[:], consts)
        k.dma(pc[:], pcols)
        k.dma(flg[:], flags)
        k.dma(wupf[:], wup)
        k.cp("dve", cstb[:], cst[:])
        k.cp("dve", wupb[:], wupf[:])
        identb = cstb[:, 0, :]
        identf = cst[:, 0, :]
        blockones = cstb[:, 5, :]
        PD_OMM, PD_HM, PD_W0H, PD_A0H, PD_OMKA = 0, 17, 34, 42, 50
        k.ts("dve", pcd[:, PD_OMM:PD_OMM + 17], pc[:, PC_MU:PC_MU + 17], -1.0, ALU.mult, 1.0, ALU.add)
        k.ts("dve", pcd[:, PD_HM:PD_HM + 17], pc[:, PC_MU:PC_MU + 17], 0.5, ALU.mult)
        k.ts("dve", pcd[:, PD_W0H:PD_W0H + 16], pc[:, PC_W0:PC_W0 + 16], 0.5, ALU.mult)
        k.ts("dve", pcd[:, PD_OMKA:PD_OMKA + 4], pc[:, PC_KA:PC_KA + 4], -1.0, ALU.mult, 1.0, ALU.add)

        alloc["base"] = alloc["off"]
        arena = sb("arena", [128, 2 * 12 * 512], F32)
        wst = arena[:, 0:DIN]
        wstb = arena[:, DIN:DIN + DIN // 2].bitcast(BF16)
        CH = [0]
        for dk in range(8):
            k.dma(wst, w_in[dk * 128:(dk + 1) * 128, :])
            if dk % 2 == 0:
                k.ts("dve", wstb[:], wst, pc[:, PC_NG + dk:PC_NG + dk + 1], ALU.mult)
            else:
                k.act(wstb[:], wst, AF.Copy, scale=pc[:, PC_NG + dk:PC_NG + dk + 1])
            k.dma(Wb.rearrange("b p (k c) -> p b k c", k=8)[:, :, dk, :],
                  wstb[:].rearrange("p (b c) -> p b c", c=128))

        xt = sb("xt", [128, 2, D], F32)
        hb = sb("hb", [128, 2, D], BF16)
        junk = sb("junk", [128, D], BF16)
        ssq = sb("ssq", [128, NT], F32)
        rstd = sb("rstd", [128, NT], F32)
        hT = sb("hT", [128, 2, 8, 514], BF16)
        wblk = sb("wblk", [128, 4, 8 * 128], BF16)
        zraw = sb("zraw", [128, 2, 514], F32)
        ztmp = sb("ztmp", [128, 2, 512], F32)
        zsl = sb("zsl", [128, 512], F32)
        zsc = sb("zsc", [128, 2, 4, 512], F32)
        lorab = sb("lorab", [128, 512], BF16)
        def T(i):
            return arena[:, (CH[0] * 12 + i) * 512:(CH[0] * 12 + i + 1) * 512]
        resetm = sb("resetm", [128, 512], F32)
        k.memset("pool", resetm[:], 1.0)
        for c in range(NM):
            k.memset("pool", resetm[:, c * 128:c * 128 + 1], 0.0)
        oAR = sb("oAR", [128, 2, 2, NM, 256], BF16)
        oBT = sb("oBT", [128, 2, 2, NM, 128], BF16)
        oKT = sb("oKT", [128, 2, 2, NM, 128], BF16)
        oBHc2 = sb("oBHc", [128, 2, 2, 512], BF16)
        oKHc2 = sb("oKHc", [128, 2, 2, 512], BF16)
        oVc2 = sb("oVc", [128, 2, 512], BF16)
        oBH = sb("oBH", [128, 2, 2, NM, 128], BF16)
        oKH = sb("oKH", [128, 2, 2, NM, 128], BF16)
        oV = sb("oV", [128, 2, NM, 128], BF16)
        oWC = sb("oWC", [128, 2, 2, NM], F32)
        oGR = sb("oGR", [128, 2, 512], F32)
        oBON = sb("oBON", [128, 2, 512], F32)
        oQ = sb("oQ", [128, 2, 512], BF16)
        oVAc2 = sb("oVAc", [128, 2, 512], BF16)
        oVA = sb("oVA", [128, 2, NM, 128], BF16)
        oGA = sb("oGA", [128, 2, 512], F32)

        ptr = [psB[2][:].bitcast(BF16), psB[3][:].bitcast(BF16)]

        def stage_x(i):
            sl = i % 2
            m = i // NM
            j = i % NM
            k.dma(xt[:, sl, :], xs[i * 128:(i + 1) * 128, :])
            S.add("act", lambda e, sl=sl, i=i: e.activation(out=junk[:], in_=xt[:, sl, :], func=AF.Square,
                                                          accum_out=ssq[:, i:i + 1]),
                  reads=[xt[:, sl, :]], writes=[junk[:], ssq[:, i:i + 1]])
            k.ts("dve", rstd[:, i:i + 1], ssq[:, i:i + 1], 1.0 / D, ALU.mult, RMS_EPS, ALU.add)
            k.tt("pool", rstd[:, i:i + 1], rstd[:, i:i + 1], cm05[:, 0:1], ALU.pow)
            k.act(hb[:, sl, :], xt[:, sl, :], AF.Copy, scale=rstd[:, i:i + 1])
            pt = ptr[i % 2].rearrange("p (a b) -> p a b", a=8)
            for dk in range(8):
                k.tr(pt[:, dk, :], hb[:, sl, dk * 128:(dk + 1) * 128], identb)
            k.cp("dve" if i % 2 == 0 else "act", hT[:, m % 2, :, 1 + j * 128:1 + (j + 1) * 128], pt)

        def halo_left(m):
            sl = m % 2
            t0 = m * NM
            if m == 0:
                k.memset("pool", hT[:, sl, :, 0:1], 0.0)
            elif t0 % SEG == 0:
                s = t0 // SEG
                k.ts("pool", hT[:, sl, :, 0:1], hT[:, 1 - sl, :, 512:513], flg[:, s:s + 1], ALU.mult)
            else:
                k.cp("pool", hT[:, sl, :, 0:1], hT[:, 1 - sl, :, 512:513])

        def halo_right(m):
            sl = m % 2
            t0 = m * NM
            if m == NMT - 1:
                k.memset("pool", hT[:, sl, :, 513:514], 0.0)
            elif (t0 + NM) % SEG == 0:
                s = (t0 + NM) // SEG
                k.ts("pool", hT[:, sl, :, 513:514], hT[:, 1 - sl, :, 1:2], flg[:, s:s + 1], ALU.mult)
            else:
                k.cp("pool", hT[:, sl, :, 513:514], hT[:, 1 - sl, :, 1:2])

        wcount = [0, 0]
        wq = [[], []]
        wready = [[], []]

        def w_begin(blocks):
            c = CH[0]
            wq[c] = list(blocks)
            wready[c] = []
            w_prefetch()
            w_prefetch()

        def w_prefetch():
            c = CH[0]
            if wq[c]:
                b = wq[c].pop(0)
                sl = c * 2 + wcount[c] % 2
                wcount[c] += 1
                k.dma(wblk[:, sl, :], Wb[b])
                wready[c].append((b, sl))

        def load_w(b):
            c = CH[0]
            if not wready[c]:
                w_begin([b])
            bb, sl = wready[c].pop(0)
            assert bb == b, (bb, b)
            return wblk[:, sl, :].rearrange("p (k c) -> p k c", k=8)

        zcount = [0]

        def proj_rwkv(m, b, dst):
            w = load_w(b)
            pz = psW[CH[0]]
            zr = zraw[:, CH[0], :]
            zt = ztmp[:, CH[0], :]
            sl = m % 2
            for half in range(2):
                for dk in range(8):
                    k.mm(pz[:, half * 512:half * 512 + 257], w[:, dk, :], hT[:, sl, dk, half * 257:(half + 1) * 257],
                         start=(dk == 0), stop=(dk == 7))
            w_prefetch()
            k.act(zr.rearrange("p (a b) -> p a b", a=2), pz.rearrange("p (a b) -> p a b", a=2)[:, :, 0:257], AF.Copy)
            k.act(dst, zr[:, 1:513], AF.Copy, scale=pcd[:, PD_OMM + b:PD_OMM + b + 1])
            if b % 3 == 0:
                k.tt("pool", zt, zr[:, 0:512], zr[:, 2:514], ALU.add)
                k.stt(dst, zt, pcd[:, PD_HM + b:PD_HM + b + 1], dst, ALU.mult, ALU.add)
            else:
                k.stt(dst, zr[:, 0:512], pcd[:, PD_HM + b:PD_HM + b + 1], dst, ALU.mult, ALU.add)
                k.stt(dst, zr[:, 2:514], pcd[:, PD_HM + b:PD_HM + b + 1], dst, ALU.mult, ALU.add)

        def proj_attn(m, b):
            w = load_w(b)
            pz = psW[CH[0]]
            sl = m % 2
            for dk in range(8):
                k.mm(pz[:, 0:512], w[:, dk, :], hT[:, sl, dk, 1:513], start=(dk == 0), stop=(dk == 7))
            w_prefetch()
            return pz[:, 0:512]

        def sigm_from_tanh(eng, out, th):
            k.ts(eng, out, th, 0.5, ALU.mult, 0.5, ALU.add)

        def passA_macro(m):
            CH[0] = 0
            proj_rwkv(m, 16, zsl[:])
            k.act(lorab[0:64, :], zsl[0:64, :], AF.Tanh)
            k.cp("pool", lorab[64:128, :], zsl[64:128, :])

            def rwkv_cb(cb):
                osl = cb % 2
                CH[0] = osl
                oVc = oVc2[:, osl]
                oBHc = oBHc2[:, osl]
                oKHc = oKHc2[:, osl]
                zs_ = zsc[:, osl]
                w_begin([cb, 4 + cb, 8 + cb, 12 + cb])
                proj_rwkv(m, 0 + cb, zs_[:, 0, :])
                proj_rwkv(m, 4 + cb, zs_[:, 1, :])
                proj_rwkv(m, 8 + cb, zs_[:, 2, :])
                proj_rwkv(m, 12 + cb, zs_[:, 3, :])
                r_, k_, v_, g_ = zs_[:, 0, :], zs_[:, 1, :], zs_[:, 2, :], zs_[:, 3, :]
                th = T(0)
                k.act(th, g_, AF.Tanh, scale=0.5)
                sigm_from_tanh("dve", th, th)
                k.tt("pool", oGR[:, osl, :], th, g_, ALU.mult)
                k.dma(S_GR[m, cb], oGR[:, osl, :])
                k.cp("dve", oVc, v_)
                pt = ptr[CH[0]].rearrange("p (a b) -> p a b", a=8)
                for j in range(NM):
                    k.tr(pt[:, j, :], oVc[:, j * 128:(j + 1) * 128], identb)
                k.cp("act", oV[:, osl, :, :], pt[:, 0:NM, :])
                k.dma(S_V[m, cb].rearrange("p (a b) -> p a b", a=NM), oV[:, osl, :, :])
                kk = T(1)
                k.ts("dve", kk, k_, pc[:, PC_KK + cb:PC_KK + cb + 1], ALU.mult)
                kk2 = T(2).bitcast(BF16)[:, 0:512]
                k.tt("pool", kk2, kk, kk, ALU.mult)
                pn = psB[CH[0]]
                k.mm(pn[:, :], blockones, kk2, start=True, stop=True)
                rn = T(2)
                k.ts("dve", rn, pn[:, :], 1e-18, ALU.max)
                k.act(rn, rn, AF.Ln)
                k.act(rn, rn, AF.Exp, scale=-0.5)
                kkn = T(1)
                k.tt("pool", kkn, kk, rn, ALU.mult)
                bon = T(3)
                for d in range(2):
                    pw = psB[CH[0]]
                    pa = psB[CH[0]]
                    k.mm(pw[:, :], wupb[0:64, d, cb * 128:(cb + 1) * 128], lorab[0:64, :])
                    lw = T(4)
                    k.act(lw, pw[:, :], AF.Tanh, bias=pcd[:, PD_W0H + d * 4 + cb:PD_W0H + d * 4 + cb + 1], scale=0.5)
                    k.mm(pa[:, :], wupb[64:128, d, cb * 128:(cb + 1) * 128], lorab[64:128, :])
                    k.ts("dve", lw, lw, -0.5 * CDEC, ALU.mult, -0.5 * CDEC, ALU.add)
                    a_ = T(5)
                    k.act(a_, pa[:, :], AF.Tanh, bias=pcd[:, PD_A0H + d * 4 + cb:PD_A0H + d * 4 + cb + 1], scale=0.5)
                    sigm_from_tanh("dve", a_, a_)
                    kd = T(6)
                    k.ts("dve", kd, a_, pc[:, PC_KA + cb:PC_KA + cb + 1], ALU.mult,
                         pcd[:, PD_OMKA + cb:PD_OMKA + cb + 1], ALU.add)
                    k.tt("pool", kd, kd, k_, ALU.mult)
                    kka = T(5)
                    k.tt("pool", kka, kkn, a_, ALU.mult)
                    rk = T(7) if d == 1 else bon
                    k.stt(rk, kd, pc[:, PC_RK + cb:PC_RK + cb + 1], r_, ALU.mult, ALU.mult)
                    if d == 1:
                        k.tt("pool", bon, bon, rk, ALU.add)
                    cl = T(7)
                    k.scan(cl, resetm[:], lw, 0.0, ALU.mult, ALU.add)
                    cl3 = cl.rearrange("p (a b) -> p a b", a=NM)
                    lw3 = lw.rearrange("p (a b) -> p a b", a=NM)
                    if d == 1:
                        tot = T(8)[:, 0:NM]
                        k.cp("dve", tot, cl3[:, :, 127])
                        k.tt("dve", cl, lw, cl, ALU.subtract)
                        k.tt("dve", cl3, cl3, tot.unsqueeze(2).to_broadcast([128, NM, 128]), ALU.add)
                    clC = T(8)[:, 8:8 + NM]
                    k.cp("dve", clC, cl3[:, :, 127] if d == 0 else cl3[:, :, 0])
                    k.act(oWC[:, osl, d, :], clC, AF.Exp)
                    Ep = T(9)
                    Em = T(10)
                    Eh = T(11)
                    k.act(Ep, cl, AF.Exp)
                    k.act(Em, cl, AF.Exp, scale=-1.0)
                    for j in range(NM):
                        k.act(Eh[:, j * 128:(j + 1) * 128], cl[:, j * 128:(j + 1) * 128], AF.Exp,
                              bias=clC[:, j:j + 1], scale=-1.0)
                    Ep3 = Ep.rearrange("p (a b) -> p a b", a=NM)
                    kkn3 = kkn.rearrange("p (a b) -> p a b", a=NM)
                    oA = oAR[:, osl, d, :, 0:128]
                    oR = oAR[:, osl, d, :, 128:256]
                    if d == 0:
                        k.stt(oA[:, :, 1:128], kkn3[:, :, 1:128], -1.0, Ep3[:, :, 0:127], ALU.mult, ALU.mult)
                        k.ts("pool", oA[:, :, 0:1], kkn3[:, :, 0:1], -1.0, ALU.mult)
                    else:
                        k.stt(oA[:, :, 0:127], kkn3[:, :, 0:127], -1.0, Ep3[:, :, 1:128], ALU.mult, ALU.mult)
                        k.ts("pool", oA[:, :, 127:128], kkn3[:, :, 127:128], -1.0, ALU.mult)
                    k.tt("pool", oR, r_.rearrange("p (a b) -> p a b", a=NM), Ep3, ALU.mult)
                    k.tt("dve", oBT[:, osl, d, :, :], kka.rearrange("p (a b) -> p a b", a=NM),
                         Em.rearrange("p (a b) -> p a b", a=NM), ALU.mult)
                    k.tt("pool", oKT[:, osl, d, :, :], kd.rearrange("p (a b) -> p a b", a=NM),
                         Em.rearrange("p (a b) -> p a b", a=NM), ALU.mult)
                    k.tt("dve", oBHc[:, d, :], kka, Eh, ALU.mult)
                    k.tt("pool", oKHc[:, d, :], kd, Eh, ALU.mult)
                    pt2 = ptr[CH[0]].rearrange("p (a b) -> p a b", a=8)
                    for j in range(NM):
                        k.tr(pt2[:, j, :], oBHc[:, d, j * 128:(j + 1) * 128], identb)
                        k.tr(pt2[:, NM + j, :], oKHc[:, d, j * 128:(j + 1) * 128], identb)
                    k.cp("act", oBH[:, osl, d, :, :], pt2[:, 0:NM, :])
                    k.cp("dve", oKH[:, osl, d, :, :], pt2[:, NM:2 * NM, :])
                    k.dma(S_AR[d, m, cb].rearrange("p (a b) -> p a b", a=NM), oAR[:, osl, d, :, :])
                    k.dma(S_BT[d, m, cb].rearrange("p (a b) -> p a b", a=NM), oBT[:, osl, d, :, :])
                    k.dma(S_KT[d, m, cb].rearrange("p (a b) -> p a b", a=NM), oKT[:, osl, d, :, :])
                    k.dma(S_BH[d, m, cb].rearrange("p (a b) -> p a b", a=NM), oBH[:, osl, d, :, :])
                    k.dma(S_KH[d, m, cb].rearrange("p (a b) -> p a b", a=NM), oKH[:, osl, d, :, :])
                    k.dma(S_WC[d, m, cb], oWC[:, osl, d, :])
                pb = psB[CH[0]]
                bonb = T(7).bitcast(BF16)[:, 0:512]
                k.cp("dve", bonb, bon)
                k.mm(pb[:, :], blockones, bonb, start=True, stop=True)
                k.tt("dve", oBON[:, osl, :], pb[:, :], v_, ALU.mult)
                k.dma(S_BON[m, cb], oBON[:, osl, :])
            def attn_cb(cb):
                osl = cb % 2
                CH[0] = osl
                oVAc = oVAc2[:, osl]
                w_begin([17 + cb, 21 + cb, 25 + cb, 29 + cb])
                for which, dst, gcol in ((0, S_Q, PC_QG), (1, S_K, PC_KG)):
                    pz = proj_attn(m, 17 + which * 4 + cb)
                    q_ = T(0)
                    k.cp("act", q_, pz)
                    q2 = T(1).bitcast(BF16)[:, 0:512]
                    k.tt("pool", q2, q_, q_, ALU.mult)
                    pn = psB[CH[0]]
                    k.mm(pn[:, :], blockones, q2, start=True, stop=True)
                    rn = T(1)
                    k.ts("dve", rn, pn[:, :], 1.0 / 64, ALU.mult, RMS_EPS, ALU.add)
                    k.act(rn, rn, AF.Ln)
                    k.act(rn, rn, AF.Exp, scale=-0.5)
                    k.stt(oQ[:, osl, :], q_, pc[:, gcol:gcol + 1], rn, ALU.mult, ALU.mult)
                    k.dma(dst[m, cb], oQ[:, osl, :])
                pz = proj_attn(m, 25 + cb)
                k.cp("act", oVAc, pz)
                pt = ptr[CH[0]].rearrange("p (a b) -> p a b", a=8)
                for j in range(NM):
                    k.tr(pt[:, j, :], oVAc[:, j * 128:(j + 1) * 128], identb)
                k.cp("dve", oVA[:, osl, :, :], pt[:, 0:NM, :])
                k.dma(S_VA[m, cb].rearrange("p (a b) -> p a b", a=NM), oVA[:, osl, :, :])
                pz = proj_attn(m, 29 + cb)
                g_ = T(2)
                k.cp("act", g_, pz)
                th = T(3)
                k.act(th, g_, AF.Tanh, scale=0.5)
                sigm_from_tanh("dve", th, th)
                k.tt("pool", oGA[:, osl, :], th, g_, ALU.mult)
                k.dma(S_GA[m, cb], oGA[:, osl, :])


            for pair in ((0, 1), (2, 3)):
                lists = []
                for cb in pair:
                    k.cap = []
                    rwkv_cb(cb)
                    lists.append(k.cap)
                    k.cap = None
                k.merge(lists, store_delay=0)
            for pair in ((0, 1), (2, 3)):
                lists = []
                for cb in pair:
                    k.cap = []
                    attn_cb(cb)
                    lists.append(k.cap)
                    k.cap = None
                k.merge(lists, store_delay=0)
            CH[0] = 0

        print("SBUF pass A bytes", alloc["off"], "base", alloc["base"])
        if cfg.upto == "W":
            k.dma(ys[0:128, :], xs[0:128, :])
            with nc.Block() as block:
                S.emit(sems, dsems, block)
            return nc
        for i in range(min(NM, NT)):
            stage_x(i)
        if cfg.upto == "X":
            k.dma(ys[0:128, :], xs[0:128, :])
            with nc.Block() as block:
                S.emit(sems, dsems, block)
            return nc
        for m in range(NMT):
            halo_left(m)
            if m + 1 < NMT:
                for j in range(NM):
                    stage_x((m + 1) * NM + j)
            halo_right(m)
            passA_macro(m)

        if cfg.upto == "A":
            with nc.Block() as block:
                S.emit(sems, dsems, block)
            return nc

        W0a, W0b = psW[0][:, 0:512], psW[0][:, 512:1024]
        W1a, W1b = psW[1][:, 0:512], psW[1][:, 512:1024]

        def rwkv_pass(d):
            pass_reset()
            ldAR = sb("ldAR", [128, 2, 4, NM, 256], BF16)
            ldBT = sb("ldBT", [128, 2, 4, NM, 128], BF16)
            ldKT = sb("ldKT", [128, 2, 4, NM, 128], BF16)
            ldBH = sb("ldBH", [128, 2, 4, NM, 128], BF16)
            ldKH = sb("ldKH", [128, 2, 4, NM, 128], BF16)
            ldV = sb("ldV", [128, 2, 4, NM, 128], BF16)
            ldWC = sb("ldWC", [128, 2, 4, NM], F32)
            MAb = sb("MA", [128, 2, 8, 256], BF16)
            AKb = sb("AK", [128, 2, 8, 256], BF16)
            N0b = sb("N0", [128, 2, 8, 128], BF16)
            Mn = sb("Mn", [128, 2, 8, 128], BF16)
            Nn = sb("Nn", [128, 2, 8, 128], BF16)
            Ttb = sb("Ttb", [128, 2, 8, 128], BF16)
            G1s = sb("G1s", [128, 512], F32)
            G12 = sb("G12", [128, 512], BF16)
            Ubf = sb("Ubf", [128, 512], BF16)
            Hs = sb("Hs", [128, 256], F32)
            Hblk = sb("Hblk", [128, 4, 128], BF16)
            yo = sb("yo", [128, 2, 512], F32)
            mk2 = sb("mk2", [128, 256], F32)
            mkN = sb("mkN", [128, 128], F32)
            k.cp("pool", mk2[:, 0:128], cst[:, 1 if d == 0 else 2, :])
            k.cp("pool", mk2[:, 128:256], cst[:, 3 if d == 0 else 4, :])
            k.cp("pool", mkN[:], cst[:, 2 if d == 0 else 1, :])
            k.memset("pool", Hs[:], 0.0)
            k.memset("pool", Hblk[:], 0.0)
            import os as _os
            for _w in range(int(_os.environ.get("WARM", "0"))):
                k.mm(W0a, identb, MAb[:, 0, 0:2, :].rearrange("p a b -> p (a b)"), start=True, stop=True)
            if d == 1:
                yf = sb("yf", [128, 2, 512], F32)
                ysum = sb("ysum", [128, 512], F32)
                lnt = sb("lnt", [128, 512], F32)
                lnm = sb("lnm", [128, 16], F32)
                ynT = sb("ynT", [128, 512], F32)
                ldBON = sb("ldBON", [128, 2, 4, 128], F32)
                ldGR = sb("ldGR", [128, 2, 4, 128], F32)
                oMR = sb("oMR", [128, 2, 4, 128], BF16)

            def load(m, sl):
                k.dma(ldAR[:, sl].rearrange("p c t x -> p c (t x)"), S_AR[d, m].rearrange("c p x -> p c x"))
                k.dma(ldBT[:, sl].rearrange("p c t x -> p c (t x)"), S_BT[d, m].rearrange("c p x -> p c x"))
                k.dma(ldKT[:, sl].rearrange("p c t x -> p c (t x)"), S_KT[d, m].rearrange("c p x -> p c x"))
                k.dma(ldBH[:, sl].rearrange("p c t x -> p c (t x)"), S_BH[d, m].rearrange("c p x -> p c x"))
                k.dma(ldKH[:, sl].rearrange("p c t x -> p c (t x)"), S_KH[d, m].rearrange("c p x -> p c x"))
                k.dma(ldV[:, sl].rearrange("p c t x -> p c (t x)"), S_V[m].rearrange("c p x -> p c x"))
                k.dma(ldWC[:, sl], S_WC[d, m].rearrange("c p x -> p c x"))

            def tile(m, sl, j, cnt):
                i = m * NM + j
                AR = ldAR[:, sl]
                BT = ldBT[:, sl]
                KT = ldKT[:, sl]
                BH = ldBH[:, sl]
                KH = ldKH[:, sl]
                V = ldV[:, sl]
                MA = MAb[:, cnt % 2]
                AK = AKb[:, cnt % 2]
                N0 = N0b[:, cnt % 2]
                bnd = None
                if d == 0 and i % SEG == 0 and i > 0:
                    bnd = i // SEG
                if d == 1 and (i + 1) % SEG == 0 and i < NT - 1:
                    bnd = (i + 1) // SEG
                if bnd is not None:
                    k.ts("pool", Hs[:], Hs[:], flg[:, bnd:bnd + 1], ALU.mult)
                    k.ts("pool", Hblk[:], Hblk[:], flg[:, bnd:bnd + 1], ALU.mult)
                m2 = mk2[:].unsqueeze(1).to_broadcast([128, 2, 256])
                mN = mkN[:].unsqueeze(1).to_broadcast([128, 2, 128])
                sbanks = {0: [psB[0], psB[1], psB[2][:, 0:256]], 1: [psB[3], W1b, psB[2][:, 256:512]]}
                for cp_ in range(2):
                    for hh in range(2):
                        po = 64 * hh
                        bs = sbanks[hh]
                        for ci in range(2):
                            cb = 2 * cp_ + ci
                            k.mm(bs[0][:, ci * 256:(ci + 1) * 256], BT[po:po + 64, cb, j, :], AR[po:po + 64, cb, j, :])
                            k.mm(bs[1][:, ci * 256:(ci + 1) * 256], KT[po:po + 64, cb, j, :], AR[po:po + 64, cb, j, :])
                            k.mm(bs[2][:, ci * 128:(ci + 1) * 128], AR[po:po + 64, cb, j, 0:128], BT[po:po + 64, cb, j, :])
                        h0 = 4 * cp_ + hh
                        k.tt("dve", MA[:, h0:h0 + 3:2, :], bs[0].rearrange("p (a b) -> p a b", a=2), m2, ALU.mult)
                        k.tt("dve", AK[:, h0:h0 + 3:2, :], bs[1].rearrange("p (a b) -> p a b", a=2), m2, ALU.mult)
                        k.tt("dve", N0[:, h0:h0 + 3:2, :], bs[2].rearrange("p (a b) -> p a b", a=2), mN, ALU.mult)
                k.tt("pool", Ttb[:, 0], MA[:, :, 0:128], cst[:, 0, :].unsqueeze(1).to_broadcast([128, 8, 128]), ALU.add)
                Ttbank = [psB[2], psB[3]]
                Mbank = [W1a, psB[0]]
                Nbank = [W1b, psB[1]]
                for half in range(2):
                    for q in range(4):
                        h = 4 * half + q
                        k.mm(Ttbank[half][:, q * 128:(q + 1) * 128], identb, Ttb[:, 0, h, :],
                             start=(q == 0), stop=False, skip=True)
                Mcur = [MA[:, h, 0:128] for h in range(8)]
                Ncur = [N0[:, h, :] for h in range(8)]
                for lev in range(7):
                    for half in range(2):
                        if lev <= 5:
                            for q in range(4):
                                h = 4 * half + q
                                k.mm(Nbank[half][:, q * 128:(q + 1) * 128], Mcur[h], Ncur[h])
                            if lev <= 4:
                                for q in range(4):
                                    h = 4 * half + q
                                    k.mm(Mbank[half][:, q * 128:(q + 1) * 128], Ncur[h], Mcur[h])
                        if lev >= 1:
                            for q in range(4):
                                h = 4 * half + q
                                k.mm(Ttbank[half][:, q * 128:(q + 1) * 128], Ncur[h], Ttb[:, (lev - 1) % 2, h, :],
                                     start=False, stop=(lev == 6), skip=True)
                            k.cp("act" if half == 0 else "dve", Ttb[:, lev % 2, 4 * half:4 * half + 4, :],
                                 Ttbank[half].rearrange("p (a b) -> p a b", a=4))
                        if lev <= 5:
                            k.cp("act" if (half == 0 and lev % 2 == 0) else "dve", Nn[:, lev % 2, 4 * half:4 * half + 4, :],
                                 Nbank[half].rearrange("p (a b) -> p a b", a=4))
                            if lev <= 4:
                                k.cp("act", Mn[:, lev % 2, 4 * half:4 * half + 4, :],
                                     Mbank[half].rearrange("p (a b) -> p a b", a=4))
                    if lev <= 5:
                        Ncur = [Nn[:, lev % 2, h, :] for h in range(8)]
                        if lev <= 4:
                            Mcur = [Mn[:, lev % 2, h, :] for h in range(8)]
                TT = Ttb[:, 0]
                for h in range(8):
                    cb, po = h // 2, 64 * (h % 2)
                    k.mm(W0a[:, h * 64:(h + 1) * 64], AK[:, h, 0:128], V[:, cb, j, po:po + 64])
                k.cp("act", G1s[:], W0a)
                for cb in range(4):
                    k.mm(W0b[:, cb * 128:(cb + 1) * 128], AR[:, cb, j, 0:128], Hblk[:, cb, :])
                k.tt("dve", G12[:], W0b, G1s[:], ALU.add)
                for h in range(8):
                    k.mm(W0a[:, h * 64:(h + 1) * 64], TT[:, h, :], G12[:, h * 64:(h + 1) * 64])
                k.cp("act", Ubf[:], W0a)
                for cb in range(4):
                    h0, h1 = 2 * cb, 2 * cb + 1
                    o0 = W0b[:, h0 * 64:(h0 + 1) * 64]
                    o1 = W0b[:, h1 * 64:(h1 + 1) * 64]
                    k.mm(o0, AK[:, h0, 128:256], V[:, cb, j, 0:64], start=True, stop=False, skip=True)
                    k.mm(o1, AK[:, h1, 128:256], V[:, cb, j, 64:128], start=False, stop=False, skip=True)
                    k.mm(W0b[:, cb * 128:(cb + 1) * 128], AR[:, cb, j, 128:256], Hblk[:, cb, :], start=False, stop=False, skip=True)
                    k.mm(o0, MA[:, h0, 128:256], Ubf[:, h0 * 64:(h0 + 1) * 64], start=False, stop=False, skip=True)
                    k.mm(o1, MA[:, h1, 128:256], Ubf[:, h1 * 64:(h1 + 1) * 64], start=False, stop=True, skip=True)
                for h in range(8):
                    cb, po = h // 2, 64 * (h % 2)
                    o = W1a[po:po + 64, cb * 64:(cb + 1) * 64]
                    k.mm(o, KH[:, cb, j, po:po + 64], V[:, cb, j, po:po + 64], start=True, stop=False)
                    k.mm(o, BH[:, cb, j, po:po + 64], Ubf[:, h * 64:(h + 1) * 64], start=False, stop=True)
                ysl = cnt % 2
                if d == 0:
                    k.cp("act", yo[:, ysl, :], W0b)
                    k.dma(S_YF[i], yo[:, ysl, :])
                else:
                    k.dma(yf[:, ysl, :], S_YF[i])
                    k.tt("dve", ysum[:], W0b, yf[:, ysl, :], ALU.add)
                for cb in range(4):
                    k.stt(Hs[:, cb * 64:(cb + 1) * 64], Hs[:, cb * 64:(cb + 1) * 64], ldWC[:, sl, cb, j:j + 1],
                          W1a[:, cb * 64:(cb + 1) * 64], ALU.mult, ALU.add)
                Hs3 = Hs[:].rearrange("p (a b) -> p a b", a=4)
                k.cp("pool", Hblk[0:64, :, 0:64], Hs3[0:64])
                k.cp("pool", Hblk[64:128, :, 64:128], Hs3[64:128])
                if d == 1:
                    y3 = ysum[:].rearrange("p (a b) -> p a b", a=8)
                    k.red(lnm[:, 0:8], y3)
                    k.ts("dve", lnm[:, 0:8], lnm[:, 0:8], 1.0 / 64, ALU.mult)
                    k.tt("dve", lnt[:].rearrange("p (a b) -> p a b", a=8), y3,
                         lnm[:, 0:8].unsqueeze(2).to_broadcast([128, 8, 64]), ALU.subtract)
                    k.tt("pool", ysum[:], lnt[:], lnt[:], ALU.mult)
                    k.red(lnm[:, 8:16], y3)
                    k.ts("dve", lnm[:, 8:16], lnm[:, 8:16], 1.0 / 64, ALU.mult, LNX_EPS, ALU.add)
                    k.tt("pool", lnm[:, 8:16], lnm[:, 8:16], cm05[:, 0:1].to_broadcast([128, 8]), ALU.pow)
                    k.tt("dve", lnt[:].rearrange("p (a b) -> p a b", a=8), lnt[:].rearrange("p (a b) -> p a b", a=8),
                         lnm[:, 8:16].unsqueeze(2).to_broadcast([128, 8, 64]), ALU.mult)
                    for cb in range(4):
                        k.tr(W1b[:, cb * 128:(cb + 1) * 128], lnt[:, cb * 128:(cb + 1) * 128], identf)
                    k.dma(ldBON[:, ysl], S_BON[m].rearrange("c p (t x) -> p c t x", t=NM)[:, :, j, :])
                    k.dma(ldGR[:, ysl], S_GR[m].rearrange("c p (t x) -> p c t x", t=NM)[:, :, j, :])
                    for cb in range(4):
                        k.act(ynT[:, cb * 128:(cb + 1) * 128], W1b[:, cb * 128:(cb + 1) * 128], AF.Identity,
                              bias=pc[:, PC_LB + cb:PC_LB + cb + 1], scale=pc[:, PC_LG + cb:PC_LG + cb + 1])
                    yn3 = ynT[:].rearrange("p (a b) -> p a b", a=4)
                    k.tt("pool", yn3, yn3, ldBON[:, ysl], ALU.add)
                    k.tt("pool", oMR[:, ysl], yn3, ldGR[:, ysl], ALU.mult)
                    k.dma(S_MR[i].rearrange("c p x -> p c x"), oMR[:, ysl])

            order = list(range(NMT)) if d == 0 else list(range(NMT - 1, -1, -1))
            load(order[0], 0)
            cnt = 0
            for idx, m in enumerate(order):
                sl = idx % 2
                if idx + 1 < len(order):
                    load(order[idx + 1], 1 - sl)
                js = range(NM) if d == 0 else range(NM - 1, -1, -1)
                for j in js:
                    tile(m, sl, j, cnt)
                    cnt += 1

        rwkv_pass(0)
        if cfg.upto == "B":
            with nc.Block() as block:
                S.emit(sems, dsems, block)
            return nc
        rwkv_pass(1)
        if cfg.upto == "C":
            with nc.Block() as block:
                S.emit(sems, dsems, block)
            return nc

        pass_reset()
        RS = 8
        EG = sb("EG", [128, 8, 15, 64], BF16)
        vts = sb("vts", [128, NT * 14], F32)
        Wo = sb("Wo", [128, 8, D], BF16)
        Kring = sb("Kring", [128, RS, 4, 128], BF16)
        Vring = sb("Vring", [128, RS, 4, 128], BF16)
        Qt = sb("Qt", [128, 2, 4, 128], BF16)
        ebuf = sb("ebuf", [128, 2, 8, 128], F32)
        pT = sb("pT", [128, 2, 7, 8, 128], BF16)
        gaT = sb("gaT", [128, 2, 4, 128], F32)
        mrT = sb("mrT", [128, 2, 4, 128], BF16)
        maT = sb("maT", [128, 2, 4, 128], BF16)
        dsb2 = sb("dsb", [128, 2, 512], F32)
        otmp2 = sb("otmp", [128, 2, 512], F32)
        xr = sb("xr", [128, 2, D], F32)
        yout = sb("yout", [128, 2, D], F32)
        onesb = sb("onesb", [128, 64], BF16)
        cm1 = sb("cm1", [128, 1], F32)
        k.memset("pool", onesb[:], 1.0)
        k.memset("pool", cm1[:], -1.0)
        k.dma(vts[:], vtab)
        wtmp = sb("wtmp", [128, 2, D], F32)
        for ek in range(8):
            k.dma(wtmp[:, ek % 2, :], w_out[ek * 128:(ek + 1) * 128, :])
            k.cp("pool" if ek % 2 == 0 else "dve", Wo[:, ek, :], wtmp[:, ek % 2, :])
        ohf = sb("ohf", [128, 4096], F32)
        ohb = sb("ohb", [128, 4096], BF16)
        rpf = sb("rpf", [128, 120], F32)
        rpe = sb("rpe", [128, 120], BF16)
        k.dma(ohf[0:31, :], ohot)
        k.dma(rpf[0:31, :], rpbT)
        k.cp("dve", ohb[0:31, :], ohf[0:31, :])
        k.act(rpe[0:31, :], rpf[0:31, :], AF.Exp)
        k.memset("pool", EG[64:128, :, 0:1, :], 0.0)
        for g in range(16):
            bank = psB[g % 2]
            for c in range(4):
                cq = g * 4 + c
                k.mm(bank[0:64, c * 120:(c + 1) * 120], ohb[0:31, cq * 64:(cq + 1) * 64], rpe[0:31, :])
                k.mm(bank[64:128, c * 120:(c + 1) * 120], ohb[0:31, cq * 64:(cq + 1) * 64], rpe[0:31, :])
            k.cp("act", EG[0:64, :, :, g * 4:(g + 1) * 4],
                 bank[0:64, 0:480].rearrange("p (c h u) -> p h u c", c=4, h=8))
            k.cp("dve", EG[64:128, :, 1:15, g * 4:(g + 1) * 4],
                 bank[64:128, 0:480].rearrange("p (c h u) -> p h u c", c=4, h=8)[:, :, 0:14, :])

        loaded = [-1]

        def ensure_keys(upto):
            while loaded[0] < min(upto, NT - 1):
                t = loaded[0] + 1
                mk, jk = t // NM, t % NM
                k.dma(Kring[:, t % RS], S_K[mk].rearrange("c p (t x) -> p c t x", t=NM)[:, :, jk, :])
                k.dma(Vring[:, t % RS], S_VA[mk].rearrange("c p (t x) -> p c t x", t=NM)[:, :, jk, :])
                loaded[0] = t

        zc = [0]
        for i in range(NT):
            m, j = i // NM, i % NM
            sl = i % 2
            pos = i % SEG
            offs = [-2, -1, 0, 1, 2]
            if pos == 0:
                offs.append(3)
            if pos == SEG - 1:
                offs.insert(0, -3)
            offs = [o for o in offs if 0 <= i + o < NT]
            ensure_keys(i + 3)
            k.dma(Qt[:, sl], S_Q[m].rearrange("c p (t x) -> p c t x", t=NM)[:, :, j, :])
            k.dma(gaT[:, sl], S_GA[m].rearrange("c p (t x) -> p c t x", t=NM)[:, :, j, :])
            k.dma(mrT[:, sl], S_MR[i].rearrange("c p x -> p c x"))
            k.dma(xr[:, sl, :], xs[i * 128:(i + 1) * 128, :])
            for ki, o in enumerate(offs):
                t = i + o
                ks = t % RS
                pz = psW[zc[0] % 2]
                es_ = zc[0] % 2
                zc[0] += 1
                for hh in range(2):
                    po = 64 * hh
                    for cb in range(4):
                        k.mm(pz[:, (hh * 4 + cb) * 128:(hh * 4 + cb + 1) * 128], Kring[po:po + 64, ks, cb, :],
                             Qt[po:po + 64, sl, cb, :])
                k.act(ebuf[:, es_].rearrange("p a b -> p (a b)"), pz[:], AF.Exp, scale=0.125)
                u0 = 7 - 2 * o
                for qh in range(2):
                    col = (i * 7 + (o + 3)) * 2 + qh
                    k.stt(pT[:, sl, ki, :, qh * 64:(qh + 1) * 64], ebuf[:, es_, :, qh * 64:(qh + 1) * 64],
                          vts[:, col:col + 1], EG[:, :, u0 + qh, :], ALU.mult, ALU.mult)
            nk = len(offs)
            bO = psB[2 * (i % 2)]
            bD = psB[2 * (i % 2) + 1]
            dsb = dsb2[:, i % 2]
            otmp = otmp2[:, i % 2]
            for hh in range(2):
                po = 64 * hh
                for cb in range(4):
                    idx = hh * 4 + cb
                    oo = bO[po:po + 64, cb * 128:(cb + 1) * 128]
                    dd = bD[po:po + 64, cb * 128:(cb + 1) * 128]
                    for ki, o in enumerate(offs):
                        ks = (i + o) % RS
                        k.mm(oo, Vring[:, ks, cb, po:po + 64], pT[:, sl, ki, idx, :], start=(ki == 0), stop=(ki == nk - 1))
                    for ki, o in enumerate(offs):
                        k.mm(dd, onesb[:, :], pT[:, sl, ki, idx, :], start=(ki == 0), stop=(ki == nk - 1))
            k.recip(dsb, bD[:])
            k.tt("dve", otmp, bO[:], dsb, ALU.mult)
            k.tt("pool", maT[:, sl].rearrange("p a b -> p (a b)"), otmp, gaT[:, sl].rearrange("p a b -> p (a b)"), ALU.mult)
            for half in range(2):
                ob = (bO if half == 0 else bD)[:]
                for ek in range(8):
                    lh = mrT[:, sl, ek, :] if ek < 4 else maT[:, sl, ek - 4, :]
                    k.mm(ob, lh, Wo[:, ek, half * 512:(half + 1) * 512], start=(ek == 0), stop=(ek == 7))
                k.tt("dve", yout[:, sl, half * 512:(half + 1) * 512], ob, xr[:, sl, half * 512:(half + 1) * 512], ALU.add)
            k.dma(ys[i * 128:(i + 1) * 128, :], yout[:, sl, :])

        with nc.Block() as block:
            S.emit(sems, dsems, block)
    return nc


def _cols(v, nblk):
    return np.ascontiguousarray(np.asarray(v, np.float32).reshape(nblk, 128).T)


def make_consts():
    c = np.zeros((128, 6, 128), np.float32)
    i = np.arange(128)
    c[:, 0, :] = np.eye(128, dtype=np.float32)
    c[:, 1, :] = (i[:, None] < i[None, :])
    c[:, 2, :] = (i[:, None] > i[None, :])
    c[:, 3, :] = (i[:, None] <= i[None, :])
    c[:, 4, :] = (i[:, None] >= i[None, :])
    c[:, 5, :] = ((i[:, None] // 64) == (i[None, :] // 64))
    return c


def pack_params(p):
    pcols = np.zeros((128, NPC), np.float32)
    pcols[:, PC_MU:PC_MU + 17] = _cols(p["shift_mu"], 17)
    for d in range(2):
        pcols[:, PC_W0 + 4 * d:PC_W0 + 4 * d + 4] = _cols(p["w0"][d], 4)
        pcols[:, PC_A0 + 4 * d:PC_A0 + 4 * d + 4] = _cols(p["a0"][d], 4)
    pcols[:, PC_KK:PC_KK + 4] = _cols(p["k_k"], 4)
    pcols[:, PC_KA:PC_KA + 4] = _cols(p["k_a"], 4)
    pcols[:, PC_RK:PC_RK + 4] = _cols(p["r_k"].reshape(-1), 4)
    pcols[:, PC_LG:PC_LG + 4] = _cols(p["lnx_g"], 4)
    pcols[:, PC_LB:PC_LB + 4] = _cols(p["lnx_b"], 4)
    pcols[:, PC_QG] = np.tile(p["q_norm_g"], 2)
    pcols[:, PC_KG] = np.tile(p["k_norm_g"], 2)
    pcols[:, PC_NG:PC_NG + 8] = _cols(p["norm_g"], 8)
    wup = np.zeros((128, 2, 512), np.float32)
    wup[0:64] = np.transpose(p["w_up"], (1, 0, 2))
    wup[64:128] = np.transpose(p["a_up"], (1, 0, 2))
    cq = np.arange(64)
    ck = np.arange(64)
    cs = np.clip(cq - 8, 0, 48)
    inw = (ck[None, :] >= cs[:, None]) & (ck[None, :] < cs[:, None] + 16)
    dc = ck[None, :] - cq[:, None] + 15
    oh = np.zeros((31, 64, 64), np.float32)
    for q_ in range(64):
        for k_ in range(64):
            if inw[q_, k_]:
                oh[dc[q_, k_], q_, k_] = 1.0
    rp = np.asarray(p["rpb"], np.float32)
    rp = rp[:, ::-1, :]
    rp = rp.reshape(4, 2, 15, 31).transpose(3, 1, 0, 2)
    return {
        "ohot": np.ascontiguousarray(oh.reshape(31, 4096)),
        "rpbT": np.ascontiguousarray(rp.reshape(31, 120)),
        "w_in": np.ascontiguousarray(p["w_in"], np.float32),
        "w_out": np.ascontiguousarray(p["w_out"], np.float32),
        "pcols": pcols,
        "wup": wup,
        "consts": make_consts(),
    }


def make_core_meta(cfg, slots):
    NT, SEG, NSEG = cfg.NT, cfg.SEG, cfg.NSEG
    flags = np.zeros((128, NSEG + 1), np.float32)
    for s_ in range(1, NSEG):
        a, b = slots[s_ - 1], slots[s_]
        if a is not None and b is not None and a[0] == b[0] and b[1] == a[1] + 1:
            flags[:, s_] = 1.0
    vt = np.zeros((128, NT, 7, 2), np.float32)
    for i in range(NT):
        sl = slots[i // SEG]
        if sl is None:
            vt[:, i, 3, :] = 1.0
            continue
        sid, sidx, nseg = sl
        R = nseg * SEG * 2
        for qh in range(2):
            r = (sidx * SEG + (i % SEG)) * 2 + qh
            R0 = min(max(r - 4, 0), R - 8)
            for o in range(-3, 4):
                t = i + o
                if t < 0 or t >= NT:
                    continue
                sk = slots[t // SEG]
                if sk is None or sk[0] != sid:
                    continue
                if sk[1] - sidx != (t // SEG) - (i // SEG):
                    continue
                for kh in range(2):
                    rho = (sk[1] * SEG + (t % SEG)) * 2 + kh
                    if R0 <= rho < R0 + 8:
                        vt[kh * 64:(kh + 1) * 64, i, o + 3, qh] = 1.0
    return flags, np.ascontiguousarray(vt.reshape(128, NT * 14))


PARAM_NAMES = ["norm_g", "w_in", "shift_mu", "w0", "w_up", "a0", "a_up", "k_k", "k_a", "r_k",
               "lnx_g", "lnx_b", "q_norm_g", "k_norm_g", "rpb", "w_out"]


def run_layer(cfg, seqs, p, n_cores):
    NT, SEG, NSEG = cfg.NT, cfg.SEG, cfg.NSEG
    segtok = SEG * 128
    order = sorted(range(len(seqs)), key=lambda i: -seqs[i].shape[0])
    cores = [[] for _ in range(n_cores)]
    used = [0] * n_cores
    for si in order:
        ns = seqs[si].shape[0] // segtok
        c = min(range(n_cores), key=lambda c_: (used[c_] + ns > NSEG, used[c_]))
        assert used[c] + ns <= NSEG
        for g in range(ns):
            cores[c].append((si, g, ns))
        used[c] += ns
    shared = pack_params(p)
    in_maps = []
    for c in range(n_cores):
        slots = cores[c] + [None] * (NSEG - len(cores[c]))
        x = np.zeros((NT * 128, D), np.float32)
        for s_, sl in enumerate(slots):
            if sl is not None:
                x[s_ * segtok:(s_ + 1) * segtok] = seqs[sl[0]][sl[1] * segtok:(sl[1] + 1) * segtok]
        flags, vt = make_core_meta(cfg, slots)
        im = dict(shared)
        im["xs"] = x
        im["flags"] = flags
        im["vtab"] = vt
        in_maps.append(im)
    nc = build_program(cfg)
    res = run_bass_kernel_spmd(nc, in_maps, core_ids=list(range(n_cores)))
    outs = [np.zeros_like(s_) for s_ in seqs]
    for c in range(n_cores):
        y = np.asarray(res.results[c]["ys"])
        for s_, sl in enumerate(cores[c]):
            outs[sl[0]][sl[1] * segtok:(sl[1] + 1) * segtok] = y[s_ * segtok:(s_ + 1) * segtok]
    return outs, res


def kernel(**inputs):
    cfg = Cfg(128, 32)
    p = {n_: np.asarray(inputs[n_], np.float32)[0] for n_ in PARAM_NAMES}
    xp = np.asarray(inputs["x_prompt"], np.float32)
    xsm = np.asarray(inputs["x_sample"], np.float32)
    seqs = [xp[b] for b in range(xp.shape[0])] + [xsm[b] for b in range(xsm.shape[0])]
    outs, _ = run_layer(cfg, seqs, p, 8)
    yp = np.stack(outs[:xp.shape[0]], 0)
    ysm = np.stack(outs[xp.shape[0]:], 0)
    return (yp, ysm)
```
